# Optimizing a Trainium2 kernel written in Bass

```python
import jax, jax.numpy as jnp
from jax import lax
import numpy as np

D_MODEL = 2048
BATCH = 2
SEQ = 8192
DEPTH = 1

NSA_HEADS = 16
NSA_KV_GROUPS = 4
NSA_HPG = NSA_HEADS // NSA_KV_GROUPS
NSA_HEAD_DIM = 64
CMP_BLOCK = 32
CMP_STRIDE = 16
CMP_HIDDEN = 256
SLC_BLOCK = 64
N_SELECT = 16
WINDOW = 512
Q_BLOCK = 128
NSA_WIDTH = NSA_HEADS * NSA_HEAD_DIM
NSA_KV_WIDTH = NSA_KV_GROUPS * NSA_HEAD_DIM
GLA_HEADS = 4
GLA_KEY_DIM = 128
GLA_VAL_DIM = 256
GLA_GATE_RANK = 16
GLA_TAU = 16.0
GLA_CHUNK = 64
GLA_KEY_WIDTH = GLA_HEADS * GLA_KEY_DIM
GLA_WIDTH = GLA_HEADS * GLA_VAL_DIM
IN_WIDTHS = (NSA_WIDTH,
             NSA_KV_WIDTH, NSA_KV_WIDTH,
             NSA_KV_WIDTH, NSA_KV_WIDTH,
             NSA_KV_WIDTH, NSA_KV_WIDTH,
             3 * NSA_HEADS,
             NSA_WIDTH,
             GLA_KEY_WIDTH, GLA_KEY_WIDTH, GLA_WIDTH,
             GLA_GATE_RANK,
             GLA_WIDTH,
             D_MODEL, D_MODEL)
IN_WIDTH = sum(IN_WIDTHS)
EPS = 1e-6
NEG = -1e30

kernel_name = "hybrid_nsa_gla_gated_merge"


def rmsnorm(x, g):
    xf = x.astype(jnp.float32)
    y = xf * lax.rsqrt(jnp.mean(xf * xf, axis=-1, keepdims=True) + EPS)
    return (y * g.astype(jnp.float32)).astype(x.dtype)


def split_cols(a, widths):
    offs = np.cumsum(np.array(widths))[:-1].tolist()
    return jnp.split(a, offs, axis=-1)


def alibi_slopes(n):
    return 2.0 ** (-8.0 * jnp.arange(1, n + 1, dtype=jnp.float32) / n)


def masked_softmax(s, valid):
    p = jax.nn.softmax(jnp.where(valid, s, NEG), axis=-1)
    return jnp.where(jnp.any(valid, axis=-1, keepdims=True), p, 0.0)


def compress(kv, pos_emb, w1, w2):
    B, G, T, DH = kv.shape
    n_sub = CMP_BLOCK // CMP_STRIDE
    n_chunks = T // CMP_STRIDE
    nc = n_chunks - n_sub + 1
    chunks = kv.reshape(B, G, n_chunks, CMP_STRIDE, DH)
    blocks = jnp.concatenate([chunks[:, :, j:j + nc] for j in range(n_sub)], axis=3)
    blocks = blocks + pos_emb
    flat = blocks.reshape(B, G, nc, CMP_BLOCK * DH)
    return jax.nn.silu(flat @ w1) @ w2


def window_bands(a):
    B, G, T, DH = a.shape
    nq = T // Q_BLOCK
    nw = WINDOW // Q_BLOCK
    ap = jnp.pad(a, ((0, 0), (0, 0), (WINDOW, 0), (0, 0))).reshape(B, G, nq + nw, Q_BLOCK, DH)
    band = jnp.stack([ap[:, :, j:j + nq] for j in range(nw + 1)], axis=3)
    return jnp.moveaxis(band.reshape(B, G, nq, (nw + 1) * Q_BLOCK, DH), 2, 0)


def to_query_blocks(a):
    B, G, HPG, T, X = a.shape
    return jnp.moveaxis(a.reshape(B, G, HPG, T // Q_BLOCK, Q_BLOCK, X), 3, 0)


def nsa_attention(q, k_cmp, v_cmp, k_slc, v_slc, k_win, v_win, gates,
                  pos_k, pos_v, w1_k, w2_k, w1_v, w2_v):
    B, G, HPG, T, DH = q.shape
    nq = T // Q_BLOCK
    scale = DH ** -0.5
    slopes = alibi_slopes(NSA_HEADS).reshape(G, HPG)[None, :, :, None, None]
    Kc = compress(k_cmp, pos_k, w1_k, w2_k)
    Vc = compress(v_cmp, pos_v, w1_v, w2_v)
    nc = Kc.shape[2]
    cmp_end = jnp.arange(nc) * CMP_STRIDE + CMP_BLOCK - 1
    nb = T // SLC_BLOCK
    n_sel = min(N_SELECT, nb)
    Ks = k_slc.reshape(B, G, nb, SLC_BLOCK, DH)
    Vs = v_slc.reshape(B, G, nb, SLC_BLOCK, DH)
    ratio = SLC_BLOCK // CMP_STRIDE
    n_sub = CMP_BLOCK // CMP_STRIDE
    gather = jax.vmap(jax.vmap(lambda blk, ix: blk[ix]))
    blk_ids = jnp.arange(nb)

    def block_fn(args):
        i, qi, gi, kwi, vwi = args
        t = i * Q_BLOCK + jnp.arange(Q_BLOCK)
        tf = t.astype(jnp.float32)
        dist = tf[:, None] - cmp_end[None, :].astype(jnp.float32)
        s = jnp.einsum('bghqd,bgnd->bghqn', qi, Kc).astype(jnp.float32) * scale - slopes * dist
        p_cmp = masked_softmax(s, cmp_end[None, :] <= t[:, None])
        o_cmp = jnp.einsum('bghqn,bgnd->bghqd', p_cmp.astype(Vc.dtype), Vc)
        P = jnp.sum(p_cmp, axis=2)
        Ppad = jnp.pad(P, ((0, 0), (0, 0), (0, 0), (n_sub - 1, ratio * nb - nc)))
        imp = jnp.zeros(P.shape[:3] + (nb,), jnp.float32)
        for m in range(ratio):
            for n in range(n_sub):
                st = m - n + n_sub - 1
                imp = imp + lax.slice_in_dim(Ppad, st, st + ratio * (nb - 1) + 1, stride=ratio, axis=3)
        cur = t // SLC_BLOCK
        forced = (blk_ids[None, :] == 0) | (blk_ids[None, :] == cur[:, None]) | (blk_ids[None, :] == cur[:, None] - 1)
        blk_valid = blk_ids[None, :] * SLC_BLOCK <= t[:, None]
        score = jnp.where(forced, jnp.inf, jnp.where(blk_valid, imp, -jnp.inf))
        _, idx = lax.top_k(score, n_sel)
        ksel = gather(Ks, idx).reshape(B, G, Q_BLOCK, n_sel * SLC_BLOCK, DH)
        vsel = gather(Vs, idx).reshape(B, G, Q_BLOCK, n_sel * SLC_BLOCK, DH)
        spos = (idx[..., None] * SLC_BLOCK + jnp.arange(SLC_BLOCK)).reshape(B, G, Q_BLOCK, n_sel * SLC_BLOCK)
        dist = (tf[:, None] - spos.astype(jnp.float32))[:, :, None]
        s = jnp.einsum('bghqd,bgqkd->bghqk', qi, ksel).astype(jnp.float32) * scale - slopes * dist
        p = masked_softmax(s, (spos <= t[:, None])[:, :, None])
        o_slc = jnp.einsum('bghqk,bgqkd->bghqd', p.astype(vsel.dtype), vsel)
        wpos = i * Q_BLOCK - WINDOW + jnp.arange(Q_BLOCK + WINDOW)
        diff = t[:, None] - wpos[None, :]
        valid = (diff >= 0) & (diff < WINDOW) & (wpos[None, :] >= 0)
        s = jnp.einsum('bghqd,bgkd->bghqk', qi, kwi).astype(jnp.float32) * scale - slopes * diff.astype(jnp.float32)
        p = masked_softmax(s, valid)
        o_win = jnp.einsum('bghqk,bgkd->bghqd', p.astype(vwi.dtype), vwi)
        return gi[..., 0:1] * o_cmp + gi[..., 1:2] * o_slc + gi[..., 2:3] * o_win

    out = lax.map(block_fn, (jnp.arange(nq), to_query_blocks(q), to_query_blocks(gates),
                             window_bands(k_win), window_bands(v_win)))
    return out.transpose(1, 0, 4, 2, 3, 5).reshape(B, T, G * HPG * DH)


def gla_attention(q, k, v, log_a):
    B, T, H, DK = q.shape
    DV = v.shape[-1]
    n = T // GLA_CHUNK
    causal = jnp.tril(jnp.ones((GLA_CHUNK, GLA_CHUNK), dtype=bool))[..., None]

    def chunks(a):
        return a.reshape(B, n, GLA_CHUNK, H, a.shape[-1]).transpose(1, 0, 3, 2, 4)

    def step(S, inp):
        qc, kc, vc, gc = inp
        b = jnp.cumsum(gc, axis=2)
        rel = jnp.where(causal, b[:, :, :, None, :] - b[:, :, None, :, :], -jnp.inf)
        A = jnp.einsum('bhik,bhjk,bhijk->bhij', qc, kc, jnp.exp(rel))
        o = jnp.einsum('bhij,bhjv->bhiv', A, vc) + jnp.einsum('bhik,bhkv->bhiv', qc * jnp.exp(b), S)
        b_last = b[:, :, -1:, :]
        S = jnp.exp(b_last)[:, :, 0, :, None] * S + jnp.einsum('bhjk,bhjv->bhkv', kc * jnp.exp(b_last - b), vc)
        return S, o

    S0 = jnp.zeros((B, H, DK, DV), jnp.float32)
    _, o = lax.scan(step, S0, (chunks(q * (DK ** -0.5)), chunks(k), chunks(v), chunks(log_a)))
    return o.transpose(1, 0, 3, 2, 4).reshape(B, T, H, DV)


def setup_inputs(seed: int = 0) -> dict:
    key = jax.random.key(seed)
    ks = jax.random.split(key, 22)
    L = DEPTH

    def nrm(k, shape, s):
        return jax.random.normal(k, shape, jnp.float32) * s

    return {
        "x": nrm(ks[0], (BATCH, SEQ, D_MODEL), 1.0),
        "c": nrm(ks[1], (BATCH, D_MODEL), 1.0),
        "w_ada": nrm(ks[2], (L, D_MODEL, 3 * D_MODEL), 0.3 * D_MODEL ** -0.5),
        "b_ada": nrm(ks[3], (L, 3 * D_MODEL), 0.01),
        "norm_gain": 1.0 + nrm(ks[4], (L, D_MODEL), 0.02),
        "w_in": nrm(ks[5], (L, D_MODEL, IN_WIDTH), D_MODEL ** -0.5),
        "b_in": nrm(ks[6], (L, IN_WIDTH), 0.01),
        "cmp_pos_k": nrm(ks[7], (L, CMP_BLOCK, NSA_HEAD_DIM), 0.02),
        "cmp_pos_v": nrm(ks[8], (L, CMP_BLOCK, NSA_HEAD_DIM), 0.02),
        "cmp_w1_k": nrm(ks[9], (L, CMP_BLOCK * NSA_HEAD_DIM, CMP_HIDDEN), (CMP_BLOCK * NSA_HEAD_DIM) ** -0.5),
        "cmp_w2_k": nrm(ks[10], (L, CMP_HIDDEN, NSA_HEAD_DIM), CMP_HIDDEN ** -0.5),
        "cmp_w1_v": nrm(ks[11], (L, CMP_BLOCK * NSA_HEAD_DIM, CMP_HIDDEN), (CMP_BLOCK * NSA_HEAD_DIM) ** -0.5),
        "cmp_w2_v": nrm(ks[12], (L, CMP_HIDDEN, NSA_HEAD_DIM), CMP_HIDDEN ** -0.5),
        "gla_w_alpha": nrm(ks[13], (L, GLA_GATE_RANK, GLA_KEY_WIDTH), GLA_GATE_RANK ** -0.5),
        "gla_b_alpha": nrm(ks[14], (L, GLA_KEY_WIDTH), 0.01),
        "gla_norm_gain": 1.0 + nrm(ks[15], (L, GLA_VAL_DIM), 0.02),
        "w_br_nsa": nrm(ks[16], (L, NSA_WIDTH, D_MODEL), NSA_WIDTH ** -0.5),
        "w_br_gla": nrm(ks[17], (L, GLA_WIDTH, D_MODEL), GLA_WIDTH ** -0.5),
        "w_out": nrm(ks[18], (L, D_MODEL, D_MODEL), D_MODEL ** -0.5),
        "final_norm_gain": 1.0 + nrm(ks[19], (D_MODEL,), 0.02),
    }


def reference(x, c, w_ada, b_ada, norm_gain, w_in, b_in, cmp_pos_k, cmp_pos_v,
              cmp_w1_k, cmp_w2_k, cmp_w1_v, cmp_w2_v, gla_w_alpha, gla_b_alpha,
              gla_norm_gain, w_br_nsa, w_br_gla, w_out, final_norm_gain):
    B, T, D = x.shape
    G, HPG, DH = NSA_KV_GROUPS, NSA_HPG, NSA_HEAD_DIM

    def kv_heads(a):
        return a.reshape(B, T, G, DH).transpose(0, 2, 1, 3)

    for l in range(DEPTH):
        shift, scale, gate = jnp.split(c @ w_ada[l] + b_ada[l], 3, axis=-1)
        h = rmsnorm(x, norm_gain[l]) * (1.0 + scale[:, None, :]) + shift[:, None, :]
        (nsa_q, ck, cv, sk, sv, wk, wv, nsa_g, nsa_z,
         gq, gk, gv, ga, gla_z, mg_nsa, mg_gla) = split_cols(h @ w_in[l] + b_in[l], IN_WIDTHS)
        q = nsa_q.reshape(B, T, G, HPG, DH).transpose(0, 2, 3, 1, 4)
        bgates = jax.nn.sigmoid(nsa_g).reshape(B, T, G, HPG, 3).transpose(0, 2, 3, 1, 4)
        o_nsa = nsa_attention(q, kv_heads(ck), kv_heads(cv), kv_heads(sk), kv_heads(sv),
                              kv_heads(wk), kv_heads(wv), bgates,
                              cmp_pos_k[l], cmp_pos_v[l], cmp_w1_k[l], cmp_w2_k[l],
                              cmp_w1_v[l], cmp_w2_v[l])
        y_nsa = o_nsa * jax.nn.silu(nsa_z)
        log_a = jax.nn.log_sigmoid((ga @ gla_w_alpha[l] + gla_b_alpha[l]).astype(jnp.float32)) / GLA_TAU
        o_gla = gla_attention(gq.reshape(B, T, GLA_HEADS, GLA_KEY_DIM),
                              gk.reshape(B, T, GLA_HEADS, GLA_KEY_DIM),
                              gv.reshape(B, T, GLA_HEADS, GLA_VAL_DIM),
                              log_a.reshape(B, T, GLA_HEADS, GLA_KEY_DIM)).astype(x.dtype)
        y_gla = rmsnorm(o_gla, gla_norm_gain[l]).reshape(B, T, GLA_WIDTH) * jax.nn.silu(gla_z)
        merged = jax.nn.sigmoid(mg_nsa) * (y_nsa @ w_br_nsa[l]) + jax.nn.sigmoid(mg_gla) * (y_gla @ w_br_gla[l])
        x = x + gate[:, None, :] * (merged @ w_out[l])
    return rmsnorm(x, final_norm_gain)
```

```python
import contextlib

ENGS = ("pe", "act", "dve", "pool", "sp")
N_DMA_SEMS = 28


class Buf:
    __slots__ = ("name", "w", "r")

    def __init__(self, name=""):
        self.name = name
        self.w = {}
        self.r = {}


class Prog:
    def __init__(self, nc):
        self.nc = nc
        self.streams = {e: [] for e in ENGS}
        self.cnt = {e: 0 for e in ENGS}
        self.known = {e: {} for e in ENGS}
        self.dma_i = 0
        self.dma_ip = 0
        self.dma_cnt = [0] * N_DMA_SEMS
        self.dma_last = [None] * N_DMA_SEMS
        self.n_ops = 0

    def _deps(self, reads, writes):
        deps = {}

        def add(d):
            for k, v in d.items():
                if deps.get(k, -1) < v:
                    deps[k] = v
        for b in reads:
            add(b.w)
        for b in writes:
            add(b.w)
            add(b.r)
        return deps

    def _commit(self, tok, reads, writes):
        k, v = tok
        for b in reads:
            b.r[k] = v
        for b in writes:
            b.w = {k: v}
            b.r = {}

    def _waits(self, eng, deps):
        out = []
        kn = self.known[eng]
        for k, v in deps.items():
            if kn.get(k, -1) >= v:
                continue
            kn[k] = v
            out.append((k, v))
        return out

    def op(self, eng, fn, reads=(), writes=()):
        deps = self._deps(reads, writes)
        waits = self._waits(eng, deps)
        self.cnt[eng] += 1
        tok = (eng, self.cnt[eng])
        self.streams[eng].append((waits, fn, tok))
        self._commit(tok, reads, writes)
        self.n_ops += 1
        return tok

    def dma(self, q, fn, reads=(), writes=()):
        deps = self._deps(reads, writes)
        if q == "pool":
            s = self.dma_ip % 10
            self.dma_ip += 1
        else:
            s = 10 + self.dma_i % (N_DMA_SEMS - 10)
            self.dma_i += 1
        if self.dma_last[s] is not None:
            k, v = self.dma_last[s]
            if deps.get(k, -1) < v:
                deps[k] = v
        waits = self._waits(q, deps)
        self.dma_cnt[s] += 16
        tok = (("dma", s), self.dma_cnt[s])
        self.dma_last[s] = tok
        if q == "pool":
            self.last_pool = tok
        self.streams[q].append((waits, fn, tok))
        self._commit(tok, reads, writes)
        self.n_ops += 1
        return tok

    def wait_all(self, eng, toks):
        deps = {}
        for k, v in toks:
            if deps.get(k, -1) < v:
                deps[k] = v
        waits = self._waits(eng, deps)
        self.streams[eng].append((waits, None, None))

    def emit(self):
        nc = self.nc
        with contextlib.ExitStack() as st:
            sems = {}
            for e in ENGS:
                sems[e] = st.enter_context(nc.semaphore("s_" + e))
            for i in range(N_DMA_SEMS):
                sems[("dma", i)] = st.enter_context(nc.semaphore("s_dma%d" % i))
            block = st.enter_context(nc.Block())
            objs = {"pe": block.tensor, "act": block.scalar, "dve": block.vector,
                    "pool": block.gpsimd, "sp": block.sync}

            needed = {e: set() for e in ENGS}
            for e in ENGS:
                for (waits, fn, tok) in self.streams[e]:
                    for (k, v) in waits:
                        if not isinstance(k, tuple):
                            needed[k].add(v)
            rank = {e: {v: i + 1 for i, v in enumerate(sorted(needed[e]))} for e in ENGS}
            self.n_incs = {e: len(needed[e]) for e in ENGS}

            def mk(e):
                def body(eng):
                    for (waits, fn, tok) in self.streams[e]:
                        for (k, v) in waits:
                            if isinstance(k, tuple):
                                eng.wait_ge(sems[k], v)
                            else:
                                eng.wait_ge(sems[k], rank[k][v])
                        if fn is None:
                            continue
                        ins = fn(eng)
                        k, v = tok
                        if isinstance(k, tuple):
                            ins.then_inc(sems[k], 16)
                        elif v in needed[k]:
                            ins.then_inc(sems[k], 1)
                return body
            for e in ENGS:
                if self.streams[e]:
                    objs[e](mk(e))


import contextlib
import numpy as np
import ml_dtypes
import concourse.bass as bass
import concourse.mybir as mybir

F32 = mybir.dt.float32
BF16 = mybir.dt.bfloat16
AF = mybir.ActivationFunctionType
ALU = mybir.AluOpType
BF = ml_dtypes.bfloat16

D = 2048
KC = 16
NEGM = -30000.0
BIG = 1.0e30
EPS = 1e-6
NFC = 10
FCH_M = [64, 64, 64, 64, 128, 64, 64, 128, 128, 16]
FCH_OFF = [0, 64, 128, 192, 256, 384, 448, 512, 640, 768]
NF = 784
NTA = 396
NTB = 512
NT_ = NTA + NTB


def split3(a):
    a = np.asarray(a, np.float64)
    hi = a.astype(BF).astype(np.float64)
    mid = (a - hi).astype(BF).astype(np.float64)
    lo = (a - hi - mid).astype(BF).astype(np.float64)
    return hi.astype(BF), mid.astype(BF), lo.astype(BF)


def host_tables(T, g):
    slopes = 2.0 ** (-8.0 * np.arange(1, 17, dtype=np.float64) / 16)[4 * g:4 * g + 4]
    t = np.arange(T, dtype=np.float64)
    qaug = np.zeros((8, 4, T), BF)
    for h in range(4):
        a, b_, c_ = split3(-slopes[h] * t)
        qaug[0, h], qaug[1, h], qaug[2, h] = a, b_, c_
        a, b_, c_ = split3(np.full(T, 128.0 * slopes[h]))
        qaug[3, h], qaug[4, h], qaug[5, h] = a, b_, c_
        a, b_, c_ = split3(np.full(T, slopes[h]))
        qaug[6, h], qaug[7, h] = a, b_

    def kside(pos):
        k = np.zeros((8, len(pos)), BF)
        k[0:3] = 1.0
        k[3:6] = (pos // 128).astype(BF)
        k[6:8] = (pos % 128).astype(BF)
        return k
    kaug = kside(np.arange(T))
    ncp = np.arange(512)
    cend = np.maximum(16 * (ncp - 1) + 31, 0)
    kcaug = kside(cend)
    M = np.zeros((512, 128), np.float32)
    w = {-1: 1.0, 0: 2.0, 1: 2.0, 2: 2.0, 3: 1.0}
    for npr in range(1, 512):
        n = npr - 1
        for blk in range(128):
            dd = n - 4 * blk
            if dd in w:
                M[npr, blk] = w[dd]
    Mtab = M.reshape(4, 128, 128).transpose(1, 0, 2).astype(BF)
    E4 = np.zeros((128, 32, 128), BF)
    for m in range(32):
        for half in range(2):
            r = 2 * m + half
            for gq in range(2):
                E4[64 * gq + r, m, 64 * half:64 * half + 64] = 1.0
    p = np.arange(128)[:, None]
    q = np.arange(128)[None, :]
    bias_diag = np.where(q - p >= 0, 0.0, NEGM).astype(BF)
    bias_far = np.where(p - q - 1 >= 0, 0.0, NEGM).astype(BF)
    bias_diag = np.tile(bias_diag, (1, 4))
    bias_far = np.tile(bias_far, (1, 4))
    gmask = np.where(q - p >= 0, 1.0, 0.0).astype(BF)
    reset = np.ones((128, 512), BF)
    reset[:, ::128] = 0.0
    qaug = np.ascontiguousarray(qaug.reshape(8, 4, T // 128, 128).transpose(0, 2, 1, 3))
    return dict(qaug=qaug, kaug=kaug, kcaug=kcaug, Mtab=Mtab, E4=E4,
                bias_diag=bias_diag, bias_far=bias_far, gmask=gmask, reset=reset,
                identb=np.eye(128).astype(BF), identf=np.eye(128, dtype=np.float32))


def host_inputs_l1(inp, b, j, T):
    g = j
    x = np.ascontiguousarray(inp["x"][b, :T])
    w_in = inp["w_in"][0]
    b_in = inp["b_in"][0]
    offs = np.cumsum([0, 1024, 256, 256, 256, 256, 256, 256, 48, 1024, 512, 512, 1024, 16, 1024, 2048, 2048])
    o_q, o_ck, o_cv, o_sk, o_sv, o_wk, o_wv, o_g, o_z, o_gq, o_gk, o_gv, o_ga, o_gz = offs[:14]
    fcols = np.concatenate([
        np.arange(o_q + 256 * g, o_q + 256 * g + 256),
        np.arange(o_ck + 64 * g, o_ck + 64 * g + 64), np.arange(o_cv + 64 * g, o_cv + 64 * g + 64),
        np.arange(o_sk + 64 * g, o_sk + 64 * g + 64), np.arange(o_wk + 64 * g, o_wk + 64 * g + 64),
        np.arange(o_gq + 128 * j, o_gq + 128 * j + 128), np.arange(o_gk + 128 * j, o_gk + 128 * j + 128),
        np.arange(o_ga, o_ga + 16)])
    tcols = np.concatenate([
        np.arange(o_sv + 64 * g, o_sv + 64 * g + 64), np.arange(o_wv + 64 * g, o_wv + 64 * g + 64),
        np.arange(o_g + 12 * g, o_g + 12 * g + 12), np.arange(o_z + 256 * g, o_z + 256 * g + 256),
        np.arange(o_gv + 256 * j, o_gv + 256 * j + 256), np.arange(o_gz + 256 * j, o_gz + 256 * j + 256)])
    assert len(fcols) == NF and len(tcols) == NT_
    wf = np.ascontiguousarray(w_in[:, fcols])
    wt = np.ascontiguousarray(w_in[:, tcols])
    bf = np.zeros((128, NFC), np.float32)
    bfl = b_in[fcols]
    for ci in range(NFC):
        bf[:FCH_M[ci], ci] = bfl[FCH_OFF[ci]:FCH_OFF[ci] + FCH_M[ci]]
    bt = np.ascontiguousarray(b_in[tcols][None, :])
    w1kv = np.concatenate([inp["cmp_w1_k"][0].reshape(32, 64, 256).transpose(1, 0, 2),
                           inp["cmp_w1_v"][0].reshape(32, 64, 256).transpose(1, 0, 2)], axis=0)
    w2k = inp["cmp_w2_k"][0].reshape(2, 128, 64).transpose(1, 0, 2)
    w2v = inp["cmp_w2_v"][0].reshape(2, 128, 64).transpose(1, 0, 2)
    posT = np.concatenate([inp["cmp_pos_k"][0].T, inp["cmp_pos_v"][0].T], axis=0)
    d = dict(
        x=x,
        cT=np.ascontiguousarray(inp["c"][b].reshape(16, 128).T),
        w_ada=np.ascontiguousarray(inp["w_ada"][0][:, :4096]),
        b_ada=np.ascontiguousarray(inp["b_ada"][0][:4096].reshape(32, 128).T),
        gainT=np.ascontiguousarray(inp["norm_gain"][0].reshape(16, 128).T),
        wf=wf, wt=wt, bf=bf, bt=bt,
        w1kv=np.ascontiguousarray(w1kv), w2k=np.ascontiguousarray(w2k), w2v=np.ascontiguousarray(w2v),
        posT=np.ascontiguousarray(posT),
        walpha=np.ascontiguousarray(inp["gla_w_alpha"][0][:, 128 * j:128 * j + 128]),
        balpha=np.ascontiguousarray(inp["gla_b_alpha"][0][128 * j:128 * j + 128].reshape(128, 1)),
        ggain=np.ascontiguousarray(np.broadcast_to(inp["gla_norm_gain"][0][None, :], (128, 256))),
    )
    d.update(host_tables(T, g))
    return d


class _Stop(Exception):
    pass


def build_l1(T, debug=False, stage=99):
    NST = T // 512
    NQT = T // 128
    nc = bass.Bass("TRN2", target_bir_lowering=False)

    def din(name, shape, dt=F32):
        return nc.dram_tensor(name, list(shape), dt, kind="ExternalInput").ap()
    x = din("x", [T, D])
    cT_d = din("cT", [128, 16]); wada_d = din("w_ada", [D, 4096]); bada_d = din("b_ada", [128, 32])
    gainT_d = din("gainT", [128, 16])
    wf_d = din("wf", [D, NF]); wt_d = din("wt", [D, NT_]); bf_d = din("bf", [128, NFC]); bt_d = din("bt", [1, NT_])
    w1kv_d = din("w1kv", [128, 32, 256]); w2k_d = din("w2k", [128, 2, 64]); w2v_d = din("w2v", [128, 2, 64])
    posT_d = din("posT", [128, 32])
    walpha_d = din("walpha", [16, 128]); balpha_d = din("balpha", [128, 1]); ggain_d = din("ggain", [128, 256])
    qaug_d = din("qaug", [8, T // 128, 4, 128], BF16); kaug_d = din("kaug", [8, T], BF16); kcaug_d = din("kcaug", [8, 512], BF16)
    Mtab_d = din("Mtab", [128, 4, 128], BF16); E4_d = din("E4", [128, 32, 128], BF16)
    bdiag_d = din("bias_diag", [128, 512], BF16); bfar_d = din("bias_far", [128, 512], BF16)
    gmask_d = din("gmask", [128, 128], BF16); reset_d = din("reset", [128, 512], BF16)
    identb_d = din("identb", [128, 128], BF16); identf_d = din("identf", [128, 128])
    y_d = nc.dram_tensor("y", [T, 512], BF16, kind="ExternalOutput").ap()
    dbg = {}
    if debug:
        dbg["qT"] = nc.dram_tensor("dbg_qT", [72, 4, 4, 128], BF16, kind="ExternalOutput").ap()
        dbg["tokA"] = nc.dram_tensor("dbg_tokA", [128, NTA], F32, kind="ExternalOutput").ap()
        dbg["ocmp"] = nc.dram_tensor("dbg_ocmp", [128, 4, 193], F32, kind="ExternalOutput").ap()
        dbg["selb"] = nc.dram_tensor("dbg_selb", [128, 128], BF16, kind="ExternalOutput").ap()
        dbg["kcT"] = nc.dram_tensor("dbg_kcT", [72, 512], BF16, kind="ExternalOutput").ap()
        dbg["oall"] = nc.dram_tensor("dbg_oall", [128, 3, 4, 65], F32, kind="ExternalOutput").ap()

    P = Prog(nc)
    with contextlib.ExitStack() as st:
        def sb(name, shape, dt=F32):
            return st.enter_context(nc.sbuf_tensor("sb_" + name, list(shape), dt))
        wf_s = sb("wf_s", [128, KC, NF], BF16); wt_s = sb("wt_s", [128, KC, NT_], BF16)
        bf_s = sb("bf_s", [128, NFC]); btb_s = sb("btb_s", [1, NT_], BF16)
        ones_s = sb("ones_s", [1, 128], BF16)
        xs = [sb("xs%d" % i, [128, D]) for i in range(2)]
        xnb = [sb("xnb0", [128, D], BF16)] * 2
        hT = sb("hT", [128, KC, 512], BF16)
        A_s = sb("A_s", [128, 16]); B_s = sb("B_s", [128, 16]); ada_s = sb("ada_s", [128, 32])
        cT_s = sb("cT_s", [128, 16]); gainT_s = sb("gainT_s", [128, 16]); bada_s = sb("bada_s", [128, 32])
        small = sb("small", [128, 64])
        qT = sb("qT", [72, 4, 4, 128], BF16)
        kslcT = sb("kslcT", [72, T], BF16)
        kwinT = sb("kwinT", [72, 1024], BF16)
        vslc = sb("vslc", [128, NQT, 66], BF16)
        vwin = sb("vwin", [128, 8, 66], BF16)
        cmpbuf = sb("cmpbuf", [128, 528], BF16)
        w1kv_s = sb("w1kv_s", [128, 32, 256], BF16)
        w2k_s = sb("w2k_s", [128, 2, 64], BF16); w2v_s = sb("w2v_s", [128, 2, 64], BF16)
        posT_s = sb("posT_s", [128, 32], BF16); posb_s = sb("posb_s", [128, 4])
        kcT = sb("kcT", [72, 512], BF16)
        vcM = sb("vcM", [128, 4, 194], BF16)
        E4_s = sb("E4_s", [128, 32, 128], BF16)
        bdiag_s = sb("bdiag_s", [128, 512], BF16); bfar_s = sb("bfar_s", [128, 512], BF16)
        gmask_s = sb("gmask_s", [128, 128], BF16); reset_s = sb("reset_s", [128, 512], BF16)
        identb = sb("identb", [128, 128], BF16); identf = sb("identf", [128, 128])
        walpha_s = sb("walpha_s", [16, 128], BF16); nbalpha_s = sb("nbalpha_s", [128, 1]); ggain_s = sb("ggain_s", [128, 256])
        gqT = sb("gqT", [128, 512], BF16); gkT = sb("gkT", [128, 512], BF16); gaT = sb("gaT", [16, 512], BF16)
        tokA = [sb("tokA%d" % i, [128, NTA]) for i in range(4)]
        tokB = [sb("tokB%d" % i, [128, 256]) for i in range(4)]
        gvb = [sb("gvb%d" % i, [128, 256], BF16) for i in range(4)]
        e1 = sb("e1", [128, 512]); spl = e1; cs = sb("cs", [128, 512])
        eb = sb("eb", [128, 512]); enb = e1
        qtl = sb("qtl", [128, 512], BF16); ktl = sb("ktl", [128, 512], BF16)
        ktok = sb("ktok", [128, 128], BF16); atm = sb("atm", [128, 128], BF16)
        S_s = sb("S_s", [128, 256]); Sb_s = sb("Sb_s", [128, 256], BF16); stmp = sb("stmp", [128, 256])
        gl1 = sb("gl1", [128, 256]); junk = gl1
        yt = [sb("yt%d" % i, [128, 512], BF16) for i in range(4)]
        hsb = sb("hsb", [128, 4, 32]); hex_ = sb("hex_", [128, 4, 32]); hidT = sb("hidT", [128, 4, 32], BF16)
        vcst = sb("vcst", [32, 64], BF16)
        pT = [sb("pT%d" % i, [128, 512], BF16) for i in range(3)]
        maskc = [sb("maskc%d" % i, [128, 512], BF16) for i in range(2)]
        zeros_b = sb("zeros_b", [128, 512], BF16)
        score = sb("score", [128, 128]); swork = sb("swork", [128, 128]); m8 = sb("m8", [128, 16])
        selb = sb("selb", [128, 128], BF16); selT4 = sb("selT4", [128, 4, 128], BF16)
        oT_sb = sb("oT_sb", [65, 2, 512])
        coef = sb("coef", [128, 12]); den = sb("den", [128, 12]); gsig = sb("gsig", [128, 12])
        szn = sb("szn", [128, 256]); acc = sb("acc", [128, 256])

        banks = [st.enter_context(nc.psum_tensor("bank%d" % i, [128, 512], F32)) for i in range(8)]
        bb = [Buf("bank%d" % i) for i in range(8)]

        def B(name):
            return Buf(name)
        b_w = B("weights"); b_tab = B("tables"); b_hT = B("hT"); b_AB = B("AB")
        b_xs = [B("xs0"), B("xs1")]; b_xnb = [B("xnb0")] * 2; b_small = B("small")
        b_qT = B("qT"); b_kslc = [B("kslc%d" % i) for i in range(NQT)]
        b_kwin = [B("kwin%d" % i) for i in range(8)]; b_vslc = [B("vslc%d" % i) for i in range(NQT)]
        b_vwin = [B("vwin%d" % i) for i in range(8)]
        b_cmpbuf = B("cmpbuf"); b_kcT = B("kcT"); b_vcM = B("vcM"); b_hid = B("hid")
        b_gT = B("gT"); b_tokA = [B("tokA%d" % i) for i in range(4)]; b_tokB = [B("tokB%d" % i) for i in range(4)]
        b_gvb = [B("gvb%d" % i) for i in range(4)]
        b_gl = B("glatemps"); b_S = B("S"); b_glc = B("glachunk"); b_yt = [B("yt%d" % i) for i in range(4)]
        b_pT = [B("pT%d" % i) for i in range(3)]; b_maskc = [B("maskc0"), B("maskc1")]
        b_sel = B("sel"); b_selT = B("selT"); b_oT = B("oTsb"); b_fin = B("fin")

        def chk(k):
            if stage == k:
                raise _Stop()
        try:
            def ld(q, out, in_, writes):
                return P.dma(q, lambda e: e.dma_start(out=out, in_=in_), writes=writes)
            wf_v = wf_d.rearrange("(kc p) n -> p kc n", p=128)
            wt_v = wt_d.rearrange("(kc p) n -> p kc n", p=128)
            for k4 in range(0, KC, 4):
                ld("pool", wf_s[:, k4:k4 + 4, :], wf_v[:, k4:k4 + 4, :], [b_w])
                ld("pool", wt_s[:, k4:k4 + 4, :], wt_v[:, k4:k4 + 4, :], [b_w])
            for pp in range(0, 32, 8):
                ld("pool", w1kv_s[:, pp:pp + 8, :], w1kv_d[:, pp:pp + 8, :], [b_w])
            ld("pool", w2k_s[:], w2k_d, [b_w]); ld("pool", w2v_s[:], w2v_d, [b_w])
            ld("pool", posT_s[:], posT_d, [b_w]); ld("pool", walpha_s[:], walpha_d, [b_w])
            ld("sp", bf_s[:], bf_d, [b_w]); ld("pool", btb_s[:], bt_d, [b_w])
            ld("sp", cT_s[:], cT_d, [b_AB]); ld("sp", gainT_s[:], gainT_d, [b_AB]); ld("sp", bada_s[:], bada_d, [b_AB])
            ld("sp", nbalpha_s[:], balpha_d, [b_w]); ld("sp", ggain_s[:], ggain_d, [b_w])
            ld("sp", E4_s[:], E4_d, [b_tab]); ld("sp", bdiag_s[:], bdiag_d, [b_tab]); ld("sp", bfar_s[:], bfar_d, [b_tab])
            ld("sp", gmask_s[:], gmask_d, [b_tab]); ld("sp", reset_s[:], reset_d, [b_tab])
            ld("sp", identb[:], identb_d, [b_tab]); ld("sp", identf[:], identf_d, [b_tab])
            ld("sp", vcM[:, :, 65:193], Mtab_d, [b_vcM])
            ld("sp", kcT[64:72, :], kcaug_d, [b_kcT])
            for i in range(NQT):
                pass
            ld("sp", kslcT[64:72, :], kaug_d, b_kslc)
            P.op("pool", lambda e: e.memset(ones_s[:], 1.0), writes=[b_w])
            P.op("pool", lambda e: e.memset(zeros_b[:], 0.0), writes=[b_tab])
            P.op("pool", lambda e: e.memset(cmpbuf[:], 0.0), writes=[b_cmpbuf])
            P.op("pool", lambda e: e.memset(S_s[:], 0.0), writes=[b_S])
            P.op("pool", lambda e: e.memset(Sb_s[:], 0.0), writes=[b_S])
            P.op("pool", lambda e: e.memset(vcM[:, :, 0:64], 0.0), writes=[b_vcM])
            P.op("pool", lambda e: e.memset(vcM[:, :, 64:65], 1.0), writes=[b_vcM])
            P.op("pool", lambda e: e.memset(kcT[0:64, :], 0.0), writes=[b_kcT])
            for i in range(NQT):
                P.op("pool", lambda e, i=i: e.memset(vslc[:, i, 64:65], 1.0), writes=[b_vslc[i]])
            for i in range(8):
                P.op("pool", lambda e, i=i: e.memset(vwin[:, i, 64:65], 1.0), writes=[b_vwin[i]])
            P.op("dve", lambda e: e.tensor_scalar(out=nbalpha_s[:], in0=nbalpha_s[:], scalar1=-1.0, scalar2=None, op0=ALU.mult),
                 reads=[b_w], writes=[b_w])

            chk(0)
            wada_v = wada_d.rearrange("(kc p) n -> p kc n", p=128)
            for cc in range(32):
                xb_ = cc % 2
                ld("sp", xs[xb_][:].rearrange("p (kc n) -> p kc n", kc=16), wada_v[:, :, cc * 128:(cc + 1) * 128], [b_xs[xb_]])
                wv = xs[xb_][:].rearrange("p (kc n) -> p kc n", kc=16)
                for kc in range(KC):
                    P.op("pe", lambda e, kc=kc, wv=wv, cc=cc: e.matmul(banks[0][:, cc:cc + 1], lhsT=wv[:, kc, :], rhs=cT_s[:, kc:kc + 1],
                                                                      start=(kc == 0), stop=(kc == KC - 1)),
                         reads=[b_xs[xb_], b_AB], writes=[bb[0]])
            P.op("dve", lambda e: e.tensor_tensor(out=ada_s[:], in0=banks[0][:, 0:32], in1=bada_s[:], op=ALU.add),
                 reads=[bb[0], b_AB], writes=[b_AB])
            P.op("dve", lambda e: e.tensor_scalar(out=A_s[:], in0=ada_s[:, 16:32], scalar1=1.0, scalar2=None, op0=ALU.add),
                 reads=[b_AB], writes=[b_AB])
            P.op("dve", lambda e: e.tensor_tensor(out=A_s[:], in0=A_s[:], in1=gainT_s[:], op=ALU.mult), reads=[b_AB], writes=[b_AB])
            P.op("dve", lambda e: e.tensor_copy(out=B_s[:], in_=ada_s[:, 0:16]), reads=[b_AB], writes=[b_AB])

            chk(1)
            for kv in range(2):
                lo = 64 * kv
                for hc in range(2):
                    col = kv * 2 + hc
                    for p_ in range(32):
                        P.op("pe", lambda e, lo=lo, hc=hc, p_=p_, col=col: e.matmul(
                            banks[1][:, col:col + 1], lhsT=w1kv_s[lo:lo + 64, p_, hc * 128:(hc + 1) * 128],
                            rhs=posT_s[lo:lo + 64, p_:p_ + 1], start=(p_ == 0), stop=(p_ == 31)),
                            reads=[b_w], writes=[bb[1]])
            P.op("dve", lambda e: e.tensor_copy(out=posb_s[:], in_=banks[1][:, 0:4]), reads=[bb[1]], writes=[b_w])

            chk(2)
            x_v = x.rearrange("(t p) d -> t p d", p=128)
            y_v = y_d.rearrange("(t p) n -> t p n", p=128)
            tokx = {}

            def load_x(t):
                if t < NQT:
                    tokx[t] = ld("sp", xs[t % 2][:], x_v[t], [b_xs[t % 2]])
            load_x(0); load_x(1)

            rr = {"mm": 0, "st": 0}
            fillc = {}

            def getfill(e):
                if "r" not in fillc:
                    fillc["r"] = e.to_reg(NEGM)
                return fillc["r"]

            def mmbank():
                i = rr["mm"] % 2
                rr["mm"] += 1
                return i

            def stbank():
                i = 4 + rr["st"] % 2
                rr["st"] += 1
                return i

            for ST in range(NST):
                for tt in range(4):
                    t = ST * 4 + tt
                    xb_ = t % 2
                    P.op("act", lambda e, xb_=xb_, tt=tt: e.activation(out=xnb[xb_][:], in_=xs[xb_][:], func=AF.Square,
                                                                       accum_out=small[:, tt:tt + 1]),
                         reads=[b_xs[xb_]], writes=[b_xnb[xb_], b_small])
                    P.op("act", lambda e, tt=tt: e.activation(out=small[:, 8 + tt:9 + tt], in_=small[:, tt:tt + 1], func=AF.Ln,
                                                              scale=1.0 / D, bias=EPS),
                         reads=[b_small], writes=[b_small])
                    P.op("act", lambda e, tt=tt: e.activation(out=small[:, 16 + tt:17 + tt], in_=small[:, 8 + tt:9 + tt], func=AF.Exp,
                                                              scale=-0.5),
                         reads=[b_small], writes=[b_small])
                    P.op("dve", lambda e, xb_=xb_, tt=tt: e.tensor_scalar(out=xnb[xb_][:], in0=xs[xb_][:], scalar1=small[:, 16 + tt:17 + tt],
                                                                          scalar2=None, op0=ALU.mult),
                         reads=[b_xs[xb_], b_small], writes=[b_xnb[xb_]])
                    load_x(t + 2)
                    for half in range(2):
                        bk = 2 + half
                        pv = banks[bk][:].bitcast(BF16)
                        for c8 in range(8):
                            kc = half * 8 + c8
                            P.op("pe", lambda e, pv=pv, c8=c8, kc=kc, xb_=xb_: e.transpose(
                                pv[:, c8 * 128:(c8 + 1) * 128], xnb[xb_][:, kc * 128:(kc + 1) * 128], identb[:]),
                                reads=[b_xnb[xb_], b_tab], writes=[bb[bk]])
                        for c8 in range(8):
                            kc = half * 8 + c8
                            eng = "dve" if c8 % 2 == 0 else "pool"
                            eng = "dve"
                            P.op(eng, lambda e, pv=pv, c8=c8, kc=kc, tt=tt: e.tensor_scalar(
                                out=hT[:, kc, tt * 128:(tt + 1) * 128], in0=pv[:, c8 * 128:(c8 + 1) * 128],
                                scalar1=A_s[:, kc:kc + 1], scalar2=B_s[:, kc:kc + 1], op0=ALU.mult, op1=ALU.add),
                                reads=[bb[bk], b_AB], writes=[b_hT])
                chk(3)
                tsl = slice(ST * 512, ST * 512 + 512)
                P.dma("sp", lambda e, ST=ST: e.dma_start(out=qT[64:72, :, :, :], in_=qaug_d[:, ST * 4:ST * 4 + 4, :, :]), writes=[b_qT])
                for ci in range(NFC):
                    M_ = FCH_M[ci]; off = FCH_OFF[ci]
                    bk = mmbank()
                    for kc in range(KC):
                        P.op("pe", lambda e, bk=bk, M_=M_, off=off, kc=kc: e.matmul(
                            banks[bk][0:M_, :], lhsT=wf_s[:, kc, off:off + M_], rhs=hT[:, kc, :], start=(kc == 0), stop=(kc == KC - 1)),
                            reads=[b_w, b_hT], writes=[bb[bk]])
                    src = banks[bk]
                    if ci < 4:
                        P.op("act", lambda e, src=src, ci=ci: e.activation(out=qT[0:64, :, ci, :], in_=src[0:64, :].rearrange("p (t q) -> p t q", q=128), func=AF.Identity,
                                                                           bias=bf_s[0:64, ci:ci + 1], scale=1.0),
                             reads=[bb[bk], b_w], writes=[b_qT])
                        P.op("dve", lambda e, ci=ci: e.tensor_scalar(out=qT[0:64, :, ci, :], in0=qT[0:64, :, ci, :], scalar1=0.125, scalar2=None,
                                                                     op0=ALU.mult), reads=[b_qT], writes=[b_qT])
                    elif ci == 4:
                        P.op("act", lambda e, src=src: e.activation(out=cmpbuf[:, 16:528], in_=src[:, :], func=AF.Identity,
                                                                    bias=bf_s[:, 4:5], scale=1.0),
                             reads=[bb[bk], b_w], writes=[b_cmpbuf])
                    elif ci == 5:
                        P.op("act", lambda e, src=src, tsl=tsl: e.activation(out=kslcT[0:64, tsl], in_=src[0:64, :], func=AF.Identity,
                                                                             bias=bf_s[0:64, 5:6], scale=1.0),
                             reads=[bb[bk], b_w], writes=[b_kslc[ST * 4 + k] for k in range(4)])
                    elif ci == 6:
                        rs = (ST % 2) * 512
                        P.op("act", lambda e, src=src, rs=rs: e.activation(out=kwinT[0:64, rs:rs + 512], in_=src[0:64, :], func=AF.Identity,
                                                                           bias=bf_s[0:64, 6:7], scale=1.0),
                             reads=[bb[bk], b_w], writes=[b_kwin[(ST % 2) * 4 + k] for k in range(4)])
                        P.dma("sp", lambda e, rs=rs, tsl=tsl: e.dma_start(out=kwinT[64:72, rs:rs + 512], in_=kaug_d[:, tsl]),
                              writes=[b_kwin[(ST % 2) * 4 + k] for k in range(4)])
                    elif ci == 7:
                        P.op("act", lambda e, src=src: e.activation(out=gqT[:], in_=src[:, :], func=AF.Identity, bias=bf_s[:, 7:8], scale=1.0),
                             reads=[bb[bk], b_w], writes=[b_gT])
                    elif ci == 8:
                        P.op("act", lambda e, src=src: e.activation(out=gkT[:], in_=src[:, :], func=AF.Identity, bias=bf_s[:, 8:9], scale=1.0),
                             reads=[bb[bk], b_w], writes=[b_gT])
                    else:
                        P.op("act", lambda e, src=src: e.activation(out=gaT[:], in_=src[0:16, :], func=AF.Identity, bias=bf_s[0:16, 9:10], scale=1.0),
                             reads=[bb[bk], b_w], writes=[b_gT])
                chk(4)
                for tt in range(4):
                    t = ST * 4 + tt
                    for grp in range(2):
                        if grp == 1:
                            chk(44)
                        if tt == 1:
                            chk(46)
                        bk = mmbank()
                        n0, n1 = (0, NTA) if grp == 0 else (NTA, NT_)
                        w_ = n1 - n0
                        for kc in range(KC):
                            P.op("pe", lambda e, bk=bk, kc=kc, tt=tt, n0=n0, n1=n1, w_=w_: e.matmul(
                                banks[bk][:, 0:w_], lhsT=hT[:, kc, tt * 128:(tt + 1) * 128], rhs=wt_s[:, kc, n0:n1], start=(kc == 0), stop=False),
                                reads=[b_w, b_hT], writes=[bb[bk]])
                        chk(41)
                        P.op("pe", lambda e, bk=bk, n0=n0, n1=n1, w_=w_: e.matmul(
                            banks[bk][:, 0:w_], lhsT=ones_s[0:1, :], rhs=btb_s[0:1, n0:n1], start=False, stop=True),
                            reads=[b_w], writes=[bb[bk]])
                        chk(42)
                        if grp == 0:
                            P.op("dve", lambda e, bk=bk, tt=tt: e.tensor_copy(out=tokA[tt][:], in_=banks[bk][:, 0:NTA]),
                                 reads=[bb[bk]], writes=[b_tokA[tt]])
                            chk(43)
                            P.op("dve", lambda e, tt=tt, t=t: e.tensor_copy(out=vslc[:, t, 0:64], in_=tokA[tt][:, 0:64]),
                                 reads=[b_tokA[tt]], writes=[b_vslc[t]])
                            P.op("dve", lambda e, tt=tt, t=t: e.tensor_copy(out=vwin[:, t % 8, 0:64], in_=tokA[tt][:, 64:128]),
                                 reads=[b_tokA[tt]], writes=[b_vwin[t % 8]])
                        else:
                            chk(45)
                            P.op("dve", lambda e, bk=bk, tt=tt: e.tensor_copy(out=tokB[tt][:], in_=banks[bk][:, 256:512]),
                                 reads=[bb[bk]], writes=[b_tokB[tt]])
                            P.op("dve", lambda e, bk=bk, tt=tt: e.tensor_copy(out=gvb[tt][:], in_=banks[bk][:, 0:256]),
                                 reads=[bb[bk]], writes=[b_gvb[tt]])
                if debug and ST == 0:
                    P.dma("pool", lambda e: e.dma_start(out=dbg["qT"], in_=qT[:]), reads=[b_qT])
                    P.dma("pool", lambda e: e.dma_start(out=dbg["tokA"], in_=tokA[0][:]), reads=[b_tokA[0]])

                chk(5)
                bk = mmbank()
                P.op("pe", lambda e, bk=bk: e.matmul(banks[bk][:, :], lhsT=walpha_s[:, :], rhs=gaT[:, :], start=True, stop=True),
                     reads=[b_w, b_gT], writes=[bb[bk]])
                P.op("act", lambda e, bk=bk: e.activation(out=e1[:], in_=banks[bk][:, :], func=AF.Exp, scale=-1.0, bias=nbalpha_s[:, 0:1]),
                     reads=[bb[bk], b_w], writes=[b_gl])
                P.op("act", lambda e: e.activation(out=spl[:], in_=e1[:], func=AF.Ln, bias=1.0, scale=1.0), reads=[b_gl], writes=[b_gl])
                P.op("dve", lambda e: e.tensor_tensor_scan(out=cs[:], data0=reset_s[:], data1=spl[:], initial=0.0, op0=ALU.mult, op1=ALU.add),
                     reads=[b_gl, b_tab], writes=[b_gl])
                P.op("act", lambda e: e.activation(out=eb[:], in_=cs[:], func=AF.Exp, scale=-1.0 / 16), reads=[b_gl], writes=[b_gl])
                P.op("act", lambda e: e.activation(out=enb[:], in_=cs[:], func=AF.Exp, scale=1.0 / 16), reads=[b_gl], writes=[b_gl])
                P.op("dve", lambda e: e.scalar_tensor_tensor(out=qtl[:], in0=gqT[:], scalar=128.0 ** -0.5, in1=eb[:], op0=ALU.mult, op1=ALU.mult),
                     reads=[b_gl, b_gT], writes=[b_gl])
                P.op("dve", lambda e: e.tensor_tensor(out=ktl[:], in0=gkT[:], in1=enb[:], op=ALU.mult), reads=[b_gl, b_gT], writes=[b_gl])
                for tt in range(4):
                    t = ST * 4 + tt
                    cs_ = slice(tt * 128, tt * 128 + 128)
                    P.op("pe", lambda e, cs_=cs_: e.transpose(banks[6][:].bitcast(BF16)[:, 0:128], ktl[:, cs_], identb[:]),
                         reads=[b_gl, b_tab], writes=[bb[6]])
                    P.op("dve", lambda e: e.tensor_copy(out=ktok[:], in_=banks[6][:].bitcast(BF16)[:, 0:128]), reads=[bb[6]], writes=[b_glc])
                    P.op("pe", lambda e, cs_=cs_: e.matmul(banks[7][:, 0:128], lhsT=ktl[:, cs_], rhs=qtl[:, cs_], start=True, stop=True),
                         reads=[b_gl], writes=[bb[7]])
                    P.op("dve", lambda e: e.tensor_tensor(out=atm[:], in0=banks[7][:, 0:128], in1=gmask_s[:], op=ALU.mult),
                         reads=[bb[7], b_tab], writes=[b_glc])
                    bo = mmbank()
                    P.op("pe", lambda e, bo=bo, tt=tt: e.matmul(banks[bo][:, 0:256], lhsT=atm[:], rhs=gvb[tt][:], start=True, stop=False),
                         reads=[b_glc, b_gvb[tt]], writes=[bb[bo]])
                    P.op("pe", lambda e, bo=bo, cs_=cs_: e.matmul(banks[bo][:, 0:256], lhsT=qtl[:, cs_], rhs=Sb_s[:], start=False, stop=True),
                         reads=[b_gl, b_S], writes=[bb[bo]])
                    P.op("pe", lambda e, tt=tt: e.matmul(banks[7][:, 256:512], lhsT=ktok[:], rhs=gvb[tt][:], start=True, stop=True),
                         reads=[b_glc, b_gvb[tt]], writes=[bb[7]])
                    P.op("dve", lambda e: e.tensor_tensor(out=stmp[:], in0=banks[7][:, 256:512], in1=S_s[:], op=ALU.add),
                         reads=[bb[7], b_S], writes=[b_glc])
                    last = tt * 128 + 127
                    P.op("dve", lambda e, last=last: e.tensor_scalar(out=S_s[:], in0=stmp[:], scalar1=eb[:, last:last + 1], scalar2=None, op0=ALU.mult),
                         reads=[b_glc, b_gl], writes=[b_S])
                    P.op("dve", lambda e: e.tensor_copy(out=Sb_s[:], in_=S_s[:]), reads=[b_S], writes=[b_S])
                    yb_ = tt
                    P.op("act", lambda e, bo=bo: e.activation(out=junk[:], in_=banks[bo][:, 0:256], func=AF.Square, accum_out=small[:, 24:25]),
                         reads=[bb[bo]], writes=[b_fin, b_small])
                    P.op("act", lambda e: e.activation(out=small[:, 25:26], in_=small[:, 24:25], func=AF.Ln, scale=1.0 / 256, bias=EPS),
                         reads=[b_small], writes=[b_small])
                    P.op("act", lambda e: e.activation(out=small[:, 26:27], in_=small[:, 25:26], func=AF.Exp, scale=-0.5),
                         reads=[b_small], writes=[b_small])
                    P.op("act", lambda e, tt=tt: e.activation(out=gl1[:], in_=tokB[tt][:], func=AF.Exp, scale=-1.0),
                         reads=[b_tokB[tt]], writes=[b_fin])
                    P.op("dve", lambda e: e.tensor_scalar(out=gl1[:], in0=gl1[:], scalar1=1.0, scalar2=None, op0=ALU.add), reads=[b_fin], writes=[b_fin])
                    P.op("dve", lambda e: e.reciprocal(out=gl1[:], in_=gl1[:]), reads=[b_fin], writes=[b_fin])
                    P.op("dve", lambda e, tt=tt: e.tensor_tensor(out=gl1[:], in0=gl1[:], in1=tokB[tt][:], op=ALU.mult),
                         reads=[b_fin, b_tokB[tt]], writes=[b_fin])
                    P.op("dve", lambda e: e.tensor_tensor(out=gl1[:], in0=gl1[:], in1=ggain_s[:], op=ALU.mult), reads=[b_fin, b_w], writes=[b_fin])
                    P.op("dve", lambda e, bo=bo, yb_=yb_: e.scalar_tensor_tensor(out=yt[yb_][:, 256:512], in0=banks[bo][:, 0:256], scalar=small[:, 26:27],
                                                                                 in1=gl1[:], op0=ALU.mult, op1=ALU.mult),
                         reads=[bb[bo], b_fin, b_small], writes=[b_yt[yb_]])

                chk(6)
                for kv in range(2):
                    lo = 64 * kv
                    bk = mmbank()
                    for hc in range(2):
                        for p_ in range(32):
                            P.op("pe", lambda e, bk=bk, lo=lo, hc=hc, p_=p_: e.matmul(
                                banks[bk][:, hc * 32:(hc + 1) * 32], lhsT=w1kv_s[lo:lo + 64, p_, hc * 128:(hc + 1) * 128],
                                rhs=cmpbuf[lo:lo + 64, p_:p_ + 497:16], start=(p_ == 0), stop=(p_ == 31)),
                                reads=[b_w, b_cmpbuf], writes=[bb[bk]])
                    for hc in range(2):
                        col = kv * 2 + hc
                        P.op("dve", lambda e, bk=bk, hc=hc, col=col: e.tensor_scalar(out=hsb[:, col, :], in0=banks[bk][:, hc * 32:(hc + 1) * 32],
                                                                                     scalar1=posb_s[:, col:col + 1], scalar2=None, op0=ALU.add),
                             reads=[bb[bk], b_w], writes=[b_hid])
                P.op("act", lambda e: e.activation(out=hex_[:], in_=hsb[:], func=AF.Exp, scale=-1.0), reads=[b_hid], writes=[b_hid])
                P.op("dve", lambda e: e.tensor_scalar(out=hex_[:], in0=hex_[:], scalar1=1.0, scalar2=None, op0=ALU.add), reads=[b_hid], writes=[b_hid])
                P.op("dve", lambda e: e.reciprocal(out=hex_[:], in_=hex_[:]), reads=[b_hid], writes=[b_hid])
                P.op("dve", lambda e: e.tensor_tensor(out=hidT[:], in0=hex_[:], in1=hsb[:], op=ALU.mult), reads=[b_hid], writes=[b_hid])
                bk = mmbank()
                for hc in range(2):
                    P.op("pe", lambda e, bk=bk, hc=hc: e.matmul(banks[bk][0:64, 0:32], lhsT=w2k_s[:, hc, :], rhs=hidT[:, hc, :],
                                                                start=(hc == 0), stop=(hc == 1)), reads=[b_w, b_hid], writes=[bb[bk]])
                P.op("dve", lambda e, bk=bk, ST=ST: e.tensor_copy(out=kcT[0:64, ST * 32:ST * 32 + 32], in_=banks[bk][0:64, 0:32]),
                     reads=[bb[bk]], writes=[b_kcT])
                for hc in range(2):
                    P.op("pe", lambda e, bk=bk, hc=hc: e.matmul(banks[bk][0:32, 64:128], lhsT=hidT[:, 2 + hc, :], rhs=w2v_s[:, hc, :],
                                                                start=(hc == 0), stop=(hc == 1)), reads=[b_w, b_hid], writes=[bb[bk]])
                P.op("dve", lambda e, bk=bk: e.tensor_copy(out=vcst[:], in_=banks[bk][0:32, 64:128]), reads=[bb[bk]], writes=[b_hid])
                pr = 32 * (ST % 4)
                P.dma("sp", lambda e, pr=pr, ST=ST: e.dma_start(out=vcM[pr:pr + 32, ST // 4, 0:64], in_=vcst[:]), reads=[b_hid], writes=[b_vcM])
                if ST == 0:
                    P.op("pool", lambda e: e.memset(vcM[0:1, 0, 0:193], 0.0), writes=[b_vcM])
                P.op("dve", lambda e: e.tensor_copy(out=cmpbuf[:, 0:16], in_=cmpbuf[:, 512:528]), reads=[b_cmpbuf], writes=[b_cmpbuf])

                chk(7)
                for tt in range(4):
                    i = ST * 4 + tt
                    qv = qT[:, tt, :, :].rearrange("p h q -> p (h q)")
                    nnt = (8 * i + 7 + 127) // 128
                    for nt in range(nnt):
                        mrows = min(128, 8 * i + 8 - 128 * nt)
                        mrows = ((mrows + 31) // 32) * 32
                        sbk = stbank()
                        mk = maskc[nt % 2]
                        P.op("pool", lambda e, mk=mk, i=i, nt=nt: e.affine_select(
                            out=mk[:], in_=zeros_b[:], pattern=[[0, 4], [1, 128]], compare_op=ALU.is_ge, fill=getfill(e),
                            base=128 * i - 2048 * nt - 15, channel_multiplier=-16), reads=[b_tab], writes=[b_maskc[nt % 2]])
                        P.op("pe", lambda e, sbk=sbk, nt=nt, mrows=mrows, qv=qv: e.matmul(
                            banks[sbk][0:mrows, :], lhsT=kcT[:, nt * 128:nt * 128 + mrows], rhs=qv, start=True, stop=False),
                            reads=[b_kcT, b_qT], writes=[bb[sbk]])
                        P.op("pe", lambda e, sbk=sbk, mk=mk, mrows=mrows: e.matmul(
                            banks[sbk][0:mrows, :], lhsT=identb[:, 0:mrows], rhs=mk[:], start=False, stop=True),
                            reads=[b_tab, b_maskc[nt % 2]], writes=[bb[sbk]])
                        pb = rr["st"] % 3
                        P.op("act", lambda e, sbk=sbk, pb=pb, mrows=mrows: e.activation(out=pT[pb][0:mrows, :], in_=banks[sbk][0:mrows, :], func=AF.Exp, scale=1.0),
                             reads=[bb[sbk]], writes=[b_pT[pb]])
                        for h in range(4):
                            obk = 2 + h // 2
                            oc = (h % 2) * 256
                            P.op("pe", lambda e, obk=obk, oc=oc, pb=pb, h=h, nt=nt, mrows=mrows: e.matmul(
                                banks[obk][:, oc:oc + 193], lhsT=pT[pb][0:mrows, h * 128:(h + 1) * 128], rhs=vcM[0:mrows, nt, 0:193],
                                start=(nt == 0 and h % 2 == 0), stop=(nt == nnt - 1), skip_group_check=True),
                                reads=[b_pT[pb], b_vcM], writes=[bb[obk]])
                    for h in range(4):
                        obk = 2 + h // 2
                        oc = (h % 2) * 256
                        P.op("dve", lambda e, obk=obk, oc=oc, h=h: e.tensor_scalar(out=den[:, h:h + 1], in0=banks[obk][:, oc + 64:oc + 65],
                                                                                  scalar1=1e-30, scalar2=None, op0=ALU.max),
                             reads=[bb[obk]], writes=[b_fin])
                    P.op("dve", lambda e: e.reciprocal(out=den[:, 0:4], in_=den[:, 0:4]), reads=[b_fin], writes=[b_fin])
                    for h in range(4):
                        obk = 2 + h // 2
                        oc = (h % 2) * 256
                        if h == 0:
                            P.op("dve", lambda e, obk=obk, oc=oc: e.tensor_scalar(out=score[:], in0=banks[obk][:, oc + 65:oc + 193], scalar1=den[:, 0:1],
                                                                                 scalar2=None, op0=ALU.mult), reads=[bb[obk], b_fin], writes=[b_sel])
                        else:
                            P.op("dve", lambda e, obk=obk, oc=oc, h=h: e.scalar_tensor_tensor(out=score[:], in0=banks[obk][:, oc + 65:oc + 193],
                                                                                              scalar=den[:, h:h + 1], in1=score[:], op0=ALU.mult, op1=ALU.add),
                                 reads=[bb[obk], b_fin, b_sel], writes=[b_sel])
                    P.op("dve", lambda e: e.memset(score[:, 0:1], BIG), writes=[b_sel])
                    P.op("dve", lambda e, i=i: e.memset(score[:, 2 * i:2 * i + 1], BIG), writes=[b_sel])
                    if i > 0:
                        P.op("dve", lambda e, i=i: e.memset(score[0:64, 2 * i - 1:2 * i], BIG), writes=[b_sel])
                    P.op("dve", lambda e, i=i: e.memset(score[64:128, 2 * i + 1:2 * i + 2], BIG), writes=[b_sel])
                    P.op("dve", lambda e: e.max(out=m8[:, 0:8], in_=score[:]), reads=[b_sel], writes=[b_sel])
                    P.op("dve", lambda e: e.match_replace(out=swork[:], in_to_replace=m8[:, 0:8], in_values=score[:], imm_value=-BIG),
                         reads=[b_sel], writes=[b_sel])
                    P.op("dve", lambda e: e.max(out=m8[:, 8:16], in_=swork[:]), reads=[b_sel], writes=[b_sel])
                    P.op("dve", lambda e: e.tensor_scalar(out=selb[:], in0=score[:], scalar1=m8[:, 15:16], scalar2=NEGM, op0=ALU.is_lt, op1=ALU.mult),
                         reads=[b_sel], writes=[b_sel])
                    P.op("pe", lambda e: e.transpose(banks[6][:].bitcast(BF16)[:, 0:128], selb[:], identb[:]), reads=[b_sel, b_tab], writes=[bb[6]])
                    for h in range(4):
                        P.op("dve", lambda e, h=h: e.tensor_copy(out=selT4[:, h, :], in_=banks[6][:].bitcast(BF16)[:, 0:128]),
                             reads=[bb[6]], writes=[b_selT])
                    if debug and i == NQT - 1:
                        P.op("dve", lambda e: e.tensor_copy(out=szn[:, 0:193], in_=banks[2][:, 0:193]), reads=[bb[2]], writes=[b_fin])
                        P.dma("pool", lambda e: e.dma_start(out=dbg["ocmp"][:, 0, :], in_=szn[:, 0:193]), reads=[b_fin])
                        P.dma("pool", lambda e: e.dma_start(out=dbg["selb"], in_=selb[:]), reads=[b_sel])
                        P.dma("pool", lambda e: e.dma_start(out=dbg["kcT"], in_=kcT[:]), reads=[b_kcT])
                    P.op("act", lambda e, tt=tt: e.activation(out=gsig[:], in_=tokA[tt][:, 128:140], func=AF.Exp, scale=-1.0), reads=[b_tokA[tt]], writes=[b_fin])
                    P.op("dve", lambda e: e.tensor_scalar(out=gsig[:], in0=gsig[:], scalar1=1.0, scalar2=None, op0=ALU.add), reads=[b_fin], writes=[b_fin])
                    P.op("dve", lambda e: e.reciprocal(out=gsig[:], in_=gsig[:]), reads=[b_fin], writes=[b_fin])
                    for h in range(4):
                        obk = 2 + h // 2
                        oc = (h % 2) * 256
                        P.op("dve", lambda e, h=h: e.tensor_tensor(out=coef[:, h:h + 1], in0=gsig[:, 3 * h:3 * h + 1], in1=den[:, h:h + 1], op=ALU.mult),
                             reads=[b_fin], writes=[b_fin])
                        P.op("dve", lambda e, obk=obk, oc=oc, h=h: e.tensor_scalar(out=acc[:, h * 64:(h + 1) * 64], in0=banks[obk][:, oc:oc + 64],
                                                                                  scalar1=coef[:, h:h + 1], scalar2=None, op0=ALU.mult),
                             reads=[bb[obk], b_fin], writes=[b_fin])

                    for br in range(2):
                        obk = 6 + br
                        if br == 0:
                            kts = list(range(0, i + 1))
                        else:
                            kts = list(range(max(0, i - 4), i + 1))
                        for idx, kt in enumerate(kts):
                            sbk = stbank()
                            if br == 0:
                                kop = kslcT[:, kt * 128:(kt + 1) * 128]; kb = b_kslc[kt]
                                vop = vslc[:, kt, 0:65]; vb = b_vslc[kt]
                            else:
                                rsl = (kt % 8) * 128
                                kop = kwinT[:, rsl:rsl + 128]; kb = b_kwin[kt % 8]
                                vop = vwin[:, kt % 8, 0:65]; vb = b_vwin[kt % 8]
                            extra = []
                            if br == 0:
                                gq_ = (2 * kt) // 64
                                m_ = kt % 32
                                extra.append((E4_s[64 * gq_:64 * gq_ + 64, m_, :], selT4[64 * gq_:64 * gq_ + 64, :, :].rearrange("p h q -> p (h q)"), [b_tab, b_selT]))
                            if kt == i:
                                extra.append((identb[:], bdiag_s[:], [b_tab]))
                            if br == 1 and kt == i - 4:
                                extra.append((identb[:], bfar_s[:], [b_tab]))
                            P.op("pe", lambda e, sbk=sbk, kop=kop, qv=qv, ne=len(extra): e.matmul(banks[sbk][:, :], lhsT=kop, rhs=qv, start=True, stop=(ne == 0)),
                                 reads=[kb, b_qT], writes=[bb[sbk]])
                            for xi, (l_, r_, rb_) in enumerate(extra):
                                P.op("pe", lambda e, sbk=sbk, l_=l_, r_=r_, lastx=(xi == len(extra) - 1): e.matmul(
                                    banks[sbk][:, :], lhsT=l_, rhs=r_, start=False, stop=lastx), reads=rb_, writes=[bb[sbk]])
                            pb = rr["st"] % 3
                            P.op("act", lambda e, sbk=sbk, pb=pb: e.activation(out=pT[pb][:], in_=banks[sbk][:, :], func=AF.Exp, scale=1.0),
                                 reads=[bb[sbk]], writes=[b_pT[pb]])
                            P.op("pe", lambda e, obk=obk, pb=pb, vop=vop, idx=idx, nk=len(kts): e.matmul(
                                banks[obk][0:65, :], lhsT=vop, rhs=pT[pb][:], start=(idx == 0), stop=(idx == nk - 1)),
                                reads=[b_pT[pb], vb], writes=[bb[obk]])
                        P.op("dve", lambda e, obk=obk, br=br: e.tensor_copy(out=oT_sb[:, br, :], in_=banks[obk][0:65, :]),
                             reads=[bb[obk]], writes=[b_oT])
                    for br in range(2):
                        tbk = mmbank()
                        for h in range(4):
                            P.op("pe", lambda e, tbk=tbk, br=br, h=h: e.matmul(
                                banks[tbk][:, h * 128:h * 128 + 65], lhsT=oT_sb[:, br, h * 128:(h + 1) * 128], rhs=identf[0:65, 0:65],
                                start=(h == 0), stop=(h == 3), skip_group_check=True), reads=[b_oT, b_tab], writes=[bb[tbk]])
                        for h in range(4):
                            dcol = 4 + br * 4 + h
                            P.op("dve", lambda e, tbk=tbk, h=h, dcol=dcol: e.tensor_scalar(out=den[:, dcol:dcol + 1], in0=banks[tbk][:, h * 128 + 64:h * 128 + 65],
                                                                                          scalar1=1e-30, scalar2=None, op0=ALU.max),
                                 reads=[bb[tbk]], writes=[b_fin])
                            P.op("dve", lambda e, dcol=dcol: e.reciprocal(out=den[:, dcol:dcol + 1], in_=den[:, dcol:dcol + 1]), reads=[b_fin], writes=[b_fin])
                            P.op("dve", lambda e, h=h, br=br, dcol=dcol: e.tensor_tensor(out=coef[:, dcol:dcol + 1], in0=gsig[:, 3 * h + 1 + br:3 * h + 2 + br],
                                                                                        in1=den[:, dcol:dcol + 1], op=ALU.mult), reads=[b_fin], writes=[b_fin])
                            P.op("dve", lambda e, tbk=tbk, h=h, dcol=dcol: e.scalar_tensor_tensor(
                                out=acc[:, h * 64:(h + 1) * 64], in0=banks[tbk][:, h * 128:h * 128 + 64], scalar=coef[:, dcol:dcol + 1],
                                in1=acc[:, h * 64:(h + 1) * 64], op0=ALU.mult, op1=ALU.add), reads=[bb[tbk], b_fin], writes=[b_fin])
                    yb_ = tt
                    P.op("act", lambda e, tt=tt: e.activation(out=szn[:], in_=tokA[tt][:, 140:396], func=AF.Exp, scale=-1.0), reads=[b_tokA[tt]], writes=[b_fin])
                    P.op("dve", lambda e: e.tensor_scalar(out=szn[:], in0=szn[:], scalar1=1.0, scalar2=None, op0=ALU.add), reads=[b_fin], writes=[b_fin])
                    P.op("dve", lambda e: e.reciprocal(out=szn[:], in_=szn[:]), reads=[b_fin], writes=[b_fin])
                    P.op("dve", lambda e, tt=tt: e.tensor_tensor(out=szn[:], in0=szn[:], in1=tokA[tt][:, 140:396], op=ALU.mult), reads=[b_fin, b_tokA[tt]], writes=[b_fin])
                    P.op("dve", lambda e, yb_=yb_: e.tensor_tensor(out=yt[yb_][:, 0:256], in0=szn[:], in1=acc[:], op=ALU.mult), reads=[b_fin], writes=[b_yt[yb_]])
                    P.dma("pool", lambda e, yb_=yb_, i=i: e.dma_start(out=y_v[i], in_=yt[yb_][:]), reads=[b_yt[yb_]])

        except _Stop:
            pass
        toks = [t_ for t_ in P.dma_last if t_ is not None]
        toks += [(e_, P.cnt[e_]) for e_ in ("pe", "act", "dve", "pool") if P.cnt[e_] > 0]
        for e_ in ("pool", "sp", "act", "dve", "pe"):
            P.wait_all(e_, toks)
        P.emit()
    return nc


import contextlib
import numpy as np
import ml_dtypes
import concourse.bass as bass
import concourse.mybir as mybir

F32 = mybir.dt.float32
BF16 = mybir.dt.bfloat16
AF = mybir.ActivationFunctionType
ALU = mybir.AluOpType
BF = ml_dtypes.bfloat16
D = 2048
KC = 16
EPS = 1e-6
CB = 256
NCB = D // CB


def host_inputs_l2(inp, ys, b, qtr, TQ):
    t0 = qtr * TQ
    ycat = np.concatenate([ys[b * 4 + j][t0:t0 + TQ, 0:256] for j in range(4)] +
                          [ys[b * 4 + j][t0:t0 + TQ, 256:512] for j in range(4)], axis=1)
    w_in = inp["w_in"][0]
    b_in = inp["b_in"][0]
    return dict(
        x2=np.ascontiguousarray(inp["x"][b, t0:t0 + TQ]),
        yin=np.ascontiguousarray(ycat),
        cT2=np.ascontiguousarray(inp["c"][b].reshape(16, 128).T),
        w_ada2=np.ascontiguousarray(inp["w_ada"][0]),
        b_ada2=np.ascontiguousarray(inp["b_ada"][0].reshape(48, 128).T),
        gainT2=np.ascontiguousarray(inp["norm_gain"][0].reshape(16, 128).T),
        wmg=np.ascontiguousarray(w_in[:, 6720:10816]),
        bmg=np.ascontiguousarray(b_in[6720:10816][None, :]),
        wbrn=np.ascontiguousarray(inp["w_br_nsa"][0]),
        wbrg=np.ascontiguousarray(inp["w_br_gla"][0]),
        wout=np.ascontiguousarray(inp["w_out"][0]),
        fgain=np.ascontiguousarray(np.broadcast_to(inp["final_norm_gain"][None, :], (128, D))),
        identb2=np.eye(128).astype(BF), identf2=np.eye(128, dtype=np.float32),
    )


def build_l2(TQ):
    NST = TQ // 512
    nc = bass.Bass("TRN2", target_bir_lowering=False)

    def din(name, shape, dt=F32):
        return nc.dram_tensor(name, list(shape), dt, kind="ExternalInput").ap()
    x = din("x2", [TQ, D]); yin = din("yin", [TQ, D], BF16)
    cT_d = din("cT2", [128, 16]); wada_d = din("w_ada2", [D, 6144]); bada_d = din("b_ada2", [128, 48])
    gainT_d = din("gainT2", [128, 16])
    wmg_d = din("wmg", [D, 4096]); bmg_d = din("bmg", [1, 4096])
    wbrn_d = din("wbrn", [1024, D]); wbrg_d = din("wbrg", [1024, D]); wout_d = din("wout", [D, D])
    fgain_d = din("fgain", [128, D]); identb_d = din("identb2", [128, 128], BF16); identf_d = din("identf2", [128, 128])
    out_d = nc.dram_tensor("out", [TQ, D], F32, kind="ExternalOutput").ap()

    P = Prog(nc)
    with contextlib.ExitStack() as st:
        def sb(name, shape, dt=F32):
            return st.enter_context(nc.sbuf_tensor("s2_" + name, list(shape), dt))
        xs = [sb("xs%d" % i, [128, D]) for i in range(2)]
        xnb = sb("xnb", [128, D], BF16)
        ytile = [sb("ytile%d" % i, [128, D], BF16) for i in range(2)]
        hT = sb("hT", [128, KC, 512], BF16)
        yT = sb("yT", [128, KC, 512], BF16)
        mtok = [sb("mtok%d" % i, [128, D], BF16) for i in range(4)]
        mT = sb("mT", [128, KC, 128], BF16)
        wmgb = [sb("wmgb%d" % i, [128, 2, KC, CB], BF16) for i in range(2)]
        wbrb = [sb("wbrb%d" % i, [128, 2, 8, CB], BF16) for i in range(2)]
        woutb = [sb("woutb%d" % i, [128, KC, 512], BF16) for i in range(2)]
        bmg_s = sb("bmg_s", [1, 4096], BF16); ones_s = sb("ones_s", [1, 128], BF16)
        onesf = sb("onesf", [128, 128])
        gate_bc = sb("gate_bc", [128, D]); fgain_s = sb("fgain_s", [128, D])
        xo = sb("xo", [128, D])
        A_s = sb("A_s", [128, 16]); B_s = sb("B_s", [128, 16]); ada_s = sb("ada_s", [128, 48])
        cT_s = sb("cT_s", [128, 16]); gainT_s = sb("gainT_s", [128, 16]); bada_s = sb("bada_s", [128, 48])
        small = sb("small", [128, 32]); diag = [sb("diag%d" % i, [128, 128]) for i in range(2)]
        identb = sb("identb", [128, 128], BF16); identf = sb("identf", [128, 128])
        g1 = sb("g1", [128, CB]); g2 = sb("g2", [128, CB]); t1 = sb("t1", [128, CB])

        banks = [st.enter_context(nc.psum_tensor("b2ank%d" % i, [128, 512], F32)) for i in range(8)]
        bb = [Buf("bank%d" % i) for i in range(8)]
        B = Buf
        b_w = B("w"); b_AB = B("AB"); b_xs = [B("xs0"), B("xs1")]; b_xnb = B("xnb"); b_small = B("small")
        b_yt = [B("yt0"), B("yt1")]; b_hT = B("hT"); b_yT = B("yT"); b_mtok = [B("mtok%d" % i) for i in range(4)]
        b_mT = B("mT"); b_wmg = [B("wmg0"), B("wmg1")]; b_wbr = [B("wbr0"), B("wbr1")]; b_wout = [B("wout0"), B("wout1")]
        b_gate = B("gate"); b_diag = [B("d0"), B("d1")]; b_g = B("g"); b_xo = B("xo"); b_tab = B("tab")

        def ld(q, out, in_, writes):
            return P.dma(q, lambda e: e.dma_start(out=out, in_=in_), writes=writes)
        ld("sp", cT_s[:], cT_d, [b_AB]); ld("sp", gainT_s[:], gainT_d, [b_AB]); ld("sp", bada_s[:], bada_d, [b_AB])
        ld("sp", fgain_s[:], fgain_d, [b_w]); ld("sp", identb[:], identb_d, [b_tab]); ld("sp", identf[:], identf_d, [b_tab])
        ld("pool", bmg_s[:], bmg_d, [b_w])
        P.op("pool", lambda e: e.memset(ones_s[:], 1.0), writes=[b_w])
        P.op("pool", lambda e: e.memset(onesf[:], 1.0), writes=[b_w])
        wada_v = wada_d.rearrange("(kc p) n -> p kc n", p=128)
        for cc in range(48):
            xb_ = cc % 2
            ld("sp", xs[xb_][:].rearrange("p (kc n) -> p kc n", kc=16), wada_v[:, :, cc * 128:(cc + 1) * 128], [b_xs[xb_]])
            wv = xs[xb_][:].rearrange("p (kc n) -> p kc n", kc=16)
            for kc in range(KC):
                P.op("pe", lambda e, kc=kc, wv=wv, cc=cc: e.matmul(banks[0][:, cc:cc + 1], lhsT=wv[:, kc, :], rhs=cT_s[:, kc:kc + 1],
                                                                  start=(kc == 0), stop=(kc == KC - 1)),
                     reads=[b_xs[xb_], b_AB], writes=[bb[0]])
        P.op("dve", lambda e: e.tensor_tensor(out=ada_s[:], in0=banks[0][:, 0:48], in1=bada_s[:], op=ALU.add),
             reads=[bb[0], b_AB], writes=[b_AB])
        P.op("dve", lambda e: e.tensor_scalar(out=A_s[:], in0=ada_s[:, 16:32], scalar1=1.0, scalar2=None, op0=ALU.add), reads=[b_AB], writes=[b_AB])
        P.op("dve", lambda e: e.tensor_tensor(out=A_s[:], in0=A_s[:], in1=gainT_s[:], op=ALU.mult), reads=[b_AB], writes=[b_AB])
        P.op("dve", lambda e: e.tensor_copy(out=B_s[:], in_=ada_s[:, 0:16]), reads=[b_AB], writes=[b_AB])
        for kc in range(KC):
            d_ = kc % 2
            P.op("dve", lambda e, kc=kc, d_=d_: e.tensor_scalar(out=diag[d_][:], in0=identf[:], scalar1=ada_s[:, 32 + kc:33 + kc], scalar2=None, op0=ALU.mult),
                 reads=[b_AB, b_tab], writes=[b_diag[d_]])
            bk = 1 + kc // 4
            P.op("pe", lambda e, kc=kc, d_=d_, bk=bk: e.matmul(banks[bk][:, (kc % 4) * 128:(kc % 4) * 128 + 128], lhsT=onesf[:], rhs=diag[d_][:],
                                                               start=(kc % 4 == 0), stop=(kc % 4 == 3), skip_group_check=True),
                 reads=[b_w, b_diag[d_]], writes=[bb[bk]])
        for q4 in range(4):
            P.op("dve", lambda e, q4=q4: e.tensor_copy(out=gate_bc[:, q4 * 512:(q4 + 1) * 512], in_=banks[1 + q4][:, :]), reads=[bb[1 + q4]], writes=[b_gate])

        x_v = x.rearrange("(t p) d -> t p d", p=128)
        y_v = yin.rearrange("(t p) d -> t p d", p=128)
        o_v = out_d.rearrange("(t p) d -> t p d", p=128)
        wmg_v = wmg_d.rearrange("(kc p) n -> p kc n", p=128)
        wbrn_v = wbrn_d.rearrange("(kc p) n -> p kc n", p=128)
        wbrg_v = wbrg_d.rearrange("(kc p) n -> p kc n", p=128)
        wout_v = wout_d.rearrange("(kc p) n -> p kc n", p=128)
        rr = {"mm": 0, "w": 0, "wo": 0, "x": 0}

        def mmbank():
            i = rr["mm"] % 4
            rr["mm"] += 1
            return i

        for ST in range(NST):
            for tt in range(4):
                t = ST * 4 + tt
                xb_ = rr["x"] % 2
                rr["x"] += 1
                ld("sp", xs[xb_][:], x_v[t], [b_xs[xb_]])
                ld("sp", ytile[tt % 2][:], y_v[t], [b_yt[tt % 2]])
                P.op("act", lambda e, xb_=xb_, tt=tt: e.activation(out=xnb[:], in_=xs[xb_][:], func=AF.Square, accum_out=small[:, tt:tt + 1]),
                     reads=[b_xs[xb_]], writes=[b_xnb, b_small])
                P.op("act", lambda e, tt=tt: e.activation(out=small[:, 8 + tt:9 + tt], in_=small[:, tt:tt + 1], func=AF.Ln, scale=1.0 / D, bias=EPS),
                     reads=[b_small], writes=[b_small])
                P.op("act", lambda e, tt=tt: e.activation(out=small[:, 16 + tt:17 + tt], in_=small[:, 8 + tt:9 + tt], func=AF.Exp, scale=-0.5),
                     reads=[b_small], writes=[b_small])
                P.op("dve", lambda e, xb_=xb_, tt=tt: e.tensor_scalar(out=xnb[:], in0=xs[xb_][:], scalar1=small[:, 16 + tt:17 + tt], scalar2=None, op0=ALU.mult),
                     reads=[b_xs[xb_], b_small], writes=[b_xnb])
                for src_i, (src, sbuf_, dst, dbuf) in enumerate(((xnb, b_xnb, hT, b_hT), (ytile[tt % 2], b_yt[tt % 2], yT, b_yT))):
                    for half in range(2):
                        bk = 4 + 2 * src_i + half
                        pv = banks[bk][:].bitcast(BF16)
                        for c8 in range(8):
                            kc = half * 8 + c8
                            P.op("pe", lambda e, pv=pv, c8=c8, kc=kc, src=src: e.transpose(pv[:, c8 * 128:(c8 + 1) * 128], src[:, kc * 128:(kc + 1) * 128], identb[:]),
                                 reads=[sbuf_, b_tab], writes=[bb[bk]])
                        if src_i == 0:
                            for c8 in range(8):
                                kc = half * 8 + c8
                                P.op("dve", lambda e, pv=pv, c8=c8, kc=kc, tt=tt: e.tensor_scalar(
                                    out=hT[:, kc, tt * 128:(tt + 1) * 128], in0=pv[:, c8 * 128:(c8 + 1) * 128],
                                    scalar1=A_s[:, kc:kc + 1], scalar2=B_s[:, kc:kc + 1], op0=ALU.mult, op1=ALU.add),
                                    reads=[bb[bk], b_AB], writes=[b_hT])
                        else:
                            P.op("dve", lambda e, pv=pv, half=half, tt=tt: e.tensor_copy(
                                out=yT[:, half * 8:half * 8 + 8, tt * 128:(tt + 1) * 128], in_=pv.rearrange("p (c q) -> p c q", q=128)),
                                reads=[bb[bk]], writes=[b_yT])
            for cb in range(NCB):
                wb = rr["w"] % 2
                rr["w"] += 1
                c0 = cb * CB
                ld("pool", wmgb[wb][:, 0, :, :], wmg_v[:, :, c0:c0 + CB], [b_wmg[wb]])
                ld("pool", wmgb[wb][:, 1, :, :], wmg_v[:, :, 2048 + c0:2048 + c0 + CB], [b_wmg[wb]])
                ld("pool", wbrb[wb][:, 0, :, :], wbrn_v[:, :, c0:c0 + CB], [b_wbr[wb]])
                ld("pool", wbrb[wb][:, 1, :, :], wbrg_v[:, :, c0:c0 + CB], [b_wbr[wb]])
                for tt in range(4):
                    tsl = slice(tt * 128, tt * 128 + 128)
                    bk = mmbank()
                    for gi in range(2):
                        oc = gi * CB
                        for kc in range(KC):
                            P.op("pe", lambda e, bk=bk, oc=oc, gi=gi, kc=kc, tsl=tsl, wb=wb: e.matmul(
                                banks[bk][:, oc:oc + CB], lhsT=hT[:, kc, tsl], rhs=wmgb[wb][:, gi, kc, :],
                                start=(kc == 0 and gi == 0), stop=False, skip_group_check=True),
                                reads=[b_hT, b_wmg[wb]], writes=[bb[bk]])
                        P.op("pe", lambda e, bk=bk, oc=oc, gi=gi, c0=c0: e.matmul(
                            banks[bk][:, oc:oc + CB], lhsT=ones_s[0:1, :], rhs=bmg_s[0:1, gi * 2048 + c0:gi * 2048 + c0 + CB],
                            start=False, stop=True, skip_group_check=True), reads=[b_w], writes=[bb[bk]])
                    bk2 = mmbank()
                    for gi in range(2):
                        oc = gi * CB
                        for kc in range(8):
                            P.op("pe", lambda e, bk2=bk2, oc=oc, gi=gi, kc=kc, tsl=tsl, wb=wb: e.matmul(
                                banks[bk2][:, oc:oc + CB], lhsT=yT[:, gi * 8 + kc, tsl], rhs=wbrb[wb][:, gi, kc, :],
                                start=(kc == 0 and gi == 0), stop=(kc == 7), skip_group_check=True),
                                reads=[b_yT, b_wbr[wb]], writes=[bb[bk2]])
                    P.op("act", lambda e, bk=bk: e.activation(out=g1[:], in_=banks[bk][:, 0:CB], func=AF.Exp, scale=-1.0), reads=[bb[bk]], writes=[b_g])
                    P.op("act", lambda e, bk=bk: e.activation(out=g2[:], in_=banks[bk][:, CB:2 * CB], func=AF.Exp, scale=-1.0), reads=[bb[bk]], writes=[b_g])
                    P.op("dve", lambda e: e.tensor_scalar(out=g1[:], in0=g1[:], scalar1=1.0, scalar2=None, op0=ALU.add), reads=[b_g], writes=[b_g])
                    P.op("dve", lambda e: e.tensor_scalar(out=g2[:], in0=g2[:], scalar1=1.0, scalar2=None, op0=ALU.add), reads=[b_g], writes=[b_g])
                    P.op("dve", lambda e: e.reciprocal(out=g1[:], in_=g1[:]), reads=[b_g], writes=[b_g])
                    P.op("dve", lambda e: e.reciprocal(out=g2[:], in_=g2[:]), reads=[b_g], writes=[b_g])
                    P.op("dve", lambda e, bk2=bk2: e.tensor_tensor(out=t1[:], in0=banks[bk2][:, 0:CB], in1=g1[:], op=ALU.mult), reads=[bb[bk2], b_g], writes=[b_g])
                    P.op("dve", lambda e, bk2=bk2: e.tensor_tensor(out=g2[:], in0=banks[bk2][:, CB:2 * CB], in1=g2[:], op=ALU.mult), reads=[bb[bk2], b_g], writes=[b_g])
                    P.op("dve", lambda e, tt=tt, c0=c0: e.tensor_tensor(out=mtok[tt][:, c0:c0 + CB], in0=t1[:], in1=g2[:], op=ALU.add), reads=[b_g], writes=[b_mtok[tt]])
            for tt in range(4):
                t = ST * 4 + tt
                for half in range(2):
                    bk = 4 + half
                    pv = banks[bk][:].bitcast(BF16)
                    for c8 in range(8):
                        kc = half * 8 + c8
                        P.op("pe", lambda e, pv=pv, c8=c8, kc=kc, tt=tt: e.transpose(pv[:, c8 * 128:(c8 + 1) * 128], mtok[tt][:, kc * 128:(kc + 1) * 128], identb[:]),
                             reads=[b_mtok[tt], b_tab], writes=[bb[bk]])
                    P.op("dve", lambda e, pv=pv, half=half: e.tensor_copy(out=mT[:, half * 8:half * 8 + 8, :], in_=pv.rearrange("p (c q) -> p c q", q=128)),
                         reads=[bb[bk]], writes=[b_mT])
                xb_ = rr["x"] % 2
                rr["x"] += 1
                ld("sp", xs[xb_][:], x_v[t], [b_xs[xb_]])
                for ob in range(4):
                    wo = rr["wo"] % 2
                    rr["wo"] += 1
                    ld("pool", woutb[wo][:], wout_v[:, :, ob * 512:(ob + 1) * 512], [b_wout[wo]])
                    bk = mmbank()
                    for kc in range(KC):
                        P.op("pe", lambda e, bk=bk, kc=kc, wo=wo: e.matmul(banks[bk][:, :], lhsT=mT[:, kc, :], rhs=woutb[wo][:, kc, :],
                                                                          start=(kc == 0), stop=(kc == KC - 1)),
                             reads=[b_mT, b_wout[wo]], writes=[bb[bk]])
                    osl = slice(ob * 512, ob * 512 + 512)
                    P.op("dve", lambda e, bk=bk, osl=osl: e.tensor_tensor(out=xo[:, osl], in0=banks[bk][:, :], in1=gate_bc[:, osl], op=ALU.mult),
                         reads=[bb[bk], b_gate], writes=[b_xo])
                    P.op("dve", lambda e, osl=osl, xb_=xb_: e.tensor_tensor(out=xo[:, osl], in0=xo[:, osl], in1=xs[xb_][:, osl], op=ALU.add),
                         reads=[b_xo, b_xs[xb_]], writes=[b_xo])
                P.op("act", lambda e, xb_=xb_: e.activation(out=xs[xb_][:], in_=xo[:], func=AF.Square, accum_out=small[:, 24:25]),
                     reads=[b_xo], writes=[b_xs[xb_], b_small])
                P.op("act", lambda e: e.activation(out=small[:, 25:26], in_=small[:, 24:25], func=AF.Ln, scale=1.0 / D, bias=EPS), reads=[b_small], writes=[b_small])
                P.op("act", lambda e: e.activation(out=small[:, 26:27], in_=small[:, 25:26], func=AF.Exp, scale=-0.5), reads=[b_small], writes=[b_small])
                P.op("dve", lambda e, xb_=xb_: e.scalar_tensor_tensor(out=xs[xb_][:], in0=xo[:], scalar=small[:, 26:27], in1=fgain_s[:], op0=ALU.mult, op1=ALU.mult),
                     reads=[b_xo, b_small, b_w], writes=[b_xs[xb_]])
                P.dma("sp", lambda e, xb_=xb_, t=t: e.dma_start(out=o_v[t], in_=xs[xb_][:]), reads=[b_xs[xb_]])
        toks = [t_ for t_ in P.dma_last if t_ is not None]
        toks += [(e_, P.cnt[e_]) for e_ in ("pe", "act", "dve", "pool") if P.cnt[e_] > 0]
        for e_ in ("pool", "sp", "act", "dve", "pe"):
            P.wait_all(e_, toks)
        P.emit()
    return nc


def kernel(**inputs):
    from concourse.bass_utils import run_bass_kernel_spmd
    inp = {k_: np.asarray(v) for k_, v in inputs.items()}
    Bn, T, Dm = inp["x"].shape
    nc1 = build_l1(T)
    in_maps = [host_inputs_l1(inp, c // 4, c % 4, T) for c in range(8)]
    res = run_bass_kernel_spmd(nc1, in_maps, core_ids=list(range(8)))
    ys = [np.asarray(r["y"]) for r in res.results]
    del in_maps
    TQ = T // 4
    nc2 = build_l2(TQ)
    in_maps2 = [host_inputs_l2(inp, ys, c // 4, c % 4, TQ) for c in range(8)]
    res2 = run_bass_kernel_spmd(nc2, in_maps2, core_ids=list(range(8)))
    out = np.empty((Bn, T, Dm), np.float32)
    for c in range(8):
        out[c // 4, (c % 4) * TQ:(c % 4 + 1) * TQ] = np.asarray(res2.results[c]["out"])
    return out
```

```python
import contextlib

ENGS = ("pe", "act", "dve", "pool", "sp")
N_DMA_SEMS = 28


class Buf:
    __slots__ = ("name", "w", "r")

    def __init__(self, name=""):
        self.name = name
        self.w = {}
        self.r = {}


class Prog:
    def __init__(self, nc):
        self.nc = nc
        self.streams = {e: [] for e in ENGS}
        self.cnt = {e: 0 for e in ENGS}
        self.known = {e: {} for e in ENGS}
        self.dma_i = 0
        self.dma_ip = 0
        self.dma_cnt = [0] * N_DMA_SEMS
        self.dma_last = [None] * N_DMA_SEMS
        self.n_ops = 0

    def _deps(self, reads, writes):
        deps = {}

        def add(d):
            for k, v in d.items():
                if deps.get(k, -1) < v:
                    deps[k] = v
        for b in reads:
            add(b.w)
        for b in writes:
            add(b.w)
            add(b.r)
        return deps

    def _commit(self, tok, reads, writes):
        k, v = tok
        for b in reads:
            b.r[k] = v
        for b in writes:
            b.w = {k: v}
            b.r = {}

    def _waits(self, eng, deps):
        out = []
        kn = self.known[eng]
        for k, v in deps.items():
            if eng == "pe" and k == "pe":
                continue
            if kn.get(k, -1) >= v:
                continue
            kn[k] = v
            out.append((k, v))
        return out

    def op(self, eng, fn, reads=(), writes=(), mode="full"):
        deps = self._deps(reads, writes)
        waits = self._waits(eng, deps)
        if eng == "pe":
            if mode != getattr(self, "pe_mode", mode) and self.cnt["pe"] > 0:
                v = self.cnt["pe"]
                if self.known["pe"].get("pe", -1) < v:
                    self.known["pe"]["pe"] = v
                    waits.append(("pe", v))
            self.pe_mode = mode
        self.cnt[eng] += 1
        tok = (eng, self.cnt[eng])
        self.streams[eng].append((waits, fn, tok))
        self._commit(tok, reads, writes)
        self.n_ops += 1
        return tok

    def dma(self, q, fn, reads=(), writes=()):
        deps = self._deps(reads, writes)
        if q == "pool":
            s = self.dma_ip % 10
            self.dma_ip += 1
        else:
            s = 10 + self.dma_i % (N_DMA_SEMS - 10)
            self.dma_i += 1
        if self.dma_last[s] is not None:
            k, v = self.dma_last[s]
            if deps.get(k, -1) < v:
                deps[k] = v
        waits = self._waits(q, deps)
        self.dma_cnt[s] += 16
        tok = (("dma", s), self.dma_cnt[s])
        self.dma_last[s] = tok
        if q == "pool":
            self.last_pool = tok
        self.streams[q].append((waits, fn, tok))
        self._commit(tok, reads, writes)
        self.n_ops += 1
        return tok

    def wait_all(self, eng, toks):
        deps = {}
        for k, v in toks:
            if deps.get(k, -1) < v:
                deps[k] = v
        waits = self._waits(eng, deps)
        self.streams[eng].append((waits, None, None))

    def emit(self):
        nc = self.nc
        with contextlib.ExitStack() as st:
            sems = {}
            for e in ENGS:
                sems[e] = st.enter_context(nc.semaphore("s_" + e))
            for i in range(N_DMA_SEMS):
                sems[("dma", i)] = st.enter_context(nc.semaphore("s_dma%d" % i))
            block = st.enter_context(nc.Block())
            objs = {"pe": block.tensor, "act": block.scalar, "dve": block.vector,
                    "pool": block.gpsimd, "sp": block.sync}

            needed = {e: set() for e in ENGS}
            for e in ENGS:
                for (waits, fn, tok) in self.streams[e]:
                    for (k, v) in waits:
                        if not isinstance(k, tuple):
                            needed[k].add(v)
            rank = {e: {v: i + 1 for i, v in enumerate(sorted(needed[e]))} for e in ENGS}
            self.n_incs = {e: len(needed[e]) for e in ENGS}

            def mk(e):
                def body(eng):
                    for (waits, fn, tok) in self.streams[e]:
                        for (k, v) in waits:
                            if isinstance(k, tuple):
                                eng.wait_ge(sems[k], v)
                            else:
                                eng.wait_ge(sems[k], rank[k][v])
                        if fn is None:
                            continue
                        ins = fn(eng)
                        k, v = tok
                        if isinstance(k, tuple):
                            ins.then_inc(sems[k], 16)
                        elif v in needed[k]:
                            ins.then_inc(sems[k], 1)
                return body
            for e in ENGS:
                if self.streams[e]:
                    objs[e](mk(e))


import contextlib
import numpy as np
import ml_dtypes
import concourse.bass as bass
import concourse.mybir as mybir

F32 = mybir.dt.float32
BF16 = mybir.dt.bfloat16
AF = mybir.ActivationFunctionType
ALU = mybir.AluOpType
BF = ml_dtypes.bfloat16

D = 2048
KC = 16
NEGM = -30000.0
BIG = 1.0e30
EPS = 1e-6
NFC = 10
FCH_M = [64, 64, 64, 64, 128, 64, 64, 128, 128, 16]
FCH_OFF = [0, 64, 128, 192, 256, 384, 448, 512, 640, 768]
NF = 784
NTA = 396
NTB = 512
NT_ = NTA + NTB


def split3(a):
    a = np.asarray(a, np.float64)
    hi = a.astype(BF).astype(np.float64)
    mid = (a - hi).astype(BF).astype(np.float64)
    lo = (a - hi - mid).astype(BF).astype(np.float64)
    return hi.astype(BF), mid.astype(BF), lo.astype(BF)


def host_tables(T, g):
    slopes = 2.0 ** (-8.0 * np.arange(1, 17, dtype=np.float64) / 16)[4 * g:4 * g + 4]
    t = np.arange(T, dtype=np.float64)
    qaug = np.zeros((8, 4, T), BF)
    for h in range(4):
        a, b_, c_ = split3(-slopes[h] * t)
        qaug[0, h], qaug[1, h], qaug[2, h] = a, b_, c_
        a, b_, c_ = split3(np.full(T, 128.0 * slopes[h]))
        qaug[3, h], qaug[4, h], qaug[5, h] = a, b_, c_
        a, b_, c_ = split3(np.full(T, slopes[h]))
        qaug[6, h], qaug[7, h] = a, b_

    def kside(pos):
        k = np.zeros((8, len(pos)), BF)
        k[0:3] = 1.0
        k[3:6] = (pos // 128).astype(BF)
        k[6:8] = (pos % 128).astype(BF)
        return k
    kaug = kside(np.arange(T))
    ncp = np.arange(512)
    cend = np.maximum(16 * (ncp - 1) + 31, 0)
    kcaug = kside(cend)
    M = np.zeros((512, 128), np.float32)
    w = {-1: 1.0, 0: 2.0, 1: 2.0, 2: 2.0, 3: 1.0}
    for npr in range(1, 512):
        n = npr - 1
        for blk in range(128):
            dd = n - 4 * blk
            if dd in w:
                M[npr, blk] = w[dd]
    Mtab = M.reshape(4, 128, 128).transpose(1, 0, 2).astype(BF)
    E4 = np.zeros((128, 32, 128), BF)
    for m in range(32):
        for half in range(2):
            r = 2 * m + half
            for gq in range(2):
                E4[64 * gq + r, m, 64 * half:64 * half + 64] = 1.0
    p = np.arange(128)[:, None]
    q = np.arange(128)[None, :]
    bias_diag = np.where(q - p >= 0, 0.0, NEGM).astype(BF)
    bias_far = np.where(p - q - 1 >= 0, 0.0, NEGM).astype(BF)
    bias_diag = np.tile(bias_diag, (1, 4))
    bias_far = np.tile(bias_far, (1, 4))
    gmask = np.where(q - p >= 0, 1.0, 0.0).astype(BF)
    reset = np.ones((128, 512), BF)
    reset[:, ::128] = 0.0
    qaug = np.ascontiguousarray(qaug.reshape(8, 4, T // 128, 128).transpose(0, 2, 1, 3))
    return dict(qaug=qaug, kaug=kaug, kcaug=kcaug, Mtab=Mtab, E4=E4,
                bias_diag=bias_diag, bias_far=bias_far, gmask=gmask, reset=reset,
                identb=np.eye(128).astype(BF), identf=np.eye(128, dtype=np.float32))


def host_inputs_l1(inp, b, j, T):
    g = j
    x = np.ascontiguousarray(inp["x"][b, :T])
    w_in = inp["w_in"][0]
    b_in = inp["b_in"][0]
    offs = np.cumsum([0, 1024, 256, 256, 256, 256, 256, 256, 48, 1024, 512, 512, 1024, 16, 1024, 2048, 2048])
    o_q, o_ck, o_cv, o_sk, o_sv, o_wk, o_wv, o_g, o_z, o_gq, o_gk, o_gv, o_ga, o_gz = offs[:14]
    fcols = np.concatenate([
        np.arange(o_q + 256 * g, o_q + 256 * g + 256),
        np.arange(o_ck + 64 * g, o_ck + 64 * g + 64), np.arange(o_cv + 64 * g, o_cv + 64 * g + 64),
        np.arange(o_sk + 64 * g, o_sk + 64 * g + 64), np.arange(o_wk + 64 * g, o_wk + 64 * g + 64),
        np.arange(o_gq + 128 * j, o_gq + 128 * j + 128), np.arange(o_gk + 128 * j, o_gk + 128 * j + 128),
        np.arange(o_ga, o_ga + 16)])
    tcols = np.concatenate([
        np.arange(o_sv + 64 * g, o_sv + 64 * g + 64), np.arange(o_wv + 64 * g, o_wv + 64 * g + 64),
        np.arange(o_g + 12 * g, o_g + 12 * g + 12), np.arange(o_z + 256 * g, o_z + 256 * g + 256),
        np.arange(o_gv + 256 * j, o_gv + 256 * j + 256), np.arange(o_gz + 256 * j, o_gz + 256 * j + 256)])
    assert len(fcols) == NF and len(tcols) == NT_
    wf = np.ascontiguousarray(w_in[:, fcols])
    wt = np.ascontiguousarray(w_in[:, tcols])
    bf = np.zeros((128, NFC), np.float32)
    bfl = b_in[fcols]
    for ci in range(NFC):
        bf[:FCH_M[ci], ci] = bfl[FCH_OFF[ci]:FCH_OFF[ci] + FCH_M[ci]]
    bt = np.ascontiguousarray(b_in[tcols][None, :])
    w1kv = np.concatenate([inp["cmp_w1_k"][0].reshape(32, 64, 256).transpose(1, 0, 2),
                           inp["cmp_w1_v"][0].reshape(32, 64, 256).transpose(1, 0, 2)], axis=0)
    w2k = inp["cmp_w2_k"][0].reshape(2, 128, 64).transpose(1, 0, 2)
    w2v = inp["cmp_w2_v"][0].reshape(2, 128, 64).transpose(1, 0, 2)
    posT = np.concatenate([inp["cmp_pos_k"][0].T, inp["cmp_pos_v"][0].T], axis=0)
    d = dict(
        x=x,
        cT=np.ascontiguousarray(inp["c"][b].reshape(16, 128).T),
        w_ada=np.ascontiguousarray(inp["w_ada"][0][:, :4096]),
        b_ada=np.ascontiguousarray(inp["b_ada"][0][:4096].reshape(32, 128).T),
        gainT=np.ascontiguousarray(inp["norm_gain"][0].reshape(16, 128).T),
        wf=wf, wt=wt, bf=bf, bt=bt,
        w1kv=np.ascontiguousarray(w1kv), w2k=np.ascontiguousarray(w2k), w2v=np.ascontiguousarray(w2v),
        posT=np.ascontiguousarray(posT),
        walpha=np.ascontiguousarray(inp["gla_w_alpha"][0][:, 128 * j:128 * j + 128]),
        balpha=np.ascontiguousarray(inp["gla_b_alpha"][0][128 * j:128 * j + 128].reshape(128, 1)),
        ggain=np.ascontiguousarray(np.broadcast_to(inp["gla_norm_gain"][0][None, :], (128, 256))),
    )
    d.update(host_tables(T, g))
    return d


class _Stop(Exception):
    pass


def build_l1(T, debug=False, stage=99):
    NST = T // 512
    NQT = T // 128
    nc = bass.Bass("TRN2", target_bir_lowering=False)

    def din(name, shape, dt=F32):
        return nc.dram_tensor(name, list(shape), dt, kind="ExternalInput").ap()
    x = din("x", [T, D])
    cT_d = din("cT", [128, 16]); wada_d = din("w_ada", [D, 4096]); bada_d = din("b_ada", [128, 32])
    gainT_d = din("gainT", [128, 16])
    wf_d = din("wf", [D, NF]); wt_d = din("wt", [D, NT_]); bf_d = din("bf", [128, NFC]); bt_d = din("bt", [1, NT_])
    w1kv_d = din("w1kv", [128, 32, 256]); w2k_d = din("w2k", [128, 2, 64]); w2v_d = din("w2v", [128, 2, 64])
    posT_d = din("posT", [128, 32])
    walpha_d = din("walpha", [16, 128]); balpha_d = din("balpha", [128, 1]); ggain_d = din("ggain", [128, 256])
    qaug_d = din("qaug", [8, T // 128, 4, 128], BF16); kaug_d = din("kaug", [8, T], BF16); kcaug_d = din("kcaug", [8, 512], BF16)
    Mtab_d = din("Mtab", [128, 4, 128], BF16); E4_d = din("E4", [128, 32, 128], BF16)
    bdiag_d = din("bias_diag", [128, 512], BF16); bfar_d = din("bias_far", [128, 512], BF16)
    gmask_d = din("gmask", [128, 128], BF16); reset_d = din("reset", [128, 512], BF16)
    identb_d = din("identb", [128, 128], BF16); identf_d = din("identf", [128, 128])
    y_d = nc.dram_tensor("y", [T, 512], BF16, kind="ExternalOutput").ap()
    dbg = {}
    if debug:
        dbg["qT"] = nc.dram_tensor("dbg_qT", [72, 4, 4, 128], BF16, kind="ExternalOutput").ap()
        dbg["tokA"] = nc.dram_tensor("dbg_tokA", [128, NTA], F32, kind="ExternalOutput").ap()
        dbg["ocmp"] = nc.dram_tensor("dbg_ocmp", [128, 4, 193], F32, kind="ExternalOutput").ap()
        dbg["selb"] = nc.dram_tensor("dbg_selb", [128, 128], BF16, kind="ExternalOutput").ap()
        dbg["kcT"] = nc.dram_tensor("dbg_kcT", [72, 512], BF16, kind="ExternalOutput").ap()
        dbg["oall"] = nc.dram_tensor("dbg_oall", [128, 3, 4, 65], F32, kind="ExternalOutput").ap()

    P = Prog(nc)
    with contextlib.ExitStack() as st:
        def sb(name, shape, dt=F32):
            return st.enter_context(nc.sbuf_tensor("sb_" + name, list(shape), dt))
        wf_s = sb("wf_s", [128, KC, NF], BF16); wt_s = sb("wt_s", [128, KC, NT_], BF16)
        bf_s = sb("bf_s", [128, NFC]); btb_s = sb("btb_s", [1, NT_], BF16)
        ones_s = sb("ones_s", [1, 128], BF16)
        xs = [sb("xs%d" % i, [128, D]) for i in range(2)]
        xnb = [sb("xnb0", [128, D], BF16)] * 2
        hT = sb("hT", [128, KC, 512], BF16)
        A_s = sb("A_s", [128, 16]); B_s = sb("B_s", [128, 16]); ada_s = sb("ada_s", [128, 32])
        cT_s = sb("cT_s", [128, 16]); gainT_s = sb("gainT_s", [128, 16]); bada_s = sb("bada_s", [128, 32])
        small = sb("small", [128, 64])
        qT = sb("qT", [72, 4, 4, 128], BF16)
        kslcT = sb("kslcT", [72, T], BF16)
        kwinT = sb("kwinT", [72, 1024], BF16)
        vslc = sb("vslc", [128, NQT, 66], BF16)
        vwin = sb("vwin", [128, 8, 66], BF16)
        cmpbuf = sb("cmpbuf", [128, 528], BF16)
        w1kv_s = sb("w1kv_s", [128, 32, 256], BF16)
        w2k_s = sb("w2k_s", [128, 2, 64], BF16); w2v_s = sb("w2v_s", [128, 2, 64], BF16)
        posT_s = sb("posT_s", [128, 32], BF16); posb_s = sb("posb_s", [128, 4])
        kcT = sb("kcT", [72, 512], BF16)
        vcM = sb("vcM", [128, 4, 194], BF16)
        E4_s = sb("E4_s", [128, 32, 128], BF16)
        bdiag_s = sb("bdiag_s", [128, 512], BF16); bfar_s = sb("bfar_s", [128, 512], BF16)
        gmask_s = sb("gmask_s", [128, 128], BF16); reset_s = sb("reset_s", [128, 512], BF16)
        identb = sb("identb", [128, 128], BF16); identf = sb("identf", [128, 128])
        walpha_s = sb("walpha_s", [16, 128], BF16); nbalpha_s = sb("nbalpha_s", [128, 1]); ggain_s = sb("ggain_s", [128, 256])
        gqT = sb("gqT", [128, 512], BF16); gkT = sb("gkT", [128, 512], BF16); gaT = sb("gaT", [16, 512], BF16)
        tokA = [sb("tokA%d" % i, [128, NTA]) for i in range(4)]
        tokB = [sb("tokB%d" % i, [128, 256]) for i in range(4)]
        gvb = [sb("gvb%d" % i, [128, 256], BF16) for i in range(4)]
        e1 = sb("e1", [128, 512]); spl = e1; cs = sb("cs", [128, 512])
        eb = sb("eb", [128, 512]); enb = e1
        qtl = sb("qtl", [128, 512], BF16); ktl = sb("ktl", [128, 512], BF16)
        ktok = sb("ktok", [128, 128], BF16); atm = sb("atm", [128, 128], BF16)
        S_s = sb("S_s", [128, 256]); Sb_s = sb("Sb_s", [128, 256], BF16); stmp = sb("stmp", [128, 256])
        gl1 = sb("gl1", [128, 256]); junk = gl1
        yt = [sb("yt%d" % i, [128, 512], BF16) for i in range(4)]
        hsb = sb("hsb", [128, 4, 32]); hex_ = sb("hex_", [128, 4, 32]); hidT = sb("hidT", [128, 4, 32], BF16)
        vcst = sb("vcst", [32, 64], BF16)
        pT = [sb("pT%d" % i, [128, 512], BF16) for i in range(3)]
        maskc = [sb("maskc%d" % i, [128, 512], BF16) for i in range(2)]
        zeros_b = sb("zeros_b", [128, 512], BF16)
        score = sb("score", [128, 128]); swork = sb("swork", [128, 128]); m8 = sb("m8", [128, 16])
        selb = [sb("selb%d" % i, [128, 128], BF16) for i in range(2)]; selT4 = [[sb("selT4%d_%d" % (i, g_), [128, 4, 128], BF16) for g_ in range(2)] for i in range(2)]
        oT_sb = sb("oT_sb", [65, 2, 512])
        coef = [sb("coef%d" % i, [128, 12]) for i in range(2)]; den = [sb("den%d" % i, [128, 12]) for i in range(2)]; gsig = [sb("gsig%d" % i, [128, 12]) for i in range(2)]
        szn = gl1; acc = [sb("acc%d" % i, [128, 256]) for i in range(2)]

        banks = [st.enter_context(nc.psum_tensor("bank%d" % i, [128, 512], F32)) for i in range(8)]
        bb = [Buf("bank%d" % i) for i in range(8)]

        def B(name):
            return Buf(name)
        b_w = B("weights"); b_tab = B("tables"); b_hT = B("hT"); b_AB = B("AB")
        b_xs = [B("xs0"), B("xs1")]; b_xnb = [B("xnb0")] * 2; b_small = B("small")
        b_qT = B("qT"); b_kslc = [B("kslc%d" % i) for i in range(NQT)]
        b_kwin = [B("kwin%d" % i) for i in range(8)]; b_vslc = [B("vslc%d" % i) for i in range(NQT)]
        b_vwin = [B("vwin%d" % i) for i in range(8)]
        b_cmpbuf = B("cmpbuf"); b_kcT = B("kcT"); b_vcM = B("vcM"); b_hid = B("hid")
        b_gT = B("gT"); b_tokA = [B("tokA%d" % i) for i in range(4)]; b_tokB = [B("tokB%d" % i) for i in range(4)]
        b_gvb = [B("gvb%d" % i) for i in range(4)]
        b_gl = B("glatemps"); b_S = B("S"); b_glc = B("glachunk"); b_yt = [B("yt%d" % i) for i in range(4)]
        b_pT = [B("pT%d" % i) for i in range(3)]; b_maskc = [B("maskc0"), B("maskc1")]
        b_sel = B("sel"); b_selT = [B("selT0"), B("selT1")]; b_selb = [B("selb0"), B("selb1")]; b_oT = B("oTsb"); b_fin = B("fin"); b_finA = [B("finA0"), B("finA1")]

        def chk(k):
            if stage == k:
                raise _Stop()
        try:
            def ld(q, out, in_, writes):
                return P.dma(q, lambda e: e.dma_start(out=out, in_=in_), writes=writes)
            wf_v = wf_d.rearrange("(kc p) n -> p kc n", p=128)
            wt_v = wt_d.rearrange("(kc p) n -> p kc n", p=128)
            for k4 in range(0, KC, 4):
                ld("pool", wf_s[:, k4:k4 + 4, :], wf_v[:, k4:k4 + 4, :], [b_w])
                ld("pool", wt_s[:, k4:k4 + 4, :], wt_v[:, k4:k4 + 4, :], [b_w])
            for pp in range(0, 32, 8):
                ld("pool", w1kv_s[:, pp:pp + 8, :], w1kv_d[:, pp:pp + 8, :], [b_w])
            ld("pool", w2k_s[:], w2k_d, [b_w]); ld("pool", w2v_s[:], w2v_d, [b_w])
            ld("pool", posT_s[:], posT_d, [b_w]); ld("pool", walpha_s[:], walpha_d, [b_w])
            ld("sp", bf_s[:], bf_d, [b_w]); ld("pool", btb_s[:], bt_d, [b_w])
            ld("sp", cT_s[:], cT_d, [b_AB]); ld("sp", gainT_s[:], gainT_d, [b_AB]); ld("sp", bada_s[:], bada_d, [b_AB])
            ld("sp", nbalpha_s[:], balpha_d, [b_w]); ld("sp", ggain_s[:], ggain_d, [b_w])
            ld("sp", E4_s[:], E4_d, [b_tab]); ld("sp", bdiag_s[:], bdiag_d, [b_tab]); ld("sp", bfar_s[:], bfar_d, [b_tab])
            ld("sp", gmask_s[:], gmask_d, [b_tab]); ld("sp", reset_s[:], reset_d, [b_tab])
            ld("sp", identb[:], identb_d, [b_tab]); ld("sp", identf[:], identf_d, [b_tab])
            ld("sp", vcM[:, :, 65:193], Mtab_d, [b_vcM])
            ld("sp", kcT[64:72, :], kcaug_d, [b_kcT])
            for i in range(NQT):
                pass
            ld("sp", kslcT[64:72, :], kaug_d, b_kslc)
            P.op("pool", lambda e: e.memset(ones_s[:], 1.0), writes=[b_w])
            P.op("pool", lambda e: e.memset(zeros_b[:], 0.0), writes=[b_tab])
            for par_ in range(2):
                for g_ in range(2):
                    P.op("pool", lambda e, par_=par_, g_=g_: e.memset(selT4[par_][g_][:], 0.0), writes=[b_selT[par_]])
            P.op("pool", lambda e: e.memset(cmpbuf[:], 0.0), writes=[b_cmpbuf])
            P.op("pool", lambda e: e.memset(S_s[:], 0.0), writes=[b_S])
            P.op("pool", lambda e: e.memset(Sb_s[:], 0.0), writes=[b_S])
            P.op("pool", lambda e: e.memset(vcM[:, :, 0:64], 0.0), writes=[b_vcM])
            P.op("pool", lambda e: e.memset(vcM[:, :, 64:65], 1.0), writes=[b_vcM])
            P.op("pool", lambda e: e.memset(kcT[0:64, :], 0.0), writes=[b_kcT])
            for i in range(NQT):
                P.op("pool", lambda e, i=i: e.memset(vslc[:, i, 64:65], 1.0), writes=[b_vslc[i]])
            for i in range(8):
                P.op("pool", lambda e, i=i: e.memset(vwin[:, i, 64:65], 1.0), writes=[b_vwin[i]])
            P.op("dve", lambda e: e.tensor_scalar(out=nbalpha_s[:], in0=nbalpha_s[:], scalar1=-1.0, scalar2=None, op0=ALU.mult),
                 reads=[b_w], writes=[b_w])

            chk(0)
            wada_v = wada_d.rearrange("(kc p) n -> p kc n", p=128)
            for cc in range(32):
                xb_ = cc % 2
                ld("sp", xs[xb_][:].rearrange("p (kc n) -> p kc n", kc=16), wada_v[:, :, cc * 128:(cc + 1) * 128], [b_xs[xb_]])
                wv = xs[xb_][:].rearrange("p (kc n) -> p kc n", kc=16)
                for kc in range(KC):
                    P.op("pe", lambda e, kc=kc, wv=wv, cc=cc: e.matmul(banks[0][:, cc:cc + 1], lhsT=wv[:, kc, :], rhs=cT_s[:, kc:kc + 1],
                                                                      start=(kc == 0), stop=(kc == KC - 1)),
                         reads=[b_xs[xb_], b_AB], writes=[bb[0]], mode="f32")
            P.op("dve", lambda e: e.tensor_tensor(out=ada_s[:], in0=banks[0][:, 0:32], in1=bada_s[:], op=ALU.add),
                 reads=[bb[0], b_AB], writes=[b_AB])
            P.op("dve", lambda e: e.tensor_scalar(out=A_s[:], in0=ada_s[:, 16:32], scalar1=1.0, scalar2=None, op0=ALU.add),
                 reads=[b_AB], writes=[b_AB])
            P.op("dve", lambda e: e.tensor_tensor(out=A_s[:], in0=A_s[:], in1=gainT_s[:], op=ALU.mult), reads=[b_AB], writes=[b_AB])
            P.op("dve", lambda e: e.tensor_copy(out=B_s[:], in_=ada_s[:, 0:16]), reads=[b_AB], writes=[b_AB])

            chk(1)
            for kv in range(2):
                lo = 64 * kv
                for hc in range(2):
                    col = kv * 2 + hc
                    for p_ in range(32):
                        P.op("pe", lambda e, lo=lo, hc=hc, p_=p_, col=col: e.matmul(
                            banks[1][:, col:col + 1], lhsT=w1kv_s[lo:lo + 64, p_, hc * 128:(hc + 1) * 128],
                            rhs=posT_s[lo:lo + 64, p_:p_ + 1], start=(p_ == 0), stop=(p_ == 31)),
                            reads=[b_w], writes=[bb[1]], mode="r64")
            P.op("dve", lambda e: e.tensor_copy(out=posb_s[:], in_=banks[1][:, 0:4]), reads=[bb[1]], writes=[b_w])

            chk(2)
            x_v = x.rearrange("(t p) d -> t p d", p=128)
            y_v = y_d.rearrange("(t p) n -> t p n", p=128)
            tokx = {}

            def load_x(t):
                if t < NQT:
                    tokx[t] = ld("sp", xs[t % 2][:], x_v[t], [b_xs[t % 2]])
            load_x(0); load_x(1)

            rr = {"mm": 0, "st": 0, "pt": 0}
            fillc = {}

            def getfill(e):
                if "r" not in fillc:
                    fillc["r"] = e.to_reg(NEGM)
                return fillc["r"]

            def mmbank():
                i = rr["mm"] % 2
                rr["mm"] += 1
                return i

            def stbank():
                i = 4 + rr["st"] % 2
                rr["st"] += 1
                return i

            for ST in range(NST):
                for tt in range(4):
                    t = ST * 4 + tt
                    xb_ = t % 2
                    P.op("act", lambda e, xb_=xb_, tt=tt: e.activation(out=xnb[xb_][:], in_=xs[xb_][:], func=AF.Square,
                                                                       accum_out=small[:, tt:tt + 1]),
                         reads=[b_xs[xb_]], writes=[b_xnb[xb_], b_small])
                    P.op("act", lambda e, tt=tt: e.activation(out=small[:, 8 + tt:9 + tt], in_=small[:, tt:tt + 1], func=AF.Ln,
                                                              scale=1.0 / D, bias=EPS),
                         reads=[b_small], writes=[b_small])
                    P.op("act", lambda e, tt=tt: e.activation(out=small[:, 16 + tt:17 + tt], in_=small[:, 8 + tt:9 + tt], func=AF.Exp,
                                                              scale=-0.5),
                         reads=[b_small], writes=[b_small])
                    P.op("dve", lambda e, xb_=xb_, tt=tt: e.tensor_scalar(out=xnb[xb_][:], in0=xs[xb_][:], scalar1=small[:, 16 + tt:17 + tt],
                                                                          scalar2=None, op0=ALU.mult),
                         reads=[b_xs[xb_], b_small], writes=[b_xnb[xb_]])
                    load_x(t + 2)
                    for half in range(2):
                        bk = 2 + half
                        pv = banks[bk][:].bitcast(BF16)
                        for c8 in range(8):
                            kc = half * 8 + c8
                            P.op("pe", lambda e, pv=pv, c8=c8, kc=kc, xb_=xb_: e.transpose(
                                pv[:, c8 * 128:(c8 + 1) * 128], xnb[xb_][:, kc * 128:(kc + 1) * 128], identb[:]),
                                reads=[b_xnb[xb_], b_tab], writes=[bb[bk]], mode="tr")
                        for c8 in range(8):
                            kc = half * 8 + c8
                            eng = "dve" if c8 % 2 == 0 else "pool"
                            eng = "dve"
                            P.op(eng, lambda e, pv=pv, c8=c8, kc=kc, tt=tt: e.tensor_scalar(
                                out=hT[:, kc, tt * 128:(tt + 1) * 128], in0=pv[:, c8 * 128:(c8 + 1) * 128],
                                scalar1=A_s[:, kc:kc + 1], scalar2=B_s[:, kc:kc + 1], op0=ALU.mult, op1=ALU.add),
                                reads=[bb[bk], b_AB], writes=[b_hT])
                chk(3)
                tsl = slice(ST * 512, ST * 512 + 512)
                P.dma("sp", lambda e, ST=ST: e.dma_start(out=qT[64:72, :, :, :], in_=qaug_d[:, ST * 4:ST * 4 + 4, :, :]), writes=[b_qT])
                for ci in range(NFC):
                    M_ = FCH_M[ci]; off = FCH_OFF[ci]
                    bk = mmbank()
                    for kc in range(KC):
                        P.op("pe", lambda e, bk=bk, M_=M_, off=off, kc=kc: e.matmul(
                            banks[bk][0:M_, :], lhsT=wf_s[:, kc, off:off + M_], rhs=hT[:, kc, :], start=(kc == 0), stop=(kc == KC - 1)),
                            reads=[b_w, b_hT], writes=[bb[bk]], mode=("full" if M_ > 64 else "c%d" % (64 if M_ > 32 else 32)))
                    src = banks[bk]
                    if ci < 4:
                        P.op("act", lambda e, src=src, ci=ci: e.activation(out=qT[0:64, :, ci, :], in_=src[0:64, :].rearrange("p (t q) -> p t q", q=128), func=AF.Identity,
                                                                           bias=bf_s[0:64, ci:ci + 1], scale=1.0),
                             reads=[bb[bk], b_w], writes=[b_qT])
                        P.op("dve", lambda e, ci=ci: e.tensor_scalar(out=qT[0:64, :, ci, :], in0=qT[0:64, :, ci, :], scalar1=0.125, scalar2=None,
                                                                     op0=ALU.mult), reads=[b_qT], writes=[b_qT])
                    elif ci == 4:
                        P.op("act", lambda e, src=src: e.activation(out=cmpbuf[:, 16:528], in_=src[:, :], func=AF.Identity,
                                                                    bias=bf_s[:, 4:5], scale=1.0),
                             reads=[bb[bk], b_w], writes=[b_cmpbuf])
                    elif ci == 5:
                        P.op("act", lambda e, src=src, tsl=tsl: e.activation(out=kslcT[0:64, tsl], in_=src[0:64, :], func=AF.Identity,
                                                                             bias=bf_s[0:64, 5:6], scale=1.0),
                             reads=[bb[bk], b_w], writes=[b_kslc[ST * 4 + k] for k in range(4)])
                    elif ci == 6:
                        rs = (ST % 2) * 512
                        P.op("act", lambda e, src=src, rs=rs: e.activation(out=kwinT[0:64, rs:rs + 512], in_=src[0:64, :], func=AF.Identity,
                                                                           bias=bf_s[0:64, 6:7], scale=1.0),
                             reads=[bb[bk], b_w], writes=[b_kwin[(ST % 2) * 4 + k] for k in range(4)])
                        P.dma("sp", lambda e, rs=rs, tsl=tsl: e.dma_start(out=kwinT[64:72, rs:rs + 512], in_=kaug_d[:, tsl]),
                              writes=[b_kwin[(ST % 2) * 4 + k] for k in range(4)])
                    elif ci == 7:
                        P.op("act", lambda e, src=src: e.activation(out=gqT[:], in_=src[:, :], func=AF.Identity, bias=bf_s[:, 7:8], scale=1.0),
                             reads=[bb[bk], b_w], writes=[b_gT])
                    elif ci == 8:
                        P.op("act", lambda e, src=src: e.activation(out=gkT[:], in_=src[:, :], func=AF.Identity, bias=bf_s[:, 8:9], scale=1.0),
                             reads=[bb[bk], b_w], writes=[b_gT])
                    else:
                        P.op("act", lambda e, src=src: e.activation(out=gaT[:], in_=src[0:16, :], func=AF.Identity, bias=bf_s[0:16, 9:10], scale=1.0),
                             reads=[bb[bk], b_w], writes=[b_gT])
                chk(4)
                for tt in range(4):
                    t = ST * 4 + tt
                    for grp in range(2):
                        if grp == 1:
                            chk(44)
                        if tt == 1:
                            chk(46)
                        bk = mmbank()
                        n0, n1 = (0, NTA) if grp == 0 else (NTA, NT_)
                        w_ = n1 - n0
                        for kc in range(KC):
                            P.op("pe", lambda e, bk=bk, kc=kc, tt=tt, n0=n0, n1=n1, w_=w_: e.matmul(
                                banks[bk][:, 0:w_], lhsT=hT[:, kc, tt * 128:(tt + 1) * 128], rhs=wt_s[:, kc, n0:n1], start=(kc == 0), stop=False),
                                reads=[b_w, b_hT], writes=[bb[bk]])
                        chk(41)
                        P.op("pe", lambda e, bk=bk, n0=n0, n1=n1, w_=w_: e.matmul(
                            banks[bk][:, 0:w_], lhsT=ones_s[0:1, :], rhs=btb_s[0:1, n0:n1], start=False, stop=True),
                            reads=[b_w], writes=[bb[bk]], mode="r32")
                        chk(42)
                        if grp == 0:
                            P.op("dve", lambda e, bk=bk, tt=tt: e.tensor_copy(out=tokA[tt][:], in_=banks[bk][:, 0:NTA]),
                                 reads=[bb[bk]], writes=[b_tokA[tt]])
                            chk(43)
                            P.op("dve", lambda e, tt=tt, t=t: e.tensor_copy(out=vslc[:, t, 0:64], in_=tokA[tt][:, 0:64]),
                                 reads=[b_tokA[tt]], writes=[b_vslc[t]])
                            P.op("dve", lambda e, tt=tt, t=t: e.tensor_copy(out=vwin[:, t % 8, 0:64], in_=tokA[tt][:, 64:128]),
                                 reads=[b_tokA[tt]], writes=[b_vwin[t % 8]])
                        else:
                            chk(45)
                            P.op("dve", lambda e, bk=bk, tt=tt: e.tensor_copy(out=tokB[tt][:], in_=banks[bk][:, 256:512]),
                                 reads=[bb[bk]], writes=[b_tokB[tt]])
                            P.op("dve", lambda e, bk=bk, tt=tt: e.tensor_copy(out=gvb[tt][:], in_=banks[bk][:, 0:256]),
                                 reads=[bb[bk]], writes=[b_gvb[tt]])
                if debug and ST == 0:
                    P.dma("pool", lambda e: e.dma_start(out=dbg["qT"], in_=qT[:]), reads=[b_qT])
                    P.dma("pool", lambda e: e.dma_start(out=dbg["tokA"], in_=tokA[0][:]), reads=[b_tokA[0]])

                chk(5)
                bk = mmbank()
                P.op("pe", lambda e, bk=bk: e.matmul(banks[bk][:, :], lhsT=walpha_s[:, :], rhs=gaT[:, :], start=True, stop=True),
                     reads=[b_w, b_gT], writes=[bb[bk]], mode="r32")
                P.op("act", lambda e, bk=bk: e.activation(out=e1[:], in_=banks[bk][:, :], func=AF.Exp, scale=-1.0, bias=nbalpha_s[:, 0:1]),
                     reads=[bb[bk], b_w], writes=[b_gl])
                P.op("act", lambda e: e.activation(out=spl[:], in_=e1[:], func=AF.Ln, bias=1.0, scale=1.0), reads=[b_gl], writes=[b_gl])
                P.op("dve", lambda e: e.tensor_tensor_scan(out=cs[:], data0=reset_s[:], data1=spl[:], initial=0.0, op0=ALU.mult, op1=ALU.add),
                     reads=[b_gl, b_tab], writes=[b_gl])
                P.op("act", lambda e: e.activation(out=eb[:], in_=cs[:], func=AF.Exp, scale=-1.0 / 16), reads=[b_gl], writes=[b_gl])
                P.op("act", lambda e: e.activation(out=enb[:], in_=cs[:], func=AF.Exp, scale=1.0 / 16), reads=[b_gl], writes=[b_gl])
                P.op("dve", lambda e: e.scalar_tensor_tensor(out=qtl[:], in0=gqT[:], scalar=128.0 ** -0.5, in1=eb[:], op0=ALU.mult, op1=ALU.mult),
                     reads=[b_gl, b_gT], writes=[b_gl])
                P.op("dve", lambda e: e.tensor_tensor(out=ktl[:], in0=gkT[:], in1=enb[:], op=ALU.mult), reads=[b_gl, b_gT], writes=[b_gl])
                for tt in range(4):
                    t = ST * 4 + tt
                    cs_ = slice(tt * 128, tt * 128 + 128)
                    P.op("pe", lambda e, cs_=cs_: e.transpose(banks[6][:].bitcast(BF16)[:, 0:128], ktl[:, cs_], identb[:]),
                         reads=[b_gl, b_tab], writes=[bb[6]], mode="tr")
                    P.op("dve", lambda e: e.tensor_copy(out=ktok[:], in_=banks[6][:].bitcast(BF16)[:, 0:128]), reads=[bb[6]], writes=[b_glc])
                    P.op("pe", lambda e, cs_=cs_: e.matmul(banks[7][:, 0:128], lhsT=ktl[:, cs_], rhs=qtl[:, cs_], start=True, stop=True),
                         reads=[b_gl], writes=[bb[7]])
                    P.op("dve", lambda e: e.tensor_tensor(out=atm[:], in0=banks[7][:, 0:128], in1=gmask_s[:], op=ALU.mult),
                         reads=[bb[7], b_tab], writes=[b_glc])
                    bo = mmbank()
                    P.op("pe", lambda e, bo=bo, tt=tt: e.matmul(banks[bo][:, 0:256], lhsT=atm[:], rhs=gvb[tt][:], start=True, stop=False),
                         reads=[b_glc, b_gvb[tt]], writes=[bb[bo]])
                    P.op("pe", lambda e, bo=bo, cs_=cs_: e.matmul(banks[bo][:, 0:256], lhsT=qtl[:, cs_], rhs=Sb_s[:], start=False, stop=True),
                         reads=[b_gl, b_S], writes=[bb[bo]])
                    P.op("pe", lambda e, tt=tt: e.matmul(banks[7][:, 256:512], lhsT=ktok[:], rhs=gvb[tt][:], start=True, stop=True),
                         reads=[b_glc, b_gvb[tt]], writes=[bb[7]])
                    P.op("dve", lambda e: e.tensor_tensor(out=stmp[:], in0=banks[7][:, 256:512], in1=S_s[:], op=ALU.add),
                         reads=[bb[7], b_S], writes=[b_glc])
                    last = tt * 128 + 127
                    P.op("dve", lambda e, last=last: e.tensor_scalar(out=S_s[:], in0=stmp[:], scalar1=eb[:, last:last + 1], scalar2=None, op0=ALU.mult),
                         reads=[b_glc, b_gl], writes=[b_S])
                    P.op("dve", lambda e: e.tensor_copy(out=Sb_s[:], in_=S_s[:]), reads=[b_S], writes=[b_S])
                    yb_ = tt
                    P.op("act", lambda e, bo=bo: e.activation(out=junk[:], in_=banks[bo][:, 0:256], func=AF.Square, accum_out=small[:, 24:25]),
                         reads=[bb[bo]], writes=[b_fin, b_small])
                    P.op("act", lambda e: e.activation(out=small[:, 25:26], in_=small[:, 24:25], func=AF.Ln, scale=1.0 / 256, bias=EPS),
                         reads=[b_small], writes=[b_small])
                    P.op("act", lambda e: e.activation(out=small[:, 26:27], in_=small[:, 25:26], func=AF.Exp, scale=-0.5),
                         reads=[b_small], writes=[b_small])
                    P.op("act", lambda e, tt=tt: e.activation(out=gl1[:], in_=tokB[tt][:], func=AF.Exp, scale=-1.0),
                         reads=[b_tokB[tt]], writes=[b_fin])
                    P.op("dve", lambda e: e.tensor_scalar(out=gl1[:], in0=gl1[:], scalar1=1.0, scalar2=None, op0=ALU.add), reads=[b_fin], writes=[b_fin])
                    P.op("dve", lambda e: e.reciprocal(out=gl1[:], in_=gl1[:]), reads=[b_fin], writes=[b_fin])
                    P.op("dve", lambda e, tt=tt: e.tensor_tensor(out=gl1[:], in0=gl1[:], in1=tokB[tt][:], op=ALU.mult),
                         reads=[b_fin, b_tokB[tt]], writes=[b_fin])
                    P.op("dve", lambda e: e.tensor_tensor(out=gl1[:], in0=gl1[:], in1=ggain_s[:], op=ALU.mult), reads=[b_fin, b_w], writes=[b_fin])
                    P.op("dve", lambda e, bo=bo, yb_=yb_: e.scalar_tensor_tensor(out=yt[yb_][:, 256:512], in0=banks[bo][:, 0:256], scalar=small[:, 26:27],
                                                                                 in1=gl1[:], op0=ALU.mult, op1=ALU.mult),
                         reads=[bb[bo], b_fin, b_small], writes=[b_yt[yb_]])

                chk(6)
                for kv in range(2):
                    lo = 64 * kv
                    bk = mmbank()
                    for hc in range(2):
                        for p_ in range(32):
                            P.op("pe", lambda e, bk=bk, lo=lo, hc=hc, p_=p_: e.matmul(
                                banks[bk][:, hc * 32:(hc + 1) * 32], lhsT=w1kv_s[lo:lo + 64, p_, hc * 128:(hc + 1) * 128],
                                rhs=cmpbuf[lo:lo + 64, p_:p_ + 497:16], start=(p_ == 0), stop=(p_ == 31)),
                                reads=[b_w, b_cmpbuf], writes=[bb[bk]], mode="r64")
                    for hc in range(2):
                        col = kv * 2 + hc
                        P.op("dve", lambda e, bk=bk, hc=hc, col=col: e.tensor_scalar(out=hsb[:, col, :], in0=banks[bk][:, hc * 32:(hc + 1) * 32],
                                                                                     scalar1=posb_s[:, col:col + 1], scalar2=None, op0=ALU.add),
                             reads=[bb[bk], b_w], writes=[b_hid])
                P.op("act", lambda e: e.activation(out=hex_[:], in_=hsb[:], func=AF.Exp, scale=-1.0), reads=[b_hid], writes=[b_hid])
                P.op("dve", lambda e: e.tensor_scalar(out=hex_[:], in0=hex_[:], scalar1=1.0, scalar2=None, op0=ALU.add), reads=[b_hid], writes=[b_hid])
                P.op("dve", lambda e: e.reciprocal(out=hex_[:], in_=hex_[:]), reads=[b_hid], writes=[b_hid])
                P.op("dve", lambda e: e.tensor_tensor(out=hidT[:], in0=hex_[:], in1=hsb[:], op=ALU.mult), reads=[b_hid], writes=[b_hid])
                bk = mmbank()
                for hc in range(2):
                    P.op("pe", lambda e, bk=bk, hc=hc: e.matmul(banks[bk][0:64, 0:32], lhsT=w2k_s[:, hc, :], rhs=hidT[:, hc, :],
                                                                start=(hc == 0), stop=(hc == 1)), reads=[b_w, b_hid], writes=[bb[bk]], mode="c64")
                P.op("dve", lambda e, bk=bk, ST=ST: e.tensor_copy(out=kcT[0:64, ST * 32:ST * 32 + 32], in_=banks[bk][0:64, 0:32]),
                     reads=[bb[bk]], writes=[b_kcT])
                for hc in range(2):
                    P.op("pe", lambda e, bk=bk, hc=hc: e.matmul(banks[bk][0:32, 64:128], lhsT=hidT[:, 2 + hc, :], rhs=w2v_s[:, hc, :],
                                                                start=(hc == 0), stop=(hc == 1)), reads=[b_w, b_hid], writes=[bb[bk]], mode="c32")
                P.op("dve", lambda e, bk=bk: e.tensor_copy(out=vcst[:], in_=banks[bk][0:32, 64:128]), reads=[bb[bk]], writes=[b_hid])
                pr = 32 * (ST % 4)
                P.dma("sp", lambda e, pr=pr, ST=ST: e.dma_start(out=vcM[pr:pr + 32, ST // 4, 0:64], in_=vcst[:]), reads=[b_hid], writes=[b_vcM])
                if ST == 0:
                    P.op("pool", lambda e: e.memset(vcM[0:1, 0, 0:193], 0.0), writes=[b_vcM])
                P.op("dve", lambda e: e.tensor_copy(out=cmpbuf[:, 0:16], in_=cmpbuf[:, 512:528]), reads=[b_cmpbuf], writes=[b_cmpbuf])

                chk(7)
                def run_pipeline(jobs, depth=1):
                    pend = []
                    for jb in jobs:
                        jb[0]()
                        pend.append(jb)
                        if len(pend) > depth:
                            pend.pop(0)[1]()
                    for jb in pend:
                        jb[1]()

                def stage_A(tt):
                    i = ST * 4 + tt
                    par = i % 2
                    qv = qT[:, tt, :, :].rearrange("p h q -> p (h q)")
                    nnt = (8 * i + 7 + 127) // 128
                    jobs = []
                    for nt in range(nnt):
                        mrows = 128
                        st_ = {}

                        def qk(nt=nt, mrows=mrows, st_=st_):
                            sbk = stbank()
                            st_["sbk"] = sbk
                            mk = maskc[nt % 2]
                            P.op("pool", lambda e, mk=mk: e.affine_select(
                                out=mk[:], in_=zeros_b[:], pattern=[[0, 4], [1, 128]], compare_op=ALU.is_ge, fill=getfill(e),
                                base=128 * i - 2048 * nt - 15, channel_multiplier=-16), reads=[b_tab], writes=[b_maskc[nt % 2]])
                            P.op("pe", lambda e: e.matmul(banks[sbk][0:mrows, :], lhsT=kcT[:, nt * 128:nt * 128 + mrows], rhs=qv, start=True, stop=False),
                                 reads=[b_kcT, b_qT], writes=[bb[sbk]])
                            P.op("pe", lambda e: e.matmul(banks[sbk][0:mrows, :], lhsT=identb[:, 0:mrows], rhs=mk[:], start=False, stop=True),
                                 reads=[b_tab, b_maskc[nt % 2]], writes=[bb[sbk]])

                        def rest(nt=nt, mrows=mrows, st_=st_):
                            sbk = st_["sbk"]
                            pb = rr["pt"] % 3
                            rr["pt"] += 1
                            P.op("act", lambda e: e.activation(out=pT[pb][0:mrows, :], in_=banks[sbk][0:mrows, :], func=AF.Exp, scale=1.0),
                                 reads=[bb[sbk]], writes=[b_pT[pb]])
                            for h in range(4):
                                obk = 2 + h // 2
                                oc = (h % 2) * 256
                                P.op("pe", lambda e, obk=obk, oc=oc, h=h: e.matmul(
                                    banks[obk][:, oc:oc + 193], lhsT=pT[pb][0:mrows, h * 128:(h + 1) * 128], rhs=vcM[0:mrows, nt, 0:193],
                                    start=(nt == 0 and h % 2 == 0), stop=(nt == nnt - 1), skip_group_check=True),
                                    reads=[b_pT[pb], b_vcM], writes=[bb[obk]])
                        jobs.append((qk, rest))
                    run_pipeline(jobs)
                    bF = b_finA[par]
                    for h in range(4):
                        obk = 2 + h // 2
                        oc = (h % 2) * 256
                        P.op("dve", lambda e, obk=obk, oc=oc, h=h: e.tensor_scalar(out=den[par][:, h:h + 1], in0=banks[obk][:, oc + 64:oc + 65],
                                                                                  scalar1=1e-30, scalar2=None, op0=ALU.max),
                             reads=[bb[obk]], writes=[bF])
                    P.op("dve", lambda e: e.reciprocal(out=den[par][:, 0:4], in_=den[par][:, 0:4]), reads=[bF], writes=[bF])
                    for h in range(4):
                        obk = 2 + h // 2
                        oc = (h % 2) * 256
                        if h == 0:
                            P.op("dve", lambda e, obk=obk, oc=oc: e.tensor_scalar(out=score[:], in0=banks[obk][:, oc + 65:oc + 193], scalar1=den[par][:, 0:1],
                                                                                 scalar2=None, op0=ALU.mult), reads=[bb[obk], bF], writes=[b_sel])
                        else:
                            P.op("dve", lambda e, obk=obk, oc=oc, h=h: e.scalar_tensor_tensor(out=score[:], in0=banks[obk][:, oc + 65:oc + 193],
                                                                                              scalar=den[par][:, h:h + 1], in1=score[:], op0=ALU.mult, op1=ALU.add),
                                 reads=[bb[obk], bF, b_sel], writes=[b_sel])
                    P.op("dve", lambda e: e.memset(score[:, 0:1], BIG), writes=[b_sel])
                    P.op("dve", lambda e: e.memset(score[:, 2 * i:2 * i + 1], BIG), writes=[b_sel])
                    if i > 0:
                        P.op("dve", lambda e: e.memset(score[0:64, 2 * i - 1:2 * i], BIG), writes=[b_sel])
                    P.op("dve", lambda e: e.memset(score[64:128, 2 * i + 1:2 * i + 2], BIG), writes=[b_sel])
                    P.op("dve", lambda e: e.max(out=m8[:, 0:8], in_=score[:]), reads=[b_sel], writes=[b_sel])
                    P.op("dve", lambda e: e.match_replace(out=swork[:], in_to_replace=m8[:, 0:8], in_values=score[:], imm_value=-BIG),
                         reads=[b_sel], writes=[b_sel])
                    P.op("dve", lambda e: e.max(out=m8[:, 8:16], in_=swork[:]), reads=[b_sel], writes=[b_sel])
                    P.op("dve", lambda e: e.tensor_scalar(out=selb[par][:], in0=score[:], scalar1=m8[:, 15:16], scalar2=NEGM, op0=ALU.is_lt, op1=ALU.mult),
                         reads=[b_sel], writes=[b_selb[par]])
                    P.op("act", lambda e: e.activation(out=gsig[par][:], in_=tokA[tt][:, 128:140], func=AF.Exp, scale=-1.0), reads=[b_tokA[tt]], writes=[bF])
                    P.op("dve", lambda e: e.tensor_scalar(out=gsig[par][:], in0=gsig[par][:], scalar1=1.0, scalar2=None, op0=ALU.add), reads=[bF], writes=[bF])
                    P.op("dve", lambda e: e.reciprocal(out=gsig[par][:], in_=gsig[par][:]), reads=[bF], writes=[bF])
                    for h in range(4):
                        obk = 2 + h // 2
                        oc = (h % 2) * 256
                        P.op("dve", lambda e, h=h: e.tensor_tensor(out=coef[par][:, h:h + 1], in0=gsig[par][:, 3 * h:3 * h + 1], in1=den[par][:, h:h + 1], op=ALU.mult),
                             reads=[bF], writes=[bF])
                        P.op("dve", lambda e, obk=obk, oc=oc, h=h: e.tensor_scalar(out=acc[par][:, h * 64:(h + 1) * 64], in0=banks[obk][:, oc:oc + 64],
                                                                                  scalar1=coef[par][:, h:h + 1], scalar2=None, op0=ALU.mult),
                             reads=[bb[obk], bF], writes=[bF])

                def stage_B(tt):
                    i = ST * 4 + tt
                    par = i % 2
                    bF = b_finA[par]
                    qv = qT[:, tt, :, :].rearrange("p h q -> p (h q)")

                    def mkjob(br, kt, idx, nk):
                        obk = 6 + br
                        if br == 0:
                            kop = kslcT[:, kt * 128:(kt + 1) * 128]; kb = b_kslc[kt]
                            vop = vslc[:, kt, 0:65]; vb = b_vslc[kt]
                        else:
                            rsl = (kt % 8) * 128
                            kop = kwinT[:, rsl:rsl + 128]; kb = b_kwin[kt % 8]
                            vop = vwin[:, kt % 8, 0:65]; vb = b_vwin[kt % 8]
                        extra = []
                        if br == 0:
                            gq_ = (2 * kt) // 64
                            m_ = kt % 32
                            extra.append((E4_s[:, m_, :], selT4[par][gq_][:, :, :].rearrange("p h q -> p (h q)"), [b_tab, b_selT[par]]))
                        if kt == i:
                            extra.append((identb[:], bdiag_s[:], [b_tab]))
                        if br == 1 and kt == i - 4:
                            extra.append((identb[:], bfar_s[:], [b_tab]))
                        st_ = {}

                        def qk():
                            sbk = stbank()
                            st_["sbk"] = sbk
                            P.op("pe", lambda e: e.matmul(banks[sbk][:, :], lhsT=kop, rhs=qv, start=True, stop=(len(extra) == 0)),
                                 reads=[kb, b_qT], writes=[bb[sbk]])
                            for xi, (l_, r_, rb_) in enumerate(extra):
                                P.op("pe", lambda e, l_=l_, r_=r_, lastx=(xi == len(extra) - 1): e.matmul(
                                    banks[sbk][:, :], lhsT=l_, rhs=r_, start=False, stop=lastx), reads=rb_, writes=[bb[sbk]])

                        def rest():
                            sbk = st_["sbk"]
                            pb = rr["pt"] % 3
                            rr["pt"] += 1
                            P.op("act", lambda e: e.activation(out=pT[pb][:], in_=banks[sbk][:, :], func=AF.Exp, scale=1.0),
                                 reads=[bb[sbk]], writes=[b_pT[pb]])
                            P.op("pe", lambda e: e.matmul(banks[obk][0:65, :], lhsT=vop, rhs=pT[pb][:], start=(idx == 0), stop=(idx == nk - 1)),
                                 reads=[b_pT[pb], vb], writes=[bb[obk]])
                        return (qk, rest)
                    kw = list(range(max(0, i - 4), i + 1))
                    ks = list(range(0, i + 1))
                    jobs_w = [mkjob(1, kt, idx, len(kw)) for idx, kt in enumerate(kw)]
                    jobs_s = [mkjob(0, kt, idx, len(ks)) for idx, kt in enumerate(ks)]
                    sbt = mmbank()
                    P.op("pe", lambda e: e.transpose(banks[sbt][:].bitcast(BF16)[:, 0:128], selb[par][:], identb[:]), reads=[b_selb[par], b_tab], writes=[bb[sbt]], mode="tr")
                    for h in range(4):
                        P.op("dve", lambda e, h=h: e.tensor_copy(out=selT4[par][0][0:64, h, :], in_=banks[sbt][:].bitcast(BF16)[0:64, 0:128]),
                             reads=[bb[sbt]], writes=[b_selT[par]])
                        P.op("dve", lambda e, h=h: e.tensor_copy(out=selT4[par][1][64:128, h, :], in_=banks[sbt][:].bitcast(BF16)[64:128, 0:128]),
                             reads=[bb[sbt]], writes=[b_selT[par]])
                    run_pipeline(jobs_w + jobs_s)
                    for br in range(2):
                        obk = 6 + br
                        P.op("dve", lambda e, obk=obk, br=br: e.tensor_copy(out=oT_sb[:, br, :], in_=banks[obk][0:65, :]),
                             reads=[bb[obk]], writes=[b_oT])
                    for br in range(2):
                        tbk = mmbank()
                        for h in range(4):
                            P.op("pe", lambda e, tbk=tbk, br=br, h=h: e.transpose(
                                banks[tbk][:, h * 128:h * 128 + 65], oT_sb[:, br, h * 128:(h + 1) * 128], identf[0:65, 0:65]),
                                reads=[b_oT, b_tab], writes=[bb[tbk]], mode="f32t")
                        for h in range(4):
                            dcol = 4 + br * 4 + h
                            P.op("dve", lambda e, tbk=tbk, h=h, dcol=dcol: e.tensor_scalar(out=den[par][:, dcol:dcol + 1], in0=banks[tbk][:, h * 128 + 64:h * 128 + 65],
                                                                                          scalar1=1e-30, scalar2=None, op0=ALU.max),
                                 reads=[bb[tbk]], writes=[bF])
                            P.op("dve", lambda e, dcol=dcol: e.reciprocal(out=den[par][:, dcol:dcol + 1], in_=den[par][:, dcol:dcol + 1]), reads=[bF], writes=[bF])
                            P.op("dve", lambda e, h=h, br=br, dcol=dcol: e.tensor_tensor(out=coef[par][:, dcol:dcol + 1], in0=gsig[par][:, 3 * h + 1 + br:3 * h + 2 + br],
                                                                                        in1=den[par][:, dcol:dcol + 1], op=ALU.mult), reads=[bF], writes=[bF])
                            P.op("dve", lambda e, tbk=tbk, h=h, dcol=dcol: e.scalar_tensor_tensor(
                                out=acc[par][:, h * 64:(h + 1) * 64], in0=banks[tbk][:, h * 128:h * 128 + 64], scalar=coef[par][:, dcol:dcol + 1],
                                in1=acc[par][:, h * 64:(h + 1) * 64], op0=ALU.mult, op1=ALU.add), reads=[bb[tbk], bF], writes=[bF])
                    yb_ = tt
                    P.op("act", lambda e: e.activation(out=szn[:], in_=tokA[tt][:, 140:396], func=AF.Exp, scale=-1.0), reads=[b_tokA[tt]], writes=[b_fin])
                    P.op("dve", lambda e: e.tensor_scalar(out=szn[:], in0=szn[:], scalar1=1.0, scalar2=None, op0=ALU.add), reads=[b_fin], writes=[b_fin])
                    P.op("dve", lambda e: e.reciprocal(out=szn[:], in_=szn[:]), reads=[b_fin], writes=[b_fin])
                    P.op("dve", lambda e: e.tensor_tensor(out=szn[:], in0=szn[:], in1=tokA[tt][:, 140:396], op=ALU.mult), reads=[b_fin, b_tokA[tt]], writes=[b_fin])
                    P.op("dve", lambda e: e.tensor_tensor(out=yt[yb_][:, 0:256], in0=szn[:], in1=acc[par][:], op=ALU.mult), reads=[b_fin, bF], writes=[b_yt[yb_]])
                    P.dma("pool", lambda e: e.dma_start(out=y_v[i], in_=yt[yb_][:]), reads=[b_yt[yb_]])

                stage_A(0)
                stage_A(1)
                stage_B(0)
                stage_A(2)
                stage_B(1)
                stage_A(3)
                stage_B(2)
                stage_B(3)
        except _Stop:
            pass
        toks = [t_ for t_ in P.dma_last if t_ is not None]
        toks += [(e_, P.cnt[e_]) for e_ in ("pe", "act", "dve", "pool") if P.cnt[e_] > 0]
        for e_ in ("pool", "sp", "act", "dve", "pe"):
            P.wait_all(e_, toks)
        P.emit()
    return nc


import contextlib
import numpy as np
import ml_dtypes
import concourse.bass as bass
import concourse.mybir as mybir

F32 = mybir.dt.float32
BF16 = mybir.dt.bfloat16
AF = mybir.ActivationFunctionType
ALU = mybir.AluOpType
BF = ml_dtypes.bfloat16
D = 2048
KC = 16
EPS = 1e-6
CB = 256
NCB = D // CB


def host_inputs_l2(inp, ys, b, qtr, TQ):
    t0 = qtr * TQ
    ycat = np.concatenate([ys[b * 4 + j][t0:t0 + TQ, 0:256] for j in range(4)] +
                          [ys[b * 4 + j][t0:t0 + TQ, 256:512] for j in range(4)], axis=1)
    w_in = inp["w_in"][0]
    b_in = inp["b_in"][0]
    return dict(
        x2=np.ascontiguousarray(inp["x"][b, t0:t0 + TQ]),
        yin=np.ascontiguousarray(ycat),
        cT2=np.ascontiguousarray(inp["c"][b].reshape(16, 128).T),
        w_ada2=np.ascontiguousarray(inp["w_ada"][0]),
        b_ada2=np.ascontiguousarray(inp["b_ada"][0].reshape(48, 128).T),
        gainT2=np.ascontiguousarray(inp["norm_gain"][0].reshape(16, 128).T),
        wmg=np.ascontiguousarray(w_in[:, 6720:10816]),
        bmg=np.ascontiguousarray(b_in[6720:10816][None, :]),
        wbrn=np.ascontiguousarray(inp["w_br_nsa"][0]),
        wbrg=np.ascontiguousarray(inp["w_br_gla"][0]),
        wout=np.ascontiguousarray(inp["w_out"][0]),
        fgain=np.ascontiguousarray(np.broadcast_to(inp["final_norm_gain"][None, :], (128, D))),
        identb2=np.eye(128).astype(BF), identf2=np.eye(128, dtype=np.float32),
    )


def build_l2(TQ):
    NST = TQ // 512
    nc = bass.Bass("TRN2", target_bir_lowering=False)

    def din(name, shape, dt=F32):
        return nc.dram_tensor(name, list(shape), dt, kind="ExternalInput").ap()
    x = din("x2", [TQ, D]); yin = din("yin", [TQ, D], BF16)
    cT_d = din("cT2", [128, 16]); wada_d = din("w_ada2", [D, 6144]); bada_d = din("b_ada2", [128, 48])
    gainT_d = din("gainT2", [128, 16])
    wmg_d = din("wmg", [D, 4096]); bmg_d = din("bmg", [1, 4096])
    wbrn_d = din("wbrn", [1024, D]); wbrg_d = din("wbrg", [1024, D]); wout_d = din("wout", [D, D])
    fgain_d = din("fgain", [128, D]); identb_d = din("identb2", [128, 128], BF16); identf_d = din("identf2", [128, 128])
    out_d = nc.dram_tensor("out", [TQ, D], F32, kind="ExternalOutput").ap()

    P = Prog(nc)
    with contextlib.ExitStack() as st:
        def sb(name, shape, dt=F32):
            return st.enter_context(nc.sbuf_tensor("s2_" + name, list(shape), dt))
        xs = [sb("xs%d" % i, [128, D]) for i in range(2)]
        xnb = sb("xnb", [128, D], BF16)
        ytile = [sb("ytile%d" % i, [128, D], BF16) for i in range(2)]
        hT = sb("hT", [128, KC, 512], BF16)
        yT = sb("yT", [128, KC, 512], BF16)
        mtok = [sb("mtok%d" % i, [128, D], BF16) for i in range(4)]
        mT = sb("mT", [128, KC, 128], BF16)
        wmgb = [sb("wmgb%d" % i, [128, 2, KC, CB], BF16) for i in range(2)]
        wbrb = [sb("wbrb%d" % i, [128, 2, 8, CB], BF16) for i in range(2)]
        woutb = [sb("woutb%d" % i, [128, KC, 512], BF16) for i in range(2)]
        bmg_s = sb("bmg_s", [1, 4096], BF16); ones_s = sb("ones_s", [1, 128], BF16)
        onesf = sb("onesf", [128, 128])
        gate_bc = sb("gate_bc", [128, D]); fgain_s = sb("fgain_s", [128, D])
        xo = sb("xo", [128, D])
        A_s = sb("A_s", [128, 16]); B_s = sb("B_s", [128, 16]); ada_s = sb("ada_s", [128, 48])
        cT_s = sb("cT_s", [128, 16]); gainT_s = sb("gainT_s", [128, 16]); bada_s = sb("bada_s", [128, 48])
        small = sb("small", [128, 32]); diag = [sb("diag%d" % i, [128, 128]) for i in range(2)]
        identb = sb("identb", [128, 128], BF16); identf = sb("identf", [128, 128])
        g1 = sb("g1", [128, CB]); g2 = sb("g2", [128, CB]); t1 = sb("t1", [128, CB])

        banks = [st.enter_context(nc.psum_tensor("b2ank%d" % i, [128, 512], F32)) for i in range(8)]
        bb = [Buf("bank%d" % i) for i in range(8)]
        B = Buf
        b_w = B("w"); b_AB = B("AB"); b_xs = [B("xs0"), B("xs1")]; b_xnb = B("xnb"); b_small = B("small")
        b_yt = [B("yt0"), B("yt1")]; b_hT = B("hT"); b_yT = B("yT"); b_mtok = [B("mtok%d" % i) for i in range(4)]
        b_mT = B("mT"); b_wmg = [B("wmg0"), B("wmg1")]; b_wbr = [B("wbr0"), B("wbr1")]; b_wout = [B("wout0"), B("wout1")]
        b_gate = B("gate"); b_diag = [B("d0"), B("d1")]; b_g = B("g"); b_xo = B("xo"); b_tab = B("tab")

        def ld(q, out, in_, writes):
            return P.dma(q, lambda e: e.dma_start(out=out, in_=in_), writes=writes)
        ld("sp", cT_s[:], cT_d, [b_AB]); ld("sp", gainT_s[:], gainT_d, [b_AB]); ld("sp", bada_s[:], bada_d, [b_AB])
        ld("sp", fgain_s[:], fgain_d, [b_w]); ld("sp", identb[:], identb_d, [b_tab]); ld("sp", identf[:], identf_d, [b_tab])
        ld("pool", bmg_s[:], bmg_d, [b_w])
        P.op("pool", lambda e: e.memset(ones_s[:], 1.0), writes=[b_w])
        P.op("pool", lambda e: e.memset(onesf[:], 1.0), writes=[b_w])
        wada_v = wada_d.rearrange("(kc p) n -> p kc n", p=128)
        for cc in range(48):
            xb_ = cc % 2
            ld("sp", xs[xb_][:].rearrange("p (kc n) -> p kc n", kc=16), wada_v[:, :, cc * 128:(cc + 1) * 128], [b_xs[xb_]])
            wv = xs[xb_][:].rearrange("p (kc n) -> p kc n", kc=16)
            for kc in range(KC):
                P.op("pe", lambda e, kc=kc, wv=wv, cc=cc: e.matmul(banks[0][:, cc:cc + 1], lhsT=wv[:, kc, :], rhs=cT_s[:, kc:kc + 1],
                                                                  start=(kc == 0), stop=(kc == KC - 1)),
                     reads=[b_xs[xb_], b_AB], writes=[bb[0]])
        P.op("dve", lambda e: e.tensor_tensor(out=ada_s[:], in0=banks[0][:, 0:48], in1=bada_s[:], op=ALU.add),
             reads=[bb[0], b_AB], writes=[b_AB])
        P.op("dve", lambda e: e.tensor_scalar(out=A_s[:], in0=ada_s[:, 16:32], scalar1=1.0, scalar2=None, op0=ALU.add), reads=[b_AB], writes=[b_AB])
        P.op("dve", lambda e: e.tensor_tensor(out=A_s[:], in0=A_s[:], in1=gainT_s[:], op=ALU.mult), reads=[b_AB], writes=[b_AB])
        P.op("dve", lambda e: e.tensor_copy(out=B_s[:], in_=ada_s[:, 0:16]), reads=[b_AB], writes=[b_AB])
        for kc in range(KC):
            d_ = kc % 2
            P.op("dve", lambda e, kc=kc, d_=d_: e.tensor_scalar(out=diag[d_][:], in0=identf[:], scalar1=ada_s[:, 32 + kc:33 + kc], scalar2=None, op0=ALU.mult),
                 reads=[b_AB, b_tab], writes=[b_diag[d_]])
            bk = 1 + kc // 4
            P.op("pe", lambda e, kc=kc, d_=d_, bk=bk: e.matmul(banks[bk][:, (kc % 4) * 128:(kc % 4) * 128 + 128], lhsT=onesf[:], rhs=diag[d_][:],
                                                               start=(kc % 4 == 0), stop=(kc % 4 == 3), skip_group_check=True),
                 reads=[b_w, b_diag[d_]], writes=[bb[bk]])
        for q4 in range(4):
            P.op("dve", lambda e, q4=q4: e.tensor_copy(out=gate_bc[:, q4 * 512:(q4 + 1) * 512], in_=banks[1 + q4][:, :]), reads=[bb[1 + q4]], writes=[b_gate])

        x_v = x.rearrange("(t p) d -> t p d", p=128)
        y_v = yin.rearrange("(t p) d -> t p d", p=128)
        o_v = out_d.rearrange("(t p) d -> t p d", p=128)
        wmg_v = wmg_d.rearrange("(kc p) n -> p kc n", p=128)
        wbrn_v = wbrn_d.rearrange("(kc p) n -> p kc n", p=128)
        wbrg_v = wbrg_d.rearrange("(kc p) n -> p kc n", p=128)
        wout_v = wout_d.rearrange("(kc p) n -> p kc n", p=128)
        rr = {"mm": 0, "w": 0, "wo": 0, "x": 0}

        def mmbank():
            i = rr["mm"] % 4
            rr["mm"] += 1
            return i

        for ST in range(NST):
            for tt in range(4):
                t = ST * 4 + tt
                xb_ = rr["x"] % 2
                rr["x"] += 1
                ld("sp", xs[xb_][:], x_v[t], [b_xs[xb_]])
                ld("sp", ytile[tt % 2][:], y_v[t], [b_yt[tt % 2]])
                P.op("act", lambda e, xb_=xb_, tt=tt: e.activation(out=xnb[:], in_=xs[xb_][:], func=AF.Square, accum_out=small[:, tt:tt + 1]),
                     reads=[b_xs[xb_]], writes=[b_xnb, b_small])
                P.op("act", lambda e, tt=tt: e.activation(out=small[:, 8 + tt:9 + tt], in_=small[:, tt:tt + 1], func=AF.Ln, scale=1.0 / D, bias=EPS),
                     reads=[b_small], writes=[b_small])
                P.op("act", lambda e, tt=tt: e.activation(out=small[:, 16 + tt:17 + tt], in_=small[:, 8 + tt:9 + tt], func=AF.Exp, scale=-0.5),
                     reads=[b_small], writes=[b_small])
                P.op("dve", lambda e, xb_=xb_, tt=tt: e.tensor_scalar(out=xnb[:], in0=xs[xb_][:], scalar1=small[:, 16 + tt:17 + tt], scalar2=None, op0=ALU.mult),
                     reads=[b_xs[xb_], b_small], writes=[b_xnb])
                for src_i, (src, sbuf_, dst, dbuf) in enumerate(((xnb, b_xnb, hT, b_hT), (ytile[tt % 2], b_yt[tt % 2], yT, b_yT))):
                    for half in range(2):
                        bk = 4 + 2 * src_i + half
                        pv = banks[bk][:].bitcast(BF16)
                        for c8 in range(8):
                            kc = half * 8 + c8
                            P.op("pe", lambda e, pv=pv, c8=c8, kc=kc, src=src: e.transpose(pv[:, c8 * 128:(c8 + 1) * 128], src[:, kc * 128:(kc + 1) * 128], identb[:]),
                                 reads=[sbuf_, b_tab], writes=[bb[bk]])
                        if src_i == 0:
                            for c8 in range(8):
                                kc = half * 8 + c8
                                P.op("dve", lambda e, pv=pv, c8=c8, kc=kc, tt=tt: e.tensor_scalar(
                                    out=hT[:, kc, tt * 128:(tt + 1) * 128], in0=pv[:, c8 * 128:(c8 + 1) * 128],
                                    scalar1=A_s[:, kc:kc + 1], scalar2=B_s[:, kc:kc + 1], op0=ALU.mult, op1=ALU.add),
                                    reads=[bb[bk], b_AB], writes=[b_hT])
                        else:
                            P.op("dve", lambda e, pv=pv, half=half, tt=tt: e.tensor_copy(
                                out=yT[:, half * 8:half * 8 + 8, tt * 128:(tt + 1) * 128], in_=pv.rearrange("p (c q) -> p c q", q=128)),
                                reads=[bb[bk]], writes=[b_yT])
            for cb in range(NCB):
                wb = rr["w"] % 2
                rr["w"] += 1
                c0 = cb * CB
                ld("pool", wmgb[wb][:, 0, :, :], wmg_v[:, :, c0:c0 + CB], [b_wmg[wb]])
                ld("pool", wmgb[wb][:, 1, :, :], wmg_v[:, :, 2048 + c0:2048 + c0 + CB], [b_wmg[wb]])
                ld("pool", wbrb[wb][:, 0, :, :], wbrn_v[:, :, c0:c0 + CB], [b_wbr[wb]])
                ld("pool", wbrb[wb][:, 1, :, :], wbrg_v[:, :, c0:c0 + CB], [b_wbr[wb]])
                for tt in range(4):
                    tsl = slice(tt * 128, tt * 128 + 128)
                    bk = mmbank()
                    for gi in range(2):
                        oc = gi * CB
                        for kc in range(KC):
                            P.op("pe", lambda e, bk=bk, oc=oc, gi=gi, kc=kc, tsl=tsl, wb=wb: e.matmul(
                                banks[bk][:, oc:oc + CB], lhsT=hT[:, kc, tsl], rhs=wmgb[wb][:, gi, kc, :],
                                start=(kc == 0 and gi == 0), stop=False, skip_group_check=True),
                                reads=[b_hT, b_wmg[wb]], writes=[bb[bk]])
                        P.op("pe", lambda e, bk=bk, oc=oc, gi=gi, c0=c0: e.matmul(
                            banks[bk][:, oc:oc + CB], lhsT=ones_s[0:1, :], rhs=bmg_s[0:1, gi * 2048 + c0:gi * 2048 + c0 + CB],
                            start=False, stop=True, skip_group_check=True), reads=[b_w], writes=[bb[bk]])
                    bk2 = mmbank()
                    for gi in range(2):
                        oc = gi * CB
                        for kc in range(8):
                            P.op("pe", lambda e, bk2=bk2, oc=oc, gi=gi, kc=kc, tsl=tsl, wb=wb: e.matmul(
                                banks[bk2][:, oc:oc + CB], lhsT=yT[:, gi * 8 + kc, tsl], rhs=wbrb[wb][:, gi, kc, :],
                                start=(kc == 0 and gi == 0), stop=(kc == 7), skip_group_check=True),
                                reads=[b_yT, b_wbr[wb]], writes=[bb[bk2]])
                    P.op("act", lambda e, bk=bk: e.activation(out=g1[:], in_=banks[bk][:, 0:CB], func=AF.Exp, scale=-1.0), reads=[bb[bk]], writes=[b_g])
                    P.op("act", lambda e, bk=bk: e.activation(out=g2[:], in_=banks[bk][:, CB:2 * CB], func=AF.Exp, scale=-1.0), reads=[bb[bk]], writes=[b_g])
                    P.op("dve", lambda e: e.tensor_scalar(out=g1[:], in0=g1[:], scalar1=1.0, scalar2=None, op0=ALU.add), reads=[b_g], writes=[b_g])
                    P.op("dve", lambda e: e.tensor_scalar(out=g2[:], in0=g2[:], scalar1=1.0, scalar2=None, op0=ALU.add), reads=[b_g], writes=[b_g])
                    P.op("dve", lambda e: e.reciprocal(out=g1[:], in_=g1[:]), reads=[b_g], writes=[b_g])
                    P.op("dve", lambda e: e.reciprocal(out=g2[:], in_=g2[:]), reads=[b_g], writes=[b_g])
                    P.op("dve", lambda e, bk2=bk2: e.tensor_tensor(out=t1[:], in0=banks[bk2][:, 0:CB], in1=g1[:], op=ALU.mult), reads=[bb[bk2], b_g], writes=[b_g])
                    P.op("dve", lambda e, bk2=bk2: e.tensor_tensor(out=g2[:], in0=banks[bk2][:, CB:2 * CB], in1=g2[:], op=ALU.mult), reads=[bb[bk2], b_g], writes=[b_g])
                    P.op("dve", lambda e, tt=tt, c0=c0: e.tensor_tensor(out=mtok[tt][:, c0:c0 + CB], in0=t1[:], in1=g2[:], op=ALU.add), reads=[b_g], writes=[b_mtok[tt]])
            for tt in range(4):
                t = ST * 4 + tt
                for half in range(2):
                    bk = 4 + half
                    pv = banks[bk][:].bitcast(BF16)
                    for c8 in range(8):
                        kc = half * 8 + c8
                        P.op("pe", lambda e, pv=pv, c8=c8, kc=kc, tt=tt: e.transpose(pv[:, c8 * 128:(c8 + 1) * 128], mtok[tt][:, kc * 128:(kc + 1) * 128], identb[:]),
                             reads=[b_mtok[tt], b_tab], writes=[bb[bk]])
                    P.op("dve", lambda e, pv=pv, half=half: e.tensor_copy(out=mT[:, half * 8:half * 8 + 8, :], in_=pv.rearrange("p (c q) -> p c q", q=128)),
                         reads=[bb[bk]], writes=[b_mT])
                xb_ = rr["x"] % 2
                rr["x"] += 1
                ld("sp", xs[xb_][:], x_v[t], [b_xs[xb_]])
                for ob in range(4):
                    wo = rr["wo"] % 2
                    rr["wo"] += 1
                    ld("pool", woutb[wo][:], wout_v[:, :, ob * 512:(ob + 1) * 512], [b_wout[wo]])
                    bk = mmbank()
                    for kc in range(KC):
                        P.op("pe", lambda e, bk=bk, kc=kc, wo=wo: e.matmul(banks[bk][:, :], lhsT=mT[:, kc, :], rhs=woutb[wo][:, kc, :],
                                                                          start=(kc == 0), stop=(kc == KC - 1)),
                             reads=[b_mT, b_wout[wo]], writes=[bb[bk]])
                    osl = slice(ob * 512, ob * 512 + 512)
                    P.op("dve", lambda e, bk=bk, osl=osl: e.tensor_tensor(out=xo[:, osl], in0=banks[bk][:, :], in1=gate_bc[:, osl], op=ALU.mult),
                         reads=[bb[bk], b_gate], writes=[b_xo])
                    P.op("dve", lambda e, osl=osl, xb_=xb_: e.tensor_tensor(out=xo[:, osl], in0=xo[:, osl], in1=xs[xb_][:, osl], op=ALU.add),
                         reads=[b_xo, b_xs[xb_]], writes=[b_xo])
                P.op("act", lambda e, xb_=xb_: e.activation(out=xs[xb_][:], in_=xo[:], func=AF.Square, accum_out=small[:, 24:25]),
                     reads=[b_xo], writes=[b_xs[xb_], b_small])
                P.op("act", lambda e: e.activation(out=small[:, 25:26], in_=small[:, 24:25], func=AF.Ln, scale=1.0 / D, bias=EPS), reads=[b_small], writes=[b_small])
                P.op("act", lambda e: e.activation(out=small[:, 26:27], in_=small[:, 25:26], func=AF.Exp, scale=-0.5), reads=[b_small], writes=[b_small])
                P.op("dve", lambda e, xb_=xb_: e.scalar_tensor_tensor(out=xs[xb_][:], in0=xo[:], scalar=small[:, 26:27], in1=fgain_s[:], op0=ALU.mult, op1=ALU.mult),
                     reads=[b_xo, b_small, b_w], writes=[b_xs[xb_]])
                P.dma("sp", lambda e, xb_=xb_, t=t: e.dma_start(out=o_v[t], in_=xs[xb_][:]), reads=[b_xs[xb_]])
        toks = [t_ for t_ in P.dma_last if t_ is not None]
        toks += [(e_, P.cnt[e_]) for e_ in ("pe", "act", "dve", "pool") if P.cnt[e_] > 0]
        for e_ in ("pool", "sp", "act", "dve", "pe"):
            P.wait_all(e_, toks)
        P.emit()
    return nc


def kernel(**inputs):
    from concourse.bass_utils import run_bass_kernel_spmd
    inp = {k_: np.asarray(v) for k_, v in inputs.items()}
    Bn, T, Dm = inp["x"].shape
    nc1 = build_l1(T)
    in_maps = [host_inputs_l1(inp, c // 4, c % 4, T) for c in range(8)]
    res = run_bass_kernel_spmd(nc1, in_maps, core_ids=list(range(8)))
    ys = [np.asarray(r["y"]) for r in res.results]
    del in_maps
    TQ = T // 4
    nc2 = build_l2(TQ)
    in_maps2 = [host_inputs_l2(inp, ys, c // 4, c % 4, TQ) for c in range(8)]
    res2 = run_bass_kernel_spmd(nc2, in_maps2, core_ids=list(range(8)))
    out = np.empty((Bn, T, Dm), np.float32)
    for c in range(8):
        out[c // 4, (c % 4) * TQ:(c % 4 + 1) * TQ] = np.asarray(res2.results[c]["out"])
    return out
```

```python
import contextlib

ENGS = ("pe", "act", "dve", "pool", "sp")
N_DMA_SEMS = 28


class Buf:
    __slots__ = ("name", "w", "r")

    def __init__(self, name=""):
        self.name = name
        self.w = {}
        self.r = {}


class Prog:
    def __init__(self, nc):
        self.nc = nc
        self.streams = {e: [] for e in ENGS}
        self.cnt = {e: 0 for e in ENGS}
        self.known = {e: {} for e in ENGS}
        self.dma_i = 0
        self.dma_ip = 0
        self.dma_cnt = [0] * N_DMA_SEMS
        self.dma_last = [None] * N_DMA_SEMS
        self.n_ops = 0

    def _deps(self, reads, writes):
        deps = {}

        def add(d):
            for k, v in d.items():
                if deps.get(k, -1) < v:
                    deps[k] = v
        for b in reads:
            add(b.w)
        for b in writes:
            add(b.w)
            add(b.r)
        return deps

    def _commit(self, tok, reads, writes):
        k, v = tok
        for b in reads:
            b.r[k] = v
        for b in writes:
            b.w = {k: v}
            b.r = {}

    def _waits(self, eng, deps):
        out = []
        kn = self.known[eng]
        for k, v in deps.items():
            if eng == "pe" and k == "pe":
                continue
            if kn.get(k, -1) >= v:
                continue
            kn[k] = v
            out.append((k, v))
        return out

    def op(self, eng, fn, reads=(), writes=(), mode="full"):
        deps = self._deps(reads, writes)
        waits = self._waits(eng, deps)
        if eng == "pe":
            if mode != getattr(self, "pe_mode", mode) and self.cnt["pe"] > 0:
                v = self.cnt["pe"]
                if self.known["pe"].get("pe", -1) < v:
                    self.known["pe"]["pe"] = v
                    waits.append(("pe", v))
            self.pe_mode = mode
        self.cnt[eng] += 1
        tok = (eng, self.cnt[eng])
        self.streams[eng].append((waits, fn, tok))
        self._commit(tok, reads, writes)
        self.n_ops += 1
        return tok

    def dma(self, q, fn, reads=(), writes=()):
        deps = self._deps(reads, writes)
        if q == "pool":
            s = self.dma_ip % 10
            self.dma_ip += 1
        else:
            s = 10 + self.dma_i % (N_DMA_SEMS - 10)
            self.dma_i += 1
        if self.dma_last[s] is not None:
            k, v = self.dma_last[s]
            if deps.get(k, -1) < v:
                deps[k] = v
        waits = self._waits(q, deps)
        self.dma_cnt[s] += 16
        tok = (("dma", s), self.dma_cnt[s])
        self.dma_last[s] = tok
        if q == "pool":
            self.last_pool = tok
        self.streams[q].append((waits, fn, tok))
        self._commit(tok, reads, writes)
        self.n_ops += 1
        return tok

    def wait_all(self, eng, toks):
        deps = {}
        for k, v in toks:
            if deps.get(k, -1) < v:
                deps[k] = v
        waits = self._waits(eng, deps)
        self.streams[eng].append((waits, None, None))

    def emit(self):
        nc = self.nc
        with contextlib.ExitStack() as st:
            sems = {}
            for e in ENGS:
                sems[e] = st.enter_context(nc.semaphore("s_" + e))
            for i in range(N_DMA_SEMS):
                sems[("dma", i)] = st.enter_context(nc.semaphore("s_dma%d" % i))
            block = st.enter_context(nc.Block())
            objs = {"pe": block.tensor, "act": block.scalar, "dve": block.vector,
                    "pool": block.gpsimd, "sp": block.sync}

            needed = {e: set() for e in ENGS}
            for e in ENGS:
                for (waits, fn, tok) in self.streams[e]:
                    for (k, v) in waits:
                        if not isinstance(k, tuple):
                            needed[k].add(v)
            rank = {e: {v: i + 1 for i, v in enumerate(sorted(needed[e]))} for e in ENGS}
            self.n_incs = {e: len(needed[e]) for e in ENGS}

            def mk(e):
                def body(eng):
                    for (waits, fn, tok) in self.streams[e]:
                        for (k, v) in waits:
                            if isinstance(k, tuple):
                                eng.wait_ge(sems[k], v)
                            else:
                                eng.wait_ge(sems[k], rank[k][v])
                        if fn is None:
                            continue
                        ins = fn(eng)
                        k, v = tok
                        if isinstance(k, tuple):
                            ins.then_inc(sems[k], 16)
                        elif v in needed[k]:
                            ins.then_inc(sems[k], 1)
                return body
            for e in ENGS:
                if self.streams[e]:
                    objs[e](mk(e))


import contextlib
import numpy as np
import ml_dtypes
import concourse.bass as bass
import concourse.mybir as mybir

F32 = mybir.dt.float32
BF16 = mybir.dt.bfloat16
AF = mybir.ActivationFunctionType
ALU = mybir.AluOpType
BF = ml_dtypes.bfloat16

D = 2048
KC = 16
NEGM = -30000.0
BIG = 1.0e30
EPS = 1e-6
NFC = 10
FCH_M = [64, 64, 64, 64, 128, 64, 64, 128, 128, 16]
FCH_OFF = [0, 64, 128, 192, 256, 384, 448, 512, 640, 768]
NF = 784
NTA = 396
NTB = 512
NT_ = NTA + NTB


def split3(a):
    a = np.asarray(a, np.float64)
    hi = a.astype(BF).astype(np.float64)
    mid = (a - hi).astype(BF).astype(np.float64)
    lo = (a - hi - mid).astype(BF).astype(np.float64)
    return hi.astype(BF), mid.astype(BF), lo.astype(BF)


def host_tables(T, g):
    slopes = 2.0 ** (-8.0 * np.arange(1, 17, dtype=np.float64) / 16)[4 * g:4 * g + 4]
    t = np.arange(T, dtype=np.float64)
    qaug = np.zeros((8, 4, T), BF)
    for h in range(4):
        a, b_, c_ = split3(-slopes[h] * t)
        qaug[0, h], qaug[1, h], qaug[2, h] = a, b_, c_
        a, b_, c_ = split3(np.full(T, 128.0 * slopes[h]))
        qaug[3, h], qaug[4, h], qaug[5, h] = a, b_, c_
        a, b_, c_ = split3(np.full(T, slopes[h]))
        qaug[6, h], qaug[7, h] = a, b_

    def kside(pos):
        k = np.zeros((8, len(pos)), BF)
        k[0:3] = 1.0
        k[3:6] = (pos // 128).astype(BF)
        k[6:8] = (pos % 128).astype(BF)
        return k
    kaug = kside(np.arange(T))
    ncp = np.arange(512)
    cend = np.maximum(16 * (ncp - 1) + 31, 0)
    kcaug = kside(cend)
    M = np.zeros((512, 128), np.float32)
    w = {-1: 1.0, 0: 2.0, 1: 2.0, 2: 2.0, 3: 1.0}
    for npr in range(1, 512):
        n = npr - 1
        for blk in range(128):
            dd = n - 4 * blk
            if dd in w:
                M[npr, blk] = w[dd]
    Mtab = M.reshape(4, 128, 128).transpose(1, 0, 2).astype(BF)
    E4 = np.zeros((128, 32, 128), BF)
    for m in range(32):
        for half in range(2):
            r = 2 * m + half
            for gq in range(2):
                E4[64 * gq + r, m, 64 * half:64 * half + 64] = 1.0
    p = np.arange(128)[:, None]
    q = np.arange(128)[None, :]
    bias_diag = np.where(q - p >= 0, 0.0, NEGM).astype(BF)
    bias_far = np.where(p - q - 1 >= 0, 0.0, NEGM).astype(BF)
    bias_diag = np.tile(bias_diag, (1, 4))
    bias_far = np.tile(bias_far, (1, 4))
    gmask = np.where(q - p >= 0, 1.0, 0.0).astype(BF)
    reset = np.ones((128, 512), BF)
    reset[:, ::128] = 0.0
    qaug = np.ascontiguousarray(qaug.reshape(8, 4, T // 128, 128).transpose(0, 2, 1, 3))
    return dict(qaug=qaug, kaug=kaug, kcaug=kcaug, Mtab=Mtab, E4=E4,
                bias_diag=bias_diag, bias_far=bias_far, gmask=gmask, reset=reset,
                identb=np.eye(128).astype(BF), identf=np.eye(128, dtype=np.float32))


def host_inputs_l1(inp, b, j, T):
    g = j
    x = np.ascontiguousarray(inp["x"][b, :T])
    w_in = inp["w_in"][0]
    b_in = inp["b_in"][0]
    offs = np.cumsum([0, 1024, 256, 256, 256, 256, 256, 256, 48, 1024, 512, 512, 1024, 16, 1024, 2048, 2048])
    o_q, o_ck, o_cv, o_sk, o_sv, o_wk, o_wv, o_g, o_z, o_gq, o_gk, o_gv, o_ga, o_gz = offs[:14]
    fcols = np.concatenate([
        np.arange(o_q + 256 * g, o_q + 256 * g + 256),
        np.arange(o_ck + 64 * g, o_ck + 64 * g + 64), np.arange(o_cv + 64 * g, o_cv + 64 * g + 64),
        np.arange(o_sk + 64 * g, o_sk + 64 * g + 64), np.arange(o_wk + 64 * g, o_wk + 64 * g + 64),
        np.arange(o_gq + 128 * j, o_gq + 128 * j + 128), np.arange(o_gk + 128 * j, o_gk + 128 * j + 128),
        np.arange(o_ga, o_ga + 16)])
    tcols = np.concatenate([
        np.arange(o_sv + 64 * g, o_sv + 64 * g + 64), np.arange(o_wv + 64 * g, o_wv + 64 * g + 64),
        np.arange(o_g + 12 * g, o_g + 12 * g + 12), np.arange(o_z + 256 * g, o_z + 256 * g + 256),
        np.arange(o_gv + 256 * j, o_gv + 256 * j + 256), np.arange(o_gz + 256 * j, o_gz + 256 * j + 256)])
    assert len(fcols) == NF and len(tcols) == NT_
    wf = np.ascontiguousarray(w_in[:, fcols])
    wt = np.ascontiguousarray(w_in[:, tcols])
    bf = np.zeros((128, NFC), np.float32)
    bfl = b_in[fcols]
    for ci in range(NFC):
        bf[:FCH_M[ci], ci] = bfl[FCH_OFF[ci]:FCH_OFF[ci] + FCH_M[ci]]
    bt = np.ascontiguousarray(b_in[tcols][None, :])
    w1kv = np.concatenate([inp["cmp_w1_k"][0].reshape(32, 64, 256).transpose(1, 0, 2),
                           inp["cmp_w1_v"][0].reshape(32, 64, 256).transpose(1, 0, 2)], axis=0)
    w2k = inp["cmp_w2_k"][0].reshape(2, 128, 64).transpose(1, 0, 2)
    w2v = inp["cmp_w2_v"][0].reshape(2, 128, 64).transpose(1, 0, 2)
    posT = np.concatenate([inp["cmp_pos_k"][0].T, inp["cmp_pos_v"][0].T], axis=0)
    d = dict(
        x=x,
        cT=np.ascontiguousarray(inp["c"][b].reshape(16, 128).T),
        w_ada=np.ascontiguousarray(inp["w_ada"][0][:, :4096]),
        b_ada=np.ascontiguousarray(inp["b_ada"][0][:4096].reshape(32, 128).T),
        gainT=np.ascontiguousarray(inp["norm_gain"][0].reshape(16, 128).T),
        wf=wf, wt=wt, bf=bf, bt=bt,
        w1kv=np.ascontiguousarray(w1kv), w2k=np.ascontiguousarray(w2k), w2v=np.ascontiguousarray(w2v),
        posT=np.ascontiguousarray(posT),
        walpha=np.ascontiguousarray(inp["gla_w_alpha"][0][:, 128 * j:128 * j + 128]),
        balpha=np.ascontiguousarray(inp["gla_b_alpha"][0][128 * j:128 * j + 128].reshape(128, 1)),
        ggain=np.ascontiguousarray(np.broadcast_to(inp["gla_norm_gain"][0][None, :], (128, 256))),
    )
    d.update(host_tables(T, g))
    return d


class _Stop(Exception):
    pass


def build_l1(T, debug=False, stage=99):
    NST = T // 512
    NQT = T // 128
    nc = bass.Bass("TRN2", target_bir_lowering=False)

    def din(name, shape, dt=F32):
        return nc.dram_tensor(name, list(shape), dt, kind="ExternalInput").ap()
    x = din("x", [T, D])
    cT_d = din("cT", [128, 16]); wada_d = din("w_ada", [D, 4096]); bada_d = din("b_ada", [128, 32])
    gainT_d = din("gainT", [128, 16])
    wf_d = din("wf", [D, NF]); wt_d = din("wt", [D, NT_]); bf_d = din("bf", [128, NFC]); bt_d = din("bt", [1, NT_])
    w1kv_d = din("w1kv", [128, 32, 256]); w2k_d = din("w2k", [128, 2, 64]); w2v_d = din("w2v", [128, 2, 64])
    posT_d = din("posT", [128, 32])
    walpha_d = din("walpha", [16, 128]); balpha_d = din("balpha", [128, 1]); ggain_d = din("ggain", [128, 256])
    qaug_d = din("qaug", [8, T // 128, 4, 128], BF16); kaug_d = din("kaug", [8, T], BF16); kcaug_d = din("kcaug", [8, 512], BF16)
    Mtab_d = din("Mtab", [128, 4, 128], BF16); E4_d = din("E4", [128, 32, 128], BF16)
    bdiag_d = din("bias_diag", [128, 512], BF16); bfar_d = din("bias_far", [128, 512], BF16)
    gmask_d = din("gmask", [128, 128], BF16); reset_d = din("reset", [128, 512], BF16)
    identb_d = din("identb", [128, 128], BF16); identf_d = din("identf", [128, 128])
    y_d = nc.dram_tensor("y", [T, 512], BF16, kind="ExternalOutput").ap()
    dbg = {}
    if debug:
        dbg["qT"] = nc.dram_tensor("dbg_qT", [72, 4, 4, 128], BF16, kind="ExternalOutput").ap()
        dbg["tokA"] = nc.dram_tensor("dbg_tokA", [128, NTA], F32, kind="ExternalOutput").ap()
        dbg["ocmp"] = nc.dram_tensor("dbg_ocmp", [128, 4, 193], F32, kind="ExternalOutput").ap()
        dbg["selb"] = nc.dram_tensor("dbg_selb", [128, 128], BF16, kind="ExternalOutput").ap()
        dbg["kcT"] = nc.dram_tensor("dbg_kcT", [72, 512], BF16, kind="ExternalOutput").ap()
        dbg["oall"] = nc.dram_tensor("dbg_oall", [128, 3, 4, 65], F32, kind="ExternalOutput").ap()

    P = Prog(nc)
    with contextlib.ExitStack() as st:
        def sb(name, shape, dt=F32):
            return st.enter_context(nc.sbuf_tensor("sb_" + name, list(shape), dt))
        wf_s = sb("wf_s", [128, KC, NF], BF16); wt_s = sb("wt_s", [128, KC, NT_], BF16)
        bf_s = sb("bf_s", [128, NFC]); btb_s = sb("btb_s", [1, NT_], BF16)
        ones_s = sb("ones_s", [1, 128], BF16)
        xs = [sb("xs%d" % i, [128, D]) for i in range(2)]
        xnb = [sb("xnb%d" % i, [128, D], BF16) for i in range(2)]
        hT = sb("hT", [128, KC, 512], BF16)
        A_s = sb("A_s", [128, 16]); B_s = sb("B_s", [128, 16]); ada_s = sb("ada_s", [128, 32])
        cT_s = sb("cT_s", [128, 16]); gainT_s = sb("gainT_s", [128, 16]); bada_s = sb("bada_s", [128, 32])
        small = sb("small", [128, 64])
        qT = sb("qT", [72, 4, 4, 128], BF16)
        kslcT = sb("kslcT", [72, T], BF16)
        kwinT = sb("kwinT", [72, 1024], BF16)
        vslc = sb("vslc", [128, NQT, 66], BF16)
        vwin = sb("vwin", [128, 8, 66], BF16)
        cmpbuf = sb("cmpbuf", [128, 528], BF16)
        w1kv_s = sb("w1kv_s", [128, 32, 256], BF16)
        w2k_s = sb("w2k_s", [128, 2, 64], BF16); w2v_s = sb("w2v_s", [128, 2, 64], BF16)
        posT_s = sb("posT_s", [128, 32], BF16); posb_s = sb("posb_s", [128, 4])
        kcT = sb("kcT", [72, 512], BF16)
        vcM = sb("vcM", [128, 4, 194], BF16)
        E4_s = sb("E4_s", [128, 32, 128], BF16)
        bdiag_s = sb("bdiag_s", [128, 512], BF16); bfar_s = sb("bfar_s", [128, 512], BF16)
        gmask_s = sb("gmask_s", [128, 128], BF16); reset_s = sb("reset_s", [128, 512], BF16)
        identb = sb("identb", [128, 128], BF16); identf = sb("identf", [128, 128])
        walpha_s = sb("walpha_s", [16, 128], BF16); nbalpha_s = sb("nbalpha_s", [128, 1]); ggain_s = sb("ggain_s", [128, 256])
        gqT = sb("gqT", [128, 512], BF16); gkT = sb("gkT", [128, 512], BF16); gaT = sb("gaT", [16, 512], BF16)
        tokA = [sb("tokA%d" % i, [128, NTA]) for i in range(4)]
        tokB = [sb("tokB%d" % i, [128, 256]) for i in range(4)]
        gvb = [sb("gvb%d" % i, [128, 256], BF16) for i in range(4)]
        e1 = sb("e1", [128, 512]); spl = e1; cs = sb("cs", [128, 512])
        eb = sb("eb", [128, 512]); enb = e1
        qtl = sb("qtl", [128, 512], BF16); ktl = sb("ktl", [128, 512], BF16)
        ktok = sb("ktok", [128, 128], BF16); atm = sb("atm", [128, 128], BF16)
        S_s = sb("S_s", [128, 256]); Sb_s = sb("Sb_s", [128, 256], BF16); stmp = sb("stmp", [128, 256])
        gl1 = sb("gl1", [128, 256]); junk = gl1
        yt = [sb("yt%d" % i, [128, 512], BF16) for i in range(4)]
        hsb = sb("hsb", [128, 4, 32]); hex_ = sb("hex_", [128, 4, 32]); hidT = sb("hidT", [128, 4, 32], BF16)
        vcst = sb("vcst", [32, 64], BF16)
        pT = [sb("pT%d" % i, [128, 512], BF16) for i in range(3)]
        maskc = [sb("maskc%d" % i, [128, 512], BF16) for i in range(2)]
        zeros_b = sb("zeros_b", [128, 512], BF16)
        score = sb("score", [128, 128]); swork = sb("swork", [128, 128]); m8 = sb("m8", [128, 16])
        selb = [sb("selb%d" % i, [128, 128], BF16) for i in range(2)]; selT4 = [[sb("selT4%d_%d" % (i, g_), [128, 4, 128], BF16) for g_ in range(2)] for i in range(2)]
        oT_sb = None
        coef = [sb("coef%d" % i, [128, 12]) for i in range(2)]; den = [sb("den%d" % i, [128, 12]) for i in range(2)]; gsig = [sb("gsig%d" % i, [128, 12]) for i in range(2)]
        szn = gl1; acc = [sb("acc%d" % i, [128, 256]) for i in range(2)]

        banks = [st.enter_context(nc.psum_tensor("bank%d" % i, [128, 512], F32)) for i in range(8)]
        bb = [Buf("bank%d" % i) for i in range(8)]

        def B(name):
            return Buf(name)
        b_w = B("weights"); b_tab = B("tables"); b_hT = B("hT"); b_AB = B("AB")
        b_xs = [B("xs0"), B("xs1")]; b_xnb = [B("xnb0"), B("xnb1")]; b_small = B("small"); b_sm = [B("sm%d" % i) for i in range(4)]
        b_qT = B("qT"); b_kslc = [B("kslc%d" % i) for i in range(NQT)]
        b_kwin = [B("kwin%d" % i) for i in range(8)]; b_vslc = [B("vslc%d" % i) for i in range(NQT)]
        b_vwin = [B("vwin%d" % i) for i in range(8)]
        b_cmpbuf = B("cmpbuf"); b_kcT = B("kcT"); b_vcM = B("vcM"); b_hid = B("hid")
        b_gT = B("gT"); b_tokA = [B("tokA%d" % i) for i in range(4)]; b_tokB = [B("tokB%d" % i) for i in range(4)]
        b_gvb = [B("gvb%d" % i) for i in range(4)]
        b_gl = B("glatemps"); b_S = B("S"); b_glc = B("glachunk"); b_yt = [B("yt%d" % i) for i in range(4)]
        b_pT = [B("pT%d" % i) for i in range(3)]; b_maskc = [B("maskc0"), B("maskc1")]
        b_sel = B("sel"); b_selT = [B("selT0"), B("selT1")]; b_selb = [B("selb0"), B("selb1")]; b_oT = B("oTsb"); b_fin = B("fin"); b_finA = [B("finA0"), B("finA1")]

        def chk(k):
            if stage == k:
                raise _Stop()
        try:
            def ld(q, out, in_, writes):
                return P.dma(q, lambda e: e.dma_start(out=out, in_=in_), writes=writes)
            wf_v = wf_d.rearrange("(kc p) n -> p kc n", p=128)
            wt_v = wt_d.rearrange("(kc p) n -> p kc n", p=128)
            for k4 in range(0, KC, 4):
                ld("pool", wf_s[:, k4:k4 + 4, :], wf_v[:, k4:k4 + 4, :], [b_w])
                ld("pool", wt_s[:, k4:k4 + 4, :], wt_v[:, k4:k4 + 4, :], [b_w])
            for pp in range(0, 32, 8):
                ld("pool", w1kv_s[:, pp:pp + 8, :], w1kv_d[:, pp:pp + 8, :], [b_w])
            ld("pool", w2k_s[:], w2k_d, [b_w]); ld("pool", w2v_s[:], w2v_d, [b_w])
            ld("pool", posT_s[:], posT_d, [b_w]); ld("pool", walpha_s[:], walpha_d, [b_w])
            ld("sp", bf_s[:], bf_d, [b_w]); ld("pool", btb_s[:], bt_d, [b_w])
            ld("sp", cT_s[:], cT_d, [b_AB]); ld("sp", gainT_s[:], gainT_d, [b_AB]); ld("sp", bada_s[:], bada_d, [b_AB])
            ld("sp", nbalpha_s[:], balpha_d, [b_w]); ld("sp", ggain_s[:], ggain_d, [b_w])
            ld("sp", E4_s[:], E4_d, [b_tab]); ld("sp", bdiag_s[:], bdiag_d, [b_tab]); ld("sp", bfar_s[:], bfar_d, [b_tab])
            ld("sp", gmask_s[:], gmask_d, [b_tab]); ld("sp", reset_s[:], reset_d, [b_tab])
            ld("sp", identb[:], identb_d, [b_tab]); ld("sp", identf[:], identf_d, [b_tab])
            ld("sp", vcM[:, :, 65:193], Mtab_d, [b_vcM])
            ld("sp", kcT[64:72, :], kcaug_d, [b_kcT])
            for i in range(NQT):
                pass
            ld("sp", kslcT[64:72, :], kaug_d, b_kslc)
            P.op("pool", lambda e: e.memset(ones_s[:], 1.0), writes=[b_w])
            P.op("pool", lambda e: e.memset(zeros_b[:], 0.0), writes=[b_tab])
            for par_ in range(2):
                for g_ in range(2):
                    P.op("pool", lambda e, par_=par_, g_=g_: e.memset(selT4[par_][g_][:], 0.0), writes=[b_selT[par_]])
            P.op("pool", lambda e: e.memset(cmpbuf[:], 0.0), writes=[b_cmpbuf])
            P.op("pool", lambda e: e.memset(S_s[:], 0.0), writes=[b_S])
            P.op("pool", lambda e: e.memset(Sb_s[:], 0.0), writes=[b_S])
            P.op("pool", lambda e: e.memset(vcM[:, :, 0:64], 0.0), writes=[b_vcM])
            P.op("pool", lambda e: e.memset(vcM[:, :, 64:65], 1.0), writes=[b_vcM])
            P.op("pool", lambda e: e.memset(kcT[0:64, :], 0.0), writes=[b_kcT])
            for i in range(NQT):
                P.op("pool", lambda e, i=i: e.memset(vslc[:, i, 64:65], 1.0), writes=[b_vslc[i]])
            for i in range(8):
                P.op("pool", lambda e, i=i: e.memset(vwin[:, i, 64:65], 1.0), writes=[b_vwin[i]])
            P.op("dve", lambda e: e.tensor_scalar(out=nbalpha_s[:], in0=nbalpha_s[:], scalar1=-1.0, scalar2=None, op0=ALU.mult),
                 reads=[b_w], writes=[b_w])

            chk(0)
            wada_v = wada_d.rearrange("(kc p) n -> p kc n", p=128)
            for cc in range(32):
                xb_ = cc % 2
                ld("sp", xs[xb_][:].rearrange("p (kc n) -> p kc n", kc=16), wada_v[:, :, cc * 128:(cc + 1) * 128], [b_xs[xb_]])
                wv = xs[xb_][:].rearrange("p (kc n) -> p kc n", kc=16)
                for kc in range(KC):
                    P.op("pe", lambda e, kc=kc, wv=wv, cc=cc: e.matmul(banks[0][:, cc:cc + 1], lhsT=wv[:, kc, :], rhs=cT_s[:, kc:kc + 1],
                                                                      start=(kc == 0), stop=(kc == KC - 1)),
                         reads=[b_xs[xb_], b_AB], writes=[bb[0]], mode="f32")
            P.op("dve", lambda e: e.tensor_tensor(out=ada_s[:], in0=banks[0][:, 0:32], in1=bada_s[:], op=ALU.add),
                 reads=[bb[0], b_AB], writes=[b_AB])
            P.op("dve", lambda e: e.tensor_scalar(out=A_s[:], in0=ada_s[:, 16:32], scalar1=1.0, scalar2=None, op0=ALU.add),
                 reads=[b_AB], writes=[b_AB])
            P.op("dve", lambda e: e.tensor_tensor(out=A_s[:], in0=A_s[:], in1=gainT_s[:], op=ALU.mult), reads=[b_AB], writes=[b_AB])
            P.op("dve", lambda e: e.tensor_copy(out=B_s[:], in_=ada_s[:, 0:16]), reads=[b_AB], writes=[b_AB])

            chk(1)
            for kv in range(2):
                lo = 64 * kv
                for hc in range(2):
                    col = kv * 2 + hc
                    for p_ in range(32):
                        P.op("pe", lambda e, lo=lo, hc=hc, p_=p_, col=col: e.matmul(
                            banks[1][:, col:col + 1], lhsT=w1kv_s[lo:lo + 64, p_, hc * 128:(hc + 1) * 128],
                            rhs=posT_s[lo:lo + 64, p_:p_ + 1], start=(p_ == 0), stop=(p_ == 31)),
                            reads=[b_w], writes=[bb[1]], mode="r64")
            P.op("dve", lambda e: e.tensor_copy(out=posb_s[:], in_=banks[1][:, 0:4]), reads=[bb[1]], writes=[b_w])

            chk(2)
            x_v = x.rearrange("(t p) d -> t p d", p=128)
            y_v = y_d.rearrange("(t p) n -> t p n", p=128)
            tokx = {}

            def load_x(t):
                if t < NQT:
                    tokx[t] = ld("sp", xs[t % 2][:], x_v[t], [b_xs[t % 2]])
            load_x(0); load_x(1)

            rr = {"mm": 0, "st": 0, "pt": 0}
            fillc = {}

            def getfill(e):
                if "r" not in fillc:
                    fillc["r"] = e.to_reg(NEGM)
                return fillc["r"]

            def mmbank():
                i = rr["mm"] % 2
                rr["mm"] += 1
                return i

            def stbank():
                i = 4 + rr["st"] % 2
                rr["st"] += 1
                return i

            def norm_tile(ST, tt):
                t = ST * 4 + tt
                xb_ = t % 2
                P.op("act", lambda e, xb_=xb_, tt=tt: e.activation(out=xnb[xb_][:], in_=xs[xb_][:], func=AF.Square,
                                                                   accum_out=small[:, tt:tt + 1]),
                     reads=[b_xs[xb_]], writes=[b_xnb[xb_], b_sm[tt]])
                P.op("act", lambda e, tt=tt: e.activation(out=small[:, 8 + tt:9 + tt], in_=small[:, tt:tt + 1], func=AF.Ln,
                                                          scale=1.0 / D, bias=EPS),
                     reads=[b_sm[tt]], writes=[b_sm[tt]])
                P.op("act", lambda e, tt=tt: e.activation(out=small[:, 16 + tt:17 + tt], in_=small[:, 8 + tt:9 + tt], func=AF.Exp,
                                                          scale=-0.5),
                     reads=[b_sm[tt]], writes=[b_sm[tt]])
                P.op("dve", lambda e, xb_=xb_, tt=tt: e.tensor_scalar(out=xnb[xb_][:], in0=xs[xb_][:], scalar1=small[:, 16 + tt:17 + tt],
                                                                      scalar2=None, op0=ALU.mult),
                     reads=[b_xs[xb_], b_sm[tt]], writes=[b_xnb[xb_]])
                load_x(t + 2)
                for half in range(2):
                    bk = 2 + half
                    pv = banks[bk][:].bitcast(BF16)
                    for c8 in range(8):
                        kc = half * 8 + c8
                        P.op("pe", lambda e, pv=pv, c8=c8, kc=kc, xb_=xb_: e.transpose(
                            pv[:, c8 * 128:(c8 + 1) * 128], xnb[xb_][:, kc * 128:(kc + 1) * 128], identb[:]),
                            reads=[b_xnb[xb_], b_tab], writes=[bb[bk]], mode="tr")
                    for c8 in range(8):
                        kc = half * 8 + c8
                        eng = "dve" if c8 % 2 == 0 else "pool"
                        eng = "dve"
                        P.op(eng, lambda e, pv=pv, c8=c8, kc=kc, tt=tt: e.tensor_scalar(
                            out=hT[:, kc, tt * 128:(tt + 1) * 128], in0=pv[:, c8 * 128:(c8 + 1) * 128],
                            scalar1=A_s[:, kc:kc + 1], scalar2=B_s[:, kc:kc + 1], op0=ALU.mult, op1=ALU.add),
                            reads=[bb[bk], b_AB], writes=[b_hT])

            for tt_ in range(4):
                norm_tile(0, tt_)
            for ST in range(NST):
                chk(3)
                tsl = slice(ST * 512, ST * 512 + 512)
                P.dma("sp", lambda e, ST=ST: e.dma_start(out=qT[64:72, :, :, :], in_=qaug_d[:, ST * 4:ST * 4 + 4, :, :]), writes=[b_qT])
                for ci in range(NFC):
                    M_ = FCH_M[ci]; off = FCH_OFF[ci]
                    bk = mmbank()
                    for kc in range(KC):
                        P.op("pe", lambda e, bk=bk, M_=M_, off=off, kc=kc: e.matmul(
                            banks[bk][0:M_, :], lhsT=wf_s[:, kc, off:off + M_], rhs=hT[:, kc, :], start=(kc == 0), stop=(kc == KC - 1)),
                            reads=[b_w, b_hT], writes=[bb[bk]], mode=("full" if M_ > 64 else "c%d" % (64 if M_ > 32 else 32)))
                    src = banks[bk]
                    if ci < 4:
                        P.op("act", lambda e, src=src, ci=ci: e.activation(out=qT[0:64, :, ci, :], in_=src[0:64, :].rearrange("p (t q) -> p t q", q=128), func=AF.Identity,
                                                                           bias=bf_s[0:64, ci:ci + 1], scale=1.0),
                             reads=[bb[bk], b_w], writes=[b_qT])
                        P.op("dve", lambda e, ci=ci: e.tensor_scalar(out=qT[0:64, :, ci, :], in0=qT[0:64, :, ci, :], scalar1=0.125, scalar2=None,
                                                                     op0=ALU.mult), reads=[b_qT], writes=[b_qT])
                    elif ci == 4:
                        P.op("act", lambda e, src=src: e.activation(out=cmpbuf[:, 16:528], in_=src[:, :], func=AF.Identity,
                                                                    bias=bf_s[:, 4:5], scale=1.0),
                             reads=[bb[bk], b_w], writes=[b_cmpbuf])
                    elif ci == 5:
                        P.op("act", lambda e, src=src, tsl=tsl: e.activation(out=kslcT[0:64, tsl], in_=src[0:64, :], func=AF.Identity,
                                                                             bias=bf_s[0:64, 5:6], scale=1.0),
                             reads=[bb[bk], b_w], writes=[b_kslc[ST * 4 + k] for k in range(4)])
                    elif ci == 6:
                        rs = (ST % 2) * 512
                        P.op("act", lambda e, src=src, rs=rs: e.activation(out=kwinT[0:64, rs:rs + 512], in_=src[0:64, :], func=AF.Identity,
                                                                           bias=bf_s[0:64, 6:7], scale=1.0),
                             reads=[bb[bk], b_w], writes=[b_kwin[(ST % 2) * 4 + k] for k in range(4)])
                        P.dma("sp", lambda e, rs=rs, tsl=tsl: e.dma_start(out=kwinT[64:72, rs:rs + 512], in_=kaug_d[:, tsl]),
                              writes=[b_kwin[(ST % 2) * 4 + k] for k in range(4)])
                    elif ci == 7:
                        P.op("act", lambda e, src=src: e.activation(out=gqT[:], in_=src[:, :], func=AF.Identity, bias=bf_s[:, 7:8], scale=1.0),
                             reads=[bb[bk], b_w], writes=[b_gT])
                    elif ci == 8:
                        P.op("act", lambda e, src=src: e.activation(out=gkT[:], in_=src[:, :], func=AF.Identity, bias=bf_s[:, 8:9], scale=1.0),
                             reads=[bb[bk], b_w], writes=[b_gT])
                    else:
                        P.op("act", lambda e, src=src: e.activation(out=gaT[:], in_=src[0:16, :], func=AF.Identity, bias=bf_s[0:16, 9:10], scale=1.0),
                             reads=[bb[bk], b_w], writes=[b_gT])
                chk(4)
                for tt in range(4):
                    t = ST * 4 + tt
                    for grp in range(2):
                        if grp == 1:
                            chk(44)
                        if tt == 1:
                            chk(46)
                        bk = mmbank()
                        n0, n1 = (0, NTA) if grp == 0 else (NTA, NT_)
                        w_ = n1 - n0
                        for kc in range(KC):
                            P.op("pe", lambda e, bk=bk, kc=kc, tt=tt, n0=n0, n1=n1, w_=w_: e.matmul(
                                banks[bk][:, 0:w_], lhsT=hT[:, kc, tt * 128:(tt + 1) * 128], rhs=wt_s[:, kc, n0:n1], start=(kc == 0), stop=False),
                                reads=[b_w, b_hT], writes=[bb[bk]])
                        chk(41)
                        P.op("pe", lambda e, bk=bk, n0=n0, n1=n1, w_=w_: e.matmul(
                            banks[bk][:, 0:w_], lhsT=ones_s[0:1, :], rhs=btb_s[0:1, n0:n1], start=False, stop=True),
                            reads=[b_w], writes=[bb[bk]], mode="r32")
                        chk(42)
                        if grp == 0:
                            P.op("dve", lambda e, bk=bk, tt=tt: e.tensor_copy(out=tokA[tt][:], in_=banks[bk][:, 0:NTA]),
                                 reads=[bb[bk]], writes=[b_tokA[tt]])
                            chk(43)
                            P.op("dve", lambda e, tt=tt, t=t: e.tensor_copy(out=vslc[:, t, 0:64], in_=tokA[tt][:, 0:64]),
                                 reads=[b_tokA[tt]], writes=[b_vslc[t]])
                            P.op("dve", lambda e, tt=tt, t=t: e.tensor_copy(out=vwin[:, t % 8, 0:64], in_=tokA[tt][:, 64:128]),
                                 reads=[b_tokA[tt]], writes=[b_vwin[t % 8]])
                        else:
                            chk(45)
                            P.op("dve", lambda e, bk=bk, tt=tt: e.tensor_copy(out=tokB[tt][:], in_=banks[bk][:, 256:512]),
                                 reads=[bb[bk]], writes=[b_tokB[tt]])
                            P.op("dve", lambda e, bk=bk, tt=tt: e.tensor_copy(out=gvb[tt][:], in_=banks[bk][:, 0:256]),
                                 reads=[bb[bk]], writes=[b_gvb[tt]])
                if debug and ST == 0:
                    P.dma("pool", lambda e: e.dma_start(out=dbg["qT"], in_=qT[:]), reads=[b_qT])
                    P.dma("pool", lambda e: e.dma_start(out=dbg["tokA"], in_=tokA[0][:]), reads=[b_tokA[0]])

                if ST + 1 < NST:
                    norm_tile(ST + 1, 0)
                chk(5)
                bk = mmbank()
                P.op("pe", lambda e, bk=bk: e.matmul(banks[bk][:, :], lhsT=walpha_s[:, :], rhs=gaT[:, :], start=True, stop=True),
                     reads=[b_w, b_gT], writes=[bb[bk]], mode="r32")
                P.op("act", lambda e, bk=bk: e.activation(out=e1[:], in_=banks[bk][:, :], func=AF.Exp, scale=-1.0, bias=nbalpha_s[:, 0:1]),
                     reads=[bb[bk], b_w], writes=[b_gl])
                P.op("act", lambda e: e.activation(out=spl[:], in_=e1[:], func=AF.Ln, bias=1.0, scale=1.0), reads=[b_gl], writes=[b_gl])
                P.op("dve", lambda e: e.tensor_tensor_scan(out=cs[:], data0=reset_s[:], data1=spl[:], initial=0.0, op0=ALU.mult, op1=ALU.add),
                     reads=[b_gl, b_tab], writes=[b_gl])
                P.op("act", lambda e: e.activation(out=eb[:], in_=cs[:], func=AF.Exp, scale=-1.0 / 16), reads=[b_gl], writes=[b_gl])
                P.op("act", lambda e: e.activation(out=enb[:], in_=cs[:], func=AF.Exp, scale=1.0 / 16), reads=[b_gl], writes=[b_gl])
                P.op("dve", lambda e: e.scalar_tensor_tensor(out=qtl[:], in0=gqT[:], scalar=128.0 ** -0.5, in1=eb[:], op0=ALU.mult, op1=ALU.mult),
                     reads=[b_gl, b_gT], writes=[b_gl])
                P.op("dve", lambda e: e.tensor_tensor(out=ktl[:], in0=gkT[:], in1=enb[:], op=ALU.mult), reads=[b_gl, b_gT], writes=[b_gl])
                for tt in range(4):
                    t = ST * 4 + tt
                    cs_ = slice(tt * 128, tt * 128 + 128)
                    P.op("pe", lambda e, cs_=cs_: e.transpose(banks[6][:].bitcast(BF16)[:, 0:128], ktl[:, cs_], identb[:]),
                         reads=[b_gl, b_tab], writes=[bb[6]], mode="tr")
                    P.op("dve", lambda e: e.tensor_copy(out=ktok[:], in_=banks[6][:].bitcast(BF16)[:, 0:128]), reads=[bb[6]], writes=[b_glc])
                    P.op("pe", lambda e, cs_=cs_: e.matmul(banks[7][:, 0:128], lhsT=ktl[:, cs_], rhs=qtl[:, cs_], start=True, stop=True),
                         reads=[b_gl], writes=[bb[7]])
                    P.op("dve", lambda e: e.tensor_tensor(out=atm[:], in0=banks[7][:, 0:128], in1=gmask_s[:], op=ALU.mult),
                         reads=[bb[7], b_tab], writes=[b_glc])
                    bo = mmbank()
                    P.op("pe", lambda e, bo=bo, tt=tt: e.matmul(banks[bo][:, 0:256], lhsT=atm[:], rhs=gvb[tt][:], start=True, stop=False),
                         reads=[b_glc, b_gvb[tt]], writes=[bb[bo]])
                    P.op("pe", lambda e, bo=bo, cs_=cs_: e.matmul(banks[bo][:, 0:256], lhsT=qtl[:, cs_], rhs=Sb_s[:], start=False, stop=True),
                         reads=[b_gl, b_S], writes=[bb[bo]])
                    P.op("pe", lambda e, tt=tt: e.matmul(banks[7][:, 256:512], lhsT=ktok[:], rhs=gvb[tt][:], start=True, stop=True),
                         reads=[b_glc, b_gvb[tt]], writes=[bb[7]])
                    P.op("dve", lambda e: e.tensor_tensor(out=stmp[:], in0=banks[7][:, 256:512], in1=S_s[:], op=ALU.add),
                         reads=[bb[7], b_S], writes=[b_glc])
                    last = tt * 128 + 127
                    P.op("dve", lambda e, last=last: e.tensor_scalar(out=S_s[:], in0=stmp[:], scalar1=eb[:, last:last + 1], scalar2=None, op0=ALU.mult),
                         reads=[b_glc, b_gl], writes=[b_S])
                    P.op("dve", lambda e: e.tensor_copy(out=Sb_s[:], in_=S_s[:]), reads=[b_S], writes=[b_S])
                    yb_ = tt
                    P.op("act", lambda e, bo=bo: e.activation(out=junk[:], in_=banks[bo][:, 0:256], func=AF.Square, accum_out=small[:, 24:25]),
                         reads=[bb[bo]], writes=[b_fin, b_small])
                    P.op("act", lambda e: e.activation(out=small[:, 25:26], in_=small[:, 24:25], func=AF.Ln, scale=1.0 / 256, bias=EPS),
                         reads=[b_small], writes=[b_small])
                    P.op("act", lambda e: e.activation(out=small[:, 26:27], in_=small[:, 25:26], func=AF.Exp, scale=-0.5),
                         reads=[b_small], writes=[b_small])
                    P.op("act", lambda e, tt=tt: e.activation(out=gl1[:], in_=tokB[tt][:], func=AF.Exp, scale=-1.0),
                         reads=[b_tokB[tt]], writes=[b_fin])
                    P.op("dve", lambda e: e.tensor_scalar(out=gl1[:], in0=gl1[:], scalar1=1.0, scalar2=None, op0=ALU.add), reads=[b_fin], writes=[b_fin])
                    P.op("dve", lambda e: e.reciprocal(out=gl1[:], in_=gl1[:]), reads=[b_fin], writes=[b_fin])
                    P.op("dve", lambda e, tt=tt: e.tensor_tensor(out=gl1[:], in0=gl1[:], in1=tokB[tt][:], op=ALU.mult),
                         reads=[b_fin, b_tokB[tt]], writes=[b_fin])
                    P.op("dve", lambda e: e.tensor_tensor(out=gl1[:], in0=gl1[:], in1=ggain_s[:], op=ALU.mult), reads=[b_fin, b_w], writes=[b_fin])
                    P.op("dve", lambda e, bo=bo, yb_=yb_: e.scalar_tensor_tensor(out=yt[yb_][:, 256:512], in0=banks[bo][:, 0:256], scalar=small[:, 26:27],
                                                                                 in1=gl1[:], op0=ALU.mult, op1=ALU.mult),
                         reads=[bb[bo], b_fin, b_small], writes=[b_yt[yb_]])

                if ST + 1 < NST:
                    norm_tile(ST + 1, 1)
                chk(6)
                for kv in range(2):
                    lo = 64 * kv
                    bk = mmbank()
                    for hc in range(2):
                        for p_ in range(32):
                            P.op("pe", lambda e, bk=bk, lo=lo, hc=hc, p_=p_: e.matmul(
                                banks[bk][:, hc * 32:(hc + 1) * 32], lhsT=w1kv_s[lo:lo + 64, p_, hc * 128:(hc + 1) * 128],
                                rhs=cmpbuf[lo:lo + 64, p_:p_ + 497:16], start=(p_ == 0), stop=(p_ == 31)),
                                reads=[b_w, b_cmpbuf], writes=[bb[bk]], mode="r64")
                    for hc in range(2):
                        col = kv * 2 + hc
                        P.op("dve", lambda e, bk=bk, hc=hc, col=col: e.tensor_scalar(out=hsb[:, col, :], in0=banks[bk][:, hc * 32:(hc + 1) * 32],
                                                                                     scalar1=posb_s[:, col:col + 1], scalar2=None, op0=ALU.add),
                             reads=[bb[bk], b_w], writes=[b_hid])
                P.op("act", lambda e: e.activation(out=hex_[:], in_=hsb[:], func=AF.Exp, scale=-1.0), reads=[b_hid], writes=[b_hid])
                P.op("dve", lambda e: e.tensor_scalar(out=hex_[:], in0=hex_[:], scalar1=1.0, scalar2=None, op0=ALU.add), reads=[b_hid], writes=[b_hid])
                P.op("dve", lambda e: e.reciprocal(out=hex_[:], in_=hex_[:]), reads=[b_hid], writes=[b_hid])
                P.op("dve", lambda e: e.tensor_tensor(out=hidT[:], in0=hex_[:], in1=hsb[:], op=ALU.mult), reads=[b_hid], writes=[b_hid])
                bk = mmbank()
                for hc in range(2):
                    P.op("pe", lambda e, bk=bk, hc=hc: e.matmul(banks[bk][0:64, 0:32], lhsT=w2k_s[:, hc, :], rhs=hidT[:, hc, :],
                                                                start=(hc == 0), stop=(hc == 1)), reads=[b_w, b_hid], writes=[bb[bk]], mode="c64")
                P.op("dve", lambda e, bk=bk, ST=ST: e.tensor_copy(out=kcT[0:64, ST * 32:ST * 32 + 32], in_=banks[bk][0:64, 0:32]),
                     reads=[bb[bk]], writes=[b_kcT])
                for hc in range(2):
                    P.op("pe", lambda e, bk=bk, hc=hc: e.matmul(banks[bk][0:32, 64:128], lhsT=hidT[:, 2 + hc, :], rhs=w2v_s[:, hc, :],
                                                                start=(hc == 0), stop=(hc == 1)), reads=[b_w, b_hid], writes=[bb[bk]], mode="c32")
                P.op("dve", lambda e, bk=bk: e.tensor_copy(out=vcst[:], in_=banks[bk][0:32, 64:128]), reads=[bb[bk]], writes=[b_hid])
                pr = 32 * (ST % 4)
                P.dma("sp", lambda e, pr=pr, ST=ST: e.dma_start(out=vcM[pr:pr + 32, ST // 4, 0:64], in_=vcst[:]), reads=[b_hid], writes=[b_vcM])
                if ST == 0:
                    P.op("pool", lambda e: e.memset(vcM[0:1, 0, 0:193], 0.0), writes=[b_vcM])
                P.op("dve", lambda e: e.tensor_copy(out=cmpbuf[:, 0:16], in_=cmpbuf[:, 512:528]), reads=[b_cmpbuf], writes=[b_cmpbuf])

                if ST + 1 < NST:
                    norm_tile(ST + 1, 2)
                    norm_tile(ST + 1, 3)
                chk(7)
                def run_pipeline(jobs, depth=1):
                    pend = []
                    for jb in jobs:
                        jb[0]()
                        pend.append(jb)
                        if len(pend) > depth:
                            pend.pop(0)[1]()
                    for jb in pend:
                        jb[1]()

                def stage_A(tt):
                    i = ST * 4 + tt
                    par = i % 2
                    qv = qT[:, tt, :, :].rearrange("p h q -> p (h q)")
                    nnt = (8 * i + 7 + 127) // 128
                    jobs = []
                    for nt in range(nnt):
                        mrows = 128
                        st_ = {}

                        def qk(nt=nt, mrows=mrows, st_=st_):
                            sbk = stbank()
                            st_["sbk"] = sbk
                            mk = maskc[nt % 2]
                            P.op("pool", lambda e, mk=mk: e.affine_select(
                                out=mk[:], in_=zeros_b[:], pattern=[[0, 4], [1, 128]], compare_op=ALU.is_ge, fill=getfill(e),
                                base=128 * i - 2048 * nt - 15, channel_multiplier=-16), reads=[b_tab], writes=[b_maskc[nt % 2]])
                            P.op("pe", lambda e: e.matmul(banks[sbk][0:mrows, :], lhsT=kcT[:, nt * 128:nt * 128 + mrows], rhs=qv, start=True, stop=False),
                                 reads=[b_kcT, b_qT], writes=[bb[sbk]])
                            P.op("pe", lambda e: e.matmul(banks[sbk][0:mrows, :], lhsT=identb[:, 0:mrows], rhs=mk[:], start=False, stop=True),
                                 reads=[b_tab, b_maskc[nt % 2]], writes=[bb[sbk]])

                        def rest(nt=nt, mrows=mrows, st_=st_):
                            sbk = st_["sbk"]
                            pb = rr["pt"] % 3
                            rr["pt"] += 1
                            P.op("act", lambda e: e.activation(out=pT[pb][0:mrows, :], in_=banks[sbk][0:mrows, :], func=AF.Exp, scale=1.0),
                                 reads=[bb[sbk]], writes=[b_pT[pb]])
                            for h in range(4):
                                obk = 2 + h // 2
                                oc = (h % 2) * 256
                                P.op("pe", lambda e, obk=obk, oc=oc, h=h: e.matmul(
                                    banks[obk][:, oc:oc + 193], lhsT=pT[pb][0:mrows, h * 128:(h + 1) * 128], rhs=vcM[0:mrows, nt, 0:193],
                                    start=(nt == 0 and h % 2 == 0), stop=(nt == nnt - 1), skip_group_check=True),
                                    reads=[b_pT[pb], b_vcM], writes=[bb[obk]])
                        jobs.append((qk, rest))
                    run_pipeline(jobs)

                def stage_Adve(tt):
                    i = ST * 4 + tt
                    par = i % 2
                    bF = b_finA[par]
                    P.op("act", lambda e: e.activation(out=gsig[par][:], in_=tokA[tt][:, 128:140], func=AF.Exp, scale=-1.0), reads=[b_tokA[tt]], writes=[bF])
                    P.op("dve", lambda e: e.tensor_scalar(out=gsig[par][:], in0=gsig[par][:], scalar1=1.0, scalar2=None, op0=ALU.add), reads=[bF], writes=[bF])
                    P.op("dve", lambda e: e.reciprocal(out=gsig[par][:], in_=gsig[par][:]), reads=[bF], writes=[bF])
                    for h in range(4):
                        obk = 2 + h // 2
                        oc = (h % 2) * 256
                        P.op("dve", lambda e, obk=obk, oc=oc, h=h: e.tensor_scalar(out=den[par][:, h:h + 1], in0=banks[obk][:, oc + 64:oc + 65],
                                                                                  scalar1=1e-30, scalar2=None, op0=ALU.max),
                             reads=[bb[obk]], writes=[bF])
                    P.op("dve", lambda e: e.reciprocal(out=den[par][:, 0:4], in_=den[par][:, 0:4]), reads=[bF], writes=[bF])
                    for h in range(4):
                        obk = 2 + h // 2
                        oc = (h % 2) * 256
                        P.op("dve", lambda e, h=h: e.tensor_tensor(out=coef[par][:, h:h + 1], in0=gsig[par][:, 3 * h:3 * h + 1], in1=den[par][:, h:h + 1], op=ALU.mult),
                             reads=[bF], writes=[bF])
                        P.op("dve", lambda e, obk=obk, oc=oc, h=h: e.tensor_scalar(out=acc[par][:, h * 64:(h + 1) * 64], in0=banks[obk][:, oc:oc + 64],
                                                                                  scalar1=coef[par][:, h:h + 1], scalar2=None, op0=ALU.mult),
                             reads=[bb[obk], bF], writes=[bF])

                    for h in range(4):
                        obk = 2 + h // 2
                        oc = (h % 2) * 256
                        if h == 0:
                            P.op("dve", lambda e, obk=obk, oc=oc: e.tensor_scalar(out=score[:], in0=banks[obk][:, oc + 65:oc + 193], scalar1=den[par][:, 0:1],
                                                                                 scalar2=None, op0=ALU.mult), reads=[bb[obk], bF], writes=[b_sel])
                        else:
                            P.op("dve", lambda e, obk=obk, oc=oc, h=h: e.scalar_tensor_tensor(out=score[:], in0=banks[obk][:, oc + 65:oc + 193],
                                                                                              scalar=den[par][:, h:h + 1], in1=score[:], op0=ALU.mult, op1=ALU.add),
                                 reads=[bb[obk], bF, b_sel], writes=[b_sel])
                    P.op("dve", lambda e: e.memset(score[:, 0:1], BIG), writes=[b_sel])
                    P.op("dve", lambda e: e.memset(score[:, 2 * i:2 * i + 1], BIG), writes=[b_sel])
                    if i > 0:
                        P.op("dve", lambda e: e.memset(score[0:64, 2 * i - 1:2 * i], BIG), writes=[b_sel])
                    P.op("dve", lambda e: e.memset(score[64:128, 2 * i + 1:2 * i + 2], BIG), writes=[b_sel])
                    P.op("dve", lambda e: e.max(out=m8[:, 0:8], in_=score[:]), reads=[b_sel], writes=[b_sel])
                    P.op("dve", lambda e: e.match_replace(out=swork[:], in_to_replace=m8[:, 0:8], in_values=score[:], imm_value=-BIG),
                         reads=[b_sel], writes=[b_sel])
                    P.op("dve", lambda e: e.max(out=m8[:, 8:16], in_=swork[:]), reads=[b_sel], writes=[b_sel])
                    P.op("dve", lambda e: e.tensor_scalar(out=selb[par][:], in0=score[:], scalar1=m8[:, 15:16], scalar2=NEGM, op0=ALU.is_lt, op1=ALU.mult),
                         reads=[b_sel], writes=[b_selb[par]])

                def stage_T(tt):
                    i = ST * 4 + tt
                    par = i % 2
                    sbt = mmbank()
                    P.op("pe", lambda e: e.transpose(banks[sbt][:].bitcast(BF16)[:, 0:128], selb[par][:], identb[:]), reads=[b_selb[par], b_tab], writes=[bb[sbt]], mode="tr")
                    for h in range(4):
                        P.op("dve", lambda e, h=h: e.tensor_copy(out=selT4[par][0][0:64, h, :], in_=banks[sbt][:].bitcast(BF16)[0:64, 0:128]),
                             reads=[bb[sbt]], writes=[b_selT[par]])
                        P.op("dve", lambda e, h=h: e.tensor_copy(out=selT4[par][1][64:128, h, :], in_=banks[sbt][:].bitcast(BF16)[64:128, 0:128]),
                             reads=[bb[sbt]], writes=[b_selT[par]])

                def stage_B(tt):
                    i = ST * 4 + tt
                    par = i % 2
                    bF = b_finA[par]
                    qv = qT[:, tt, :, :].rearrange("p h q -> p (h q)")

                    def mkjob(br, kt, idx, nk):
                        obk = 6 + br
                        if br == 0:
                            kop = kslcT[:, kt * 128:(kt + 1) * 128]; kb = b_kslc[kt]
                            vop = vslc[:, kt, 0:65]; vb = b_vslc[kt]
                        else:
                            rsl = (kt % 8) * 128
                            kop = kwinT[:, rsl:rsl + 128]; kb = b_kwin[kt % 8]
                            vop = vwin[:, kt % 8, 0:65]; vb = b_vwin[kt % 8]
                        extra = []
                        if br == 0:
                            gq_ = (2 * kt) // 64
                            m_ = kt % 32
                            extra.append((E4_s[:, m_, :], selT4[par][gq_][:, :, :].rearrange("p h q -> p (h q)"), [b_tab, b_selT[par]]))
                        if kt == i:
                            extra.append((identb[:], bdiag_s[:], [b_tab]))
                        if br == 1 and kt == i - 4:
                            extra.append((identb[:], bfar_s[:], [b_tab]))
                        st_ = {}

                        def qk():
                            sbk = stbank()
                            st_["sbk"] = sbk
                            P.op("pe", lambda e: e.matmul(banks[sbk][:, :], lhsT=kop, rhs=qv, start=True, stop=(len(extra) == 0)),
                                 reads=[kb, b_qT], writes=[bb[sbk]])
                            for xi, (l_, r_, rb_) in enumerate(extra):
                                P.op("pe", lambda e, l_=l_, r_=r_, lastx=(xi == len(extra) - 1): e.matmul(
                                    banks[sbk][:, :], lhsT=l_, rhs=r_, start=False, stop=lastx), reads=rb_, writes=[bb[sbk]])

                        def rest():
                            sbk = st_["sbk"]
                            pb = rr["pt"] % 3
                            rr["pt"] += 1
                            P.op("act", lambda e: e.activation(out=pT[pb][:], in_=banks[sbk][:, :], func=AF.Exp, scale=1.0),
                                 reads=[bb[sbk]], writes=[b_pT[pb]])
                            P.op("pe", lambda e: e.matmul(banks[obk][0:65, :], lhsT=vop, rhs=pT[pb][:], start=(idx == 0), stop=(idx == nk - 1)),
                                 reads=[b_pT[pb], vb], writes=[bb[obk]])
                        return (qk, rest)
                    kw = list(range(max(0, i - 4), i + 1))
                    ks = list(range(0, i + 1))
                    jobs_w = [mkjob(1, kt, idx, len(kw)) for idx, kt in enumerate(kw)]
                    jobs_s = [mkjob(0, kt, idx, len(ks)) for idx, kt in enumerate(ks)]
                    run_pipeline(jobs_w + jobs_s)
                    for br in range(2):
                        obk = 6 + br
                        P.op("dve", lambda e, obk=obk, br=br: e.tensor_copy(out=(e1 if br == 0 else cs)[0:65, :], in_=banks[obk][0:65, :]),
                             reads=[bb[obk]], writes=[b_gl])
                    for br in range(2):
                        tbk = mmbank()
                        for h in range(4):
                            P.op("pe", lambda e, tbk=tbk, br=br, h=h: e.transpose(
                                banks[tbk][:, h * 128:h * 128 + 65], (e1 if br == 0 else cs)[0:65, h * 128:(h + 1) * 128], identf[0:65, 0:65]),
                                reads=[b_gl, b_tab], writes=[bb[tbk]], mode="f32t")
                        for h in range(4):
                            dcol = 4 + br * 4 + h
                            P.op("dve", lambda e, tbk=tbk, h=h, dcol=dcol: e.tensor_scalar(out=den[par][:, dcol:dcol + 1], in0=banks[tbk][:, h * 128 + 64:h * 128 + 65],
                                                                                          scalar1=1e-30, scalar2=None, op0=ALU.max),
                                 reads=[bb[tbk]], writes=[bF])
                            P.op("dve", lambda e, dcol=dcol: e.reciprocal(out=den[par][:, dcol:dcol + 1], in_=den[par][:, dcol:dcol + 1]), reads=[bF], writes=[bF])
                            P.op("dve", lambda e, h=h, br=br, dcol=dcol: e.tensor_tensor(out=coef[par][:, dcol:dcol + 1], in0=gsig[par][:, 3 * h + 1 + br:3 * h + 2 + br],
                                                                                        in1=den[par][:, dcol:dcol + 1], op=ALU.mult), reads=[bF], writes=[bF])
                            P.op("dve", lambda e, tbk=tbk, h=h, dcol=dcol: e.scalar_tensor_tensor(
                                out=acc[par][:, h * 64:(h + 1) * 64], in0=banks[tbk][:, h * 128:h * 128 + 64], scalar=coef[par][:, dcol:dcol + 1],
                                in1=acc[par][:, h * 64:(h + 1) * 64], op0=ALU.mult, op1=ALU.add), reads=[bb[tbk], bF], writes=[bF])
                    yb_ = tt
                    P.op("act", lambda e: e.activation(out=szn[:], in_=tokA[tt][:, 140:396], func=AF.Exp, scale=-1.0), reads=[b_tokA[tt]], writes=[b_fin])
                    P.op("dve", lambda e: e.tensor_scalar(out=szn[:], in0=szn[:], scalar1=1.0, scalar2=None, op0=ALU.add), reads=[b_fin], writes=[b_fin])
                    P.op("dve", lambda e: e.reciprocal(out=szn[:], in_=szn[:]), reads=[b_fin], writes=[b_fin])
                    P.op("dve", lambda e: e.tensor_tensor(out=szn[:], in0=szn[:], in1=tokA[tt][:, 140:396], op=ALU.mult), reads=[b_fin, b_tokA[tt]], writes=[b_fin])
                    P.op("dve", lambda e: e.tensor_tensor(out=yt[yb_][:, 0:256], in0=szn[:], in1=acc[par][:], op=ALU.mult), reads=[b_fin, bF], writes=[b_yt[yb_]])
                    P.dma("pool", lambda e: e.dma_start(out=y_v[i], in_=yt[yb_][:]), reads=[b_yt[yb_]])

                stage_A(0)
                stage_Adve(0)
                stage_A(1)
                stage_T(0)
                stage_Adve(1)
                stage_B(0)
                stage_A(2)
                stage_T(1)
                stage_Adve(2)
                stage_B(1)
                stage_A(3)
                stage_T(2)
                stage_Adve(3)
                stage_B(2)
                stage_T(3)
                stage_B(3)
        except _Stop:
            pass
        toks = [t_ for t_ in P.dma_last if t_ is not None]
        toks += [(e_, P.cnt[e_]) for e_ in ("pe", "act", "dve", "pool") if P.cnt[e_] > 0]
        for e_ in ("pool", "sp", "act", "dve", "pe"):
            P.wait_all(e_, toks)
        P.emit()
    return nc


import contextlib
import numpy as np
import ml_dtypes
import concourse.bass as bass
import concourse.mybir as mybir

F32 = mybir.dt.float32
BF16 = mybir.dt.bfloat16
AF = mybir.ActivationFunctionType
ALU = mybir.AluOpType
BF = ml_dtypes.bfloat16
D = 2048
KC = 16
EPS = 1e-6
CB = 256
NCB = D // CB


def host_inputs_l2(inp, ys, b, qtr, TQ):
    t0 = qtr * TQ
    ycat = np.concatenate([ys[b * 4 + j][t0:t0 + TQ, 0:256] for j in range(4)] +
                          [ys[b * 4 + j][t0:t0 + TQ, 256:512] for j in range(4)], axis=1)
    w_in = inp["w_in"][0]
    b_in = inp["b_in"][0]
    return dict(
        x2=np.ascontiguousarray(inp["x"][b, t0:t0 + TQ]),
        yin=np.ascontiguousarray(ycat),
        cT2=np.ascontiguousarray(inp["c"][b].reshape(16, 128).T),
        w_ada2=np.ascontiguousarray(inp["w_ada"][0]),
        b_ada2=np.ascontiguousarray(inp["b_ada"][0].reshape(48, 128).T),
        gainT2=np.ascontiguousarray(inp["norm_gain"][0].reshape(16, 128).T),
        wmg=np.ascontiguousarray(w_in[:, 6720:10816]),
        bmg=np.ascontiguousarray(b_in[6720:10816][None, :]),
        wbrn=np.ascontiguousarray(inp["w_br_nsa"][0]),
        wbrg=np.ascontiguousarray(inp["w_br_gla"][0]),
        wout=np.ascontiguousarray(inp["w_out"][0]),
        fgain=np.ascontiguousarray(np.broadcast_to(inp["final_norm_gain"][None, :], (128, D))),
        identb2=np.eye(128).astype(BF), identf2=np.eye(128, dtype=np.float32),
    )


def build_l2(TQ):
    NST = TQ // 512
    nc = bass.Bass("TRN2", target_bir_lowering=False)

    def din(name, shape, dt=F32):
        return nc.dram_tensor(name, list(shape), dt, kind="ExternalInput").ap()
    x = din("x2", [TQ, D]); yin = din("yin", [TQ, D], BF16)
    cT_d = din("cT2", [128, 16]); wada_d = din("w_ada2", [D, 6144]); bada_d = din("b_ada2", [128, 48])
    gainT_d = din("gainT2", [128, 16])
    wmg_d = din("wmg", [D, 4096]); bmg_d = din("bmg", [1, 4096])
    wbrn_d = din("wbrn", [1024, D]); wbrg_d = din("wbrg", [1024, D]); wout_d = din("wout", [D, D])
    fgain_d = din("fgain", [128, D]); identb_d = din("identb2", [128, 128], BF16); identf_d = din("identf2", [128, 128])
    out_d = nc.dram_tensor("out", [TQ, D], F32, kind="ExternalOutput").ap()

    P = Prog(nc)
    with contextlib.ExitStack() as st:
        def sb(name, shape, dt=F32):
            return st.enter_context(nc.sbuf_tensor("s2_" + name, list(shape), dt))
        xs = [sb("xs%d" % i, [128, D]) for i in range(2)]
        xnb = sb("xnb", [128, D], BF16)
        ytile = [sb("ytile%d" % i, [128, D], BF16) for i in range(2)]
        hT = sb("hT", [128, KC, 512], BF16)
        yT = sb("yT", [128, KC, 512], BF16)
        mtok = [sb("mtok%d" % i, [128, D], BF16) for i in range(4)]
        mT = sb("mT", [128, KC, 128], BF16)
        wmgb = [sb("wmgb%d" % i, [128, 2, KC, CB], BF16) for i in range(2)]
        wbrb = [sb("wbrb%d" % i, [128, 2, 8, CB], BF16) for i in range(2)]
        woutb = [sb("woutb%d" % i, [128, KC, 512], BF16) for i in range(2)]
        bmg_s = sb("bmg_s", [1, 4096], BF16); ones_s = sb("ones_s", [1, 128], BF16)
        onesf = sb("onesf", [128, 128])
        gate_bc = sb("gate_bc", [128, D]); fgain_s = sb("fgain_s", [128, D])
        xo = sb("xo", [128, D])
        A_s = sb("A_s", [128, 16]); B_s = sb("B_s", [128, 16]); ada_s = sb("ada_s", [128, 48])
        cT_s = sb("cT_s", [128, 16]); gainT_s = sb("gainT_s", [128, 16]); bada_s = sb("bada_s", [128, 48])
        small = sb("small", [128, 32]); diag = [sb("diag%d" % i, [128, 128]) for i in range(2)]
        identb = sb("identb", [128, 128], BF16); identf = sb("identf", [128, 128])
        g1 = sb("g1", [128, CB]); g2 = sb("g2", [128, CB]); t1 = sb("t1", [128, CB])

        banks = [st.enter_context(nc.psum_tensor("b2ank%d" % i, [128, 512], F32)) for i in range(8)]
        bb = [Buf("bank%d" % i) for i in range(8)]
        B = Buf
        b_w = B("w"); b_AB = B("AB"); b_xs = [B("xs0"), B("xs1")]; b_xnb = B("xnb"); b_small = B("small")
        b_yt = [B("yt0"), B("yt1")]; b_hT = B("hT"); b_yT = B("yT"); b_mtok = [B("mtok%d" % i) for i in range(4)]
        b_mT = B("mT"); b_wmg = [B("wmg0"), B("wmg1")]; b_wbr = [B("wbr0"), B("wbr1")]; b_wout = [B("wout0"), B("wout1")]
        b_gate = B("gate"); b_diag = [B("d0"), B("d1")]; b_g = B("g"); b_xo = B("xo"); b_tab = B("tab")

        def ld(q, out, in_, writes):
            return P.dma(q, lambda e: e.dma_start(out=out, in_=in_), writes=writes)
        ld("sp", cT_s[:], cT_d, [b_AB]); ld("sp", gainT_s[:], gainT_d, [b_AB]); ld("sp", bada_s[:], bada_d, [b_AB])
        ld("sp", fgain_s[:], fgain_d, [b_w]); ld("sp", identb[:], identb_d, [b_tab]); ld("sp", identf[:], identf_d, [b_tab])
        ld("pool", bmg_s[:], bmg_d, [b_w])
        P.op("pool", lambda e: e.memset(ones_s[:], 1.0), writes=[b_w])
        P.op("pool", lambda e: e.memset(onesf[:], 1.0), writes=[b_w])
        wada_v = wada_d.rearrange("(kc p) n -> p kc n", p=128)
        for cc in range(48):
            xb_ = cc % 2
            ld("sp", xs[xb_][:].rearrange("p (kc n) -> p kc n", kc=16), wada_v[:, :, cc * 128:(cc + 1) * 128], [b_xs[xb_]])
            wv = xs[xb_][:].rearrange("p (kc n) -> p kc n", kc=16)
            for kc in range(KC):
                P.op("pe", lambda e, kc=kc, wv=wv, cc=cc: e.matmul(banks[0][:, cc:cc + 1], lhsT=wv[:, kc, :], rhs=cT_s[:, kc:kc + 1],
                                                                  start=(kc == 0), stop=(kc == KC - 1)),
                     reads=[b_xs[xb_], b_AB], writes=[bb[0]])
        P.op("dve", lambda e: e.tensor_tensor(out=ada_s[:], in0=banks[0][:, 0:48], in1=bada_s[:], op=ALU.add),
             reads=[bb[0], b_AB], writes=[b_AB])
        P.op("dve", lambda e: e.tensor_scalar(out=A_s[:], in0=ada_s[:, 16:32], scalar1=1.0, scalar2=None, op0=ALU.add), reads=[b_AB], writes=[b_AB])
        P.op("dve", lambda e: e.tensor_tensor(out=A_s[:], in0=A_s[:], in1=gainT_s[:], op=ALU.mult), reads=[b_AB], writes=[b_AB])
        P.op("dve", lambda e: e.tensor_copy(out=B_s[:], in_=ada_s[:, 0:16]), reads=[b_AB], writes=[b_AB])
        for kc in range(KC):
            d_ = kc % 2
            P.op("dve", lambda e, kc=kc, d_=d_: e.tensor_scalar(out=diag[d_][:], in0=identf[:], scalar1=ada_s[:, 32 + kc:33 + kc], scalar2=None, op0=ALU.mult),
                 reads=[b_AB, b_tab], writes=[b_diag[d_]])
            bk = 1 + kc // 4
            P.op("pe", lambda e, kc=kc, d_=d_, bk=bk: e.matmul(banks[bk][:, (kc % 4) * 128:(kc % 4) * 128 + 128], lhsT=onesf[:], rhs=diag[d_][:],
                                                               start=(kc % 4 == 0), stop=(kc % 4 == 3), skip_group_check=True),
                 reads=[b_w, b_diag[d_]], writes=[bb[bk]])
        for q4 in range(4):
            P.op("dve", lambda e, q4=q4: e.tensor_copy(out=gate_bc[:, q4 * 512:(q4 + 1) * 512], in_=banks[1 + q4][:, :]), reads=[bb[1 + q4]], writes=[b_gate])

        x_v = x.rearrange("(t p) d -> t p d", p=128)
        y_v = yin.rearrange("(t p) d -> t p d", p=128)
        o_v = out_d.rearrange("(t p) d -> t p d", p=128)
        wmg_v = wmg_d.rearrange("(kc p) n -> p kc n", p=128)
        wbrn_v = wbrn_d.rearrange("(kc p) n -> p kc n", p=128)
        wbrg_v = wbrg_d.rearrange("(kc p) n -> p kc n", p=128)
        wout_v = wout_d.rearrange("(kc p) n -> p kc n", p=128)
        rr = {"mm": 0, "w": 0, "wo": 0, "x": 0}

        def mmbank():
            i = rr["mm"] % 4
            rr["mm"] += 1
            return i

        for ST in range(NST):
            for tt in range(4):
                t = ST * 4 + tt
                xb_ = rr["x"] % 2
                rr["x"] += 1
                ld("sp", xs[xb_][:], x_v[t], [b_xs[xb_]])
                ld("sp", ytile[tt % 2][:], y_v[t], [b_yt[tt % 2]])
                P.op("act", lambda e, xb_=xb_, tt=tt: e.activation(out=xnb[:], in_=xs[xb_][:], func=AF.Square, accum_out=small[:, tt:tt + 1]),
                     reads=[b_xs[xb_]], writes=[b_xnb, b_small])
                P.op("act", lambda e, tt=tt: e.activation(out=small[:, 8 + tt:9 + tt], in_=small[:, tt:tt + 1], func=AF.Ln, scale=1.0 / D, bias=EPS),
                     reads=[b_small], writes=[b_small])
                P.op("act", lambda e, tt=tt: e.activation(out=small[:, 16 + tt:17 + tt], in_=small[:, 8 + tt:9 + tt], func=AF.Exp, scale=-0.5),
                     reads=[b_small], writes=[b_small])
                P.op("dve", lambda e, xb_=xb_, tt=tt: e.tensor_scalar(out=xnb[:], in0=xs[xb_][:], scalar1=small[:, 16 + tt:17 + tt], scalar2=None, op0=ALU.mult),
                     reads=[b_xs[xb_], b_small], writes=[b_xnb])
                for src_i, (src, sbuf_, dst, dbuf) in enumerate(((xnb, b_xnb, hT, b_hT), (ytile[tt % 2], b_yt[tt % 2], yT, b_yT))):
                    for half in range(2):
                        bk = 4 + 2 * src_i + half
                        pv = banks[bk][:].bitcast(BF16)
                        for c8 in range(8):
                            kc = half * 8 + c8
                            P.op("pe", lambda e, pv=pv, c8=c8, kc=kc, src=src: e.transpose(pv[:, c8 * 128:(c8 + 1) * 128], src[:, kc * 128:(kc + 1) * 128], identb[:]),
                                 reads=[sbuf_, b_tab], writes=[bb[bk]])
                        if src_i == 0:
                            for c8 in range(8):
                                kc = half * 8 + c8
                                P.op("dve", lambda e, pv=pv, c8=c8, kc=kc, tt=tt: e.tensor_scalar(
                                    out=hT[:, kc, tt * 128:(tt + 1) * 128], in0=pv[:, c8 * 128:(c8 + 1) * 128],
                                    scalar1=A_s[:, kc:kc + 1], scalar2=B_s[:, kc:kc + 1], op0=ALU.mult, op1=ALU.add),
                                    reads=[bb[bk], b_AB], writes=[b_hT])
                        else:
                            P.op("dve", lambda e, pv=pv, half=half, tt=tt: e.tensor_copy(
                                out=yT[:, half * 8:half * 8 + 8, tt * 128:(tt + 1) * 128], in_=pv.rearrange("p (c q) -> p c q", q=128)),
                                reads=[bb[bk]], writes=[b_yT])
            for cb in range(NCB):
                wb = rr["w"] % 2
                rr["w"] += 1
                c0 = cb * CB
                ld("pool", wmgb[wb][:, 0, :, :], wmg_v[:, :, c0:c0 + CB], [b_wmg[wb]])
                ld("pool", wmgb[wb][:, 1, :, :], wmg_v[:, :, 2048 + c0:2048 + c0 + CB], [b_wmg[wb]])
                ld("pool", wbrb[wb][:, 0, :, :], wbrn_v[:, :, c0:c0 + CB], [b_wbr[wb]])
                ld("pool", wbrb[wb][:, 1, :, :], wbrg_v[:, :, c0:c0 + CB], [b_wbr[wb]])
                for tt in range(4):
                    tsl = slice(tt * 128, tt * 128 + 128)
                    bk = mmbank()
                    for gi in range(2):
                        oc = gi * CB
                        for kc in range(KC):
                            P.op("pe", lambda e, bk=bk, oc=oc, gi=gi, kc=kc, tsl=tsl, wb=wb: e.matmul(
                                banks[bk][:, oc:oc + CB], lhsT=hT[:, kc, tsl], rhs=wmgb[wb][:, gi, kc, :],
                                start=(kc == 0 and gi == 0), stop=False, skip_group_check=True),
                                reads=[b_hT, b_wmg[wb]], writes=[bb[bk]])
                        P.op("pe", lambda e, bk=bk, oc=oc, gi=gi, c0=c0: e.matmul(
                            banks[bk][:, oc:oc + CB], lhsT=ones_s[0:1, :], rhs=bmg_s[0:1, gi * 2048 + c0:gi * 2048 + c0 + CB],
                            start=False, stop=True, skip_group_check=True), reads=[b_w], writes=[bb[bk]])
                    bk2 = mmbank()
                    for gi in range(2):
                        oc = gi * CB
                        for kc in range(8):
                            P.op("pe", lambda e, bk2=bk2, oc=oc, gi=gi, kc=kc, tsl=tsl, wb=wb: e.matmul(
                                banks[bk2][:, oc:oc + CB], lhsT=yT[:, gi * 8 + kc, tsl], rhs=wbrb[wb][:, gi, kc, :],
                                start=(kc == 0 and gi == 0), stop=(kc == 7), skip_group_check=True),
                                reads=[b_yT, b_wbr[wb]], writes=[bb[bk2]])
                    P.op("act", lambda e, bk=bk: e.activation(out=g1[:], in_=banks[bk][:, 0:CB], func=AF.Exp, scale=-1.0), reads=[bb[bk]], writes=[b_g])
                    P.op("act", lambda e, bk=bk: e.activation(out=g2[:], in_=banks[bk][:, CB:2 * CB], func=AF.Exp, scale=-1.0), reads=[bb[bk]], writes=[b_g])
                    P.op("dve", lambda e: e.tensor_scalar(out=g1[:], in0=g1[:], scalar1=1.0, scalar2=None, op0=ALU.add), reads=[b_g], writes=[b_g])
                    P.op("dve", lambda e: e.tensor_scalar(out=g2[:], in0=g2[:], scalar1=1.0, scalar2=None, op0=ALU.add), reads=[b_g], writes=[b_g])
                    P.op("dve", lambda e: e.reciprocal(out=g1[:], in_=g1[:]), reads=[b_g], writes=[b_g])
                    P.op("dve", lambda e: e.reciprocal(out=g2[:], in_=g2[:]), reads=[b_g], writes=[b_g])
                    P.op("dve", lambda e, bk2=bk2: e.tensor_tensor(out=t1[:], in0=banks[bk2][:, 0:CB], in1=g1[:], op=ALU.mult), reads=[bb[bk2], b_g], writes=[b_g])
                    P.op("dve", lambda e, bk2=bk2: e.tensor_tensor(out=g2[:], in0=banks[bk2][:, CB:2 * CB], in1=g2[:], op=ALU.mult), reads=[bb[bk2], b_g], writes=[b_g])
                    P.op("dve", lambda e, tt=tt, c0=c0: e.tensor_tensor(out=mtok[tt][:, c0:c0 + CB], in0=t1[:], in1=g2[:], op=ALU.add), reads=[b_g], writes=[b_mtok[tt]])
            for tt in range(4):
                t = ST * 4 + tt
                for half in range(2):
                    bk = 4 + half
                    pv = banks[bk][:].bitcast(BF16)
                    for c8 in range(8):
                        kc = half * 8 + c8
                        P.op("pe", lambda e, pv=pv, c8=c8, kc=kc, tt=tt: e.transpose(pv[:, c8 * 128:(c8 + 1) * 128], mtok[tt][:, kc * 128:(kc + 1) * 128], identb[:]),
                             reads=[b_mtok[tt], b_tab], writes=[bb[bk]])
                    P.op("dve", lambda e, pv=pv, half=half: e.tensor_copy(out=mT[:, half * 8:half * 8 + 8, :], in_=pv.rearrange("p (c q) -> p c q", q=128)),
                         reads=[bb[bk]], writes=[b_mT])
                xb_ = rr["x"] % 2
                rr["x"] += 1
                ld("sp", xs[xb_][:], x_v[t], [b_xs[xb_]])
                for ob in range(4):
                    wo = rr["wo"] % 2
                    rr["wo"] += 1
                    ld("pool", woutb[wo][:], wout_v[:, :, ob * 512:(ob + 1) * 512], [b_wout[wo]])
                    bk = mmbank()
                    for kc in range(KC):
                        P.op("pe", lambda e, bk=bk, kc=kc, wo=wo: e.matmul(banks[bk][:, :], lhsT=mT[:, kc, :], rhs=woutb[wo][:, kc, :],
                                                                          start=(kc == 0), stop=(kc == KC - 1)),
                             reads=[b_mT, b_wout[wo]], writes=[bb[bk]])
                    osl = slice(ob * 512, ob * 512 + 512)
                    P.op("dve", lambda e, bk=bk, osl=osl: e.tensor_tensor(out=xo[:, osl], in0=banks[bk][:, :], in1=gate_bc[:, osl], op=ALU.mult),
                         reads=[bb[bk], b_gate], writes=[b_xo])
                    P.op("dve", lambda e, osl=osl, xb_=xb_: e.tensor_tensor(out=xo[:, osl], in0=xo[:, osl], in1=xs[xb_][:, osl], op=ALU.add),
                         reads=[b_xo, b_xs[xb_]], writes=[b_xo])
                P.op("act", lambda e, xb_=xb_: e.activation(out=xs[xb_][:], in_=xo[:], func=AF.Square, accum_out=small[:, 24:25]),
                     reads=[b_xo], writes=[b_xs[xb_], b_small])
                P.op("act", lambda e: e.activation(out=small[:, 25:26], in_=small[:, 24:25], func=AF.Ln, scale=1.0 / D, bias=EPS), reads=[b_small], writes=[b_small])
                P.op("act", lambda e: e.activation(out=small[:, 26:27], in_=small[:, 25:26], func=AF.Exp, scale=-0.5), reads=[b_small], writes=[b_small])
                P.op("dve", lambda e, xb_=xb_: e.scalar_tensor_tensor(out=xs[xb_][:], in0=xo[:], scalar=small[:, 26:27], in1=fgain_s[:], op0=ALU.mult, op1=ALU.mult),
                     reads=[b_xo, b_small, b_w], writes=[b_xs[xb_]])
                P.dma("sp", lambda e, xb_=xb_, t=t: e.dma_start(out=o_v[t], in_=xs[xb_][:]), reads=[b_xs[xb_]])
        toks = [t_ for t_ in P.dma_last if t_ is not None]
        toks += [(e_, P.cnt[e_]) for e_ in ("pe", "act", "dve", "pool") if P.cnt[e_] > 0]
        for e_ in ("pool", "sp", "act", "dve", "pe"):
            P.wait_all(e_, toks)
        P.emit()
    return nc


def kernel(**inputs):
    from concourse.bass_utils import run_bass_kernel_spmd
    inp = {k_: np.asarray(v) for k_, v in inputs.items()}
    Bn, T, Dm = inp["x"].shape
    nc1 = build_l1(T)
    in_maps = [host_inputs_l1(inp, c // 4, c % 4, T) for c in range(8)]
    res = run_bass_kernel_spmd(nc1, in_maps, core_ids=list(range(8)))
    ys = [np.asarray(r["y"]) for r in res.results]
    del in_maps
    TQ = T // 4
    nc2 = build_l2(TQ)
    in_maps2 = [host_inputs_l2(inp, ys, c // 4, c % 4, TQ) for c in range(8)]
    res2 = run_bass_kernel_spmd(nc2, in_maps2, core_ids=list(range(8)))
    out = np.empty((Bn, T, Dm), np.float32)
    for c in range(8):
        out[c // 4, (c % 4) * TQ:(c % 4 + 1) * TQ] = np.asarray(res2.results[c]["out"])
    return out
```

```python
import contextlib

ENGS = ("pe", "act", "dve", "pool", "sp")
USE_MODE_DRAINS = False
N_DMA_SEMS = 28


class Buf:
    __slots__ = ("name", "w", "r")

    def __init__(self, name=""):
        self.name = name
        self.w = {}
        self.r = {}


class Prog:
    def __init__(self, nc):
        self.nc = nc
        self.streams = {e: [] for e in ENGS}
        self.cnt = {e: 0 for e in ENGS}
        self.known = {e: {} for e in ENGS}
        self.dma_i = 0
        self.dma_ip = 0
        self.dma_cnt = [0] * N_DMA_SEMS
        self.dma_last = [None] * N_DMA_SEMS
        self.n_ops = 0

    def _deps(self, reads, writes):
        deps = {}

        def add(d):
            for k, v in d.items():
                if deps.get(k, -1) < v:
                    deps[k] = v
        for b in reads:
            add(b.w)
        for b in writes:
            add(b.w)
            add(b.r)
        return deps

    def _commit(self, tok, reads, writes):
        k, v = tok
        for b in reads:
            b.r[k] = v
        for b in writes:
            b.w = {k: v}
            b.r = {}

    def _waits(self, eng, deps):
        out = []
        kn = self.known[eng]
        for k, v in deps.items():
            if eng == "pe" and k == "pe":
                continue
            if kn.get(k, -1) >= v:
                continue
            kn[k] = v
            out.append((k, v))
        return out

    def op(self, eng, fn, reads=(), writes=(), mode="full"):
        deps = self._deps(reads, writes)
        waits = self._waits(eng, deps)
        if eng == "pe":
            if USE_MODE_DRAINS and mode != getattr(self, "pe_mode", mode) and self.cnt["pe"] > 0:
                v = self.cnt["pe"]
                if self.known["pe"].get("pe", -1) < v:
                    self.known["pe"]["pe"] = v
                    waits.append(("pe", v))
            self.pe_mode = mode
        self.cnt[eng] += 1
        tok = (eng, self.cnt[eng])
        self.streams[eng].append((waits, fn, tok))
        self._commit(tok, reads, writes)
        self.n_ops += 1
        return tok

    def dma(self, q, fn, reads=(), writes=()):
        deps = self._deps(reads, writes)
        if q == "pool":
            s = self.dma_ip % 10
            self.dma_ip += 1
        else:
            s = 10 + self.dma_i % (N_DMA_SEMS - 10)
            self.dma_i += 1
        if self.dma_last[s] is not None:
            k, v = self.dma_last[s]
            if deps.get(k, -1) < v:
                deps[k] = v
        waits = self._waits(q, deps)
        self.dma_cnt[s] += 16
        tok = (("dma", s), self.dma_cnt[s])
        self.dma_last[s] = tok
        if q == "pool":
            self.last_pool = tok
        self.streams[q].append((waits, fn, tok))
        self._commit(tok, reads, writes)
        self.n_ops += 1
        return tok

    def wait_all(self, eng, toks):
        deps = {}
        for k, v in toks:
            if deps.get(k, -1) < v:
                deps[k] = v
        waits = self._waits(eng, deps)
        self.streams[eng].append((waits, None, None))

    def emit(self):
        nc = self.nc
        with contextlib.ExitStack() as st:
            sems = {}
            for e in ENGS:
                sems[e] = st.enter_context(nc.semaphore("s_" + e))
            for i in range(N_DMA_SEMS):
                sems[("dma", i)] = st.enter_context(nc.semaphore("s_dma%d" % i))
            block = st.enter_context(nc.Block())
            objs = {"pe": block.tensor, "act": block.scalar, "dve": block.vector,
                    "pool": block.gpsimd, "sp": block.sync}

            needed = {e: set() for e in ENGS}
            for e in ENGS:
                for (waits, fn, tok) in self.streams[e]:
                    for (k, v) in waits:
                        if not isinstance(k, tuple):
                            needed[k].add(v)
            rank = {e: {v: i + 1 for i, v in enumerate(sorted(needed[e]))} for e in ENGS}
            self.n_incs = {e: len(needed[e]) for e in ENGS}

            def mk(e):
                def body(eng):
                    for (waits, fn, tok) in self.streams[e]:
                        for (k, v) in waits:
                            if isinstance(k, tuple):
                                eng.wait_ge(sems[k], v)
                            else:
                                eng.wait_ge(sems[k], rank[k][v])
                        if fn is None:
                            continue
                        ins = fn(eng)
                        k, v = tok
                        if isinstance(k, tuple):
                            ins.then_inc(sems[k], 16)
                        elif v in needed[k]:
                            ins.then_inc(sems[k], 1)
                return body
            for e in ENGS:
                if self.streams[e]:
                    objs[e](mk(e))


import contextlib
import numpy as np
import ml_dtypes
import concourse.bass as bass
import concourse.mybir as mybir

F32 = mybir.dt.float32
BF16 = mybir.dt.bfloat16
AF = mybir.ActivationFunctionType
ALU = mybir.AluOpType
BF = ml_dtypes.bfloat16

D = 2048
KC = 16
NEGM = -30000.0
BIG = 1.0e30
EPS = 1e-6
NFC = 10
FCH_M = [64, 64, 64, 64, 128, 64, 64, 128, 128, 16]
FCH_OFF = [0, 64, 128, 192, 256, 384, 448, 512, 640, 768]
NF = 784
NTA = 396
NTB = 512
NT_ = NTA + NTB


def split3(a):
    a = np.asarray(a, np.float64)
    hi = a.astype(BF).astype(np.float64)
    mid = (a - hi).astype(BF).astype(np.float64)
    lo = (a - hi - mid).astype(BF).astype(np.float64)
    return hi.astype(BF), mid.astype(BF), lo.astype(BF)


def host_tables(T, g):
    slopes = 2.0 ** (-8.0 * np.arange(1, 17, dtype=np.float64) / 16)[4 * g:4 * g + 4]
    t = np.arange(T, dtype=np.float64)
    qaug = np.zeros((8, 4, T), BF)
    for h in range(4):
        a, b_, c_ = split3(-slopes[h] * t)
        qaug[0, h], qaug[1, h], qaug[2, h] = a, b_, c_
        a, b_, c_ = split3(np.full(T, 128.0 * slopes[h]))
        qaug[3, h], qaug[4, h], qaug[5, h] = a, b_, c_
        a, b_, c_ = split3(np.full(T, slopes[h]))
        qaug[6, h], qaug[7, h] = a, b_

    def kside(pos):
        k = np.zeros((8, len(pos)), BF)
        k[0:3] = 1.0
        k[3:6] = (pos // 128).astype(BF)
        k[6:8] = (pos % 128).astype(BF)
        return k
    kaug = kside(np.arange(T))
    ncp = np.arange(512)
    cend = np.maximum(16 * (ncp - 1) + 31, 0)
    kcaug = kside(cend)
    M = np.zeros((512, 128), np.float32)
    w = {-1: 1.0, 0: 2.0, 1: 2.0, 2: 2.0, 3: 1.0}
    for npr in range(1, 512):
        n = npr - 1
        for blk in range(128):
            dd = n - 4 * blk
            if dd in w:
                M[npr, blk] = w[dd]
    Mtab = M.reshape(4, 128, 128).transpose(1, 0, 2).astype(BF)
    E4 = np.zeros((128, 32, 128), BF)
    for m in range(32):
        for half in range(2):
            r = 2 * m + half
            for gq in range(2):
                E4[64 * gq + r, m, 64 * half:64 * half + 64] = 1.0
    p = np.arange(128)[:, None]
    q = np.arange(128)[None, :]
    bias_diag = np.where(q - p >= 0, 0.0, NEGM).astype(BF)
    bias_far = np.where(p - q - 1 >= 0, 0.0, NEGM).astype(BF)
    bias_diag = np.tile(bias_diag, (1, 4))
    bias_far = np.tile(bias_far, (1, 4))
    gmask = np.where(q - p >= 0, 1.0, 0.0).astype(BF)
    reset = np.ones((128, 512), BF)
    reset[:, ::128] = 0.0
    qaug = np.ascontiguousarray(qaug.reshape(8, 4, T // 128, 128).transpose(0, 2, 1, 3))
    return dict(qaug=qaug, kaug=kaug, kcaug=kcaug, Mtab=Mtab, E4=E4,
                bias_diag=bias_diag, bias_far=bias_far, gmask=gmask, reset=reset,
                identb=np.eye(128).astype(BF), identf=np.eye(128, dtype=np.float32))


def host_inputs_l1(inp, b, j, T):
    g = j
    x = np.ascontiguousarray(inp["x"][b, :T])
    w_in = inp["w_in"][0]
    b_in = inp["b_in"][0]
    offs = np.cumsum([0, 1024, 256, 256, 256, 256, 256, 256, 48, 1024, 512, 512, 1024, 16, 1024, 2048, 2048])
    o_q, o_ck, o_cv, o_sk, o_sv, o_wk, o_wv, o_g, o_z, o_gq, o_gk, o_gv, o_ga, o_gz = offs[:14]
    fcols = np.concatenate([
        np.arange(o_q + 256 * g, o_q + 256 * g + 256),
        np.arange(o_ck + 64 * g, o_ck + 64 * g + 64), np.arange(o_cv + 64 * g, o_cv + 64 * g + 64),
        np.arange(o_sk + 64 * g, o_sk + 64 * g + 64), np.arange(o_wk + 64 * g, o_wk + 64 * g + 64),
        np.arange(o_gq + 128 * j, o_gq + 128 * j + 128), np.arange(o_gk + 128 * j, o_gk + 128 * j + 128),
        np.arange(o_ga, o_ga + 16)])
    tcols = np.concatenate([
        np.arange(o_sv + 64 * g, o_sv + 64 * g + 64), np.arange(o_wv + 64 * g, o_wv + 64 * g + 64),
        np.arange(o_g + 12 * g, o_g + 12 * g + 12), np.arange(o_z + 256 * g, o_z + 256 * g + 256),
        np.arange(o_gv + 256 * j, o_gv + 256 * j + 256), np.arange(o_gz + 256 * j, o_gz + 256 * j + 256)])
    assert len(fcols) == NF and len(tcols) == NT_
    wf = np.ascontiguousarray(w_in[:, fcols])
    wt = np.ascontiguousarray(w_in[:, tcols])
    bf = np.zeros((128, NFC), np.float32)
    bfl = b_in[fcols]
    for ci in range(NFC):
        bf[:FCH_M[ci], ci] = bfl[FCH_OFF[ci]:FCH_OFF[ci] + FCH_M[ci]]
    bt = np.ascontiguousarray(b_in[tcols][None, :])
    w1kv = np.concatenate([inp["cmp_w1_k"][0].reshape(32, 64, 256).transpose(1, 0, 2),
                           inp["cmp_w1_v"][0].reshape(32, 64, 256).transpose(1, 0, 2)], axis=0)
    w2k = inp["cmp_w2_k"][0].reshape(2, 128, 64).transpose(1, 0, 2)
    w2v = inp["cmp_w2_v"][0].reshape(2, 128, 64).transpose(1, 0, 2)
    posT = np.concatenate([inp["cmp_pos_k"][0].T, inp["cmp_pos_v"][0].T], axis=0)
    d = dict(
        x=x,
        cT=np.ascontiguousarray(inp["c"][b].reshape(16, 128).T),
        w_ada=np.ascontiguousarray(inp["w_ada"][0]),
        b_ada=np.ascontiguousarray(inp["b_ada"][0].reshape(48, 128).T),
        gainT=np.ascontiguousarray(inp["norm_gain"][0].reshape(16, 128).T),
        wf=wf, wt=wt, bf=bf, bt=bt,
        w1kv=np.ascontiguousarray(w1kv), w2k=np.ascontiguousarray(w2k), w2v=np.ascontiguousarray(w2v),
        posT=np.ascontiguousarray(posT),
        walpha=np.ascontiguousarray(inp["gla_w_alpha"][0][:, 128 * j:128 * j + 128]),
        balpha=np.ascontiguousarray(inp["gla_b_alpha"][0][128 * j:128 * j + 128].reshape(128, 1)),
        ggain=np.ascontiguousarray(np.broadcast_to(inp["gla_norm_gain"][0][None, :], (128, 256))),
    )
    d.update(host_tables(T, g))
    return d


class _Stop(Exception):
    pass


def build_l1(T, debug=False, stage=99):
    NST = T // 512
    NQT = T // 128
    nc = bass.Bass("TRN2", target_bir_lowering=False)

    def din(name, shape, dt=F32):
        return nc.dram_tensor(name, list(shape), dt, kind="ExternalInput").ap()
    x = din("x", [T, D])
    cT_d = din("cT", [128, 16]); wada_d = din("w_ada", [D, 6144]); bada_d = din("b_ada", [128, 48])
    gainT_d = din("gainT", [128, 16])
    wf_d = din("wf", [D, NF]); wt_d = din("wt", [D, NT_]); bf_d = din("bf", [128, NFC]); bt_d = din("bt", [1, NT_])
    w1kv_d = din("w1kv", [128, 32, 256]); w2k_d = din("w2k", [128, 2, 64]); w2v_d = din("w2v", [128, 2, 64])
    posT_d = din("posT", [128, 32])
    walpha_d = din("walpha", [16, 128]); balpha_d = din("balpha", [128, 1]); ggain_d = din("ggain", [128, 256])
    qaug_d = din("qaug", [8, T // 128, 4, 128], BF16); kaug_d = din("kaug", [8, T], BF16); kcaug_d = din("kcaug", [8, 512], BF16)
    Mtab_d = din("Mtab", [128, 4, 128], BF16); E4_d = din("E4", [128, 32, 128], BF16)
    bdiag_d = din("bias_diag", [128, 512], BF16); bfar_d = din("bias_far", [128, 512], BF16)
    gmask_d = din("gmask", [128, 128], BF16); reset_d = din("reset", [128, 512], BF16)
    identb_d = din("identb", [128, 128], BF16); identf_d = din("identf", [128, 128])
    y_d = nc.dram_tensor("y", [T, 512], BF16, kind="ExternalOutput").ap()
    adao_d = nc.dram_tensor("ada_out", [128, 48], F32, kind="ExternalOutput").ap()
    dbg = {}
    if debug:
        dbg["qT"] = nc.dram_tensor("dbg_qT", [72, 4, 4, 128], BF16, kind="ExternalOutput").ap()
        dbg["tokA"] = nc.dram_tensor("dbg_tokA", [128, NTA], F32, kind="ExternalOutput").ap()
        dbg["ocmp"] = nc.dram_tensor("dbg_ocmp", [128, 4, 193], BF16, kind="ExternalOutput").ap()
        dbg["selb"] = nc.dram_tensor("dbg_selb", [128, 128], BF16, kind="ExternalOutput").ap()
        dbg["kcT"] = nc.dram_tensor("dbg_kcT", [72, 512], BF16, kind="ExternalOutput").ap()
        dbg["oT0"] = nc.dram_tensor("dbg_oT0", [65, 512], F32, kind="ExternalOutput").ap()
        dbg["oT1"] = nc.dram_tensor("dbg_oT1", [65, 512], F32, kind="ExternalOutput").ap()
        dbg["acc"] = nc.dram_tensor("dbg_acc", [128, 256], F32, kind="ExternalOutput").ap()
        dbg["pT"] = nc.dram_tensor("dbg_pT", [128, 512], BF16, kind="ExternalOutput").ap()
        dbg["den"] = nc.dram_tensor("dbg_den", [128, 12], F32, kind="ExternalOutput").ap()
        dbg["gsig"] = nc.dram_tensor("dbg_gsig", [128, 12], F32, kind="ExternalOutput").ap()
        dbg["coef"] = nc.dram_tensor("dbg_coef", [128, 12], F32, kind="ExternalOutput").ap()

    P = Prog(nc)
    with contextlib.ExitStack() as st:
        def sb(name, shape, dt=F32):
            return st.enter_context(nc.sbuf_tensor("sb_" + name, list(shape), dt))
        wf_s = sb("wf_s", [128, KC, NF], BF16); wt_s = sb("wt_s", [128, KC, NT_], BF16)
        bf_s = sb("bf_s", [128, NFC]); btb_s = sb("btb_s", [1, NT_], BF16)
        ones_s = sb("ones_s", [1, 128], BF16)
        xs = [sb("xs%d" % i, [128, D]) for i in range(2)]
        xnb = [sb("xnb%d" % i, [128, D], BF16) for i in range(2)]
        hT = sb("hT", [128, KC, 512], BF16)
        A_s = sb("A_s", [128, 16]); B_s = sb("B_s", [128, 16]); ada_s = sb("ada_s", [128, 48])
        cT_s = sb("cT_s", [128, 16]); gainT_s = sb("gainT_s", [128, 16]); bada_s = sb("bada_s", [128, 48])
        small = sb("small", [128, 64])
        qT = sb("qT", [128, 4, 4, 128], BF16)
        kslcT = sb("kslcT", [128, T], BF16)
        kwinT = sb("kwinT", [128, 1024], BF16)
        vslc = sb("vslc", [128, NQT, 66], BF16)
        vwin = sb("vwin", [128, 8, 66], BF16)
        cmpbuf = sb("cmpbuf", [128, 528], BF16)
        w1kv_s = sb("w1kv_s", [128, 32, 256], BF16)
        w2k_s = sb("w2k_s", [128, 2, 64], BF16); w2v_s = sb("w2v_s", [128, 2, 64], BF16)
        posT_s = sb("posT_s", [128, 32], BF16); posb_s = sb("posb_s", [128, 4])
        kcT = sb("kcT", [128, 512], BF16)
        vcM = sb("vcM", [128, 4, 194], BF16)
        E4_s = sb("E4_s", [128, 32, 128], BF16)
        bdiag_s = sb("bdiag_s", [128, 512], BF16); bfar_s = sb("bfar_s", [128, 512], BF16)
        gmask_s = sb("gmask_s", [128, 128], BF16); reset_s = sb("reset_s", [128, 512], BF16)
        identb = sb("identb", [128, 128], BF16); identf = sb("identf", [128, 128])
        walpha_s = sb("walpha_s", [16, 128], BF16); nbalpha_s = sb("nbalpha_s", [128, 1]); ggain_s = sb("ggain_s", [128, 256])
        gqT = sb("gqT", [128, 512], BF16); gkT = sb("gkT", [128, 512], BF16); gaT = sb("gaT", [16, 512], BF16)
        tokA = [sb("tokA%d" % i, [128, NTA]) for i in range(4)]
        tokB = [sb("tokB%d" % i, [128, 256]) for i in range(4)]
        gvb = [sb("gvb%d" % i, [128, 256], BF16) for i in range(4)]
        e1 = sb("e1", [128, 512]); spl = e1; cs = sb("cs", [128, 512])
        eb = sb("eb", [128, 512]); enb = e1
        qtl = sb("qtl", [128, 512], BF16); ktl = sb("ktl", [128, 512], BF16)
        ktok = sb("ktok", [128, 128], BF16); atm = sb("atm", [128, 128], BF16)
        S_s = sb("S_s", [128, 256]); Sb_s = sb("Sb_s", [128, 256], BF16); stmp = sb("stmp", [128, 256])
        gl1 = sb("gl1", [128, 256]); junk = gl1
        yt = [sb("yt%d" % i, [128, 512], BF16) for i in range(4)]
        hsb = sb("hsb", [128, 4, 32]); hex_ = sb("hex_", [128, 4, 32]); hidT = sb("hidT", [128, 4, 32], BF16)
        vcst = sb("vcst", [32, 64], BF16)
        pT = [sb("pT%d" % i, [128, 512], BF16) for i in range(3)]
        maskc = [sb("maskc%d" % i, [128, 512], BF16) for i in range(2)]
        zeros_b = sb("zeros_b", [128, 512], BF16)
        score = sb("score", [128, 128]); swork = sb("swork", [128, 128]); m8 = sb("m8", [128, 16])
        selb = [sb("selb%d" % i, [128, 128], BF16) for i in range(2)]; selT4 = [[sb("selT4%d_%d" % (i, g_), [128, 4, 128], BF16) for g_ in range(2)] for i in range(2)]
        oT_sb = None
        coef = [sb("coef%d" % i, [128, 12]) for i in range(2)]; den = [sb("den%d" % i, [128, 12]) for i in range(2)]; gsig = [sb("gsig%d" % i, [128, 12]) for i in range(2)]
        szn = gl1; acc = [sb("acc%d" % i, [128, 256]) for i in range(2)]

        banks = [st.enter_context(nc.psum_tensor("bank%d" % i, [128, 512], F32)) for i in range(8)]
        bb = [Buf("bank%d" % i) for i in range(8)]

        def B(name):
            return Buf(name)
        b_w = B("weights"); b_tab = B("tables"); b_hT = B("hT"); b_AB = B("AB")
        b_xs = [B("xs0"), B("xs1")]; b_xnb = [B("xnb0"), B("xnb1")]; b_small = B("small"); b_sm = [B("sm%d" % i) for i in range(4)]
        b_qT = B("qT"); b_kslc = [B("kslc%d" % i) for i in range(NQT)]
        b_kwin = [B("kwin%d" % i) for i in range(8)]; b_vslc = [B("vslc%d" % i) for i in range(NQT)]
        b_vwin = [B("vwin%d" % i) for i in range(8)]
        b_cmpbuf = B("cmpbuf"); b_kcT = B("kcT"); b_vcM = B("vcM"); b_hid = B("hid")
        b_gT = B("gT"); b_tokA = [B("tokA%d" % i) for i in range(4)]; b_tokB = [B("tokB%d" % i) for i in range(4)]
        b_gvb = [B("gvb%d" % i) for i in range(4)]
        b_gl = B("glatemps"); b_S = B("S"); b_glc = B("glachunk"); b_yt = [B("yt%d" % i) for i in range(4)]
        b_pT = [B("pT%d" % i) for i in range(3)]; b_maskc = [B("maskc0"), B("maskc1")]
        b_sel = B("sel"); b_selT = [B("selT0"), B("selT1")]; b_selb = [B("selb0"), B("selb1")]; b_oT = B("oTsb"); b_fin = B("fin"); b_finA = [B("finA0"), B("finA1")]

        def chk(k):
            if stage == k:
                raise _Stop()
        try:
            def ld(q, out, in_, writes):
                return P.dma(q, lambda e: e.dma_start(out=out, in_=in_), writes=writes)
            wf_v = wf_d.rearrange("(kc p) n -> p kc n", p=128)
            wt_v = wt_d.rearrange("(kc p) n -> p kc n", p=128)
            for k4 in range(0, KC, 4):
                ld("pool", wf_s[:, k4:k4 + 4, :], wf_v[:, k4:k4 + 4, :], [b_w])
                ld("pool", wt_s[:, k4:k4 + 4, :], wt_v[:, k4:k4 + 4, :], [b_w])
            for pp in range(0, 32, 8):
                ld("pool", w1kv_s[:, pp:pp + 8, :], w1kv_d[:, pp:pp + 8, :], [b_w])
            ld("pool", w2k_s[:], w2k_d, [b_w]); ld("pool", w2v_s[:], w2v_d, [b_w])
            ld("pool", posT_s[:], posT_d, [b_w]); ld("pool", walpha_s[:], walpha_d, [b_w])
            ld("sp", bf_s[:], bf_d, [b_w]); ld("pool", btb_s[:], bt_d, [b_w])
            ld("sp", cT_s[:], cT_d, [b_AB]); ld("sp", gainT_s[:], gainT_d, [b_AB]); ld("sp", bada_s[:], bada_d, [b_AB])
            ld("sp", nbalpha_s[:], balpha_d, [b_w]); ld("sp", ggain_s[:], ggain_d, [b_w])
            ld("sp", E4_s[:], E4_d, [b_tab]); ld("sp", bdiag_s[:], bdiag_d, [b_tab]); ld("sp", bfar_s[:], bfar_d, [b_tab])
            ld("sp", gmask_s[:], gmask_d, [b_tab]); ld("sp", reset_s[:], reset_d, [b_tab])
            ld("sp", identb[:], identb_d, [b_tab]); ld("sp", identf[:], identf_d, [b_tab])
            ld("sp", vcM[:, :, 65:193], Mtab_d, [b_vcM])
            for i in range(NQT):
                pass
            P.op("pool", lambda e: e.memset(ones_s[:], 1.0), writes=[b_w])
            P.op("pool", lambda e: e.memset(zeros_b[:], 0.0), writes=[b_tab])
            for par_ in range(2):
                for g_ in range(2):
                    P.op("pool", lambda e, par_=par_, g_=g_: e.memset(selT4[par_][g_][:], 0.0), writes=[b_selT[par_]])
            P.op("pool", lambda e: e.memset(cmpbuf[:], 0.0), writes=[b_cmpbuf])
            P.op("pool", lambda e: e.memset(S_s[:], 0.0), writes=[b_S])
            P.op("pool", lambda e: e.memset(Sb_s[:], 0.0), writes=[b_S])
            P.op("pool", lambda e: e.memset(vcM[:, :, 0:64], 0.0), writes=[b_vcM])
            P.op("pool", lambda e: e.memset(vcM[:, :, 64:65], 1.0), writes=[b_vcM])
            P.op("pool", lambda e: e.memset(kcT[:, :], 0.0), writes=[b_kcT])
            ld("sp", kcT[64:72, :], kcaug_d, [b_kcT])
            P.op("pool", lambda e: e.memset(kslcT[:, :], 0.0), writes=b_kslc)
            ld("sp", kslcT[64:72, :], kaug_d, b_kslc)
            P.op("pool", lambda e: e.memset(kwinT[:, :], 0.0), writes=b_kwin)
            P.op("pool", lambda e: e.memset(qT[:].rearrange("p a b c -> p (a b c)"), 0.0), writes=[b_qT])
            for i in range(NQT):
                P.op("pool", lambda e, i=i: e.memset(vslc[:, i, 64:65], 1.0), writes=[b_vslc[i]])
            for i in range(8):
                P.op("pool", lambda e, i=i: e.memset(vwin[:, i, 64:65], 1.0), writes=[b_vwin[i]])
            P.op("dve", lambda e: e.tensor_scalar(out=nbalpha_s[:], in0=nbalpha_s[:], scalar1=-1.0, scalar2=None, op0=ALU.mult),
                 reads=[b_w], writes=[b_w])

            chk(0)
            wada_v = wada_d.rearrange("(kc p) n -> p kc n", p=128)
            for cc in range(48):
                xb_ = cc % 2
                ld("sp", xs[xb_][:].rearrange("p (kc n) -> p kc n", kc=16), wada_v[:, :, cc * 128:(cc + 1) * 128], [b_xs[xb_]])
                wv = xs[xb_][:].rearrange("p (kc n) -> p kc n", kc=16)
                for kc in range(KC):
                    P.op("pe", lambda e, kc=kc, wv=wv, cc=cc: e.matmul(banks[0][:, cc:cc + 1], lhsT=wv[:, kc, :], rhs=cT_s[:, kc:kc + 1],
                                                                      start=(kc == 0), stop=(kc == KC - 1)),
                         reads=[b_xs[xb_], b_AB], writes=[bb[0]], mode="f32")
            P.op("dve", lambda e: e.tensor_tensor(out=ada_s[:], in0=banks[0][:, 0:48], in1=bada_s[:], op=ALU.add),
                 reads=[bb[0], b_AB], writes=[b_AB])
            P.op("dve", lambda e: e.tensor_scalar(out=A_s[:], in0=ada_s[:, 16:32], scalar1=1.0, scalar2=None, op0=ALU.add),
                 reads=[b_AB], writes=[b_AB])
            P.op("dve", lambda e: e.tensor_tensor(out=A_s[:], in0=A_s[:], in1=gainT_s[:], op=ALU.mult), reads=[b_AB], writes=[b_AB])
            P.op("dve", lambda e: e.tensor_copy(out=B_s[:], in_=ada_s[:, 0:16]), reads=[b_AB], writes=[b_AB])
            P.dma("pool", lambda e: e.dma_start(out=adao_d, in_=ada_s[:]), reads=[b_AB])

            chk(1)
            for kv in range(2):
                lo = 64 * kv
                for hc in range(2):
                    col = kv * 2 + hc
                    for p_ in range(32):
                        P.op("pe", lambda e, lo=lo, hc=hc, p_=p_, col=col: e.matmul(
                            banks[1][:, col:col + 1], lhsT=w1kv_s[lo:lo + 64, p_, hc * 128:(hc + 1) * 128],
                            rhs=posT_s[lo:lo + 64, p_:p_ + 1], start=(p_ == 0), stop=(p_ == 31)),
                            reads=[b_w], writes=[bb[1]], mode="r64")
            P.op("dve", lambda e: e.tensor_copy(out=posb_s[:], in_=banks[1][:, 0:4]), reads=[bb[1]], writes=[b_w])

            chk(2)
            x_v = x.rearrange("(t p) d -> t p d", p=128)
            y_v = y_d.rearrange("(t p) n -> t p n", p=128)
            tokx = {}

            def load_x(t):
                if t < NQT:
                    tokx[t] = ld("sp", xs[t % 2][:], x_v[t], [b_xs[t % 2]])
            load_x(0); load_x(1)

            rr = {"mm": 0, "st": 0, "pt": 0}
            fillc = {}

            def getfill(e):
                if "r" not in fillc:
                    fillc["r"] = e.to_reg(NEGM)
                return fillc["r"]

            def mmbank():
                i = rr["mm"] % 2
                rr["mm"] += 1
                return i

            def stbank():
                i = 4 + rr["st"] % 2
                rr["st"] += 1
                return i

            def norm_tile(ST, tt):
                t = ST * 4 + tt
                xb_ = t % 2
                P.op("act", lambda e, xb_=xb_, tt=tt: e.activation(out=xnb[xb_][:], in_=xs[xb_][:], func=AF.Square,
                                                                   accum_out=small[:, tt:tt + 1]),
                     reads=[b_xs[xb_]], writes=[b_xnb[xb_], b_sm[tt]])
                P.op("act", lambda e, tt=tt: e.activation(out=small[:, 8 + tt:9 + tt], in_=small[:, tt:tt + 1], func=AF.Ln,
                                                          scale=1.0 / D, bias=EPS),
                     reads=[b_sm[tt]], writes=[b_sm[tt]])
                P.op("act", lambda e, tt=tt: e.activation(out=small[:, 16 + tt:17 + tt], in_=small[:, 8 + tt:9 + tt], func=AF.Exp,
                                                          scale=-0.5),
                     reads=[b_sm[tt]], writes=[b_sm[tt]])
                P.op("dve", lambda e, xb_=xb_, tt=tt: e.tensor_scalar(out=xnb[xb_][:], in0=xs[xb_][:], scalar1=small[:, 16 + tt:17 + tt],
                                                                      scalar2=None, op0=ALU.mult),
                     reads=[b_xs[xb_], b_sm[tt]], writes=[b_xnb[xb_]])
                load_x(t + 2)
                for half in range(2):
                    bk = 2 + half
                    pv = banks[bk][:].bitcast(BF16)
                    for c8 in range(8):
                        kc = half * 8 + c8
                        P.op("pe", lambda e, pv=pv, c8=c8, kc=kc, xb_=xb_: e.transpose(
                            pv[:, c8 * 128:(c8 + 1) * 128], xnb[xb_][:, kc * 128:(kc + 1) * 128], identb[:]),
                            reads=[b_xnb[xb_], b_tab], writes=[bb[bk]], mode="tr")
                    for c8 in range(8):
                        kc = half * 8 + c8
                        eng = "dve" if c8 % 2 == 0 else "pool"
                        eng = "dve"
                        P.op(eng, lambda e, pv=pv, c8=c8, kc=kc, tt=tt: e.tensor_scalar(
                            out=hT[:, kc, tt * 128:(tt + 1) * 128], in0=pv[:, c8 * 128:(c8 + 1) * 128],
                            scalar1=A_s[:, kc:kc + 1], scalar2=B_s[:, kc:kc + 1], op0=ALU.mult, op1=ALU.add),
                            reads=[bb[bk], b_AB], writes=[b_hT])

            for tt_ in range(4):
                norm_tile(0, tt_)
            for ST in range(NST):
                chk(3)
                tsl = slice(ST * 512, ST * 512 + 512)
                P.dma("sp", lambda e, ST=ST: e.dma_start(out=qT[64:72, :, :, :], in_=qaug_d[:, ST * 4:ST * 4 + 4, :, :]), writes=[b_qT])
                for ci in range(NFC):
                    M_ = FCH_M[ci]; off = FCH_OFF[ci]
                    bk = mmbank()
                    for kc in range(KC):
                        P.op("pe", lambda e, bk=bk, M_=M_, off=off, kc=kc: e.matmul(
                            banks[bk][0:M_, :], lhsT=wf_s[:, kc, off:off + M_], rhs=hT[:, kc, :], start=(kc == 0), stop=(kc == KC - 1)),
                            reads=[b_w, b_hT], writes=[bb[bk]], mode=("full" if M_ > 64 else "c%d" % (64 if M_ > 32 else 32)))
                    src = banks[bk]
                    if ci < 4:
                        P.op("act", lambda e, src=src, ci=ci: e.activation(out=qT[0:64, :, ci, :], in_=src[0:64, :].rearrange("p (t q) -> p t q", q=128), func=AF.Identity,
                                                                           bias=bf_s[0:64, ci:ci + 1], scale=1.0),
                             reads=[bb[bk], b_w], writes=[b_qT])
                        P.op("dve", lambda e, ci=ci: e.tensor_scalar(out=qT[0:64, :, ci, :], in0=qT[0:64, :, ci, :], scalar1=0.125, scalar2=None,
                                                                     op0=ALU.mult), reads=[b_qT], writes=[b_qT])
                    elif ci == 4:
                        P.op("act", lambda e, src=src: e.activation(out=cmpbuf[:, 16:528], in_=src[:, :], func=AF.Identity,
                                                                    bias=bf_s[:, 4:5], scale=1.0),
                             reads=[bb[bk], b_w], writes=[b_cmpbuf])
                    elif ci == 5:
                        P.op("act", lambda e, src=src, tsl=tsl: e.activation(out=kslcT[0:64, tsl], in_=src[0:64, :], func=AF.Identity,
                                                                             bias=bf_s[0:64, 5:6], scale=1.0),
                             reads=[bb[bk], b_w], writes=[b_kslc[ST * 4 + k] for k in range(4)])
                    elif ci == 6:
                        rs = (ST % 2) * 512
                        P.op("act", lambda e, src=src, rs=rs: e.activation(out=kwinT[0:64, rs:rs + 512], in_=src[0:64, :], func=AF.Identity,
                                                                           bias=bf_s[0:64, 6:7], scale=1.0),
                             reads=[bb[bk], b_w], writes=[b_kwin[(ST % 2) * 4 + k] for k in range(4)])
                        P.dma("sp", lambda e, rs=rs, tsl=tsl: e.dma_start(out=kwinT[64:72, rs:rs + 512], in_=kaug_d[:, tsl]),
                              writes=[b_kwin[(ST % 2) * 4 + k] for k in range(4)])
                    elif ci == 7:
                        P.op("act", lambda e, src=src: e.activation(out=gqT[:], in_=src[:, :], func=AF.Identity, bias=bf_s[:, 7:8], scale=1.0),
                             reads=[bb[bk], b_w], writes=[b_gT])
                    elif ci == 8:
                        P.op("act", lambda e, src=src: e.activation(out=gkT[:], in_=src[:, :], func=AF.Identity, bias=bf_s[:, 8:9], scale=1.0),
                             reads=[bb[bk], b_w], writes=[b_gT])
                    else:
                        P.op("act", lambda e, src=src: e.activation(out=gaT[:], in_=src[0:16, :], func=AF.Identity, bias=bf_s[0:16, 9:10], scale=1.0),
                             reads=[bb[bk], b_w], writes=[b_gT])
                chk(4)
                for tt in range(4):
                    t = ST * 4 + tt
                    for grp in range(2):
                        if grp == 1:
                            chk(44)
                        if tt == 1:
                            chk(46)
                        bk = mmbank()
                        n0, n1 = (0, NTA) if grp == 0 else (NTA, NT_)
                        w_ = n1 - n0
                        for kc in range(KC):
                            P.op("pe", lambda e, bk=bk, kc=kc, tt=tt, n0=n0, n1=n1, w_=w_: e.matmul(
                                banks[bk][:, 0:w_], lhsT=hT[:, kc, tt * 128:(tt + 1) * 128], rhs=wt_s[:, kc, n0:n1], start=(kc == 0), stop=False),
                                reads=[b_w, b_hT], writes=[bb[bk]])
                        chk(41)
                        P.op("pe", lambda e, bk=bk, n0=n0, n1=n1, w_=w_: e.matmul(
                            banks[bk][:, 0:w_], lhsT=ones_s[0:1, :], rhs=btb_s[0:1, n0:n1], start=False, stop=True),
                            reads=[b_w], writes=[bb[bk]], mode="r32")
                        chk(42)
                        if grp == 0:
                            P.op("dve", lambda e, bk=bk, tt=tt: e.tensor_copy(out=tokA[tt][:], in_=banks[bk][:, 0:NTA]),
                                 reads=[bb[bk]], writes=[b_tokA[tt]])
                            chk(43)
                            P.op("dve", lambda e, tt=tt, t=t: e.tensor_copy(out=vslc[:, t, 0:64], in_=tokA[tt][:, 0:64]),
                                 reads=[b_tokA[tt]], writes=[b_vslc[t]])
                            P.op("dve", lambda e, tt=tt, t=t: e.tensor_copy(out=vwin[:, t % 8, 0:64], in_=tokA[tt][:, 64:128]),
                                 reads=[b_tokA[tt]], writes=[b_vwin[t % 8]])
                        else:
                            chk(45)
                            P.op("dve", lambda e, bk=bk, tt=tt: e.tensor_copy(out=tokB[tt][:], in_=banks[bk][:, 256:512]),
                                 reads=[bb[bk]], writes=[b_tokB[tt]])
                            P.op("dve", lambda e, bk=bk, tt=tt: e.tensor_copy(out=gvb[tt][:], in_=banks[bk][:, 0:256]),
                                 reads=[bb[bk]], writes=[b_gvb[tt]])
                if debug and ST == 0:
                    P.dma("pool", lambda e: e.dma_start(out=dbg["qT"], in_=qT[0:72]), reads=[b_qT])
                    P.dma("pool", lambda e: e.dma_start(out=dbg["tokA"], in_=tokA[0][:]), reads=[b_tokA[0]])

                if ST + 1 < NST:
                    norm_tile(ST + 1, 0)
                chk(5)
                bk = mmbank()
                P.op("pe", lambda e, bk=bk: e.matmul(banks[bk][:, :], lhsT=walpha_s[:, :], rhs=gaT[:, :], start=True, stop=True),
                     reads=[b_w, b_gT], writes=[bb[bk]], mode="r32")
                P.op("act", lambda e, bk=bk: e.activation(out=e1[:], in_=banks[bk][:, :], func=AF.Exp, scale=-1.0, bias=nbalpha_s[:, 0:1]),
                     reads=[bb[bk], b_w], writes=[b_gl])
                P.op("act", lambda e: e.activation(out=spl[:], in_=e1[:], func=AF.Ln, bias=1.0, scale=1.0), reads=[b_gl], writes=[b_gl])
                P.op("dve", lambda e: e.tensor_tensor_scan(out=cs[:], data0=reset_s[:], data1=spl[:], initial=0.0, op0=ALU.mult, op1=ALU.add),
                     reads=[b_gl, b_tab], writes=[b_gl])
                P.op("act", lambda e: e.activation(out=eb[:], in_=cs[:], func=AF.Exp, scale=-1.0 / 16), reads=[b_gl], writes=[b_gl])
                P.op("act", lambda e: e.activation(out=enb[:], in_=cs[:], func=AF.Exp, scale=1.0 / 16), reads=[b_gl], writes=[b_gl])
                P.op("dve", lambda e: e.scalar_tensor_tensor(out=qtl[:], in0=gqT[:], scalar=128.0 ** -0.5, in1=eb[:], op0=ALU.mult, op1=ALU.mult),
                     reads=[b_gl, b_gT], writes=[b_gl])
                P.op("dve", lambda e: e.tensor_tensor(out=ktl[:], in0=gkT[:], in1=enb[:], op=ALU.mult), reads=[b_gl, b_gT], writes=[b_gl])
                for tt in range(4):
                    t = ST * 4 + tt
                    cs_ = slice(tt * 128, tt * 128 + 128)
                    P.op("pe", lambda e, cs_=cs_: e.transpose(banks[6][:].bitcast(BF16)[:, 0:128], ktl[:, cs_], identb[:]),
                         reads=[b_gl, b_tab], writes=[bb[6]], mode="tr")
                    P.op("dve", lambda e: e.tensor_copy(out=ktok[:], in_=banks[6][:].bitcast(BF16)[:, 0:128]), reads=[bb[6]], writes=[b_glc])
                    P.op("pe", lambda e, cs_=cs_: e.matmul(banks[7][:, 0:128], lhsT=ktl[:, cs_], rhs=qtl[:, cs_], start=True, stop=True),
                         reads=[b_gl], writes=[bb[7]])
                    P.op("dve", lambda e: e.tensor_tensor(out=atm[:], in0=banks[7][:, 0:128], in1=gmask_s[:], op=ALU.mult),
                         reads=[bb[7], b_tab], writes=[b_glc])
                    bo = mmbank()
                    P.op("pe", lambda e, bo=bo, tt=tt: e.matmul(banks[bo][:, 0:256], lhsT=atm[:], rhs=gvb[tt][:], start=True, stop=False),
                         reads=[b_glc, b_gvb[tt]], writes=[bb[bo]])
                    P.op("pe", lambda e, bo=bo, cs_=cs_: e.matmul(banks[bo][:, 0:256], lhsT=qtl[:, cs_], rhs=Sb_s[:], start=False, stop=True),
                         reads=[b_gl, b_S], writes=[bb[bo]])
                    P.op("pe", lambda e, tt=tt: e.matmul(banks[7][:, 256:512], lhsT=ktok[:], rhs=gvb[tt][:], start=True, stop=True),
                         reads=[b_glc, b_gvb[tt]], writes=[bb[7]])
                    P.op("dve", lambda e: e.tensor_tensor(out=stmp[:], in0=banks[7][:, 256:512], in1=S_s[:], op=ALU.add),
                         reads=[bb[7], b_S], writes=[b_glc])
                    last = tt * 128 + 127
                    P.op("dve", lambda e, last=last: e.tensor_scalar(out=S_s[:], in0=stmp[:], scalar1=eb[:, last:last + 1], scalar2=None, op0=ALU.mult),
                         reads=[b_glc, b_gl], writes=[b_S])
                    P.op("dve", lambda e: e.tensor_copy(out=Sb_s[:], in_=S_s[:]), reads=[b_S], writes=[b_S])
                    yb_ = tt
                    P.op("act", lambda e, bo=bo: e.activation(out=junk[:], in_=banks[bo][:, 0:256], func=AF.Square, accum_out=small[:, 24:25]),
                         reads=[bb[bo]], writes=[b_fin, b_small])
                    P.op("act", lambda e: e.activation(out=small[:, 25:26], in_=small[:, 24:25], func=AF.Ln, scale=1.0 / 256, bias=EPS),
                         reads=[b_small], writes=[b_small])
                    P.op("act", lambda e: e.activation(out=small[:, 26:27], in_=small[:, 25:26], func=AF.Exp, scale=-0.5),
                         reads=[b_small], writes=[b_small])
                    P.op("act", lambda e, tt=tt: e.activation(out=gl1[:], in_=tokB[tt][:], func=AF.Exp, scale=-1.0),
                         reads=[b_tokB[tt]], writes=[b_fin])
                    P.op("dve", lambda e: e.tensor_scalar(out=gl1[:], in0=gl1[:], scalar1=1.0, scalar2=None, op0=ALU.add), reads=[b_fin], writes=[b_fin])
                    P.op("dve", lambda e: e.reciprocal(out=gl1[:], in_=gl1[:]), reads=[b_fin], writes=[b_fin])
                    P.op("dve", lambda e, tt=tt: e.tensor_tensor(out=gl1[:], in0=gl1[:], in1=tokB[tt][:], op=ALU.mult),
                         reads=[b_fin, b_tokB[tt]], writes=[b_fin])
                    P.op("dve", lambda e: e.tensor_tensor(out=gl1[:], in0=gl1[:], in1=ggain_s[:], op=ALU.mult), reads=[b_fin, b_w], writes=[b_fin])
                    P.op("dve", lambda e, bo=bo, yb_=yb_: e.scalar_tensor_tensor(out=yt[yb_][:, 256:512], in0=banks[bo][:, 0:256], scalar=small[:, 26:27],
                                                                                 in1=gl1[:], op0=ALU.mult, op1=ALU.mult),
                         reads=[bb[bo], b_fin, b_small], writes=[b_yt[yb_]])

                if ST + 1 < NST:
                    norm_tile(ST + 1, 1)
                chk(6)
                for kv in range(2):
                    lo = 64 * kv
                    bk = mmbank()
                    for hc in range(2):
                        for p_ in range(32):
                            P.op("pe", lambda e, bk=bk, lo=lo, hc=hc, p_=p_: e.matmul(
                                banks[bk][:, hc * 32:(hc + 1) * 32], lhsT=w1kv_s[lo:lo + 64, p_, hc * 128:(hc + 1) * 128],
                                rhs=cmpbuf[lo:lo + 64, p_:p_ + 497:16], start=(p_ == 0), stop=(p_ == 31)),
                                reads=[b_w, b_cmpbuf], writes=[bb[bk]], mode="r64")
                    for hc in range(2):
                        col = kv * 2 + hc
                        P.op("dve", lambda e, bk=bk, hc=hc, col=col: e.tensor_scalar(out=hsb[:, col, :], in0=banks[bk][:, hc * 32:(hc + 1) * 32],
                                                                                     scalar1=posb_s[:, col:col + 1], scalar2=None, op0=ALU.add),
                             reads=[bb[bk], b_w], writes=[b_hid])
                P.op("act", lambda e: e.activation(out=hex_[:], in_=hsb[:], func=AF.Exp, scale=-1.0), reads=[b_hid], writes=[b_hid])
                P.op("dve", lambda e: e.tensor_scalar(out=hex_[:], in0=hex_[:], scalar1=1.0, scalar2=None, op0=ALU.add), reads=[b_hid], writes=[b_hid])
                P.op("dve", lambda e: e.reciprocal(out=hex_[:], in_=hex_[:]), reads=[b_hid], writes=[b_hid])
                P.op("dve", lambda e: e.tensor_tensor(out=hidT[:], in0=hex_[:], in1=hsb[:], op=ALU.mult), reads=[b_hid], writes=[b_hid])
                bk = mmbank()
                for hc in range(2):
                    P.op("pe", lambda e, bk=bk, hc=hc: e.matmul(banks[bk][0:64, 0:32], lhsT=w2k_s[:, hc, :], rhs=hidT[:, hc, :],
                                                                start=(hc == 0), stop=(hc == 1)), reads=[b_w, b_hid], writes=[bb[bk]], mode="c64")
                P.op("dve", lambda e, bk=bk, ST=ST: e.tensor_copy(out=kcT[0:64, ST * 32:ST * 32 + 32], in_=banks[bk][0:64, 0:32]),
                     reads=[bb[bk]], writes=[b_kcT])
                for hc in range(2):
                    P.op("pe", lambda e, bk=bk, hc=hc: e.matmul(banks[bk][0:32, 64:128], lhsT=hidT[:, 2 + hc, :], rhs=w2v_s[:, hc, :],
                                                                start=(hc == 0), stop=(hc == 1)), reads=[b_w, b_hid], writes=[bb[bk]], mode="c32")
                P.op("dve", lambda e, bk=bk: e.tensor_copy(out=vcst[:], in_=banks[bk][0:32, 64:128]), reads=[bb[bk]], writes=[b_hid])
                pr = 32 * (ST % 4)
                P.dma("sp", lambda e, pr=pr, ST=ST: e.dma_start(out=vcM[pr:pr + 32, ST // 4, 0:64], in_=vcst[:]), reads=[b_hid], writes=[b_vcM])
                if ST == 0:
                    P.op("pool", lambda e: e.memset(vcM[0:1, 0, 0:193], 0.0), writes=[b_vcM])
                P.op("dve", lambda e: e.tensor_copy(out=cmpbuf[:, 0:16], in_=cmpbuf[:, 512:528]), reads=[b_cmpbuf], writes=[b_cmpbuf])

                if ST + 1 < NST:
                    norm_tile(ST + 1, 2)
                    norm_tile(ST + 1, 3)
                chk(7)
                def run_pipeline(jobs, depth=1):
                    pend = []
                    for jb in jobs:
                        jb[0]()
                        pend.append(jb)
                        if len(pend) > depth:
                            pend.pop(0)[1]()
                    for jb in pend:
                        jb[1]()

                def stage_A(tt):
                    i = ST * 4 + tt
                    par = i % 2
                    qv = qT[:, tt, :, :].rearrange("p h q -> p (h q)")
                    nnt = (8 * i + 7 + 127) // 128
                    jobs = []
                    for nt in range(nnt):
                        mrows = 128
                        st_ = {}

                        def qk(nt=nt, mrows=mrows, st_=st_):
                            sbk = stbank()
                            st_["sbk"] = sbk
                            mk = maskc[nt % 2]
                            P.op("pool", lambda e, mk=mk: e.affine_select(
                                out=mk[:], in_=zeros_b[:], pattern=[[0, 4], [1, 128]], compare_op=ALU.is_ge, fill=getfill(e),
                                base=128 * i - 2048 * nt - 15, channel_multiplier=-16), reads=[b_tab], writes=[b_maskc[nt % 2]])
                            P.op("pe", lambda e: e.matmul(banks[sbk][0:mrows, :], lhsT=kcT[:, nt * 128:nt * 128 + mrows], rhs=qv, start=True, stop=False),
                                 reads=[b_kcT, b_qT], writes=[bb[sbk]])
                            P.op("pe", lambda e: e.matmul(banks[sbk][0:mrows, :], lhsT=identb[:, 0:mrows], rhs=mk[:], start=False, stop=True),
                                 reads=[b_tab, b_maskc[nt % 2]], writes=[bb[sbk]])

                        def rest(nt=nt, mrows=mrows, st_=st_):
                            sbk = st_["sbk"]
                            pb = rr["pt"] % 3
                            rr["pt"] += 1
                            P.op("act", lambda e: e.activation(out=pT[pb][0:mrows, :], in_=banks[sbk][0:mrows, :], func=AF.Exp, scale=1.0),
                                 reads=[bb[sbk]], writes=[b_pT[pb]])
                            for h in range(4):
                                obk = 2 + h // 2
                                oc = (h % 2) * 256
                                P.op("pe", lambda e, obk=obk, oc=oc, h=h: e.matmul(
                                    banks[obk][:, oc:oc + 193], lhsT=pT[pb][0:mrows, h * 128:(h + 1) * 128], rhs=vcM[0:mrows, nt, 0:193],
                                    start=(nt == 0 and h % 2 == 0), stop=(nt == nnt - 1), skip_group_check=True),
                                    reads=[b_pT[pb], b_vcM], writes=[bb[obk]])
                        jobs.append((qk, rest))
                    run_pipeline(jobs)

                def stage_Adve(tt):
                    i = ST * 4 + tt
                    par = i % 2
                    bF = b_finA[par]
                    P.op("act", lambda e: e.activation(out=gsig[par][:], in_=tokA[tt][:, 128:140], func=AF.Exp, scale=-1.0), reads=[b_tokA[tt]], writes=[bF])
                    P.op("dve", lambda e: e.tensor_scalar(out=gsig[par][:], in0=gsig[par][:], scalar1=1.0, scalar2=None, op0=ALU.add), reads=[bF], writes=[bF])
                    P.op("dve", lambda e: e.reciprocal(out=gsig[par][:], in_=gsig[par][:]), reads=[bF], writes=[bF])
                    for h in range(4):
                        obk = 2 + h // 2
                        oc = (h % 2) * 256
                        P.op("dve", lambda e, obk=obk, oc=oc, h=h: e.tensor_scalar(out=den[par][:, h:h + 1], in0=banks[obk][:, oc + 64:oc + 65],
                                                                                  scalar1=1e-30, scalar2=None, op0=ALU.max),
                             reads=[bb[obk]], writes=[bF])
                    P.op("dve", lambda e: e.reciprocal(out=den[par][:, 0:4], in_=den[par][:, 0:4]), reads=[bF], writes=[bF])
                    for h in range(4):
                        obk = 2 + h // 2
                        oc = (h % 2) * 256
                        P.op("dve", lambda e, h=h: e.tensor_tensor(out=coef[par][:, h:h + 1], in0=gsig[par][:, 3 * h:3 * h + 1], in1=den[par][:, h:h + 1], op=ALU.mult),
                             reads=[bF], writes=[bF])
                        P.op("dve", lambda e, obk=obk, oc=oc, h=h: e.tensor_scalar(out=acc[par][:, h * 64:(h + 1) * 64], in0=banks[obk][:, oc:oc + 64],
                                                                                  scalar1=coef[par][:, h:h + 1], scalar2=None, op0=ALU.mult),
                             reads=[bb[obk], bF], writes=[bF])

                    for h in range(4):
                        obk = 2 + h // 2
                        oc = (h % 2) * 256
                        if h == 0:
                            P.op("dve", lambda e, obk=obk, oc=oc: e.tensor_scalar(out=score[:], in0=banks[obk][:, oc + 65:oc + 193], scalar1=den[par][:, 0:1],
                                                                                 scalar2=None, op0=ALU.mult), reads=[bb[obk], bF], writes=[b_sel])
                        else:
                            P.op("dve", lambda e, obk=obk, oc=oc, h=h: e.scalar_tensor_tensor(out=score[:], in0=banks[obk][:, oc + 65:oc + 193],
                                                                                              scalar=den[par][:, h:h + 1], in1=score[:], op0=ALU.mult, op1=ALU.add),
                                 reads=[bb[obk], bF, b_sel], writes=[b_sel])
                    P.op("dve", lambda e: e.memset(score[:, 0:1], BIG), writes=[b_sel])
                    P.op("dve", lambda e: e.memset(score[:, 2 * i:2 * i + 1], BIG), writes=[b_sel])
                    if i > 0:
                        P.op("dve", lambda e: e.memset(score[0:64, 2 * i - 1:2 * i], BIG), writes=[b_sel])
                    P.op("dve", lambda e: e.memset(score[64:128, 2 * i + 1:2 * i + 2], BIG), writes=[b_sel])
                    P.op("dve", lambda e: e.max(out=m8[:, 0:8], in_=score[:]), reads=[b_sel], writes=[b_sel])
                    P.op("dve", lambda e: e.match_replace(out=swork[:], in_to_replace=m8[:, 0:8], in_values=score[:], imm_value=-BIG),
                         reads=[b_sel], writes=[b_sel])
                    P.op("dve", lambda e: e.max(out=m8[:, 8:16], in_=swork[:]), reads=[b_sel], writes=[b_sel])
                    P.op("dve", lambda e: e.tensor_scalar(out=selb[par][:], in0=score[:], scalar1=m8[:, 15:16], scalar2=NEGM, op0=ALU.is_lt, op1=ALU.mult),
                         reads=[b_sel], writes=[b_selb[par]])

                def stage_T(tt):
                    i = ST * 4 + tt
                    par = i % 2
                    sbt = mmbank()
                    P.op("pe", lambda e: e.transpose(banks[sbt][:].bitcast(BF16)[:, 0:128], selb[par][:], identb[:]), reads=[b_selb[par], b_tab], writes=[bb[sbt]], mode="tr")
                    for h in range(4):
                        P.op("dve", lambda e, h=h: e.tensor_copy(out=selT4[par][0][0:64, h, :], in_=banks[sbt][:].bitcast(BF16)[0:64, 0:128]),
                             reads=[bb[sbt]], writes=[b_selT[par]])
                        P.op("dve", lambda e, h=h: e.tensor_copy(out=selT4[par][1][64:128, h, :], in_=banks[sbt][:].bitcast(BF16)[64:128, 0:128]),
                             reads=[bb[sbt]], writes=[b_selT[par]])

                def stage_B(tt):
                    i = ST * 4 + tt
                    par = i % 2
                    bF = b_finA[par]
                    qv = qT[:, tt, :, :].rearrange("p h q -> p (h q)")

                    def mkjob(br, kt, idx, nk):
                        obk = 6 + br
                        if br == 0:
                            kop = kslcT[:, kt * 128:(kt + 1) * 128]; kb = b_kslc[kt]
                            vop = vslc[:, kt, 0:65]; vb = b_vslc[kt]
                        else:
                            rsl = (kt % 8) * 128
                            kop = kwinT[:, rsl:rsl + 128]; kb = b_kwin[kt % 8]
                            vop = vwin[:, kt % 8, 0:65]; vb = b_vwin[kt % 8]
                        extra = []
                        if br == 0:
                            gq_ = (2 * kt) // 64
                            m_ = kt % 32
                            extra.append((E4_s[:, m_, :], selT4[par][gq_][:, :, :].rearrange("p h q -> p (h q)"), [b_tab, b_selT[par]]))
                        if kt == i:
                            extra.append((identb[:], bdiag_s[:], [b_tab]))
                        if br == 1 and kt == i - 4:
                            extra.append((identb[:], bfar_s[:], [b_tab]))
                        st_ = {}

                        def qk():
                            sbk = stbank()
                            st_["sbk"] = sbk
                            P.op("pe", lambda e: e.matmul(banks[sbk][:, :], lhsT=kop, rhs=qv, start=True, stop=(len(extra) == 0)),
                                 reads=[kb, b_qT], writes=[bb[sbk]])
                            for xi, (l_, r_, rb_) in enumerate(extra):
                                P.op("pe", lambda e, l_=l_, r_=r_, lastx=(xi == len(extra) - 1): e.matmul(
                                    banks[sbk][:, :], lhsT=l_, rhs=r_, start=False, stop=lastx), reads=rb_, writes=[bb[sbk]])

                        def rest():
                            sbk = st_["sbk"]
                            pb = rr["pt"] % 3
                            rr["pt"] += 1
                            P.op("act", lambda e: e.activation(out=pT[pb][:], in_=banks[sbk][:, :], func=AF.Exp, scale=1.0),
                                 reads=[bb[sbk]], writes=[b_pT[pb]])
                            P.op("pe", lambda e: e.matmul(banks[obk][0:65, :], lhsT=vop, rhs=pT[pb][:], start=(idx == 0), stop=(idx == nk - 1)),
                                 reads=[b_pT[pb], vb], writes=[bb[obk]])
                        return (qk, rest)
                    kw = list(range(max(0, i - 4), i + 1))
                    ks = list(range(0, i + 1))
                    jobs_w = [mkjob(1, kt, idx, len(kw)) for idx, kt in enumerate(kw)]
                    jobs_s = [mkjob(0, kt, idx, len(ks)) for idx, kt in enumerate(ks)]
                    run_pipeline(jobs_w + jobs_s)
                    for br in range(2):
                        obk = 6 + br
                        P.op("dve", lambda e, obk=obk, br=br: e.tensor_copy(out=(e1 if br == 0 else cs)[0:65, :], in_=banks[obk][0:65, :]),
                             reads=[bb[obk]], writes=[b_gl])
                    for br in range(2):
                        tbk = mmbank()
                        for h in range(4):
                            P.op("pe", lambda e, tbk=tbk, br=br, h=h: e.transpose(
                                banks[tbk][:, h * 128:h * 128 + 65], (e1 if br == 0 else cs)[0:65, h * 128:(h + 1) * 128], identf[0:65, 0:65]),
                                reads=[b_gl, b_tab], writes=[bb[tbk]], mode="f32t")
                        for h in range(4):
                            dcol = 4 + br * 4 + h
                            P.op("dve", lambda e, tbk=tbk, h=h, dcol=dcol: e.tensor_scalar(out=den[par][:, dcol:dcol + 1], in0=banks[tbk][:, h * 128 + 64:h * 128 + 65],
                                                                                          scalar1=1e-30, scalar2=None, op0=ALU.max),
                                 reads=[bb[tbk]], writes=[bF])
                            P.op("dve", lambda e, dcol=dcol: e.reciprocal(out=den[par][:, dcol:dcol + 1], in_=den[par][:, dcol:dcol + 1]), reads=[bF], writes=[bF])
                            P.op("dve", lambda e, h=h, br=br, dcol=dcol: e.tensor_tensor(out=coef[par][:, dcol:dcol + 1], in0=gsig[par][:, 3 * h + 1 + br:3 * h + 2 + br],
                                                                                        in1=den[par][:, dcol:dcol + 1], op=ALU.mult), reads=[bF], writes=[bF])
                            P.op("dve", lambda e, tbk=tbk, h=h, dcol=dcol: e.scalar_tensor_tensor(
                                out=acc[par][:, h * 64:(h + 1) * 64], in0=banks[tbk][:, h * 128:h * 128 + 64], scalar=coef[par][:, dcol:dcol + 1],
                                in1=acc[par][:, h * 64:(h + 1) * 64], op0=ALU.mult, op1=ALU.add), reads=[bb[tbk], bF], writes=[bF])
                    if debug and i == NQT - 1:
                        P.dma("pool", lambda e: e.dma_start(out=dbg["oT0"], in_=e1[0:65, :]), reads=[b_gl])
                        P.dma("pool", lambda e: e.dma_start(out=dbg["oT1"], in_=cs[0:65, :]), reads=[b_gl])
                        P.dma("pool", lambda e: e.dma_start(out=dbg["acc"], in_=acc[par][:]), reads=[bF])
                        P.dma("pool", lambda e: e.dma_start(out=dbg["kcT"], in_=kcT[0:72]), reads=[b_kcT])
                        P.dma("pool", lambda e: e.dma_start(out=dbg["ocmp"], in_=vcM[:, :, 0:193]), reads=[b_vcM])
                        P.dma("pool", lambda e: e.dma_start(out=dbg["den"], in_=den[par][:]), reads=[bF])
                        P.dma("pool", lambda e: e.dma_start(out=dbg["gsig"], in_=gsig[par][:]), reads=[bF])
                        P.dma("pool", lambda e: e.dma_start(out=dbg["coef"], in_=coef[par][:]), reads=[bF])
                        P.dma("pool", lambda e: e.dma_start(out=dbg["pT"], in_=pT[(rr["pt"] - 1) % 3][:]), reads=[b_pT[(rr["pt"] - 1) % 3]])
                    yb_ = tt
                    P.op("act", lambda e: e.activation(out=szn[:], in_=tokA[tt][:, 140:396], func=AF.Exp, scale=-1.0), reads=[b_tokA[tt]], writes=[b_fin])
                    P.op("dve", lambda e: e.tensor_scalar(out=szn[:], in0=szn[:], scalar1=1.0, scalar2=None, op0=ALU.add), reads=[b_fin], writes=[b_fin])
                    P.op("dve", lambda e: e.reciprocal(out=szn[:], in_=szn[:]), reads=[b_fin], writes=[b_fin])
                    P.op("dve", lambda e: e.tensor_tensor(out=szn[:], in0=szn[:], in1=tokA[tt][:, 140:396], op=ALU.mult), reads=[b_fin, b_tokA[tt]], writes=[b_fin])
                    P.op("dve", lambda e: e.tensor_tensor(out=yt[yb_][:, 0:256], in0=szn[:], in1=acc[par][:], op=ALU.mult), reads=[b_fin, bF], writes=[b_yt[yb_]])
                    P.dma("pool", lambda e: e.dma_start(out=y_v[i], in_=yt[yb_][:]), reads=[b_yt[yb_]])

                stage_A(0)
                stage_Adve(0)
                stage_A(1)
                stage_T(0)
                stage_Adve(1)
                stage_B(0)
                stage_A(2)
                stage_T(1)
                stage_Adve(2)
                stage_B(1)
                stage_A(3)
                stage_T(2)
                stage_Adve(3)
                stage_B(2)
                stage_T(3)
                stage_B(3)
        except _Stop:
            pass
        toks = [t_ for t_ in P.dma_last if t_ is not None]
        toks += [(e_, P.cnt[e_]) for e_ in ("pe", "act", "dve", "pool") if P.cnt[e_] > 0]
        for e_ in ("pool", "sp", "act", "dve", "pe"):
            P.wait_all(e_, toks)
        P.emit()
    return nc


import contextlib
import numpy as np
import ml_dtypes
import concourse.bass as bass
import concourse.mybir as mybir

F32 = mybir.dt.float32
BF16 = mybir.dt.bfloat16
AF = mybir.ActivationFunctionType
ALU = mybir.AluOpType
BF = ml_dtypes.bfloat16
D = 2048
KC = 16
EPS = 1e-6
CB = 512
NCB = D // CB


def host_inputs_l2(inp, ys, adas, b, qtr, TQ):
    t0 = qtr * TQ
    ycat = np.concatenate([ys[b * 4 + j][t0:t0 + TQ, 0:256] for j in range(4)] +
                          [ys[b * 4 + j][t0:t0 + TQ, 256:512] for j in range(4)], axis=1)
    w_in = inp["w_in"][0]
    b_in = inp["b_in"][0]
    return dict(
        x2=np.ascontiguousarray(inp["x"][b, t0:t0 + TQ]),
        yin=np.ascontiguousarray(ycat),
        ada_in=np.ascontiguousarray(adas[b * 4]),
        gainT2=np.ascontiguousarray(inp["norm_gain"][0].reshape(16, 128).T),
        wmg=np.ascontiguousarray(w_in[:, 6720:10816]),
        bmg=np.ascontiguousarray(b_in[6720:10816][None, :]),
        wbrn=np.ascontiguousarray(inp["w_br_nsa"][0]),
        wbrg=np.ascontiguousarray(inp["w_br_gla"][0]),
        wout=np.ascontiguousarray(inp["w_out"][0]),
        fgain=np.ascontiguousarray(np.broadcast_to(inp["final_norm_gain"][None, :], (128, D))),
        identb2=np.eye(128).astype(BF), identf2=np.eye(128, dtype=np.float32),
    )


def build_l2(TQ):
    NTT = 4
    NST = TQ // (128 * NTT)
    nc = bass.Bass("TRN2", target_bir_lowering=False)

    def din(name, shape, dt=F32):
        return nc.dram_tensor(name, list(shape), dt, kind="ExternalInput").ap()
    x = din("x2", [TQ, D]); yin = din("yin", [TQ, D], BF16)
    adain_d = din("ada_in", [128, 48])
    gainT_d = din("gainT2", [128, 16])
    wmg_d = din("wmg", [D, 4096]); bmg_d = din("bmg", [1, 4096])
    wbrn_d = din("wbrn", [1024, D]); wbrg_d = din("wbrg", [1024, D]); wout_d = din("wout", [D, D])
    fgain_d = din("fgain", [128, D]); identb_d = din("identb2", [128, 128], BF16); identf_d = din("identf2", [128, 128])
    out_d = nc.dram_tensor("out", [TQ, D], F32, kind="ExternalOutput").ap()
    wmg_b = nc.dram_tensor("wmg_b16", [D, 4096], BF16).ap()
    wbrn_b = nc.dram_tensor("wbrn_b16", [1024, D], BF16).ap()
    wbrg_b = nc.dram_tensor("wbrg_b16", [1024, D], BF16).ap()
    wout_b = nc.dram_tensor("wout_b16", [D, D], BF16).ap()

    P = Prog(nc)
    with contextlib.ExitStack() as st:
        def sb(name, shape, dt=F32):
            return st.enter_context(nc.sbuf_tensor("s2_" + name, list(shape), dt))
        xs = [sb("xs%d" % i, [128, D]) for i in range(2)]
        hT = sb("hT", [128, KC, 128 * NTT], BF16)
        yT = sb("yT", [128, KC, 128 * NTT], BF16)
        mtok = [sb("mtok%d" % i, [128, D], BF16) for i in range(NTT)]
        xnb = mtok[NTT - 1]; ytile = [mtok[NTT - 2], mtok[NTT - 3]]
        mT = sb("mT", [128, KC, 128], BF16)
        wmgb = [sb("wmgb%d" % i, [128, 2, KC, CB], BF16) for i in range(2)]
        wbrb = [sb("wbrb%d" % i, [128, 2, 8, CB], BF16) for i in range(2)]
        woutb = [sb("woutb%d" % i, [128, KC, 128], BF16) for i in range(2)]
        bmg_s = [sb("bmg_s%d" % i, [1, 2 * CB], BF16) for i in range(2)]; ones_s = sb("ones_s", [1, 128], BF16)
        onesf = sb("onesf", [128, 128])
        gate_bc = sb("gate_bc", [128, D]); fgain_s = sb("fgain_s", [128, D])
        xo = sb("xo", [128, 512])
        A_s = sb("A_s", [128, 16]); B_s = sb("B_s", [128, 16]); ada_s = sb("ada_s", [128, 48])
        cT_s = sb("cT_s", [128, 16]); gainT_s = sb("gainT_s", [128, 16]); bada_s = sb("bada_s", [128, 48])
        small = sb("small", [128, 32]); diag = [sb("diag%d" % i, [128, 128]) for i in range(2)]
        identb = sb("identb", [128, 128], BF16); identf = sb("identf", [128, 128])
        g1 = sb("g1", [128, CB]); g2 = sb("g2", [128, CB]); t1 = sb("t1", [128, CB])

        banks = [st.enter_context(nc.psum_tensor("b2ank%d" % i, [128, 512], F32)) for i in range(8)]
        bb = [Buf("bank%d" % i) for i in range(8)]
        B = Buf
        b_w = B("w"); b_AB = B("AB"); b_xs = [B("xs0"), B("xs1")]; b_small = B("small")
        b_hT = B("hT"); b_yT = B("yT"); b_mtok = [B("mtok%d" % i) for i in range(NTT)]
        b_xnb = b_mtok[NTT - 1]; b_yt = [b_mtok[NTT - 2], b_mtok[NTT - 3]]
        b_mT = B("mT"); b_wmg = [B("wmg0"), B("wmg1")]; b_wbr = [B("wbr0"), B("wbr1")]; b_wout = [B("wout0"), B("wout1")]
        b_gate = B("gate"); b_diag = [B("d0"), B("d1")]; b_g = B("g"); b_xo = B("xo"); b_tab = B("tab")

        def ld(q, out, in_, writes):
            return P.dma(q, lambda e: e.dma_start(out=out, in_=in_), writes=writes)
        ld("sp", gainT_s[:], gainT_d, [b_AB])
        ld("sp", fgain_s[:], fgain_d, [b_w]); ld("sp", identb[:], identb_d, [b_tab]); ld("sp", identf[:], identf_d, [b_tab])
        P.op("pool", lambda e: e.memset(ones_s[:], 1.0), writes=[b_w])
        b_wc = B("wcast")
        for h2 in range(2):
            for r0 in range(0, D, 1024):
                P.dma("pool", lambda e, h2=h2, r0=r0: e.dma_start(out=wmg_b[r0:r0 + 1024, h2 * 2048:(h2 + 1) * 2048], in_=wmg_d[r0:r0 + 1024, h2 * 2048:(h2 + 1) * 2048]), writes=[b_wc])
        P.dma("pool", lambda e: e.dma_start(out=wbrn_b, in_=wbrn_d), writes=[b_wc])
        P.dma("pool", lambda e: e.dma_start(out=wbrg_b, in_=wbrg_d), writes=[b_wc])
        for r0 in range(0, D, 1024):
            P.dma("pool", lambda e, r0=r0: e.dma_start(out=wout_b[r0:r0 + 1024, :], in_=wout_d[r0:r0 + 1024, :]), writes=[b_wc])
        P.op("pool", lambda e: e.memset(onesf[:], 1.0), writes=[b_w])
        ld("sp", ada_s[:], adain_d, [b_AB])
        P.op("dve", lambda e: e.tensor_scalar(out=A_s[:], in0=ada_s[:, 16:32], scalar1=1.0, scalar2=None, op0=ALU.add), reads=[b_AB], writes=[b_AB])
        P.op("dve", lambda e: e.tensor_tensor(out=A_s[:], in0=A_s[:], in1=gainT_s[:], op=ALU.mult), reads=[b_AB], writes=[b_AB])
        P.op("dve", lambda e: e.tensor_copy(out=B_s[:], in_=ada_s[:, 0:16]), reads=[b_AB], writes=[b_AB])
        for kc in range(KC):
            d_ = kc % 2
            P.op("dve", lambda e, kc=kc, d_=d_: e.tensor_scalar(out=diag[d_][:], in0=identf[:], scalar1=ada_s[:, 32 + kc:33 + kc], scalar2=None, op0=ALU.mult),
                 reads=[b_AB, b_tab], writes=[b_diag[d_]])
            bk = 1 + kc // 4
            P.op("pe", lambda e, kc=kc, d_=d_, bk=bk: e.matmul(banks[bk][:, (kc % 4) * 128:(kc % 4) * 128 + 128], lhsT=onesf[:], rhs=diag[d_][:],
                                                               start=(kc % 4 == 0), stop=(kc % 4 == 3), skip_group_check=True),
                 reads=[b_w, b_diag[d_]], writes=[bb[bk]])
        for q4 in range(4):
            P.op("dve", lambda e, q4=q4: e.tensor_copy(out=gate_bc[:, q4 * 512:(q4 + 1) * 512], in_=banks[1 + q4][:, :]), reads=[bb[1 + q4]], writes=[b_gate])

        x_v = x.rearrange("(t p) d -> t p d", p=128)
        y_v = yin.rearrange("(t p) d -> t p d", p=128)
        o_v = out_d.rearrange("(t p) d -> t p d", p=128)
        wmg_v = wmg_b.rearrange("(kc p) n -> p kc n", p=128)
        wbrn_v = wbrn_b.rearrange("(kc p) n -> p kc n", p=128)
        wbrg_v = wbrg_b.rearrange("(kc p) n -> p kc n", p=128)
        wout_v = wout_b.rearrange("(kc p) n -> p kc n", p=128)
        rr = {"mm": 0, "w": 0, "wo": 0, "x": 0, "ob": 0}

        def mmbank():
            i = rr["ob"] % 4
            rr["ob"] += 1
            return i

        for ST in range(NST):
            for tt in range(NTT):
                t = ST * NTT + tt
                xb_ = rr["x"] % 2
                rr["x"] += 1
                ld("sp", xs[xb_][:], x_v[t], [b_xs[xb_]])
                ld("sp", ytile[tt % 2][:], y_v[t], [b_yt[tt % 2]])
                P.op("act", lambda e, xb_=xb_, tt=tt: e.activation(out=xnb[:], in_=xs[xb_][:], func=AF.Square, accum_out=small[:, tt % 4:tt % 4 + 1]),
                     reads=[b_xs[xb_]], writes=[b_xnb, b_small])
                P.op("act", lambda e, tt=tt: e.activation(out=small[:, 8 + tt % 4:9 + tt % 4], in_=small[:, tt % 4:tt % 4 + 1], func=AF.Ln, scale=1.0 / D, bias=EPS),
                     reads=[b_small], writes=[b_small])
                P.op("act", lambda e, tt=tt: e.activation(out=small[:, 16 + tt % 4:17 + tt % 4], in_=small[:, 8 + tt % 4:9 + tt % 4], func=AF.Exp, scale=-0.5),
                     reads=[b_small], writes=[b_small])
                P.op("dve", lambda e, xb_=xb_, tt=tt: e.tensor_scalar(out=xnb[:], in0=xs[xb_][:], scalar1=small[:, 16 + tt % 4:17 + tt % 4], scalar2=None, op0=ALU.mult),
                     reads=[b_xs[xb_], b_small], writes=[b_xnb])
                for src_i, (src, sbuf_, dst, dbuf) in enumerate(((xnb, b_xnb, hT, b_hT), (ytile[tt % 2], b_yt[tt % 2], yT, b_yT))):
                    for half in range(2):
                        bk = 4 + 2 * src_i + half
                        pv = banks[bk][:].bitcast(BF16)
                        for c8 in range(8):
                            kc = half * 8 + c8
                            P.op("pe", lambda e, pv=pv, c8=c8, kc=kc, src=src: e.transpose(pv[:, c8 * 128:(c8 + 1) * 128], src[:, kc * 128:(kc + 1) * 128], identb[:]),
                                 reads=[sbuf_, b_tab], writes=[bb[bk]])
                        if src_i == 0:
                            for c8 in range(8):
                                kc = half * 8 + c8
                                P.op("dve", lambda e, pv=pv, c8=c8, kc=kc, tt=tt: e.tensor_scalar(
                                    out=hT[:, kc, tt * 128:(tt + 1) * 128], in0=pv[:, c8 * 128:(c8 + 1) * 128],
                                    scalar1=A_s[:, kc:kc + 1], scalar2=B_s[:, kc:kc + 1], op0=ALU.mult, op1=ALU.add),
                                    reads=[bb[bk], b_AB], writes=[b_hT])
                        else:
                            P.op("dve", lambda e, pv=pv, half=half, tt=tt: e.tensor_copy(
                                out=yT[:, half * 8:half * 8 + 8, tt * 128:(tt + 1) * 128], in_=pv.rearrange("p (c q) -> p c q", q=128)),
                                reads=[bb[bk]], writes=[b_yT])
            for cb in range(NCB):
                wb = rr["w"] % 2
                rr["w"] += 1
                c0 = cb * CB
                P.dma("sp", lambda e, wb=wb, c0=c0: e.dma_start(out=wmgb[wb][:, 0, :, :], in_=wmg_v[:, :, c0:c0 + CB]), reads=[b_wc], writes=[b_wmg[wb]])
                P.dma("sp", lambda e, wb=wb, c0=c0: e.dma_start(out=wmgb[wb][:, 1, :, :], in_=wmg_v[:, :, 2048 + c0:2048 + c0 + CB]), reads=[b_wc], writes=[b_wmg[wb]])
                P.dma("sp", lambda e, wb=wb, c0=c0: e.dma_start(out=wbrb[wb][:, 0, :, :], in_=wbrn_v[:, :, c0:c0 + CB]), reads=[b_wc], writes=[b_wbr[wb]])
                P.dma("sp", lambda e, wb=wb, c0=c0: e.dma_start(out=wbrb[wb][:, 1, :, :], in_=wbrg_v[:, :, c0:c0 + CB]), reads=[b_wc], writes=[b_wbr[wb]])
                ld("pool", bmg_s[wb][:, 0:CB], bmg_d[:, c0:c0 + CB], [b_wmg[wb]])
                ld("pool", bmg_s[wb][:, CB:2 * CB], bmg_d[:, 2048 + c0:2048 + c0 + CB], [b_wmg[wb]])
                for tt in range(NTT):
                    tsl = slice(tt * 128, tt * 128 + 128)
                    base = 4 * (rr["mm"] % 2)
                    rr["mm"] += 1
                    for gi in range(2):
                        bk = base + gi
                        for kc in range(KC):
                            P.op("pe", lambda e, bk=bk, gi=gi, kc=kc, tsl=tsl, wb=wb: e.matmul(
                                banks[bk][:, :], lhsT=hT[:, kc, tsl], rhs=wmgb[wb][:, gi, kc, :], start=(kc == 0), stop=False),
                                reads=[b_hT, b_wmg[wb]], writes=[bb[bk]])
                        P.op("pe", lambda e, bk=bk, gi=gi, wb=wb: e.matmul(
                            banks[bk][:, :], lhsT=ones_s[0:1, :], rhs=bmg_s[wb][0:1, gi * CB:(gi + 1) * CB], start=False, stop=True),
                            reads=[b_w, b_wmg[wb]], writes=[bb[bk]], mode="r32")
                    for gi in range(2):
                        bk = base + 2 + gi
                        for kc in range(8):
                            P.op("pe", lambda e, bk=bk, gi=gi, kc=kc, tsl=tsl, wb=wb: e.matmul(
                                banks[bk][:, :], lhsT=yT[:, gi * 8 + kc, tsl], rhs=wbrb[wb][:, gi, kc, :], start=(kc == 0), stop=(kc == 7)),
                                reads=[b_yT, b_wbr[wb]], writes=[bb[bk]])
                    P.op("act", lambda e, base=base: e.activation(out=g1[:], in_=banks[base][:, :], func=AF.Exp, scale=-1.0), reads=[bb[base]], writes=[b_g])
                    P.op("act", lambda e, base=base: e.activation(out=g2[:], in_=banks[base + 1][:, :], func=AF.Exp, scale=-1.0), reads=[bb[base + 1]], writes=[b_g])
                    P.op("dve", lambda e: e.tensor_scalar(out=g1[:], in0=g1[:], scalar1=1.0, scalar2=None, op0=ALU.add), reads=[b_g], writes=[b_g])
                    P.op("dve", lambda e: e.tensor_scalar(out=g2[:], in0=g2[:], scalar1=1.0, scalar2=None, op0=ALU.add), reads=[b_g], writes=[b_g])
                    P.op("dve", lambda e: e.reciprocal(out=g1[:], in_=g1[:]), reads=[b_g], writes=[b_g])
                    P.op("dve", lambda e: e.reciprocal(out=g2[:], in_=g2[:]), reads=[b_g], writes=[b_g])
                    P.op("dve", lambda e, base=base: e.tensor_tensor(out=t1[:], in0=banks[base + 2][:, :], in1=g1[:], op=ALU.mult), reads=[bb[base + 2], b_g], writes=[b_g])
                    P.op("dve", lambda e, base=base: e.tensor_tensor(out=g2[:], in0=banks[base + 3][:, :], in1=g2[:], op=ALU.mult), reads=[bb[base + 3], b_g], writes=[b_g])
                    P.op("dve", lambda e, tt=tt, c0=c0: e.tensor_tensor(out=mtok[tt][:, c0:c0 + CB], in0=t1[:], in1=g2[:], op=ALU.add), reads=[b_g], writes=[b_mtok[tt]])
            for tt in range(NTT):
                t = ST * NTT + tt
                for half in range(2):
                    bk = 4 + half
                    pv = banks[bk][:].bitcast(BF16)
                    for c8 in range(8):
                        kc = half * 8 + c8
                        P.op("pe", lambda e, pv=pv, c8=c8, kc=kc, tt=tt: e.transpose(pv[:, c8 * 128:(c8 + 1) * 128], mtok[tt][:, kc * 128:(kc + 1) * 128], identb[:]),
                             reads=[b_mtok[tt], b_tab], writes=[bb[bk]])
                    P.op("dve", lambda e, pv=pv, half=half: e.tensor_copy(out=mT[:, half * 8:half * 8 + 8, :], in_=pv.rearrange("p (c q) -> p c q", q=128)),
                         reads=[bb[bk]], writes=[b_mT])
                xb_ = rr["x"] % 2
                rr["x"] += 1
                ld("sp", xs[xb_][:], x_v[t], [b_xs[xb_]])
                for ob in range(D // 128):
                    wo = rr["wo"] % 2
                    rr["wo"] += 1
                    P.dma("sp", lambda e, wo=wo, ob=ob: e.dma_start(out=woutb[wo][:], in_=wout_v[:, :, ob * 128:(ob + 1) * 128]), reads=[b_wc], writes=[b_wout[wo]])
                    if ob % 4 == 0:
                        bk = mmbank()
                    oc = (ob % 4) * 128
                    for kc in range(KC):
                        P.op("pe", lambda e, bk=bk, kc=kc, wo=wo, oc=oc, ob=ob: e.matmul(banks[bk][:, oc:oc + 128], lhsT=mT[:, kc, :], rhs=woutb[wo][:, kc, :],
                                                                                 start=(kc == 0 and ob % 4 == 0), stop=(kc == KC - 1), skip_group_check=True),
                             reads=[b_mT, b_wout[wo]], writes=[bb[bk]])
                    if ob % 4 == 3:
                        osl = slice((ob // 4) * 512, (ob // 4) * 512 + 512)
                        P.op("dve", lambda e, bk=bk, osl=osl: e.tensor_tensor(out=xo[:, :], in0=banks[bk][:, :], in1=gate_bc[:, osl], op=ALU.mult),
                             reads=[bb[bk], b_gate], writes=[b_xo])
                        P.op("dve", lambda e, osl=osl, xb_=xb_: e.tensor_tensor(out=xs[xb_][:, osl], in0=xo[:, :], in1=xs[xb_][:, osl], op=ALU.add),
                             reads=[b_xo, b_xs[xb_]], writes=[b_xs[xb_]])
                P.op("act", lambda e, xb_=xb_, tt=tt: e.activation(out=hT[:, 0:4, :], in_=xs[xb_][:].rearrange("p (a b) -> p a b", a=4), func=AF.Square, accum_out=small[:, 24:25]),
                     reads=[b_xs[xb_]], writes=[b_hT, b_small])
                P.op("act", lambda e: e.activation(out=small[:, 25:26], in_=small[:, 24:25], func=AF.Ln, scale=1.0 / D, bias=EPS), reads=[b_small], writes=[b_small])
                P.op("act", lambda e: e.activation(out=small[:, 26:27], in_=small[:, 25:26], func=AF.Exp, scale=-0.5), reads=[b_small], writes=[b_small])
                P.op("dve", lambda e, xb_=xb_: e.scalar_tensor_tensor(out=xs[xb_][:], in0=xs[xb_][:], scalar=small[:, 26:27], in1=fgain_s[:], op0=ALU.mult, op1=ALU.mult),
                     reads=[b_small, b_w], writes=[b_xs[xb_]])
                P.dma("act", lambda e, xb_=xb_, t=t: e.dma_start(out=o_v[t], in_=xs[xb_][:]), reads=[b_xs[xb_]])
        toks = [t_ for t_ in P.dma_last if t_ is not None]
        toks += [(e_, P.cnt[e_]) for e_ in ("pe", "act", "dve", "pool") if P.cnt[e_] > 0]
        for e_ in ("pool", "sp", "act", "dve", "pe"):
            P.wait_all(e_, toks)
        P.emit()
    return nc


def kernel(**inputs):
    from concourse.bass_utils import run_bass_kernel_spmd
    inp = {k_: np.asarray(v) for k_, v in inputs.items()}
    Bn, T, Dm = inp["x"].shape
    nc1 = build_l1(T)
    in_maps = [host_inputs_l1(inp, c // 4, c % 4, T) for c in range(8)]
    res = run_bass_kernel_spmd(nc1, in_maps, core_ids=list(range(8)))
    ys = [np.asarray(r["y"]) for r in res.results]
    adas = [np.asarray(r["ada_out"]) for r in res.results]
    del in_maps
    TQ = T // 4
    nc2 = build_l2(TQ)
    in_maps2 = [host_inputs_l2(inp, ys, adas, c // 4, c % 4, TQ) for c in range(8)]
    res2 = run_bass_kernel_spmd(nc2, in_maps2, core_ids=list(range(8)))
    out = np.empty((Bn, T, Dm), np.float32)
    for c in range(8):
        out[c // 4, (c % 4) * TQ:(c % 4 + 1) * TQ] = np.asarray(res2.results[c]["out"])
    return out
```

```python
import contextlib

ENGS = ("pe", "act", "dve", "pool", "sp")
USE_MODE_DRAINS = False
N_DMA_SEMS = 28


class Buf:
    __slots__ = ("name", "w", "r")

    def __init__(self, name=""):
        self.name = name
        self.w = {}
        self.r = {}


class Prog:
    def __init__(self, nc):
        self.nc = nc
        self.streams = {e: [] for e in ENGS}
        self.cnt = {e: 0 for e in ENGS}
        self.known = {e: {} for e in ENGS}
        self.dma_i = 0
        self.dma_ip = 0
        self.dma_cnt = [0] * N_DMA_SEMS
        self.dma_last = [None] * N_DMA_SEMS
        self.n_ops = 0

    def _deps(self, reads, writes):
        deps = {}

        def add(d):
            for k, v in d.items():
                if deps.get(k, -1) < v:
                    deps[k] = v
        for b in reads:
            add(b.w)
        for b in writes:
            add(b.w)
            add(b.r)
        return deps

    def _commit(self, tok, reads, writes):
        k, v = tok
        for b in reads:
            b.r[k] = v
        for b in writes:
            b.w = {k: v}
            b.r = {}

    def _waits(self, eng, deps):
        out = []
        kn = self.known[eng]
        for k, v in deps.items():
            if eng == "pe" and k == "pe":
                continue
            if kn.get(k, -1) >= v:
                continue
            kn[k] = v
            out.append((k, v))
        return out

    def op(self, eng, fn, reads=(), writes=(), mode="full"):
        deps = self._deps(reads, writes)
        waits = self._waits(eng, deps)
        if eng == "pe":
            if USE_MODE_DRAINS and mode != getattr(self, "pe_mode", mode) and self.cnt["pe"] > 0:
                v = self.cnt["pe"]
                if self.known["pe"].get("pe", -1) < v:
                    self.known["pe"]["pe"] = v
                    waits.append(("pe", v))
            self.pe_mode = mode
        self.cnt[eng] += 1
        tok = (eng, self.cnt[eng])
        self.streams[eng].append((waits, fn, tok))
        self._commit(tok, reads, writes)
        self.n_ops += 1
        return tok

    def dma(self, q, fn, reads=(), writes=()):
        deps = self._deps(reads, writes)
        if q == "pool":
            s = self.dma_ip % 10
            self.dma_ip += 1
        else:
            s = 10 + self.dma_i % (N_DMA_SEMS - 10)
            self.dma_i += 1
        if self.dma_last[s] is not None:
            k, v = self.dma_last[s]
            if deps.get(k, -1) < v:
                deps[k] = v
        waits = self._waits(q, deps)
        self.dma_cnt[s] += 16
        tok = (("dma", s), self.dma_cnt[s])
        self.dma_last[s] = tok
        if q == "pool":
            self.last_pool = tok
        self.streams[q].append((waits, fn, tok))
        self._commit(tok, reads, writes)
        self.n_ops += 1
        return tok

    def wait_all(self, eng, toks):
        deps = {}
        for k, v in toks:
            if deps.get(k, -1) < v:
                deps[k] = v
        waits = self._waits(eng, deps)
        self.streams[eng].append((waits, None, None))

    def emit(self):
        nc = self.nc
        with contextlib.ExitStack() as st:
            sems = {}
            for e in ENGS:
                sems[e] = st.enter_context(nc.semaphore("s_" + e))
            for i in range(N_DMA_SEMS):
                sems[("dma", i)] = st.enter_context(nc.semaphore("s_dma%d" % i))
            block = st.enter_context(nc.Block())
            objs = {"pe": block.tensor, "act": block.scalar, "dve": block.vector,
                    "pool": block.gpsimd, "sp": block.sync}

            needed = {e: set() for e in ENGS}
            for e in ENGS:
                for (waits, fn, tok) in self.streams[e]:
                    for (k, v) in waits:
                        if not isinstance(k, tuple):
                            needed[k].add(v)
            rank = {e: {v: i + 1 for i, v in enumerate(sorted(needed[e]))} for e in ENGS}
            self.n_incs = {e: len(needed[e]) for e in ENGS}

            def mk(e):
                def body(eng):
                    for (waits, fn, tok) in self.streams[e]:
                        for (k, v) in waits:
                            if isinstance(k, tuple):
                                eng.wait_ge(sems[k], v)
                            else:
                                eng.wait_ge(sems[k], rank[k][v])
                        if fn is None:
                            continue
                        ins = fn(eng)
                        k, v = tok
                        if isinstance(k, tuple):
                            ins.then_inc(sems[k], 16)
                        elif v in needed[k]:
                            ins.then_inc(sems[k], 1)
                return body
            for e in ENGS:
                if self.streams[e]:
                    objs[e](mk(e))


import contextlib
import numpy as np
import ml_dtypes
import concourse.bass as bass
import concourse.mybir as mybir

F32 = mybir.dt.float32
BF16 = mybir.dt.bfloat16
AF = mybir.ActivationFunctionType
ALU = mybir.AluOpType
BF = ml_dtypes.bfloat16

D = 2048
KC = 16
NEGM = -30000.0
BIG = 1.0e30
EPS = 1e-6
NFC = 10
FCH_M = [64, 64, 64, 64, 128, 64, 64, 128, 128, 16]
FCH_OFF = [0, 64, 128, 192, 256, 384, 448, 512, 640, 768]
NF = 784
NTA = 396
NTB = 512
NT_ = NTA + NTB


def split3(a):
    a = np.asarray(a, np.float64)
    hi = a.astype(BF).astype(np.float64)
    mid = (a - hi).astype(BF).astype(np.float64)
    lo = (a - hi - mid).astype(BF).astype(np.float64)
    return hi.astype(BF), mid.astype(BF), lo.astype(BF)


def host_tables(T, g):
    slopes = 2.0 ** (-8.0 * np.arange(1, 17, dtype=np.float64) / 16)[4 * g:4 * g + 4]
    t = np.arange(T, dtype=np.float64)
    qaug = np.zeros((8, 4, T), BF)
    for h in range(4):
        a, b_, c_ = split3(-slopes[h] * t)
        qaug[0, h], qaug[1, h], qaug[2, h] = a, b_, c_
        a, b_, c_ = split3(np.full(T, 128.0 * slopes[h]))
        qaug[3, h], qaug[4, h], qaug[5, h] = a, b_, c_
        a, b_, c_ = split3(np.full(T, slopes[h]))
        qaug[6, h], qaug[7, h] = a, b_

    def kside(pos):
        k = np.zeros((8, len(pos)), BF)
        k[0:3] = 1.0
        k[3:6] = (pos // 128).astype(BF)
        k[6:8] = (pos % 128).astype(BF)
        return k
    kaug = kside(np.arange(T))
    ncp = np.arange(512)
    cend = np.maximum(16 * (ncp - 1) + 31, 0)
    kcaug = kside(cend)
    M = np.zeros((512, 128), np.float32)
    w = {-1: 1.0, 0: 2.0, 1: 2.0, 2: 2.0, 3: 1.0}
    for npr in range(1, 512):
        n = npr - 1
        for blk in range(128):
            dd = n - 4 * blk
            if dd in w:
                M[npr, blk] = w[dd]
    Mtab = M.reshape(4, 128, 128).transpose(1, 0, 2).astype(BF)
    E4 = np.zeros((128, 32, 128), BF)
    for m in range(32):
        for half in range(2):
            r = 2 * m + half
            for gq in range(2):
                E4[64 * gq + r, m, 64 * half:64 * half + 64] = 1.0
    p = np.arange(128)[:, None]
    q = np.arange(128)[None, :]
    bias_diag = np.where(q - p >= 0, 0.0, NEGM).astype(BF)
    bias_far = np.where(p - q - 1 >= 0, 0.0, NEGM).astype(BF)
    bias_diag = np.tile(bias_diag, (1, 4))
    bias_far = np.tile(bias_far, (1, 4))
    gmask = np.where(q - p >= 0, 1.0, 0.0).astype(BF)
    reset = np.ones((128, 512), BF)
    reset[:, ::128] = 0.0
    qaug = np.ascontiguousarray(qaug.reshape(8, 4, T // 128, 128).transpose(0, 2, 1, 3))
    return dict(qaug=qaug, kaug=kaug, kcaug=kcaug, Mtab=Mtab, E4=E4,
                bias_diag=bias_diag, bias_far=bias_far, gmask=gmask, reset=reset,
                identb=np.eye(128).astype(BF), identf=np.eye(128, dtype=np.float32))


def host_inputs_l1(inp, b, j, T):
    g = j
    x = np.ascontiguousarray(inp["x"][b, :T])
    w_in = inp["w_in"][0]
    b_in = inp["b_in"][0]
    offs = np.cumsum([0, 1024, 256, 256, 256, 256, 256, 256, 48, 1024, 512, 512, 1024, 16, 1024, 2048, 2048])
    o_q, o_ck, o_cv, o_sk, o_sv, o_wk, o_wv, o_g, o_z, o_gq, o_gk, o_gv, o_ga, o_gz = offs[:14]
    fcols = np.concatenate([
        np.arange(o_q + 256 * g, o_q + 256 * g + 256),
        np.arange(o_ck + 64 * g, o_ck + 64 * g + 64), np.arange(o_cv + 64 * g, o_cv + 64 * g + 64),
        np.arange(o_sk + 64 * g, o_sk + 64 * g + 64), np.arange(o_wk + 64 * g, o_wk + 64 * g + 64),
        np.arange(o_gq + 128 * j, o_gq + 128 * j + 128), np.arange(o_gk + 128 * j, o_gk + 128 * j + 128),
        np.arange(o_ga, o_ga + 16)])
    tcols = np.concatenate([
        np.arange(o_sv + 64 * g, o_sv + 64 * g + 64), np.arange(o_wv + 64 * g, o_wv + 64 * g + 64),
        np.arange(o_g + 12 * g, o_g + 12 * g + 12), np.arange(o_z + 256 * g, o_z + 256 * g + 256),
        np.arange(o_gv + 256 * j, o_gv + 256 * j + 256), np.arange(o_gz + 256 * j, o_gz + 256 * j + 256)])
    assert len(fcols) == NF and len(tcols) == NT_
    wf = np.ascontiguousarray(w_in[:, fcols])
    wt = np.ascontiguousarray(w_in[:, tcols])
    bf = np.zeros((128, NFC), np.float32)
    bfl = b_in[fcols]
    for ci in range(NFC):
        bf[:FCH_M[ci], ci] = bfl[FCH_OFF[ci]:FCH_OFF[ci] + FCH_M[ci]]
    bt = np.ascontiguousarray(b_in[tcols][None, :])
    w1kv = np.concatenate([inp["cmp_w1_k"][0].reshape(32, 64, 256).transpose(1, 0, 2),
                           inp["cmp_w1_v"][0].reshape(32, 64, 256).transpose(1, 0, 2)], axis=0)
    w2k = inp["cmp_w2_k"][0].reshape(2, 128, 64).transpose(1, 0, 2)
    w2v = inp["cmp_w2_v"][0].reshape(2, 128, 64).transpose(1, 0, 2)
    posT = np.concatenate([inp["cmp_pos_k"][0].T, inp["cmp_pos_v"][0].T], axis=0)
    d = dict(
        x=x,
        cT=np.ascontiguousarray(inp["c"][b].reshape(16, 128).T),
        w_ada=np.ascontiguousarray(inp["w_ada"][0]),
        b_ada=np.ascontiguousarray(inp["b_ada"][0].reshape(48, 128).T),
        gainT=np.ascontiguousarray(inp["norm_gain"][0].reshape(16, 128).T),
        wf=wf, wt=wt, bf=bf, bt=bt,
        w1kv=np.ascontiguousarray(w1kv), w2k=np.ascontiguousarray(w2k), w2v=np.ascontiguousarray(w2v),
        posT=np.ascontiguousarray(posT),
        walpha=np.ascontiguousarray(inp["gla_w_alpha"][0][:, 128 * j:128 * j + 128]),
        balpha=np.ascontiguousarray(inp["gla_b_alpha"][0][128 * j:128 * j + 128].reshape(128, 1)),
        ggain=np.ascontiguousarray(np.broadcast_to(inp["gla_norm_gain"][0][None, :], (128, 256))),
    )
    d.update(host_tables(T, g))
    return d


class _Stop(Exception):
    pass


def build_l1(T, debug=False, stage=99):
    NST = T // 512
    NQT = T // 128
    nc = bass.Bass("TRN2", target_bir_lowering=False)

    def din(name, shape, dt=F32):
        return nc.dram_tensor(name, list(shape), dt, kind="ExternalInput").ap()
    x = din("x", [T, D])
    cT_d = din("cT", [128, 16]); wada_d = din("w_ada", [D, 6144]); bada_d = din("b_ada", [128, 48])
    gainT_d = din("gainT", [128, 16])
    wf_d = din("wf", [D, NF]); wt_d = din("wt", [D, NT_]); bf_d = din("bf", [128, NFC]); bt_d = din("bt", [1, NT_])
    w1kv_d = din("w1kv", [128, 32, 256]); w2k_d = din("w2k", [128, 2, 64]); w2v_d = din("w2v", [128, 2, 64])
    posT_d = din("posT", [128, 32])
    walpha_d = din("walpha", [16, 128]); balpha_d = din("balpha", [128, 1]); ggain_d = din("ggain", [128, 256])
    qaug_d = din("qaug", [8, T // 128, 4, 128], BF16); kaug_d = din("kaug", [8, T], BF16); kcaug_d = din("kcaug", [8, 512], BF16)
    Mtab_d = din("Mtab", [128, 4, 128], BF16); E4_d = din("E4", [128, 32, 128], BF16)
    bdiag_d = din("bias_diag", [128, 512], BF16); bfar_d = din("bias_far", [128, 512], BF16)
    gmask_d = din("gmask", [128, 128], BF16); reset_d = din("reset", [128, 512], BF16)
    identb_d = din("identb", [128, 128], BF16); identf_d = din("identf", [128, 128])
    y_d = nc.dram_tensor("y", [T, 512], BF16, kind="ExternalOutput").ap()
    adao_d = nc.dram_tensor("ada_out", [128, 48], F32, kind="ExternalOutput").ap()
    dbg = {}
    if debug:
        dbg["qT"] = nc.dram_tensor("dbg_qT", [72, 4, 4, 128], BF16, kind="ExternalOutput").ap()
        dbg["tokA"] = nc.dram_tensor("dbg_tokA", [128, NTA], F32, kind="ExternalOutput").ap()
        dbg["ocmp"] = nc.dram_tensor("dbg_ocmp", [128, 4, 193], BF16, kind="ExternalOutput").ap()
        dbg["selb"] = nc.dram_tensor("dbg_selb", [128, 128], BF16, kind="ExternalOutput").ap()
        dbg["kcT"] = nc.dram_tensor("dbg_kcT", [72, 512], BF16, kind="ExternalOutput").ap()
        dbg["oT0"] = nc.dram_tensor("dbg_oT0", [65, 512], F32, kind="ExternalOutput").ap()
        dbg["oT1"] = nc.dram_tensor("dbg_oT1", [65, 512], F32, kind="ExternalOutput").ap()
        dbg["acc"] = nc.dram_tensor("dbg_acc", [128, 256], F32, kind="ExternalOutput").ap()
        dbg["pT"] = nc.dram_tensor("dbg_pT", [128, 512], BF16, kind="ExternalOutput").ap()
        dbg["den"] = nc.dram_tensor("dbg_den", [128, 12], F32, kind="ExternalOutput").ap()
        dbg["gsig"] = nc.dram_tensor("dbg_gsig", [128, 12], F32, kind="ExternalOutput").ap()
        dbg["coef"] = nc.dram_tensor("dbg_coef", [128, 12], F32, kind="ExternalOutput").ap()

    P = Prog(nc)
    with contextlib.ExitStack() as st:
        def sb(name, shape, dt=F32):
            return st.enter_context(nc.sbuf_tensor("sb_" + name, list(shape), dt))
        wf_s = sb("wf_s", [128, KC, NF], BF16); wt_s = sb("wt_s", [128, KC, NT_], BF16)
        bf_s = sb("bf_s", [128, NFC]); btb_s = sb("btb_s", [1, NT_], BF16)
        ones_s = sb("ones_s", [1, 128], BF16)
        xs = [sb("xs%d" % i, [128, D]) for i in range(2)]
        xnb = [sb("xnb%d" % i, [128, D], BF16) for i in range(2)]
        hT = sb("hT", [128, KC, 512], BF16)
        A_s = sb("A_s", [128, 16]); B_s = sb("B_s", [128, 16]); ada_s = sb("ada_s", [128, 48])
        cT_s = sb("cT_s", [128, 16]); gainT_s = sb("gainT_s", [128, 16]); bada_s = sb("bada_s", [128, 48])
        small = sb("small", [128, 64])
        qT = sb("qT", [128, 4, 4, 128], BF16)
        kslcT = sb("kslcT", [128, T], BF16)
        kwinT = sb("kwinT", [128, 1024], BF16)
        vslc = sb("vslc", [128, NQT, 66], BF16)
        vwin = sb("vwin", [128, 8, 66], BF16)
        cmpbuf = sb("cmpbuf", [128, 528], BF16)
        w1kv_s = sb("w1kv_s", [128, 32, 256], BF16)
        w2k_s = sb("w2k_s", [128, 2, 64], BF16); w2v_s = sb("w2v_s", [128, 2, 64], BF16)
        posT_s = sb("posT_s", [128, 32], BF16); posb_s = sb("posb_s", [128, 4])
        kcT = sb("kcT", [128, 512], BF16)
        vcM = sb("vcM", [128, 4, 194], BF16)
        E4_s = sb("E4_s", [128, 32, 128], BF16)
        bdiag_s = sb("bdiag_s", [128, 512], BF16); bfar_s = sb("bfar_s", [128, 512], BF16)
        gmask_s = sb("gmask_s", [128, 128], BF16); reset_s = sb("reset_s", [128, 512], BF16)
        identb = sb("identb", [128, 128], BF16); identf = sb("identf", [128, 128])
        walpha_s = sb("walpha_s", [16, 128], BF16); nbalpha_s = sb("nbalpha_s", [128, 1]); ggain_s = sb("ggain_s", [128, 256])
        gqT = sb("gqT", [128, 512], BF16); gkT = sb("gkT", [128, 512], BF16); gaT = sb("gaT", [16, 512], BF16)
        tokA = [sb("tokA%d" % i, [128, NTA]) for i in range(4)]
        tokB = [sb("tokB%d" % i, [128, 256]) for i in range(4)]
        gvb = [sb("gvb%d" % i, [128, 256], BF16) for i in range(4)]
        e1 = sb("e1", [128, 512]); spl = e1; cs = sb("cs", [128, 512])
        eb = sb("eb", [128, 512]); enb = e1
        qtl = sb("qtl", [128, 512], BF16); ktl = sb("ktl", [128, 512], BF16)
        ktok = sb("ktok", [128, 128], BF16); atm = sb("atm", [128, 128], BF16)
        S_s = sb("S_s", [128, 256]); Sb_s = sb("Sb_s", [128, 256], BF16); stmp = sb("stmp", [128, 256])
        gl1 = sb("gl1", [128, 256]); junk = gl1
        yt = [sb("yt%d" % i, [128, 512], BF16) for i in range(4)]
        hsb = sb("hsb", [128, 4, 32]); hex_ = sb("hex_", [128, 4, 32]); hidT = sb("hidT", [128, 4, 32], BF16)
        vcst = sb("vcst", [32, 64], BF16)
        pT = [sb("pT%d" % i, [128, 512], BF16) for i in range(3)]
        maskc = [sb("maskc%d" % i, [128, 512], BF16) for i in range(2)]
        zeros_b = sb("zeros_b", [128, 512], BF16)
        score = sb("score", [128, 128]); swork = sb("swork", [128, 128]); m8 = sb("m8", [128, 16])
        selb = [sb("selb%d" % i, [128, 128], BF16) for i in range(2)]; selT4 = [[sb("selT4%d_%d" % (i, g_), [128, 4, 128], BF16) for g_ in range(2)] for i in range(2)]
        oT_sb = None
        coef = [sb("coef%d" % i, [128, 12]) for i in range(2)]; den = [sb("den%d" % i, [128, 12]) for i in range(2)]; gsig = [sb("gsig%d" % i, [128, 12]) for i in range(4)]
        szn = gl1; acc = [sb("acc%d" % i, [128, 256]) for i in range(2)]

        banks = [st.enter_context(nc.psum_tensor("bank%d" % i, [128, 512], F32)) for i in range(8)]
        bb = [Buf("bank%d" % i) for i in range(8)]

        def B(name):
            return Buf(name)
        b_w = B("weights"); b_tab = B("tables"); b_hT = B("hT"); b_AB = B("AB")
        b_xs = [B("xs0"), B("xs1")]; b_xnb = [B("xnb0"), B("xnb1")]; b_small = B("small"); b_sm = [B("sm%d" % i) for i in range(4)]
        b_qT = B("qT"); b_kslc = [B("kslc%d" % i) for i in range(NQT)]
        b_kwin = [B("kwin%d" % i) for i in range(8)]; b_vslc = [B("vslc%d" % i) for i in range(NQT)]
        b_vwin = [B("vwin%d" % i) for i in range(8)]
        b_cmpbuf = B("cmpbuf"); b_kcT = B("kcT"); b_vcM = B("vcM"); b_hid = B("hid")
        b_gT = B("gT"); b_tokA = [B("tokA%d" % i) for i in range(4)]; b_tokB = [B("tokB%d" % i) for i in range(4)]
        b_gvb = [B("gvb%d" % i) for i in range(4)]
        b_gl = B("glatemps"); b_S = B("S"); b_glc = B("glachunk"); b_yt = [B("yt%d" % i) for i in range(4)]
        b_pT = [B("pT%d" % i) for i in range(3)]; b_maskc = [B("maskc0"), B("maskc1")]
        b_sel = B("sel"); b_selT = [B("selT0"), B("selT1")]; b_selb = [B("selb0"), B("selb1")]; b_oT = B("oTsb"); b_fin = B("fin"); b_finA = [B("finA0"), B("finA1")]; b_gs = [B("gs%d" % i) for i in range(4)]

        def chk(k):
            if stage == k:
                raise _Stop()
        try:
            def ld(q, out, in_, writes):
                return P.dma(q, lambda e: e.dma_start(out=out, in_=in_), writes=writes)
            wf_v = wf_d.rearrange("(kc p) n -> p kc n", p=128)
            wt_v = wt_d.rearrange("(kc p) n -> p kc n", p=128)
            for k4 in range(0, KC, 4):
                ld("pool", wf_s[:, k4:k4 + 4, :], wf_v[:, k4:k4 + 4, :], [b_w])
                ld("pool", wt_s[:, k4:k4 + 4, :], wt_v[:, k4:k4 + 4, :], [b_w])
            for pp in range(0, 32, 8):
                ld("pool", w1kv_s[:, pp:pp + 8, :], w1kv_d[:, pp:pp + 8, :], [b_w])
            ld("pool", w2k_s[:], w2k_d, [b_w]); ld("pool", w2v_s[:], w2v_d, [b_w])
            ld("pool", posT_s[:], posT_d, [b_w]); ld("pool", walpha_s[:], walpha_d, [b_w])
            ld("sp", bf_s[:], bf_d, [b_w]); ld("pool", btb_s[:], bt_d, [b_w])
            ld("sp", cT_s[:], cT_d, [b_AB]); ld("sp", gainT_s[:], gainT_d, [b_AB]); ld("sp", bada_s[:], bada_d, [b_AB])
            ld("sp", nbalpha_s[:], balpha_d, [b_w]); ld("sp", ggain_s[:], ggain_d, [b_w])
            ld("sp", E4_s[:], E4_d, [b_tab]); ld("sp", bdiag_s[:], bdiag_d, [b_tab]); ld("sp", bfar_s[:], bfar_d, [b_tab])
            ld("sp", gmask_s[:], gmask_d, [b_tab]); ld("sp", reset_s[:], reset_d, [b_tab])
            ld("sp", identb[:], identb_d, [b_tab]); ld("sp", identf[:], identf_d, [b_tab])
            ld("sp", vcM[:, :, 65:193], Mtab_d, [b_vcM])
            for i in range(NQT):
                pass
            P.op("pool", lambda e: e.memset(ones_s[:], 1.0), writes=[b_w])
            P.op("pool", lambda e: e.memset(zeros_b[:], 0.0), writes=[b_tab])
            for par_ in range(2):
                for g_ in range(2):
                    P.op("pool", lambda e, par_=par_, g_=g_: e.memset(selT4[par_][g_][:], 0.0), writes=[b_selT[par_]])
            P.op("pool", lambda e: e.memset(cmpbuf[:], 0.0), writes=[b_cmpbuf])
            P.op("pool", lambda e: e.memset(S_s[:], 0.0), writes=[b_S])
            P.op("pool", lambda e: e.memset(Sb_s[:], 0.0), writes=[b_S])
            P.op("pool", lambda e: e.memset(vcM[:, :, 0:64], 0.0), writes=[b_vcM])
            P.op("pool", lambda e: e.memset(vcM[:, :, 64:65], 1.0), writes=[b_vcM])
            P.op("pool", lambda e: e.memset(kcT[:, :], 0.0), writes=[b_kcT])
            ld("sp", kcT[64:72, :], kcaug_d, [b_kcT])
            P.op("pool", lambda e: e.memset(kslcT[:, :], 0.0), writes=b_kslc)
            ld("sp", kslcT[64:72, :], kaug_d, b_kslc)
            P.op("pool", lambda e: e.memset(kwinT[:, :], 0.0), writes=b_kwin)
            P.op("pool", lambda e: e.memset(qT[:].rearrange("p a b c -> p (a b c)"), 0.0), writes=[b_qT])
            for i in range(NQT):
                P.op("pool", lambda e, i=i: e.memset(vslc[:, i, 64:65], 1.0), writes=[b_vslc[i]])
            for i in range(8):
                P.op("pool", lambda e, i=i: e.memset(vwin[:, i, 64:65], 1.0), writes=[b_vwin[i]])
            P.op("dve", lambda e: e.tensor_scalar(out=nbalpha_s[:], in0=nbalpha_s[:], scalar1=-1.0, scalar2=None, op0=ALU.mult),
                 reads=[b_w], writes=[b_w])

            chk(0)
            wada_v = wada_d.rearrange("(kc p) n -> p kc n", p=128)
            for cc in range(48):
                xb_ = cc % 2
                ld("sp", xs[xb_][:].rearrange("p (kc n) -> p kc n", kc=16), wada_v[:, :, cc * 128:(cc + 1) * 128], [b_xs[xb_]])
                wv = xs[xb_][:].rearrange("p (kc n) -> p kc n", kc=16)
                for kc in range(KC):
                    P.op("pe", lambda e, kc=kc, wv=wv, cc=cc: e.matmul(banks[0][:, cc:cc + 1], lhsT=wv[:, kc, :], rhs=cT_s[:, kc:kc + 1],
                                                                      start=(kc == 0), stop=(kc == KC - 1)),
                         reads=[b_xs[xb_], b_AB], writes=[bb[0]], mode="f32")
            P.op("dve", lambda e: e.tensor_tensor(out=ada_s[:], in0=banks[0][:, 0:48], in1=bada_s[:], op=ALU.add),
                 reads=[bb[0], b_AB], writes=[b_AB])
            P.op("dve", lambda e: e.tensor_scalar(out=A_s[:], in0=ada_s[:, 16:32], scalar1=1.0, scalar2=None, op0=ALU.add),
                 reads=[b_AB], writes=[b_AB])
            P.op("dve", lambda e: e.tensor_tensor(out=A_s[:], in0=A_s[:], in1=gainT_s[:], op=ALU.mult), reads=[b_AB], writes=[b_AB])
            P.op("dve", lambda e: e.tensor_copy(out=B_s[:], in_=ada_s[:, 0:16]), reads=[b_AB], writes=[b_AB])
            P.dma("pool", lambda e: e.dma_start(out=adao_d, in_=ada_s[:]), reads=[b_AB])

            chk(1)
            for kv in range(2):
                lo = 64 * kv
                for hc in range(2):
                    col = kv * 2 + hc
                    for p_ in range(32):
                        P.op("pe", lambda e, lo=lo, hc=hc, p_=p_, col=col: e.matmul(
                            banks[1][:, col:col + 1], lhsT=w1kv_s[lo:lo + 64, p_, hc * 128:(hc + 1) * 128],
                            rhs=posT_s[lo:lo + 64, p_:p_ + 1], start=(p_ == 0), stop=(p_ == 31)),
                            reads=[b_w], writes=[bb[1]], mode="r64")
            P.op("dve", lambda e: e.tensor_copy(out=posb_s[:], in_=banks[1][:, 0:4]), reads=[bb[1]], writes=[b_w])

            chk(2)
            x_v = x.rearrange("(t p) d -> t p d", p=128)
            y_v = y_d.rearrange("(t p) n -> t p n", p=128)
            tokx = {}

            def load_x(t):
                if t < NQT:
                    tokx[t] = ld("sp", xs[t % 2][:], x_v[t], [b_xs[t % 2]])
            load_x(0); load_x(1)

            rr = {"mm": 0, "st": 0, "pt": 0}
            fillc = {}

            def getfill(e):
                if "r" not in fillc:
                    fillc["r"] = e.to_reg(NEGM)
                return fillc["r"]

            def mmbank():
                i = rr["mm"] % 2
                rr["mm"] += 1
                return i

            def stbank():
                i = 4 + rr["st"] % 2
                rr["st"] += 1
                return i

            def norm_tile(ST, tt):
                t = ST * 4 + tt
                xb_ = t % 2
                P.op("act", lambda e, xb_=xb_, tt=tt: e.activation(out=xnb[xb_][:], in_=xs[xb_][:], func=AF.Square,
                                                                   accum_out=small[:, tt:tt + 1]),
                     reads=[b_xs[xb_]], writes=[b_xnb[xb_], b_sm[tt]])
                P.op("act", lambda e, tt=tt: e.activation(out=small[:, 8 + tt:9 + tt], in_=small[:, tt:tt + 1], func=AF.Ln,
                                                          scale=1.0 / D, bias=EPS),
                     reads=[b_sm[tt]], writes=[b_sm[tt]])
                P.op("act", lambda e, tt=tt: e.activation(out=small[:, 16 + tt:17 + tt], in_=small[:, 8 + tt:9 + tt], func=AF.Exp,
                                                          scale=-0.5),
                     reads=[b_sm[tt]], writes=[b_sm[tt]])
                P.op("dve", lambda e, xb_=xb_, tt=tt: e.tensor_scalar(out=xnb[xb_][:], in0=xs[xb_][:], scalar1=small[:, 16 + tt:17 + tt],
                                                                      scalar2=None, op0=ALU.mult),
                     reads=[b_xs[xb_], b_sm[tt]], writes=[b_xnb[xb_]])
                load_x(t + 2)
                for half in range(2):
                    bk = 2 + half
                    pv = banks[bk][:].bitcast(BF16)
                    for c8 in range(8):
                        kc = half * 8 + c8
                        P.op("pe", lambda e, pv=pv, c8=c8, kc=kc, xb_=xb_: e.transpose(
                            pv[:, c8 * 128:(c8 + 1) * 128], xnb[xb_][:, kc * 128:(kc + 1) * 128], identb[:]),
                            reads=[b_xnb[xb_], b_tab], writes=[bb[bk]], mode="tr")
                    for c8 in range(8):
                        kc = half * 8 + c8
                        eng = "dve" if c8 % 2 == 0 else "pool"
                        eng = "dve"
                        P.op(eng, lambda e, pv=pv, c8=c8, kc=kc, tt=tt: e.tensor_scalar(
                            out=hT[:, kc, tt * 128:(tt + 1) * 128], in0=pv[:, c8 * 128:(c8 + 1) * 128],
                            scalar1=A_s[:, kc:kc + 1], scalar2=B_s[:, kc:kc + 1], op0=ALU.mult, op1=ALU.add),
                            reads=[bb[bk], b_AB], writes=[b_hT])

            for tt_ in range(4):
                norm_tile(0, tt_)
            for ST in range(NST):
                chk(3)
                tsl = slice(ST * 512, ST * 512 + 512)
                P.dma("sp", lambda e, ST=ST: e.dma_start(out=qT[64:72, :, :, :], in_=qaug_d[:, ST * 4:ST * 4 + 4, :, :]), writes=[b_qT])
                for ci in range(NFC):
                    M_ = FCH_M[ci]; off = FCH_OFF[ci]
                    bk = mmbank()
                    for kc in range(KC):
                        P.op("pe", lambda e, bk=bk, M_=M_, off=off, kc=kc: e.matmul(
                            banks[bk][0:M_, :], lhsT=wf_s[:, kc, off:off + M_], rhs=hT[:, kc, :], start=(kc == 0), stop=(kc == KC - 1)),
                            reads=[b_w, b_hT], writes=[bb[bk]], mode=("full" if M_ > 64 else "c%d" % (64 if M_ > 32 else 32)))
                    src = banks[bk]
                    if ci < 4:
                        P.op("act", lambda e, src=src, ci=ci: e.activation(out=qT[0:64, :, ci, :], in_=src[0:64, :].rearrange("p (t q) -> p t q", q=128), func=AF.Identity,
                                                                           bias=bf_s[0:64, ci:ci + 1], scale=1.0),
                             reads=[bb[bk], b_w], writes=[b_qT])
                        P.op("dve", lambda e, ci=ci: e.tensor_scalar(out=qT[0:64, :, ci, :], in0=qT[0:64, :, ci, :], scalar1=0.125, scalar2=None,
                                                                     op0=ALU.mult), reads=[b_qT], writes=[b_qT])
                    elif ci == 4:
                        P.op("act", lambda e, src=src: e.activation(out=cmpbuf[:, 16:528], in_=src[:, :], func=AF.Identity,
                                                                    bias=bf_s[:, 4:5], scale=1.0),
                             reads=[bb[bk], b_w], writes=[b_cmpbuf])
                    elif ci == 5:
                        P.op("act", lambda e, src=src, tsl=tsl: e.activation(out=kslcT[0:64, tsl], in_=src[0:64, :], func=AF.Identity,
                                                                             bias=bf_s[0:64, 5:6], scale=1.0),
                             reads=[bb[bk], b_w], writes=[b_kslc[ST * 4 + k] for k in range(4)])
                    elif ci == 6:
                        rs = (ST % 2) * 512
                        P.op("act", lambda e, src=src, rs=rs: e.activation(out=kwinT[0:64, rs:rs + 512], in_=src[0:64, :], func=AF.Identity,
                                                                           bias=bf_s[0:64, 6:7], scale=1.0),
                             reads=[bb[bk], b_w], writes=[b_kwin[(ST % 2) * 4 + k] for k in range(4)])
                        P.dma("sp", lambda e, rs=rs, tsl=tsl: e.dma_start(out=kwinT[64:72, rs:rs + 512], in_=kaug_d[:, tsl]),
                              writes=[b_kwin[(ST % 2) * 4 + k] for k in range(4)])
                    elif ci == 7:
                        P.op("act", lambda e, src=src: e.activation(out=gqT[:], in_=src[:, :], func=AF.Identity, bias=bf_s[:, 7:8], scale=1.0),
                             reads=[bb[bk], b_w], writes=[b_gT])
                    elif ci == 8:
                        P.op("act", lambda e, src=src: e.activation(out=gkT[:], in_=src[:, :], func=AF.Identity, bias=bf_s[:, 8:9], scale=1.0),
                             reads=[bb[bk], b_w], writes=[b_gT])
                    else:
                        P.op("act", lambda e, src=src: e.activation(out=gaT[:], in_=src[0:16, :], func=AF.Identity, bias=bf_s[0:16, 9:10], scale=1.0),
                             reads=[bb[bk], b_w], writes=[b_gT])
                chk(4)
                for tt in range(4):
                    t = ST * 4 + tt
                    for grp in range(2):
                        if grp == 1:
                            chk(44)
                        if tt == 1:
                            chk(46)
                        bk = mmbank()
                        n0, n1 = (0, NTA) if grp == 0 else (NTA, NT_)
                        w_ = n1 - n0
                        for kc in range(KC):
                            P.op("pe", lambda e, bk=bk, kc=kc, tt=tt, n0=n0, n1=n1, w_=w_: e.matmul(
                                banks[bk][:, 0:w_], lhsT=hT[:, kc, tt * 128:(tt + 1) * 128], rhs=wt_s[:, kc, n0:n1], start=(kc == 0), stop=False),
                                reads=[b_w, b_hT], writes=[bb[bk]])
                        chk(41)
                        P.op("pe", lambda e, bk=bk, n0=n0, n1=n1, w_=w_: e.matmul(
                            banks[bk][:, 0:w_], lhsT=ones_s[0:1, :], rhs=btb_s[0:1, n0:n1], start=False, stop=True),
                            reads=[b_w], writes=[bb[bk]], mode="r32")
                        chk(42)
                        if grp == 0:
                            P.op("dve", lambda e, bk=bk, tt=tt: e.tensor_copy(out=tokA[tt][:], in_=banks[bk][:, 0:NTA]),
                                 reads=[bb[bk]], writes=[b_tokA[tt]])
                            chk(43)
                            P.op("dve", lambda e, tt=tt, t=t: e.tensor_copy(out=vslc[:, t, 0:64], in_=tokA[tt][:, 0:64]),
                                 reads=[b_tokA[tt]], writes=[b_vslc[t]])
                            P.op("dve", lambda e, tt=tt, t=t: e.tensor_copy(out=vwin[:, t % 8, 0:64], in_=tokA[tt][:, 64:128]),
                                 reads=[b_tokA[tt]], writes=[b_vwin[t % 8]])
                        else:
                            chk(45)
                            P.op("dve", lambda e, bk=bk, tt=tt: e.tensor_copy(out=tokB[tt][:], in_=banks[bk][:, 256:512]),
                                 reads=[bb[bk]], writes=[b_tokB[tt]])
                            P.op("dve", lambda e, bk=bk, tt=tt: e.tensor_copy(out=gvb[tt][:], in_=banks[bk][:, 0:256]),
                                 reads=[bb[bk]], writes=[b_gvb[tt]])
                if debug and ST == 0:
                    P.dma("pool", lambda e: e.dma_start(out=dbg["qT"], in_=qT[0:72]), reads=[b_qT])
                    P.dma("pool", lambda e: e.dma_start(out=dbg["tokA"], in_=tokA[0][:]), reads=[b_tokA[0]])

                if ST + 1 < NST:
                    norm_tile(ST + 1, 0)
                chk(5)
                bk = mmbank()
                P.op("pe", lambda e, bk=bk: e.matmul(banks[bk][:, :], lhsT=walpha_s[:, :], rhs=gaT[:, :], start=True, stop=True),
                     reads=[b_w, b_gT], writes=[bb[bk]], mode="r32")
                P.op("act", lambda e, bk=bk: e.activation(out=e1[:], in_=banks[bk][:, :], func=AF.Exp, scale=-1.0, bias=nbalpha_s[:, 0:1]),
                     reads=[bb[bk], b_w], writes=[b_gl])
                P.op("act", lambda e: e.activation(out=spl[:], in_=e1[:], func=AF.Ln, bias=1.0, scale=1.0), reads=[b_gl], writes=[b_gl])
                P.op("dve", lambda e: e.tensor_tensor_scan(out=cs[:], data0=reset_s[:], data1=spl[:], initial=0.0, op0=ALU.mult, op1=ALU.add),
                     reads=[b_gl, b_tab], writes=[b_gl])
                P.op("act", lambda e: e.activation(out=eb[:], in_=cs[:], func=AF.Exp, scale=-1.0 / 16), reads=[b_gl], writes=[b_gl])
                P.op("act", lambda e: e.activation(out=enb[:], in_=cs[:], func=AF.Exp, scale=1.0 / 16), reads=[b_gl], writes=[b_gl])
                P.op("dve", lambda e: e.scalar_tensor_tensor(out=qtl[:], in0=gqT[:], scalar=128.0 ** -0.5, in1=eb[:], op0=ALU.mult, op1=ALU.mult),
                     reads=[b_gl, b_gT], writes=[b_gl])
                P.op("dve", lambda e: e.tensor_tensor(out=ktl[:], in0=gkT[:], in1=enb[:], op=ALU.mult), reads=[b_gl, b_gT], writes=[b_gl])
                bf6 = banks[6][:].bitcast(BF16)
                for tt in range(4):
                    cs_ = slice(tt * 128, tt * 128 + 128)
                    P.op("pe", lambda e, cs_=cs_: e.transpose(bf6[:, cs_], ktl[:, cs_], identb[:]), reads=[b_gl, b_tab], writes=[bb[6]], mode="tr")
                for tt in range(4):
                    cs_ = slice(tt * 128, tt * 128 + 128)
                    P.op("pe", lambda e, cs_=cs_: e.matmul(banks[7][:, cs_], lhsT=ktl[:, cs_], rhs=qtl[:, cs_], start=True, stop=True),
                         reads=[b_gl], writes=[bb[7]])
                P.op("dve", lambda e: e.tensor_copy(out=gkT[:], in_=bf6[:, 0:512]), reads=[bb[6], b_gl], writes=[b_gT])
                for tt in range(4):
                    cs_ = slice(tt * 128, tt * 128 + 128)
                    P.op("dve", lambda e, cs_=cs_: e.tensor_tensor(out=gqT[:, cs_], in0=banks[7][:, cs_], in1=gmask_s[:], op=ALU.mult),
                         reads=[bb[7], b_tab, b_gl], writes=[b_gT])
                for tt in range(4):
                    t = ST * 4 + tt
                    cs_ = slice(tt * 128, tt * 128 + 128)
                    bo = mmbank()
                    P.op("pe", lambda e, bo=bo, tt=tt, cs_=cs_: e.matmul(banks[bo][:, 0:256], lhsT=gqT[:, cs_], rhs=gvb[tt][:], start=True, stop=False),
                         reads=[b_gT, b_gvb[tt]], writes=[bb[bo]])
                    P.op("pe", lambda e, tt=tt, cs_=cs_: e.matmul(banks[6][:, 256:512], lhsT=gkT[:, cs_], rhs=gvb[tt][:], start=True, stop=True),
                         reads=[b_gT, b_gvb[tt]], writes=[bb[6]])
                    P.op("pe", lambda e, bo=bo, cs_=cs_: e.matmul(banks[bo][:, 0:256], lhsT=qtl[:, cs_], rhs=Sb_s[:], start=False, stop=True),
                         reads=[b_gl, b_S], writes=[bb[bo]])
                    P.op("dve", lambda e: e.tensor_tensor(out=stmp[:], in0=banks[6][:, 256:512], in1=S_s[:], op=ALU.add),
                         reads=[bb[6], b_S], writes=[b_glc])
                    last = tt * 128 + 127
                    P.op("dve", lambda e, last=last: e.tensor_scalar(out=S_s[:], in0=stmp[:], scalar1=eb[:, last:last + 1], scalar2=None, op0=ALU.mult),
                         reads=[b_glc, b_gl], writes=[b_S])
                    P.op("dve", lambda e: e.tensor_copy(out=Sb_s[:], in_=S_s[:]), reads=[b_S], writes=[b_S])
                    yb_ = tt
                    P.op("act", lambda e, bo=bo: e.activation(out=junk[:], in_=banks[bo][:, 0:256], func=AF.Square, accum_out=small[:, 24:25]),
                         reads=[bb[bo]], writes=[b_fin, b_small])
                    P.op("act", lambda e: e.activation(out=small[:, 25:26], in_=small[:, 24:25], func=AF.Ln, scale=1.0 / 256, bias=EPS),
                         reads=[b_small], writes=[b_small])
                    P.op("act", lambda e: e.activation(out=small[:, 26:27], in_=small[:, 25:26], func=AF.Exp, scale=-0.5),
                         reads=[b_small], writes=[b_small])
                    P.op("act", lambda e, tt=tt: e.activation(out=gl1[:], in_=tokB[tt][:], func=AF.Exp, scale=-1.0),
                         reads=[b_tokB[tt]], writes=[b_fin])
                    P.op("dve", lambda e: e.tensor_scalar(out=gl1[:], in0=gl1[:], scalar1=1.0, scalar2=None, op0=ALU.add), reads=[b_fin], writes=[b_fin])
                    P.op("dve", lambda e: e.reciprocal(out=gl1[:], in_=gl1[:]), reads=[b_fin], writes=[b_fin])
                    P.op("dve", lambda e, tt=tt: e.tensor_tensor(out=gl1[:], in0=gl1[:], in1=tokB[tt][:], op=ALU.mult),
                         reads=[b_fin, b_tokB[tt]], writes=[b_fin])
                    P.op("dve", lambda e: e.tensor_tensor(out=gl1[:], in0=gl1[:], in1=ggain_s[:], op=ALU.mult), reads=[b_fin, b_w], writes=[b_fin])
                    P.op("dve", lambda e, bo=bo, yb_=yb_: e.scalar_tensor_tensor(out=yt[yb_][:, 256:512], in0=banks[bo][:, 0:256], scalar=small[:, 26:27],
                                                                                 in1=gl1[:], op0=ALU.mult, op1=ALU.mult),
                         reads=[bb[bo], b_fin, b_small], writes=[b_yt[yb_]])

                if ST + 1 < NST:
                    norm_tile(ST + 1, 1)
                chk(6)
                for kv in range(2):
                    lo = 64 * kv
                    bk = mmbank()
                    for hc in range(2):
                        for p_ in range(32):
                            P.op("pe", lambda e, bk=bk, lo=lo, hc=hc, p_=p_: e.matmul(
                                banks[bk][:, hc * 32:(hc + 1) * 32], lhsT=w1kv_s[lo:lo + 64, p_, hc * 128:(hc + 1) * 128],
                                rhs=cmpbuf[lo:lo + 64, p_:p_ + 497:16], start=(p_ == 0), stop=(p_ == 31)),
                                reads=[b_w, b_cmpbuf], writes=[bb[bk]], mode="r64")
                    for hc in range(2):
                        col = kv * 2 + hc
                        P.op("dve", lambda e, bk=bk, hc=hc, col=col: e.tensor_scalar(out=hsb[:, col, :], in0=banks[bk][:, hc * 32:(hc + 1) * 32],
                                                                                     scalar1=posb_s[:, col:col + 1], scalar2=None, op0=ALU.add),
                             reads=[bb[bk], b_w], writes=[b_hid])
                P.op("act", lambda e: e.activation(out=hex_[:], in_=hsb[:], func=AF.Exp, scale=-1.0), reads=[b_hid], writes=[b_hid])
                P.op("dve", lambda e: e.tensor_scalar(out=hex_[:], in0=hex_[:], scalar1=1.0, scalar2=None, op0=ALU.add), reads=[b_hid], writes=[b_hid])
                P.op("dve", lambda e: e.reciprocal(out=hex_[:], in_=hex_[:]), reads=[b_hid], writes=[b_hid])
                P.op("dve", lambda e: e.tensor_tensor(out=hidT[:], in0=hex_[:], in1=hsb[:], op=ALU.mult), reads=[b_hid], writes=[b_hid])
                bk = mmbank()
                for hc in range(2):
                    P.op("pe", lambda e, bk=bk, hc=hc: e.matmul(banks[bk][0:64, 0:32], lhsT=w2k_s[:, hc, :], rhs=hidT[:, hc, :],
                                                                start=(hc == 0), stop=(hc == 1)), reads=[b_w, b_hid], writes=[bb[bk]], mode="c64")
                P.op("dve", lambda e, bk=bk, ST=ST: e.tensor_copy(out=kcT[0:64, ST * 32:ST * 32 + 32], in_=banks[bk][0:64, 0:32]),
                     reads=[bb[bk]], writes=[b_kcT])
                for hc in range(2):
                    P.op("pe", lambda e, bk=bk, hc=hc: e.matmul(banks[bk][0:32, 64:128], lhsT=hidT[:, 2 + hc, :], rhs=w2v_s[:, hc, :],
                                                                start=(hc == 0), stop=(hc == 1)), reads=[b_w, b_hid], writes=[bb[bk]], mode="c32")
                P.op("dve", lambda e, bk=bk: e.tensor_copy(out=vcst[:], in_=banks[bk][0:32, 64:128]), reads=[bb[bk]], writes=[b_hid])
                pr = 32 * (ST % 4)
                P.dma("sp", lambda e, pr=pr, ST=ST: e.dma_start(out=vcM[pr:pr + 32, ST // 4, 0:64], in_=vcst[:]), reads=[b_hid], writes=[b_vcM])
                if ST == 0:
                    P.op("pool", lambda e: e.memset(vcM[0:1, 0, 0:193], 0.0), writes=[b_vcM])
                P.op("dve", lambda e: e.tensor_copy(out=cmpbuf[:, 0:16], in_=cmpbuf[:, 512:528]), reads=[b_cmpbuf], writes=[b_cmpbuf])

                if ST + 1 < NST:
                    norm_tile(ST + 1, 2)
                    norm_tile(ST + 1, 3)
                chk(7)
                def run_pipeline(jobs, depth=1):
                    pend = []
                    for jb in jobs:
                        jb[0]()
                        pend.append(jb)
                        if len(pend) > depth:
                            pend.pop(0)[1]()
                    for jb in pend:
                        jb[1]()

                def stage_A(tt):
                    i = ST * 4 + tt
                    par = i % 2
                    qv = qT[:, tt, :, :].rearrange("p h q -> p (h q)")
                    nnt = (8 * i + 7 + 127) // 128
                    jobs = []
                    for nt in range(nnt):
                        mrows = 128
                        st_ = {}

                        def qk(nt=nt, mrows=mrows, st_=st_):
                            sbk = stbank()
                            st_["sbk"] = sbk
                            mk = maskc[nt % 2]
                            P.op("pool", lambda e, mk=mk: e.affine_select(
                                out=mk[:], in_=zeros_b[:], pattern=[[0, 4], [1, 128]], compare_op=ALU.is_ge, fill=getfill(e),
                                base=128 * i - 2048 * nt - 15, channel_multiplier=-16), reads=[b_tab], writes=[b_maskc[nt % 2]])
                            P.op("pe", lambda e: e.matmul(banks[sbk][0:mrows, :], lhsT=kcT[:, nt * 128:nt * 128 + mrows], rhs=qv, start=True, stop=False),
                                 reads=[b_kcT, b_qT], writes=[bb[sbk]])
                            P.op("pe", lambda e: e.matmul(banks[sbk][0:mrows, :], lhsT=identb[:, 0:mrows], rhs=mk[:], start=False, stop=True),
                                 reads=[b_tab, b_maskc[nt % 2]], writes=[bb[sbk]])

                        def rest(nt=nt, mrows=mrows, st_=st_):
                            sbk = st_["sbk"]
                            pb = rr["pt"] % 3
                            rr["pt"] += 1
                            P.op("act", lambda e: e.activation(out=pT[pb][0:mrows, :], in_=banks[sbk][0:mrows, :], func=AF.Exp, scale=1.0),
                                 reads=[bb[sbk]], writes=[b_pT[pb]])
                            for h in range(4):
                                obk = 2 + h // 2
                                oc = (h % 2) * 256
                                P.op("pe", lambda e, obk=obk, oc=oc, h=h: e.matmul(
                                    banks[obk][:, oc:oc + 193], lhsT=pT[pb][0:mrows, h * 128:(h + 1) * 128], rhs=vcM[0:mrows, nt, 0:193],
                                    start=(nt == 0 and h % 2 == 0), stop=(nt == nnt - 1), skip_group_check=True),
                                    reads=[b_pT[pb], b_vcM], writes=[bb[obk]])
                        jobs.append((qk, rest))
                    run_pipeline(jobs)

                def stage_Adve(tt):
                    i = ST * 4 + tt
                    par = i % 2
                    bF = b_finA[par]
                    P.op("act", lambda e: e.activation(out=gsig[tt][:], in_=tokA[tt][:, 128:140], func=AF.Exp, scale=-1.0), reads=[b_tokA[tt]], writes=[b_gs[tt]])
                    P.op("dve", lambda e: e.tensor_scalar(out=gsig[tt][:], in0=gsig[tt][:], scalar1=1.0, scalar2=None, op0=ALU.add), reads=[b_gs[tt]], writes=[b_gs[tt]])
                    P.op("dve", lambda e: e.reciprocal(out=gsig[tt][:], in_=gsig[tt][:]), reads=[b_gs[tt]], writes=[b_gs[tt]])
                    for h in range(4):
                        obk = 2 + h // 2
                        oc = (h % 2) * 256
                        P.op("dve", lambda e, obk=obk, oc=oc, h=h: e.tensor_scalar(out=den[par][:, h:h + 1], in0=banks[obk][:, oc + 64:oc + 65],
                                                                                  scalar1=1e-30, scalar2=None, op0=ALU.max),
                             reads=[bb[obk]], writes=[bF])
                    P.op("dve", lambda e: e.reciprocal(out=den[par][:, 0:4], in_=den[par][:, 0:4]), reads=[bF], writes=[bF])
                    for h in range(4):
                        obk = 2 + h // 2
                        oc = (h % 2) * 256
                        P.op("dve", lambda e, h=h: e.tensor_tensor(out=coef[par][:, h:h + 1], in0=gsig[tt][:, 3 * h:3 * h + 1], in1=den[par][:, h:h + 1], op=ALU.mult),
                             reads=[bF, b_gs[tt]], writes=[bF])
                        P.op("dve", lambda e, obk=obk, oc=oc, h=h: e.tensor_scalar(out=acc[par][:, h * 64:(h + 1) * 64], in0=banks[obk][:, oc:oc + 64],
                                                                                  scalar1=coef[par][:, h:h + 1], scalar2=None, op0=ALU.mult),
                             reads=[bb[obk], bF], writes=[bF])

                    for h in range(4):
                        obk = 2 + h // 2
                        oc = (h % 2) * 256
                        if h == 0:
                            P.op("dve", lambda e, obk=obk, oc=oc: e.tensor_scalar(out=score[:], in0=banks[obk][:, oc + 65:oc + 193], scalar1=den[par][:, 0:1],
                                                                                 scalar2=None, op0=ALU.mult), reads=[bb[obk], bF], writes=[b_sel])
                        else:
                            P.op("dve", lambda e, obk=obk, oc=oc, h=h: e.scalar_tensor_tensor(out=score[:], in0=banks[obk][:, oc + 65:oc + 193],
                                                                                              scalar=den[par][:, h:h + 1], in1=score[:], op0=ALU.mult, op1=ALU.add),
                                 reads=[bb[obk], bF, b_sel], writes=[b_sel])
                    P.op("dve", lambda e: e.memset(score[:, 0:1], BIG), writes=[b_sel])
                    P.op("dve", lambda e: e.memset(score[:, 2 * i:2 * i + 1], BIG), writes=[b_sel])
                    if i > 0:
                        P.op("dve", lambda e: e.memset(score[0:64, 2 * i - 1:2 * i], BIG), writes=[b_sel])
                    P.op("dve", lambda e: e.memset(score[64:128, 2 * i + 1:2 * i + 2], BIG), writes=[b_sel])
                    P.op("dve", lambda e: e.max(out=m8[:, 0:8], in_=score[:]), reads=[b_sel], writes=[b_sel])
                    P.op("dve", lambda e: e.match_replace(out=swork[:], in_to_replace=m8[:, 0:8], in_values=score[:], imm_value=-BIG),
                         reads=[b_sel], writes=[b_sel])
                    P.op("dve", lambda e: e.max(out=m8[:, 8:16], in_=swork[:]), reads=[b_sel], writes=[b_sel])
                    P.op("dve", lambda e: e.tensor_scalar(out=selb[par][:], in0=score[:], scalar1=m8[:, 15:16], scalar2=NEGM, op0=ALU.is_lt, op1=ALU.mult),
                         reads=[b_sel], writes=[b_selb[par]])

                def stage_T(tt):
                    i = ST * 4 + tt
                    par = i % 2
                    sbt = mmbank()
                    P.op("pe", lambda e: e.transpose(banks[sbt][:].bitcast(BF16)[:, 0:128], selb[par][:], identb[:]), reads=[b_selb[par], b_tab], writes=[bb[sbt]], mode="tr")
                    for h in range(4):
                        P.op("dve", lambda e, h=h: e.tensor_copy(out=selT4[par][0][0:64, h, :], in_=banks[sbt][:].bitcast(BF16)[0:64, 0:128]),
                             reads=[bb[sbt]], writes=[b_selT[par]])
                        P.op("dve", lambda e, h=h: e.tensor_copy(out=selT4[par][1][64:128, h, :], in_=banks[sbt][:].bitcast(BF16)[64:128, 0:128]),
                             reads=[bb[sbt]], writes=[b_selT[par]])

                def stage_Bm(tt):
                    i = ST * 4 + tt
                    par = i % 2
                    bF = b_finA[par]
                    qv = qT[:, tt, :, :].rearrange("p h q -> p (h q)")

                    def mkjob(br, kt, idx, nk):
                        obk = 6 + br
                        if br == 0:
                            kop = kslcT[:, kt * 128:(kt + 1) * 128]; kb = b_kslc[kt]
                            vop = vslc[:, kt, 0:65]; vb = b_vslc[kt]
                        else:
                            rsl = (kt % 8) * 128
                            kop = kwinT[:, rsl:rsl + 128]; kb = b_kwin[kt % 8]
                            vop = vwin[:, kt % 8, 0:65]; vb = b_vwin[kt % 8]
                        extra = []
                        if br == 0:
                            gq_ = (2 * kt) // 64
                            m_ = kt % 32
                            extra.append((E4_s[:, m_, :], selT4[par][gq_][:, :, :].rearrange("p h q -> p (h q)"), [b_tab, b_selT[par]]))
                        if kt == i:
                            extra.append((identb[:], bdiag_s[:], [b_tab]))
                        if br == 1 and kt == i - 4:
                            extra.append((identb[:], bfar_s[:], [b_tab]))
                        st_ = {}

                        def qk():
                            sbk = stbank()
                            st_["sbk"] = sbk
                            P.op("pe", lambda e: e.matmul(banks[sbk][:, :], lhsT=kop, rhs=qv, start=True, stop=(len(extra) == 0)),
                                 reads=[kb, b_qT], writes=[bb[sbk]])
                            for xi, (l_, r_, rb_) in enumerate(extra):
                                P.op("pe", lambda e, l_=l_, r_=r_, lastx=(xi == len(extra) - 1): e.matmul(
                                    banks[sbk][:, :], lhsT=l_, rhs=r_, start=False, stop=lastx), reads=rb_, writes=[bb[sbk]])

                        def rest():
                            sbk = st_["sbk"]
                            pb = rr["pt"] % 3
                            rr["pt"] += 1
                            P.op("act", lambda e: e.activation(out=pT[pb][:], in_=banks[sbk][:, :], func=AF.Exp, scale=1.0),
                                 reads=[bb[sbk]], writes=[b_pT[pb]])
                            P.op("pe", lambda e: e.matmul(banks[obk][0:65, :], lhsT=vop, rhs=pT[pb][:], start=(idx == 0), stop=(idx == nk - 1)),
                                 reads=[b_pT[pb], vb], writes=[bb[obk]])
                        return (qk, rest)
                    kw = list(range(max(0, i - 4), i + 1))
                    ks = list(range(0, i + 1))
                    jobs_w = [mkjob(1, kt, idx, len(kw)) for idx, kt in enumerate(kw)]
                    jobs_s = [mkjob(0, kt, idx, len(ks)) for idx, kt in enumerate(ks)]
                    run_pipeline(jobs_w + jobs_s)

                def stage_Bf(tt):
                    i = ST * 4 + tt
                    par = i % 2
                    bF = b_finA[par]
                    for br in range(2):
                        obk = 6 + br
                        P.op("dve", lambda e, obk=obk, br=br: e.tensor_copy(out=(e1 if br == 0 else cs)[0:65, :], in_=banks[obk][0:65, :]),
                             reads=[bb[obk]], writes=[b_gl])
                    for br in range(2):
                        tbk = mmbank()
                        for h in range(4):
                            P.op("pe", lambda e, tbk=tbk, br=br, h=h: e.transpose(
                                banks[tbk][:, h * 128:h * 128 + 65], (e1 if br == 0 else cs)[0:65, h * 128:(h + 1) * 128], identf[0:65, 0:65]),
                                reads=[b_gl, b_tab], writes=[bb[tbk]], mode="f32t")
                        for h in range(4):
                            dcol = 4 + br * 4 + h
                            P.op("dve", lambda e, tbk=tbk, h=h, dcol=dcol: e.tensor_scalar(out=den[par][:, dcol:dcol + 1], in0=banks[tbk][:, h * 128 + 64:h * 128 + 65],
                                                                                          scalar1=1e-30, scalar2=None, op0=ALU.max),
                                 reads=[bb[tbk]], writes=[bF])
                            P.op("dve", lambda e, dcol=dcol: e.reciprocal(out=den[par][:, dcol:dcol + 1], in_=den[par][:, dcol:dcol + 1]), reads=[bF], writes=[bF])
                            P.op("dve", lambda e, h=h, br=br, dcol=dcol: e.tensor_tensor(out=coef[par][:, dcol:dcol + 1], in0=gsig[tt][:, 3 * h + 1 + br:3 * h + 2 + br],
                                                                                        in1=den[par][:, dcol:dcol + 1], op=ALU.mult), reads=[bF, b_gs[tt]], writes=[bF])
                            P.op("dve", lambda e, tbk=tbk, h=h, dcol=dcol: e.scalar_tensor_tensor(
                                out=acc[par][:, h * 64:(h + 1) * 64], in0=banks[tbk][:, h * 128:h * 128 + 64], scalar=coef[par][:, dcol:dcol + 1],
                                in1=acc[par][:, h * 64:(h + 1) * 64], op0=ALU.mult, op1=ALU.add), reads=[bb[tbk], bF], writes=[bF])
                    if debug and i == NQT - 1:
                        P.dma("pool", lambda e: e.dma_start(out=dbg["oT0"], in_=e1[0:65, :]), reads=[b_gl])
                        P.dma("pool", lambda e: e.dma_start(out=dbg["oT1"], in_=cs[0:65, :]), reads=[b_gl])
                        P.dma("pool", lambda e: e.dma_start(out=dbg["acc"], in_=acc[par][:]), reads=[bF])
                        P.dma("pool", lambda e: e.dma_start(out=dbg["kcT"], in_=kcT[0:72]), reads=[b_kcT])
                        P.dma("pool", lambda e: e.dma_start(out=dbg["ocmp"], in_=vcM[:, :, 0:193]), reads=[b_vcM])
                        P.dma("pool", lambda e: e.dma_start(out=dbg["den"], in_=den[par][:]), reads=[bF])
                        P.dma("pool", lambda e: e.dma_start(out=dbg["gsig"], in_=gsig[par][:]), reads=[bF])
                        P.dma("pool", lambda e: e.dma_start(out=dbg["coef"], in_=coef[par][:]), reads=[bF])
                        P.dma("pool", lambda e: e.dma_start(out=dbg["pT"], in_=pT[(rr["pt"] - 1) % 3][:]), reads=[b_pT[(rr["pt"] - 1) % 3]])
                    yb_ = tt
                    P.op("act", lambda e: e.activation(out=szn[:], in_=tokA[tt][:, 140:396], func=AF.Exp, scale=-1.0), reads=[b_tokA[tt]], writes=[b_fin])
                    P.op("dve", lambda e: e.tensor_scalar(out=szn[:], in0=szn[:], scalar1=1.0, scalar2=None, op0=ALU.add), reads=[b_fin], writes=[b_fin])
                    P.op("dve", lambda e: e.reciprocal(out=szn[:], in_=szn[:]), reads=[b_fin], writes=[b_fin])
                    P.op("dve", lambda e: e.tensor_tensor(out=szn[:], in0=szn[:], in1=tokA[tt][:, 140:396], op=ALU.mult), reads=[b_fin, b_tokA[tt]], writes=[b_fin])
                    P.op("dve", lambda e: e.tensor_tensor(out=yt[yb_][:, 0:256], in0=szn[:], in1=acc[par][:], op=ALU.mult), reads=[b_fin, bF], writes=[b_yt[yb_]])
                    P.dma("pool", lambda e: e.dma_start(out=y_v[i], in_=yt[yb_][:]), reads=[b_yt[yb_]])

                stage_A(0)
                stage_Adve(0)
                stage_A(1)
                stage_T(0)
                stage_Adve(1)
                stage_Bm(0)
                stage_A(2)
                stage_T(1)
                stage_Bf(0)
                stage_Adve(2)
                stage_Bm(1)
                stage_A(3)
                stage_T(2)
                stage_Bf(1)
                stage_Adve(3)
                stage_Bm(2)
                stage_T(3)
                stage_Bf(2)
                stage_Bm(3)
                stage_Bf(3)
        except _Stop:
            pass
        toks = [t_ for t_ in P.dma_last if t_ is not None]
        toks += [(e_, P.cnt[e_]) for e_ in ("pe", "act", "dve", "pool") if P.cnt[e_] > 0]
        for e_ in ("pool", "sp", "act", "dve", "pe"):
            P.wait_all(e_, toks)
        P.emit()
    return nc


import contextlib
import numpy as np
import ml_dtypes
import concourse.bass as bass
import concourse.mybir as mybir

F32 = mybir.dt.float32
BF16 = mybir.dt.bfloat16
AF = mybir.ActivationFunctionType
ALU = mybir.AluOpType
BF = ml_dtypes.bfloat16
D = 2048
KC = 16
EPS = 1e-6
CB = 512
NCB = D // CB


def host_inputs_l2(inp, ys, adas, b, qtr, TQ):
    t0 = qtr * TQ
    ycat = np.concatenate([ys[b * 4 + j][t0:t0 + TQ, 0:256] for j in range(4)] +
                          [ys[b * 4 + j][t0:t0 + TQ, 256:512] for j in range(4)], axis=1)
    w_in = inp["w_in"][0]
    b_in = inp["b_in"][0]
    return dict(
        x2=np.ascontiguousarray(inp["x"][b, t0:t0 + TQ]),
        yin=np.ascontiguousarray(ycat),
        ada_in=np.ascontiguousarray(adas[b * 4]),
        gainT2=np.ascontiguousarray(inp["norm_gain"][0].reshape(16, 128).T),
        wmg=np.ascontiguousarray(w_in[:, 6720:10816]),
        bmg=np.ascontiguousarray(b_in[6720:10816][None, :]),
        wbrn=np.ascontiguousarray(inp["w_br_nsa"][0]),
        wbrg=np.ascontiguousarray(inp["w_br_gla"][0]),
        wout=np.ascontiguousarray(inp["w_out"][0]),
        fgain=np.ascontiguousarray(np.broadcast_to(inp["final_norm_gain"][None, :], (128, D))),
        identb2=np.eye(128).astype(BF), identf2=np.eye(128, dtype=np.float32),
    )


def build_l2(TQ):
    NTT = 4
    NST = TQ // (128 * NTT)
    nc = bass.Bass("TRN2", target_bir_lowering=False)

    def din(name, shape, dt=F32):
        return nc.dram_tensor(name, list(shape), dt, kind="ExternalInput").ap()
    x = din("x2", [TQ, D]); yin = din("yin", [TQ, D], BF16)
    adain_d = din("ada_in", [128, 48])
    gainT_d = din("gainT2", [128, 16])
    wmg_d = din("wmg", [D, 4096]); bmg_d = din("bmg", [1, 4096])
    wbrn_d = din("wbrn", [1024, D]); wbrg_d = din("wbrg", [1024, D]); wout_d = din("wout", [D, D])
    fgain_d = din("fgain", [128, D]); identb_d = din("identb2", [128, 128], BF16); identf_d = din("identf2", [128, 128])
    out_d = nc.dram_tensor("out", [TQ, D], F32, kind="ExternalOutput").ap()
    wmg_b = nc.dram_tensor("wmg_b16", [D, 4096], BF16).ap()
    wbrn_b = nc.dram_tensor("wbrn_b16", [1024, D], BF16).ap()
    wbrg_b = nc.dram_tensor("wbrg_b16", [1024, D], BF16).ap()
    wout_b = nc.dram_tensor("wout_b16", [D, D], BF16).ap()

    P = Prog(nc)
    with contextlib.ExitStack() as st:
        def sb(name, shape, dt=F32):
            return st.enter_context(nc.sbuf_tensor("s2_" + name, list(shape), dt))
        xs = [sb("xs%d" % i, [128, D]) for i in range(2)]
        hT = sb("hT", [128, KC, 128 * NTT], BF16)
        yT = sb("yT", [128, KC, 128 * NTT], BF16)
        mtok = [sb("mtok%d" % i, [128, D], BF16) for i in range(NTT)]
        xnb = mtok[NTT - 1]; ytile = [mtok[NTT - 2], mtok[NTT - 3]]
        mT = sb("mT", [128, KC, 128], BF16)
        wmgb = [sb("wmgb%d" % i, [128, 2, KC, CB], BF16) for i in range(2)]
        wbrb = [sb("wbrb%d" % i, [128, 2, 8, CB], BF16) for i in range(2)]
        woutb = [sb("woutb%d" % i, [128, KC, 128], BF16) for i in range(2)]
        bmg_s = [sb("bmg_s%d" % i, [1, 2 * CB], BF16) for i in range(2)]; ones_s = sb("ones_s", [1, 128], BF16)
        onesf = sb("onesf", [128, 128])
        gate_bc = sb("gate_bc", [128, D]); fgain_s = sb("fgain_s", [128, D])
        xo = sb("xo", [128, 512])
        A_s = sb("A_s", [128, 16]); B_s = sb("B_s", [128, 16]); ada_s = sb("ada_s", [128, 48])
        cT_s = sb("cT_s", [128, 16]); gainT_s = sb("gainT_s", [128, 16]); bada_s = sb("bada_s", [128, 48])
        small = sb("small", [128, 32]); diag = [sb("diag%d" % i, [128, 128]) for i in range(2)]
        identb = sb("identb", [128, 128], BF16); identf = sb("identf", [128, 128])
        g1 = sb("g1", [128, CB]); g2 = sb("g2", [128, CB]); t1 = sb("t1", [128, CB])

        banks = [st.enter_context(nc.psum_tensor("b2ank%d" % i, [128, 512], F32)) for i in range(8)]
        bb = [Buf("bank%d" % i) for i in range(8)]
        B = Buf
        b_w = B("w"); b_AB = B("AB"); b_xs = [B("xs0"), B("xs1")]; b_small = B("small")
        b_hT = B("hT"); b_yT = B("yT"); b_mtok = [B("mtok%d" % i) for i in range(NTT)]
        b_xnb = b_mtok[NTT - 1]; b_yt = [b_mtok[NTT - 2], b_mtok[NTT - 3]]
        b_mT = B("mT"); b_wmg = [B("wmg0"), B("wmg1")]; b_wbr = [B("wbr0"), B("wbr1")]; b_wout = [B("wout0"), B("wout1")]
        b_gate = B("gate"); b_diag = [B("d0"), B("d1")]; b_g = B("g"); b_xo = B("xo"); b_tab = B("tab")

        def ld(q, out, in_, writes):
            return P.dma(q, lambda e: e.dma_start(out=out, in_=in_), writes=writes)
        ld("sp", gainT_s[:], gainT_d, [b_AB])
        ld("sp", fgain_s[:], fgain_d, [b_w]); ld("sp", identb[:], identb_d, [b_tab]); ld("sp", identf[:], identf_d, [b_tab])
        P.op("pool", lambda e: e.memset(ones_s[:], 1.0), writes=[b_w])
        b_wc = B("wcast")
        for h2 in range(2):
            for r0 in range(0, D, 1024):
                P.dma("pool", lambda e, h2=h2, r0=r0: e.dma_start(out=wmg_b[r0:r0 + 1024, h2 * 2048:(h2 + 1) * 2048], in_=wmg_d[r0:r0 + 1024, h2 * 2048:(h2 + 1) * 2048]), writes=[b_wc])
        P.dma("pool", lambda e: e.dma_start(out=wbrn_b, in_=wbrn_d), writes=[b_wc])
        P.dma("pool", lambda e: e.dma_start(out=wbrg_b, in_=wbrg_d), writes=[b_wc])
        for r0 in range(0, D, 1024):
            P.dma("pool", lambda e, r0=r0: e.dma_start(out=wout_b[r0:r0 + 1024, :], in_=wout_d[r0:r0 + 1024, :]), writes=[b_wc])
        P.op("pool", lambda e: e.memset(onesf[:], 1.0), writes=[b_w])
        ld("sp", ada_s[:], adain_d, [b_AB])
        P.op("dve", lambda e: e.tensor_scalar(out=A_s[:], in0=ada_s[:, 16:32], scalar1=1.0, scalar2=None, op0=ALU.add), reads=[b_AB], writes=[b_AB])
        P.op("dve", lambda e: e.tensor_tensor(out=A_s[:], in0=A_s[:], in1=gainT_s[:], op=ALU.mult), reads=[b_AB], writes=[b_AB])
        P.op("dve", lambda e: e.tensor_copy(out=B_s[:], in_=ada_s[:, 0:16]), reads=[b_AB], writes=[b_AB])
        for kc in range(KC):
            d_ = kc % 2
            P.op("dve", lambda e, kc=kc, d_=d_: e.tensor_scalar(out=diag[d_][:], in0=identf[:], scalar1=ada_s[:, 32 + kc:33 + kc], scalar2=None, op0=ALU.mult),
                 reads=[b_AB, b_tab], writes=[b_diag[d_]])
            bk = 1 + kc // 4
            P.op("pe", lambda e, kc=kc, d_=d_, bk=bk: e.matmul(banks[bk][:, (kc % 4) * 128:(kc % 4) * 128 + 128], lhsT=onesf[:], rhs=diag[d_][:],
                                                               start=(kc % 4 == 0), stop=(kc % 4 == 3), skip_group_check=True),
                 reads=[b_w, b_diag[d_]], writes=[bb[bk]])
        for q4 in range(4):
            P.op("dve", lambda e, q4=q4: e.tensor_copy(out=gate_bc[:, q4 * 512:(q4 + 1) * 512], in_=banks[1 + q4][:, :]), reads=[bb[1 + q4]], writes=[b_gate])

        x_v = x.rearrange("(t p) d -> t p d", p=128)
        y_v = yin.rearrange("(t p) d -> t p d", p=128)
        o_v = out_d.rearrange("(t p) d -> t p d", p=128)
        wmg_v = wmg_b.rearrange("(kc p) n -> p kc n", p=128)
        wbrn_v = wbrn_b.rearrange("(kc p) n -> p kc n", p=128)
        wbrg_v = wbrg_b.rearrange("(kc p) n -> p kc n", p=128)
        wout_v = wout_b.rearrange("(kc p) n -> p kc n", p=128)
        rr = {"mm": 0, "w": 0, "wo": 0, "x": 0, "ob": 0}

        def mmbank():
            i = rr["ob"] % 4
            rr["ob"] += 1
            return i

        for ST in range(NST):
            for tt in range(NTT):
                t = ST * NTT + tt
                xb_ = rr["x"] % 2
                rr["x"] += 1
                ld("sp", xs[xb_][:], x_v[t], [b_xs[xb_]])
                ld("sp", ytile[tt % 2][:], y_v[t], [b_yt[tt % 2]])
                P.op("act", lambda e, xb_=xb_, tt=tt: e.activation(out=xnb[:], in_=xs[xb_][:], func=AF.Square, accum_out=small[:, tt % 4:tt % 4 + 1]),
                     reads=[b_xs[xb_]], writes=[b_xnb, b_small])
                P.op("act", lambda e, tt=tt: e.activation(out=small[:, 8 + tt % 4:9 + tt % 4], in_=small[:, tt % 4:tt % 4 + 1], func=AF.Ln, scale=1.0 / D, bias=EPS),
                     reads=[b_small], writes=[b_small])
                P.op("act", lambda e, tt=tt: e.activation(out=small[:, 16 + tt % 4:17 + tt % 4], in_=small[:, 8 + tt % 4:9 + tt % 4], func=AF.Exp, scale=-0.5),
                     reads=[b_small], writes=[b_small])
                P.op("dve", lambda e, xb_=xb_, tt=tt: e.tensor_scalar(out=xnb[:], in0=xs[xb_][:], scalar1=small[:, 16 + tt % 4:17 + tt % 4], scalar2=None, op0=ALU.mult),
                     reads=[b_xs[xb_], b_small], writes=[b_xnb])
                for src_i, (src, sbuf_, dst, dbuf) in enumerate(((xnb, b_xnb, hT, b_hT), (ytile[tt % 2], b_yt[tt % 2], yT, b_yT))):
                    for half in range(2):
                        bk = 4 + 2 * src_i + half
                        pv = banks[bk][:].bitcast(BF16)
                        for c8 in range(8):
                            kc = half * 8 + c8
                            P.op("pe", lambda e, pv=pv, c8=c8, kc=kc, src=src: e.transpose(pv[:, c8 * 128:(c8 + 1) * 128], src[:, kc * 128:(kc + 1) * 128], identb[:]),
                                 reads=[sbuf_, b_tab], writes=[bb[bk]])
                        if src_i == 0:
                            for c8 in range(8):
                                kc = half * 8 + c8
                                P.op("dve", lambda e, pv=pv, c8=c8, kc=kc, tt=tt: e.tensor_scalar(
                                    out=hT[:, kc, tt * 128:(tt + 1) * 128], in0=pv[:, c8 * 128:(c8 + 1) * 128],
                                    scalar1=A_s[:, kc:kc + 1], scalar2=B_s[:, kc:kc + 1], op0=ALU.mult, op1=ALU.add),
                                    reads=[bb[bk], b_AB], writes=[b_hT])
                        else:
                            P.op("dve", lambda e, pv=pv, half=half, tt=tt: e.tensor_copy(
                                out=yT[:, half * 8:half * 8 + 8, tt * 128:(tt + 1) * 128], in_=pv.rearrange("p (c q) -> p c q", q=128)),
                                reads=[bb[bk]], writes=[b_yT])
            for cb in range(NCB):
                wb = rr["w"] % 2
                rr["w"] += 1
                c0 = cb * CB
                P.dma("sp", lambda e, wb=wb, c0=c0: e.dma_start(out=wmgb[wb][:, 0, :, :], in_=wmg_v[:, :, c0:c0 + CB]), reads=[b_wc], writes=[b_wmg[wb]])
                P.dma("sp", lambda e, wb=wb, c0=c0: e.dma_start(out=wmgb[wb][:, 1, :, :], in_=wmg_v[:, :, 2048 + c0:2048 + c0 + CB]), reads=[b_wc], writes=[b_wmg[wb]])
                P.dma("sp", lambda e, wb=wb, c0=c0: e.dma_start(out=wbrb[wb][:, 0, :, :], in_=wbrn_v[:, :, c0:c0 + CB]), reads=[b_wc], writes=[b_wbr[wb]])
                P.dma("sp", lambda e, wb=wb, c0=c0: e.dma_start(out=wbrb[wb][:, 1, :, :], in_=wbrg_v[:, :, c0:c0 + CB]), reads=[b_wc], writes=[b_wbr[wb]])
                ld("pool", bmg_s[wb][:, 0:CB], bmg_d[:, c0:c0 + CB], [b_wmg[wb]])
                ld("pool", bmg_s[wb][:, CB:2 * CB], bmg_d[:, 2048 + c0:2048 + c0 + CB], [b_wmg[wb]])
                for tt in range(NTT):
                    tsl = slice(tt * 128, tt * 128 + 128)
                    base = 4 * (rr["mm"] % 2)
                    rr["mm"] += 1
                    for gi in range(2):
                        bk = base + gi
                        for kc in range(KC):
                            P.op("pe", lambda e, bk=bk, gi=gi, kc=kc, tsl=tsl, wb=wb: e.matmul(
                                banks[bk][:, :], lhsT=hT[:, kc, tsl], rhs=wmgb[wb][:, gi, kc, :], start=(kc == 0), stop=False),
                                reads=[b_hT, b_wmg[wb]], writes=[bb[bk]])
                        P.op("pe", lambda e, bk=bk, gi=gi, wb=wb: e.matmul(
                            banks[bk][:, :], lhsT=ones_s[0:1, :], rhs=bmg_s[wb][0:1, gi * CB:(gi + 1) * CB], start=False, stop=True),
                            reads=[b_w, b_wmg[wb]], writes=[bb[bk]], mode="r32")
                    for gi in range(2):
                        bk = base + 2 + gi
                        for kc in range(8):
                            P.op("pe", lambda e, bk=bk, gi=gi, kc=kc, tsl=tsl, wb=wb: e.matmul(
                                banks[bk][:, :], lhsT=yT[:, gi * 8 + kc, tsl], rhs=wbrb[wb][:, gi, kc, :], start=(kc == 0), stop=(kc == 7)),
                                reads=[b_yT, b_wbr[wb]], writes=[bb[bk]])
                    P.op("act", lambda e, base=base: e.activation(out=g1[:], in_=banks[base][:, :], func=AF.Exp, scale=-1.0), reads=[bb[base]], writes=[b_g])
                    P.op("act", lambda e, base=base: e.activation(out=g2[:], in_=banks[base + 1][:, :], func=AF.Exp, scale=-1.0), reads=[bb[base + 1]], writes=[b_g])
                    P.op("dve", lambda e: e.tensor_scalar(out=g1[:], in0=g1[:], scalar1=1.0, scalar2=None, op0=ALU.add), reads=[b_g], writes=[b_g])
                    P.op("dve", lambda e: e.tensor_scalar(out=g2[:], in0=g2[:], scalar1=1.0, scalar2=None, op0=ALU.add), reads=[b_g], writes=[b_g])
                    P.op("dve", lambda e: e.reciprocal(out=g1[:], in_=g1[:]), reads=[b_g], writes=[b_g])
                    P.op("dve", lambda e: e.reciprocal(out=g2[:], in_=g2[:]), reads=[b_g], writes=[b_g])
                    P.op("dve", lambda e, base=base: e.tensor_tensor(out=t1[:], in0=banks[base + 2][:, :], in1=g1[:], op=ALU.mult), reads=[bb[base + 2], b_g], writes=[b_g])
                    P.op("dve", lambda e, base=base: e.tensor_tensor(out=g2[:], in0=banks[base + 3][:, :], in1=g2[:], op=ALU.mult), reads=[bb[base + 3], b_g], writes=[b_g])
                    P.op("dve", lambda e, tt=tt, c0=c0: e.tensor_tensor(out=mtok[tt][:, c0:c0 + CB], in0=t1[:], in1=g2[:], op=ALU.add), reads=[b_g], writes=[b_mtok[tt]])
            for tt in range(NTT):
                t = ST * NTT + tt
                for half in range(2):
                    bk = 4 + half
                    pv = banks[bk][:].bitcast(BF16)
                    for c8 in range(8):
                        kc = half * 8 + c8
                        P.op("pe", lambda e, pv=pv, c8=c8, kc=kc, tt=tt: e.transpose(pv[:, c8 * 128:(c8 + 1) * 128], mtok[tt][:, kc * 128:(kc + 1) * 128], identb[:]),
                             reads=[b_mtok[tt], b_tab], writes=[bb[bk]])
                    P.op("dve", lambda e, pv=pv, half=half: e.tensor_copy(out=mT[:, half * 8:half * 8 + 8, :], in_=pv.rearrange("p (c q) -> p c q", q=128)),
                         reads=[bb[bk]], writes=[b_mT])
                xb_ = rr["x"] % 2
                rr["x"] += 1
                ld("sp", xs[xb_][:], x_v[t], [b_xs[xb_]])
                for ob in range(D // 128):
                    wo = rr["wo"] % 2
                    rr["wo"] += 1
                    P.dma("sp", lambda e, wo=wo, ob=ob: e.dma_start(out=woutb[wo][:], in_=wout_v[:, :, ob * 128:(ob + 1) * 128]), reads=[b_wc], writes=[b_wout[wo]])
                    if ob % 4 == 0:
                        bk = mmbank()
                    oc = (ob % 4) * 128
                    for kc in range(KC):
                        P.op("pe", lambda e, bk=bk, kc=kc, wo=wo, oc=oc, ob=ob: e.matmul(banks[bk][:, oc:oc + 128], lhsT=mT[:, kc, :], rhs=woutb[wo][:, kc, :],
                                                                                 start=(kc == 0 and ob % 4 == 0), stop=(kc == KC - 1), skip_group_check=True),
                             reads=[b_mT, b_wout[wo]], writes=[bb[bk]])
                    if ob % 4 == 3:
                        osl = slice((ob // 4) * 512, (ob // 4) * 512 + 512)
                        P.op("dve", lambda e, bk=bk, osl=osl: e.tensor_tensor(out=xo[:, :], in0=banks[bk][:, :], in1=gate_bc[:, osl], op=ALU.mult),
                             reads=[bb[bk], b_gate], writes=[b_xo])
                        P.op("dve", lambda e, osl=osl, xb_=xb_: e.tensor_tensor(out=xs[xb_][:, osl], in0=xo[:, :], in1=xs[xb_][:, osl], op=ALU.add),
                             reads=[b_xo, b_xs[xb_]], writes=[b_xs[xb_]])
                P.op("act", lambda e, xb_=xb_, tt=tt: e.activation(out=hT[:, 0:4, :], in_=xs[xb_][:].rearrange("p (a b) -> p a b", a=4), func=AF.Square, accum_out=small[:, 24:25]),
                     reads=[b_xs[xb_]], writes=[b_hT, b_small])
                P.op("act", lambda e: e.activation(out=small[:, 25:26], in_=small[:, 24:25], func=AF.Ln, scale=1.0 / D, bias=EPS), reads=[b_small], writes=[b_small])
                P.op("act", lambda e: e.activation(out=small[:, 26:27], in_=small[:, 25:26], func=AF.Exp, scale=-0.5), reads=[b_small], writes=[b_small])
                P.op("dve", lambda e, xb_=xb_: e.scalar_tensor_tensor(out=xs[xb_][:], in0=xs[xb_][:], scalar=small[:, 26:27], in1=fgain_s[:], op0=ALU.mult, op1=ALU.mult),
                     reads=[b_small, b_w], writes=[b_xs[xb_]])
                P.dma("act", lambda e, xb_=xb_, t=t: e.dma_start(out=o_v[t], in_=xs[xb_][:]), reads=[b_xs[xb_]])
        toks = [t_ for t_ in P.dma_last if t_ is not None]
        toks += [(e_, P.cnt[e_]) for e_ in ("pe", "act", "dve", "pool") if P.cnt[e_] > 0]
        for e_ in ("pool", "sp", "act", "dve", "pe"):
            P.wait_all(e_, toks)
        P.emit()
    return nc


def kernel(**inputs):
    from concourse.bass_utils import run_bass_kernel_spmd
    inp = {k_: np.asarray(v) for k_, v in inputs.items()}
    Bn, T, Dm = inp["x"].shape
    nc1 = build_l1(T)
    in_maps = [host_inputs_l1(inp, c // 4, c % 4, T) for c in range(8)]
    res = run_bass_kernel_spmd(nc1, in_maps, core_ids=list(range(8)))
    ys = [np.asarray(r["y"]) for r in res.results]
    adas = [np.asarray(r["ada_out"]) for r in res.results]
    del in_maps
    TQ = T // 4
    nc2 = build_l2(TQ)
    in_maps2 = [host_inputs_l2(inp, ys, adas, c // 4, c % 4, TQ) for c in range(8)]
    res2 = run_bass_kernel_spmd(nc2, in_maps2, core_ids=list(range(8)))
    out = np.empty((Bn, T, Dm), np.float32)
    for c in range(8):
        out[c // 4, (c % 4) * TQ:(c % 4 + 1) * TQ] = np.asarray(res2.results[c]["out"])
    return out
```

```python
import contextlib

ENGS = ("pe", "act", "dve", "pool", "sp")
USE_MODE_DRAINS = False
N_DMA_SEMS = 28


class Buf:
    __slots__ = ("name", "w", "r")

    def __init__(self, name=""):
        self.name = name
        self.w = {}
        self.r = {}


class Prog:
    def __init__(self, nc):
        self.nc = nc
        self.streams = {e: [] for e in ENGS}
        self.cnt = {e: 0 for e in ENGS}
        self.known = {e: {} for e in ENGS}
        self.dma_i = 0
        self.dma_ip = 0
        self.dma_cnt = [0] * N_DMA_SEMS
        self.dma_last = [None] * N_DMA_SEMS
        self.n_ops = 0

    def _deps(self, reads, writes):
        deps = {}

        def add(d):
            for k, v in d.items():
                if deps.get(k, -1) < v:
                    deps[k] = v
        for b in reads:
            add(b.w)
        for b in writes:
            add(b.w)
            add(b.r)
        return deps

    def _commit(self, tok, reads, writes):
        k, v = tok
        for b in reads:
            b.r[k] = v
        for b in writes:
            b.w = {k: v}
            b.r = {}

    def _waits(self, eng, deps):
        out = []
        kn = self.known[eng]
        for k, v in deps.items():
            if eng == "pe" and k == "pe":
                continue
            if kn.get(k, -1) >= v:
                continue
            kn[k] = v
            out.append((k, v))
        return out

    def op(self, eng, fn, reads=(), writes=(), mode="full"):
        deps = self._deps(reads, writes)
        waits = self._waits(eng, deps)
        if eng == "pe":
            if USE_MODE_DRAINS and mode != getattr(self, "pe_mode", mode) and self.cnt["pe"] > 0:
                v = self.cnt["pe"]
                if self.known["pe"].get("pe", -1) < v:
                    self.known["pe"]["pe"] = v
                    waits.append(("pe", v))
            self.pe_mode = mode
        self.cnt[eng] += 1
        tok = (eng, self.cnt[eng])
        self.streams[eng].append((waits, fn, tok))
        self._commit(tok, reads, writes)
        self.n_ops += 1
        return tok

    def dma(self, q, fn, reads=(), writes=()):
        deps = self._deps(reads, writes)
        if q == "pool":
            s = self.dma_ip % 10
            self.dma_ip += 1
        else:
            s = 10 + self.dma_i % (N_DMA_SEMS - 10)
            self.dma_i += 1
        if self.dma_last[s] is not None:
            k, v = self.dma_last[s]
            if deps.get(k, -1) < v:
                deps[k] = v
        waits = self._waits(q, deps)
        self.dma_cnt[s] += 16
        tok = (("dma", s), self.dma_cnt[s])
        self.dma_last[s] = tok
        if q == "pool":
            self.last_pool = tok
        self.streams[q].append((waits, fn, tok))
        self._commit(tok, reads, writes)
        self.n_ops += 1
        return tok

    def wait_all(self, eng, toks):
        deps = {}
        for k, v in toks:
            if deps.get(k, -1) < v:
                deps[k] = v
        waits = self._waits(eng, deps)
        self.streams[eng].append((waits, None, None))

    def emit(self):
        nc = self.nc
        with contextlib.ExitStack() as st:
            sems = {}
            for e in ENGS:
                sems[e] = st.enter_context(nc.semaphore("s_" + e))
            for i in range(N_DMA_SEMS):
                sems[("dma", i)] = st.enter_context(nc.semaphore("s_dma%d" % i))
            block = st.enter_context(nc.Block())
            objs = {"pe": block.tensor, "act": block.scalar, "dve": block.vector,
                    "pool": block.gpsimd, "sp": block.sync}

            needed = {e: set() for e in ENGS}
            for e in ENGS:
                for (waits, fn, tok) in self.streams[e]:
                    for (k, v) in waits:
                        if not isinstance(k, tuple):
                            needed[k].add(v)
            rank = {e: {v: i + 1 for i, v in enumerate(sorted(needed[e]))} for e in ENGS}
            self.n_incs = {e: len(needed[e]) for e in ENGS}

            def mk(e):
                def body(eng):
                    for (waits, fn, tok) in self.streams[e]:
                        for (k, v) in waits:
                            if isinstance(k, tuple):
                                eng.wait_ge(sems[k], v)
                            else:
                                eng.wait_ge(sems[k], rank[k][v])
                        if fn is None:
                            continue
                        ins = fn(eng)
                        k, v = tok
                        if isinstance(k, tuple):
                            ins.then_inc(sems[k], 16)
                        elif v in needed[k]:
                            ins.then_inc(sems[k], 1)
                return body
            for e in ENGS:
                if self.streams[e]:
                    objs[e](mk(e))


import contextlib
import numpy as np
import ml_dtypes
import concourse.bass as bass
import concourse.mybir as mybir

F32 = mybir.dt.float32
BF16 = mybir.dt.bfloat16
AF = mybir.ActivationFunctionType
ALU = mybir.AluOpType
BF = ml_dtypes.bfloat16

D = 2048
KC = 16
NEGM = -30000.0
BIG = 1.0e30
EPS = 1e-6
NFC = 10
FCH_M = [64, 64, 64, 64, 128, 64, 64, 128, 128, 16]
FCH_OFF = [0, 64, 128, 192, 256, 384, 448, 512, 640, 768]
NF = 784
NTA = 396
NTB = 512
NT_ = NTA + NTB


def split3(a):
    a = np.asarray(a, np.float64)
    hi = a.astype(BF).astype(np.float64)
    mid = (a - hi).astype(BF).astype(np.float64)
    lo = (a - hi - mid).astype(BF).astype(np.float64)
    return hi.astype(BF), mid.astype(BF), lo.astype(BF)


def host_tables(T, g):
    slopes = 2.0 ** (-8.0 * np.arange(1, 17, dtype=np.float64) / 16)[4 * g:4 * g + 4]
    t = np.arange(T, dtype=np.float64)
    qaug = np.zeros((8, 4, T), BF)
    for h in range(4):
        a, b_, c_ = split3(-slopes[h] * t)
        qaug[0, h], qaug[1, h], qaug[2, h] = a, b_, c_
        a, b_, c_ = split3(np.full(T, 128.0 * slopes[h]))
        qaug[3, h], qaug[4, h], qaug[5, h] = a, b_, c_
        a, b_, c_ = split3(np.full(T, slopes[h]))
        qaug[6, h], qaug[7, h] = a, b_

    def kside(pos):
        k = np.zeros((8, len(pos)), BF)
        k[0:3] = 1.0
        k[3:6] = (pos // 128).astype(BF)
        k[6:8] = (pos % 128).astype(BF)
        return k
    kaug = kside(np.arange(T))
    ncp = np.arange(512)
    cend = np.maximum(16 * (ncp - 1) + 31, 0)
    kcaug = kside(cend)
    M = np.zeros((512, 128), np.float32)
    w = {-1: 1.0, 0: 2.0, 1: 2.0, 2: 2.0, 3: 1.0}
    for npr in range(1, 512):
        n = npr - 1
        for blk in range(128):
            dd = n - 4 * blk
            if dd in w:
                M[npr, blk] = w[dd]
    Mtab = M.reshape(4, 128, 128).transpose(1, 0, 2).astype(BF)
    E4 = np.zeros((128, 32, 128), BF)
    for m in range(32):
        for half in range(2):
            r = 2 * m + half
            for gq in range(2):
                E4[64 * gq + r, m, 64 * half:64 * half + 64] = 1.0
    p = np.arange(128)[:, None]
    q = np.arange(128)[None, :]
    bias_diag = np.where(q - p >= 0, 0.0, NEGM).astype(BF)
    bias_far = np.where(p - q - 1 >= 0, 0.0, NEGM).astype(BF)
    bias_diag = np.tile(bias_diag, (1, 4))
    bias_far = np.tile(bias_far, (1, 4))
    gmask = np.where(q - p >= 0, 1.0, 0.0).astype(BF)
    reset = np.ones((128, 512), BF)
    reset[:, ::128] = 0.0
    qaug = np.ascontiguousarray(qaug.reshape(8, 4, T // 128, 128).transpose(0, 2, 1, 3))
    return dict(qaug=qaug, kaug=kaug, kcaug=kcaug, Mtab=Mtab, E4=E4,
                bias_diag=bias_diag, bias_far=bias_far, gmask=gmask, reset=reset,
                identb=np.eye(128).astype(BF), identf=np.eye(128, dtype=np.float32))


def host_inputs_l1(inp, b, j, T):
    g = j
    x = np.ascontiguousarray(inp["x"][b, :T])
    w_in = inp["w_in"][0]
    b_in = inp["b_in"][0]
    offs = np.cumsum([0, 1024, 256, 256, 256, 256, 256, 256, 48, 1024, 512, 512, 1024, 16, 1024, 2048, 2048])
    o_q, o_ck, o_cv, o_sk, o_sv, o_wk, o_wv, o_g, o_z, o_gq, o_gk, o_gv, o_ga, o_gz = offs[:14]
    fcols = np.concatenate([
        np.arange(o_q + 256 * g, o_q + 256 * g + 256),
        np.arange(o_ck + 64 * g, o_ck + 64 * g + 64), np.arange(o_cv + 64 * g, o_cv + 64 * g + 64),
        np.arange(o_sk + 64 * g, o_sk + 64 * g + 64), np.arange(o_wk + 64 * g, o_wk + 64 * g + 64),
        np.arange(o_gq + 128 * j, o_gq + 128 * j + 128), np.arange(o_gk + 128 * j, o_gk + 128 * j + 128),
        np.arange(o_ga, o_ga + 16)])
    tcols = np.concatenate([
        np.arange(o_sv + 64 * g, o_sv + 64 * g + 64), np.arange(o_wv + 64 * g, o_wv + 64 * g + 64),
        np.arange(o_g + 12 * g, o_g + 12 * g + 12), np.arange(o_z + 256 * g, o_z + 256 * g + 256),
        np.arange(o_gv + 256 * j, o_gv + 256 * j + 256), np.arange(o_gz + 256 * j, o_gz + 256 * j + 256)])
    assert len(fcols) == NF and len(tcols) == NT_
    wf = np.ascontiguousarray(w_in[:, fcols])
    wt = np.ascontiguousarray(w_in[:, tcols])
    bf = np.zeros((128, NFC), np.float32)
    bfl = b_in[fcols]
    for ci in range(NFC):
        bf[:FCH_M[ci], ci] = bfl[FCH_OFF[ci]:FCH_OFF[ci] + FCH_M[ci]]
    bt = np.ascontiguousarray(b_in[tcols][None, :])
    w1kv = np.concatenate([inp["cmp_w1_k"][0].reshape(32, 64, 256).transpose(1, 0, 2),
                           inp["cmp_w1_v"][0].reshape(32, 64, 256).transpose(1, 0, 2)], axis=0)
    w2k = inp["cmp_w2_k"][0].reshape(2, 128, 64).transpose(1, 0, 2)
    w2v = inp["cmp_w2_v"][0].reshape(2, 128, 64).transpose(1, 0, 2)
    posT = np.concatenate([inp["cmp_pos_k"][0].T, inp["cmp_pos_v"][0].T], axis=0)
    d = dict(
        x=x,
        cT=np.ascontiguousarray(inp["c"][b].reshape(16, 128).T),
        w_ada=np.ascontiguousarray(inp["w_ada"][0]),
        b_ada=np.ascontiguousarray(inp["b_ada"][0].reshape(48, 128).T),
        gainT=np.ascontiguousarray(inp["norm_gain"][0].reshape(16, 128).T),
        wf=wf, wt=wt, bf=bf, bt=bt,
        w1kv=np.ascontiguousarray(w1kv), w2k=np.ascontiguousarray(w2k), w2v=np.ascontiguousarray(w2v),
        posT=np.ascontiguousarray(posT),
        walpha=np.ascontiguousarray(inp["gla_w_alpha"][0][:, 128 * j:128 * j + 128]),
        balpha=np.ascontiguousarray(inp["gla_b_alpha"][0][128 * j:128 * j + 128].reshape(128, 1)),
        ggain=np.ascontiguousarray(np.broadcast_to(inp["gla_norm_gain"][0][None, :], (128, 256))),
    )
    d.update(host_tables(T, g))
    return d


class _Stop(Exception):
    pass


def build_l1(T, debug=False, stage=99):
    NST = T // 512
    NQT = T // 128
    nc = bass.Bass("TRN2", target_bir_lowering=False)

    def din(name, shape, dt=F32):
        return nc.dram_tensor(name, list(shape), dt, kind="ExternalInput").ap()
    x = din("x", [T, D])
    cT_d = din("cT", [128, 16]); wada_d = din("w_ada", [D, 6144]); bada_d = din("b_ada", [128, 48])
    gainT_d = din("gainT", [128, 16])
    wf_d = din("wf", [D, NF]); wt_d = din("wt", [D, NT_]); bf_d = din("bf", [128, NFC]); bt_d = din("bt", [1, NT_])
    w1kv_d = din("w1kv", [128, 32, 256]); w2k_d = din("w2k", [128, 2, 64]); w2v_d = din("w2v", [128, 2, 64])
    posT_d = din("posT", [128, 32])
    walpha_d = din("walpha", [16, 128]); balpha_d = din("balpha", [128, 1]); ggain_d = din("ggain", [128, 256])
    qaug_d = din("qaug", [8, T // 128, 4, 128], BF16); kaug_d = din("kaug", [8, T], BF16); kcaug_d = din("kcaug", [8, 512], BF16)
    Mtab_d = din("Mtab", [128, 4, 128], BF16); E4_d = din("E4", [128, 32, 128], BF16)
    bdiag_d = din("bias_diag", [128, 512], BF16); bfar_d = din("bias_far", [128, 512], BF16)
    gmask_d = din("gmask", [128, 128], BF16); reset_d = din("reset", [128, 512], BF16)
    identb_d = din("identb", [128, 128], BF16); identf_d = din("identf", [128, 128])
    y_d = nc.dram_tensor("y", [T, 512], BF16, kind="ExternalOutput").ap()
    adao_d = nc.dram_tensor("ada_out", [128, 48], F32, kind="ExternalOutput").ap()
    dbg = {}
    if debug:
        dbg["qT"] = nc.dram_tensor("dbg_qT", [72, 4, 4, 128], BF16, kind="ExternalOutput").ap()
        dbg["tokA"] = nc.dram_tensor("dbg_tokA", [128, NTA], F32, kind="ExternalOutput").ap()
        dbg["ocmp"] = nc.dram_tensor("dbg_ocmp", [128, 4, 193], BF16, kind="ExternalOutput").ap()
        dbg["selb"] = nc.dram_tensor("dbg_selb", [128, 128], BF16, kind="ExternalOutput").ap()
        dbg["kcT"] = nc.dram_tensor("dbg_kcT", [72, 512], BF16, kind="ExternalOutput").ap()
        dbg["oT0"] = nc.dram_tensor("dbg_oT0", [65, 512], F32, kind="ExternalOutput").ap()
        dbg["oT1"] = nc.dram_tensor("dbg_oT1", [65, 512], F32, kind="ExternalOutput").ap()
        dbg["acc"] = nc.dram_tensor("dbg_acc", [128, 256], F32, kind="ExternalOutput").ap()
        dbg["pT"] = nc.dram_tensor("dbg_pT", [128, 512], BF16, kind="ExternalOutput").ap()
        dbg["den"] = nc.dram_tensor("dbg_den", [128, 12], F32, kind="ExternalOutput").ap()
        dbg["gsig"] = nc.dram_tensor("dbg_gsig", [128, 12], F32, kind="ExternalOutput").ap()
        dbg["coef"] = nc.dram_tensor("dbg_coef", [128, 12], F32, kind="ExternalOutput").ap()

    P = Prog(nc)
    with contextlib.ExitStack() as st:
        def sb(name, shape, dt=F32):
            return st.enter_context(nc.sbuf_tensor("sb_" + name, list(shape), dt))
        wf_s = sb("wf_s", [128, KC, NF], BF16); wt_s = sb("wt_s", [128, KC, NT_], BF16)
        bf_s = sb("bf_s", [128, NFC]); btb_s = sb("btb_s", [1, NT_], BF16)
        ones_s = sb("ones_s", [1, 128], BF16)
        xs = [sb("xs%d" % i, [128, D]) for i in range(2)]
        xnb = [sb("xnb%d" % i, [128, D], BF16) for i in range(2)]
        hT = sb("hT", [128, KC, 512], BF16)
        A_s = sb("A_s", [128, 16]); B_s = sb("B_s", [128, 16]); ada_s = sb("ada_s", [128, 48])
        cT_s = sb("cT_s", [128, 16]); gainT_s = sb("gainT_s", [128, 16]); bada_s = sb("bada_s", [128, 48])
        small = sb("small", [128, 64])
        qT = sb("qT", [128, 4, 4, 128], BF16)
        kslcT = sb("kslcT", [128, T], BF16)
        kwinT = sb("kwinT", [128, 1024], BF16)
        vslc = sb("vslc", [128, NQT, 66], BF16)
        vwin = sb("vwin", [128, 8, 66], BF16)
        cmpbuf = sb("cmpbuf", [128, 528], BF16)
        w1kv_s = sb("w1kv_s", [128, 32, 256], BF16)
        w2k_s = sb("w2k_s", [128, 2, 64], BF16); w2v_s = sb("w2v_s", [128, 2, 64], BF16)
        posT_s = sb("posT_s", [128, 32], BF16); posb_s = sb("posb_s", [128, 4])
        kcT = sb("kcT", [128, 512], BF16)
        vcM = sb("vcM", [128, 4, 194], BF16)
        E4_s = sb("E4_s", [128, 32, 128], BF16)
        bdiag_s = sb("bdiag_s", [128, 512], BF16); bfar_s = sb("bfar_s", [128, 512], BF16)
        gmask_s = sb("gmask_s", [128, 128], BF16); reset_s = sb("reset_s", [128, 512], BF16)
        identb = sb("identb", [128, 128], BF16); identf = sb("identf", [128, 128])
        walpha_s = sb("walpha_s", [16, 128], BF16); nbalpha_s = sb("nbalpha_s", [128, 1]); ggain_s = sb("ggain_s", [128, 256])
        gqT = sb("gqT", [128, 512], BF16); gkT = sb("gkT", [128, 512], BF16); gaT = sb("gaT", [16, 512], BF16)
        tokA = [sb("tokA%d" % i, [128, NTA]) for i in range(4)]
        tokB = [sb("tokB%d" % i, [128, 256]) for i in range(4)]
        gvb = [sb("gvb%d" % i, [128, 256], BF16) for i in range(4)]
        e1 = sb("e1", [128, 512]); spl = e1; cs = sb("cs", [128, 512])
        eb = sb("eb", [128, 512]); enb = e1
        qtl = sb("qtl", [128, 512], BF16); ktl = sb("ktl", [128, 512], BF16)
        ktok = sb("ktok", [128, 128], BF16); atm = sb("atm", [128, 128], BF16)
        S_s = sb("S_s", [128, 256]); Sb_s = sb("Sb_s", [128, 256], BF16); stmp = sb("stmp", [128, 256])
        gl1 = sb("gl1", [128, 256]); junk = gl1
        yt = [sb("yt%d" % i, [128, 512], BF16) for i in range(4)]
        hsb = sb("hsb", [128, 4, 32]); hex_ = sb("hex_", [128, 4, 32]); hidT = sb("hidT", [128, 4, 32], BF16)
        vcst = sb("vcst", [32, 64], BF16)
        pT = [sb("pT%d" % i, [128, 512], BF16) for i in range(3)]
        maskc = [sb("maskc%d" % i, [128, 512], BF16) for i in range(2)]
        zeros_b = sb("zeros_b", [128, 512], BF16)
        score = sb("score", [128, 128]); swork = sb("swork", [128, 128]); m8 = sb("m8", [128, 16])
        selb = [sb("selb%d" % i, [128, 128], BF16) for i in range(2)]; selT4 = [[sb("selT4%d_%d" % (i, g_), [128, 4, 128], BF16) for g_ in range(2)] for i in range(2)]
        oT_sb = None
        coef = [sb("coef%d" % i, [128, 12]) for i in range(2)]; den = [sb("den%d" % i, [128, 12]) for i in range(2)]; gsig = [sb("gsig%d" % i, [128, 12]) for i in range(4)]
        szn = gl1; acc = [sb("acc%d" % i, [128, 256]) for i in range(2)]

        banks = [st.enter_context(nc.psum_tensor("bank%d" % i, [128, 512], F32)) for i in range(8)]
        bb = [Buf("bank%d" % i) for i in range(8)]

        def B(name):
            return Buf(name)
        b_w = B("weights"); b_tab = B("tables"); b_hT = B("hT"); b_AB = B("AB")
        b_xs = [B("xs0"), B("xs1")]; b_xnb = [B("xnb0"), B("xnb1")]; b_small = B("small"); b_sm = [B("sm%d" % i) for i in range(4)]
        b_qT = B("qT"); b_kslc = [B("kslc%d" % i) for i in range(NQT)]
        b_kwin = [B("kwin%d" % i) for i in range(8)]; b_vslc = [B("vslc%d" % i) for i in range(NQT)]
        b_vwin = [B("vwin%d" % i) for i in range(8)]
        b_cmpbuf = B("cmpbuf"); b_kcT = B("kcT"); b_vcM = B("vcM"); b_hid = B("hid")
        b_gT = B("gT"); b_tokA = [B("tokA%d" % i) for i in range(4)]; b_tokB = [B("tokB%d" % i) for i in range(4)]
        b_gvb = [B("gvb%d" % i) for i in range(4)]
        b_gl = B("glatemps"); b_S = B("S"); b_glc = B("glachunk"); b_yt = [B("yt%d" % i) for i in range(4)]
        b_pT = [B("pT%d" % i) for i in range(3)]; b_maskc = [B("maskc0"), B("maskc1")]
        b_sel = B("sel"); b_selT = [B("selT0"), B("selT1")]; b_selb = [B("selb0"), B("selb1")]; b_oT = B("oTsb"); b_fin = B("fin"); b_finA = [B("finA0"), B("finA1")]; b_gs = [B("gs%d" % i) for i in range(4)]

        def chk(k):
            if stage == k:
                raise _Stop()
        try:
            def ld(q, out, in_, writes):
                return P.dma(q, lambda e: e.dma_start(out=out, in_=in_), writes=writes)
            wf_v = wf_d.rearrange("(kc p) n -> p kc n", p=128)
            wt_v = wt_d.rearrange("(kc p) n -> p kc n", p=128)
            for k4 in range(0, KC, 4):
                ld("pool", wf_s[:, k4:k4 + 4, :], wf_v[:, k4:k4 + 4, :], [b_w])
                ld("pool", wt_s[:, k4:k4 + 4, :], wt_v[:, k4:k4 + 4, :], [b_w])
            for pp in range(0, 32, 8):
                ld("pool", w1kv_s[:, pp:pp + 8, :], w1kv_d[:, pp:pp + 8, :], [b_w])
            ld("pool", w2k_s[:], w2k_d, [b_w]); ld("pool", w2v_s[:], w2v_d, [b_w])
            ld("pool", posT_s[:], posT_d, [b_w]); ld("pool", walpha_s[:], walpha_d, [b_w])
            ld("sp", bf_s[:], bf_d, [b_w]); ld("pool", btb_s[:], bt_d, [b_w])
            ld("sp", cT_s[:], cT_d, [b_AB]); ld("sp", gainT_s[:], gainT_d, [b_AB]); ld("sp", bada_s[:], bada_d, [b_AB])
            ld("sp", nbalpha_s[:], balpha_d, [b_w]); ld("sp", ggain_s[:], ggain_d, [b_w])
            ld("sp", E4_s[:], E4_d, [b_tab]); ld("sp", bdiag_s[:], bdiag_d, [b_tab]); ld("sp", bfar_s[:], bfar_d, [b_tab])
            ld("sp", gmask_s[:], gmask_d, [b_tab]); ld("sp", reset_s[:], reset_d, [b_tab])
            ld("sp", identb[:], identb_d, [b_tab]); ld("sp", identf[:], identf_d, [b_tab])
            ld("sp", vcM[:, :, 65:193], Mtab_d, [b_vcM])
            for i in range(NQT):
                pass
            P.op("pool", lambda e: e.memset(ones_s[:], 1.0), writes=[b_w])
            P.op("pool", lambda e: e.memset(zeros_b[:], 0.0), writes=[b_tab])
            for par_ in range(2):
                for g_ in range(2):
                    P.op("pool", lambda e, par_=par_, g_=g_: e.memset(selT4[par_][g_][:], 0.0), writes=[b_selT[par_]])
            P.op("pool", lambda e: e.memset(cmpbuf[:], 0.0), writes=[b_cmpbuf])
            P.op("pool", lambda e: e.memset(S_s[:], 0.0), writes=[b_S])
            P.op("pool", lambda e: e.memset(Sb_s[:], 0.0), writes=[b_S])
            P.op("pool", lambda e: e.memset(vcM[:, :, 0:64], 0.0), writes=[b_vcM])
            P.op("pool", lambda e: e.memset(vcM[:, :, 64:65], 1.0), writes=[b_vcM])
            P.op("pool", lambda e: e.memset(kcT[:, :], 0.0), writes=[b_kcT])
            ld("sp", kcT[64:72, :], kcaug_d, [b_kcT])
            P.op("pool", lambda e: e.memset(kslcT[:, :], 0.0), writes=b_kslc)
            ld("sp", kslcT[64:72, :], kaug_d, b_kslc)
            P.op("pool", lambda e: e.memset(kwinT[:, :], 0.0), writes=b_kwin)
            P.op("pool", lambda e: e.memset(qT[:].rearrange("p a b c -> p (a b c)"), 0.0), writes=[b_qT])
            for i in range(NQT):
                P.op("pool", lambda e, i=i: e.memset(vslc[:, i, 64:65], 1.0), writes=[b_vslc[i]])
            for i in range(8):
                P.op("pool", lambda e, i=i: e.memset(vwin[:, i, 64:65], 1.0), writes=[b_vwin[i]])
            P.op("dve", lambda e: e.tensor_scalar(out=nbalpha_s[:], in0=nbalpha_s[:], scalar1=-1.0, scalar2=None, op0=ALU.mult),
                 reads=[b_w], writes=[b_w])

            chk(0)
            wada_v = wada_d.rearrange("(kc p) n -> p kc n", p=128)
            for cc in range(48):
                xb_ = cc % 2
                ld("sp", xs[xb_][:].rearrange("p (kc n) -> p kc n", kc=16), wada_v[:, :, cc * 128:(cc + 1) * 128], [b_xs[xb_]])
                wv = xs[xb_][:].rearrange("p (kc n) -> p kc n", kc=16)
                for kc in range(KC):
                    P.op("pe", lambda e, kc=kc, wv=wv, cc=cc: e.matmul(banks[0][:, cc:cc + 1], lhsT=wv[:, kc, :], rhs=cT_s[:, kc:kc + 1],
                                                                      start=(kc == 0), stop=(kc == KC - 1)),
                         reads=[b_xs[xb_], b_AB], writes=[bb[0]], mode="f32")
            P.op("dve", lambda e: e.tensor_tensor(out=ada_s[:], in0=banks[0][:, 0:48], in1=bada_s[:], op=ALU.add),
                 reads=[bb[0], b_AB], writes=[b_AB])
            P.op("dve", lambda e: e.tensor_scalar(out=A_s[:], in0=ada_s[:, 16:32], scalar1=1.0, scalar2=None, op0=ALU.add),
                 reads=[b_AB], writes=[b_AB])
            P.op("dve", lambda e: e.tensor_tensor(out=A_s[:], in0=A_s[:], in1=gainT_s[:], op=ALU.mult), reads=[b_AB], writes=[b_AB])
            P.op("dve", lambda e: e.tensor_copy(out=B_s[:], in_=ada_s[:, 0:16]), reads=[b_AB], writes=[b_AB])
            P.dma("pool", lambda e: e.dma_start(out=adao_d, in_=ada_s[:]), reads=[b_AB])

            chk(1)
            for kv in range(2):
                lo = 64 * kv
                for hc in range(2):
                    col = kv * 2 + hc
                    for p_ in range(32):
                        P.op("pe", lambda e, lo=lo, hc=hc, p_=p_, col=col: e.matmul(
                            banks[1][:, col:col + 1], lhsT=w1kv_s[lo:lo + 64, p_, hc * 128:(hc + 1) * 128],
                            rhs=posT_s[lo:lo + 64, p_:p_ + 1], start=(p_ == 0), stop=(p_ == 31)),
                            reads=[b_w], writes=[bb[1]], mode="r64")
            P.op("dve", lambda e: e.tensor_copy(out=posb_s[:], in_=banks[1][:, 0:4]), reads=[bb[1]], writes=[b_w])

            chk(2)
            x_v = x.rearrange("(t p) d -> t p d", p=128)
            y_v = y_d.rearrange("(t p) n -> t p n", p=128)
            tokx = {}

            def load_x(t):
                if t < NQT:
                    tokx[t] = ld("sp", xs[t % 2][:], x_v[t], [b_xs[t % 2]])
            load_x(0); load_x(1)

            rr = {"mm": 0, "st": 0, "pt": 0}
            fillc = {}

            def getfill(e):
                if "r" not in fillc:
                    fillc["r"] = e.to_reg(NEGM)
                return fillc["r"]

            def mmbank():
                i = rr["mm"] % 2
                rr["mm"] += 1
                return i

            def stbank():
                i = 4 + rr["st"] % 2
                rr["st"] += 1
                return i

            def norm_tile(ST, tt):
                t = ST * 4 + tt
                xb_ = t % 2
                P.op("act", lambda e, xb_=xb_, tt=tt: e.activation(out=xnb[xb_][:], in_=xs[xb_][:], func=AF.Square,
                                                                   accum_out=small[:, tt:tt + 1]),
                     reads=[b_xs[xb_]], writes=[b_xnb[xb_], b_sm[tt]])
                P.op("act", lambda e, tt=tt: e.activation(out=small[:, 8 + tt:9 + tt], in_=small[:, tt:tt + 1], func=AF.Ln,
                                                          scale=1.0 / D, bias=EPS),
                     reads=[b_sm[tt]], writes=[b_sm[tt]])
                P.op("act", lambda e, tt=tt: e.activation(out=small[:, 16 + tt:17 + tt], in_=small[:, 8 + tt:9 + tt], func=AF.Exp,
                                                          scale=-0.5),
                     reads=[b_sm[tt]], writes=[b_sm[tt]])
                P.op("dve", lambda e, xb_=xb_, tt=tt: e.tensor_scalar(out=xnb[xb_][:], in0=xs[xb_][:], scalar1=small[:, 16 + tt:17 + tt],
                                                                      scalar2=None, op0=ALU.mult),
                     reads=[b_xs[xb_], b_sm[tt]], writes=[b_xnb[xb_]])
                load_x(t + 2)
                for half in range(2):
                    bk = 2 + half
                    pv = banks[bk][:].bitcast(BF16)
                    for c8 in range(8):
                        kc = half * 8 + c8
                        P.op("pe", lambda e, pv=pv, c8=c8, kc=kc, xb_=xb_: e.transpose(
                            pv[:, c8 * 128:(c8 + 1) * 128], xnb[xb_][:, kc * 128:(kc + 1) * 128], identb[:]),
                            reads=[b_xnb[xb_], b_tab], writes=[bb[bk]], mode="tr")
                    for c8 in range(8):
                        kc = half * 8 + c8
                        eng = "dve" if c8 % 2 == 0 else "pool"
                        eng = "dve"
                        P.op(eng, lambda e, pv=pv, c8=c8, kc=kc, tt=tt: e.tensor_scalar(
                            out=hT[:, kc, tt * 128:(tt + 1) * 128], in0=pv[:, c8 * 128:(c8 + 1) * 128],
                            scalar1=A_s[:, kc:kc + 1], scalar2=B_s[:, kc:kc + 1], op0=ALU.mult, op1=ALU.add),
                            reads=[bb[bk], b_AB], writes=[b_hT])

            for tt_ in range(4):
                norm_tile(0, tt_)
            for ST in range(NST):
                chk(3)
                tsl = slice(ST * 512, ST * 512 + 512)
                P.dma("sp", lambda e, ST=ST: e.dma_start(out=qT[64:72, :, :, :], in_=qaug_d[:, ST * 4:ST * 4 + 4, :, :]), writes=[b_qT])
                for ci in range(NFC):
                    M_ = FCH_M[ci]; off = FCH_OFF[ci]
                    bk = mmbank()
                    for kc in range(KC):
                        P.op("pe", lambda e, bk=bk, M_=M_, off=off, kc=kc: e.matmul(
                            banks[bk][0:M_, :], lhsT=wf_s[:, kc, off:off + M_], rhs=hT[:, kc, :], start=(kc == 0), stop=(kc == KC - 1)),
                            reads=[b_w, b_hT], writes=[bb[bk]], mode=("full" if M_ > 64 else "c%d" % (64 if M_ > 32 else 32)))
                    src = banks[bk]
                    if ci < 4:
                        P.op("act", lambda e, src=src, ci=ci: e.activation(out=qT[0:64, :, ci, :], in_=src[0:64, :].rearrange("p (t q) -> p t q", q=128), func=AF.Identity,
                                                                           bias=bf_s[0:64, ci:ci + 1], scale=1.0),
                             reads=[bb[bk], b_w], writes=[b_qT])
                        P.op("dve", lambda e, ci=ci: e.tensor_scalar(out=qT[0:64, :, ci, :], in0=qT[0:64, :, ci, :], scalar1=0.125, scalar2=None,
                                                                     op0=ALU.mult), reads=[b_qT], writes=[b_qT])
                    elif ci == 4:
                        P.op("act", lambda e, src=src: e.activation(out=cmpbuf[:, 16:528], in_=src[:, :], func=AF.Identity,
                                                                    bias=bf_s[:, 4:5], scale=1.0),
                             reads=[bb[bk], b_w], writes=[b_cmpbuf])
                    elif ci == 5:
                        P.op("act", lambda e, src=src, tsl=tsl: e.activation(out=kslcT[0:64, tsl], in_=src[0:64, :], func=AF.Identity,
                                                                             bias=bf_s[0:64, 5:6], scale=1.0),
                             reads=[bb[bk], b_w], writes=[b_kslc[ST * 4 + k] for k in range(4)])
                    elif ci == 6:
                        rs = (ST % 2) * 512
                        P.op("act", lambda e, src=src, rs=rs: e.activation(out=kwinT[0:64, rs:rs + 512], in_=src[0:64, :], func=AF.Identity,
                                                                           bias=bf_s[0:64, 6:7], scale=1.0),
                             reads=[bb[bk], b_w], writes=[b_kwin[(ST % 2) * 4 + k] for k in range(4)])
                        P.dma("sp", lambda e, rs=rs, tsl=tsl: e.dma_start(out=kwinT[64:72, rs:rs + 512], in_=kaug_d[:, tsl]),
                              writes=[b_kwin[(ST % 2) * 4 + k] for k in range(4)])
                    elif ci == 7:
                        P.op("act", lambda e, src=src: e.activation(out=gqT[:], in_=src[:, :], func=AF.Identity, bias=bf_s[:, 7:8], scale=1.0),
                             reads=[bb[bk], b_w], writes=[b_gT])
                    elif ci == 8:
                        P.op("act", lambda e, src=src: e.activation(out=gkT[:], in_=src[:, :], func=AF.Identity, bias=bf_s[:, 8:9], scale=1.0),
                             reads=[bb[bk], b_w], writes=[b_gT])
                    else:
                        P.op("act", lambda e, src=src: e.activation(out=gaT[:], in_=src[0:16, :], func=AF.Identity, bias=bf_s[0:16, 9:10], scale=1.0),
                             reads=[bb[bk], b_w], writes=[b_gT])
                bk = mmbank()
                P.op("pe", lambda e, bk=bk: e.matmul(banks[bk][:, :], lhsT=walpha_s[:, :], rhs=gaT[:, :], start=True, stop=True),
                     reads=[b_w, b_gT], writes=[bb[bk]], mode="r32")
                P.op("act", lambda e, bk=bk: e.activation(out=e1[:], in_=banks[bk][:, :], func=AF.Exp, scale=-1.0, bias=nbalpha_s[:, 0:1]),
                     reads=[bb[bk], b_w], writes=[b_gl])
                P.op("act", lambda e: e.activation(out=spl[:], in_=e1[:], func=AF.Ln, bias=1.0, scale=1.0), reads=[b_gl], writes=[b_gl])
                P.op("dve", lambda e: e.tensor_tensor_scan(out=cs[:], data0=reset_s[:], data1=spl[:], initial=0.0, op0=ALU.mult, op1=ALU.add),
                     reads=[b_gl, b_tab], writes=[b_gl])
                P.op("act", lambda e: e.activation(out=eb[:], in_=cs[:], func=AF.Exp, scale=-1.0 / 16), reads=[b_gl], writes=[b_gl])
                P.op("act", lambda e: e.activation(out=enb[:], in_=cs[:], func=AF.Exp, scale=1.0 / 16), reads=[b_gl], writes=[b_gl])
                P.op("dve", lambda e: e.scalar_tensor_tensor(out=qtl[:], in0=gqT[:], scalar=128.0 ** -0.5, in1=eb[:], op0=ALU.mult, op1=ALU.mult),
                     reads=[b_gl, b_gT], writes=[b_gl])
                P.op("dve", lambda e: e.tensor_tensor(out=ktl[:], in0=gkT[:], in1=enb[:], op=ALU.mult), reads=[b_gl, b_gT], writes=[b_gl])
                chk(4)
                for tt in range(4):
                    t = ST * 4 + tt
                    for grp in range(2):
                        if grp == 1:
                            chk(44)
                        if tt == 1:
                            chk(46)
                        bk = mmbank()
                        n0, n1 = (0, NTA) if grp == 0 else (NTA, NT_)
                        w_ = n1 - n0
                        for kc in range(KC):
                            P.op("pe", lambda e, bk=bk, kc=kc, tt=tt, n0=n0, n1=n1, w_=w_: e.matmul(
                                banks[bk][:, 0:w_], lhsT=hT[:, kc, tt * 128:(tt + 1) * 128], rhs=wt_s[:, kc, n0:n1], start=(kc == 0), stop=False),
                                reads=[b_w, b_hT], writes=[bb[bk]])
                        chk(41)
                        P.op("pe", lambda e, bk=bk, n0=n0, n1=n1, w_=w_: e.matmul(
                            banks[bk][:, 0:w_], lhsT=ones_s[0:1, :], rhs=btb_s[0:1, n0:n1], start=False, stop=True),
                            reads=[b_w], writes=[bb[bk]], mode="r32")
                        chk(42)
                        if grp == 0:
                            P.op("dve", lambda e, bk=bk, tt=tt: e.tensor_copy(out=tokA[tt][:], in_=banks[bk][:, 0:NTA]),
                                 reads=[bb[bk]], writes=[b_tokA[tt]])
                            chk(43)
                            P.op("dve", lambda e, tt=tt, t=t: e.tensor_copy(out=vslc[:, t, 0:64], in_=tokA[tt][:, 0:64]),
                                 reads=[b_tokA[tt]], writes=[b_vslc[t]])
                            P.op("dve", lambda e, tt=tt, t=t: e.tensor_copy(out=vwin[:, t % 8, 0:64], in_=tokA[tt][:, 64:128]),
                                 reads=[b_tokA[tt]], writes=[b_vwin[t % 8]])
                        else:
                            chk(45)
                            P.op("dve", lambda e, bk=bk, tt=tt: e.tensor_copy(out=tokB[tt][:], in_=banks[bk][:, 256:512]),
                                 reads=[bb[bk]], writes=[b_tokB[tt]])
                            P.op("dve", lambda e, bk=bk, tt=tt: e.tensor_copy(out=gvb[tt][:], in_=banks[bk][:, 0:256]),
                                 reads=[bb[bk]], writes=[b_gvb[tt]])
                if debug and ST == 0:
                    P.dma("pool", lambda e: e.dma_start(out=dbg["qT"], in_=qT[0:72]), reads=[b_qT])
                    P.dma("pool", lambda e: e.dma_start(out=dbg["tokA"], in_=tokA[0][:]), reads=[b_tokA[0]])

                if ST + 1 < NST:
                    norm_tile(ST + 1, 0)
                    norm_tile(ST + 1, 1)
                chk(5)
                bf6 = banks[6][:].bitcast(BF16)
                for tt in range(4):
                    cs_ = slice(tt * 128, tt * 128 + 128)
                    P.op("pe", lambda e, cs_=cs_: e.transpose(bf6[:, cs_], ktl[:, cs_], identb[:]), reads=[b_gl, b_tab], writes=[bb[6]], mode="tr")
                for tt in range(4):
                    cs_ = slice(tt * 128, tt * 128 + 128)
                    P.op("pe", lambda e, cs_=cs_: e.matmul(banks[7][:, cs_], lhsT=ktl[:, cs_], rhs=qtl[:, cs_], start=True, stop=True),
                         reads=[b_gl], writes=[bb[7]])
                P.op("dve", lambda e: e.tensor_copy(out=gkT[:], in_=bf6[:, 0:512]), reads=[bb[6], b_gl], writes=[b_gT])
                for tt in range(4):
                    cs_ = slice(tt * 128, tt * 128 + 128)
                    P.op("dve", lambda e, cs_=cs_: e.tensor_tensor(out=gqT[:, cs_], in0=banks[7][:, cs_], in1=gmask_s[:], op=ALU.mult),
                         reads=[bb[7], b_tab, b_gl], writes=[b_gT])
                for tt in range(4):
                    t = ST * 4 + tt
                    cs_ = slice(tt * 128, tt * 128 + 128)
                    bo = mmbank()
                    P.op("pe", lambda e, bo=bo, tt=tt, cs_=cs_: e.matmul(banks[bo][:, 0:256], lhsT=gqT[:, cs_], rhs=gvb[tt][:], start=True, stop=False),
                         reads=[b_gT, b_gvb[tt]], writes=[bb[bo]])
                    P.op("pe", lambda e, tt=tt, cs_=cs_: e.matmul(banks[6][:, 256:512], lhsT=gkT[:, cs_], rhs=gvb[tt][:], start=True, stop=True),
                         reads=[b_gT, b_gvb[tt]], writes=[bb[6]])
                    P.op("pe", lambda e, bo=bo, cs_=cs_: e.matmul(banks[bo][:, 0:256], lhsT=qtl[:, cs_], rhs=Sb_s[:], start=False, stop=True),
                         reads=[b_gl, b_S], writes=[bb[bo]])
                    P.op("dve", lambda e: e.tensor_tensor(out=stmp[:], in0=banks[6][:, 256:512], in1=S_s[:], op=ALU.add),
                         reads=[bb[6], b_S], writes=[b_glc])
                    last = tt * 128 + 127
                    P.op("dve", lambda e, last=last: e.tensor_scalar(out=S_s[:], in0=stmp[:], scalar1=eb[:, last:last + 1], scalar2=None, op0=ALU.mult),
                         reads=[b_glc, b_gl], writes=[b_S])
                    P.op("dve", lambda e: e.tensor_copy(out=Sb_s[:], in_=S_s[:]), reads=[b_S], writes=[b_S])
                    yb_ = tt
                    P.op("act", lambda e, bo=bo: e.activation(out=junk[:], in_=banks[bo][:, 0:256], func=AF.Square, accum_out=small[:, 24:25]),
                         reads=[bb[bo]], writes=[b_fin, b_small])
                    P.op("act", lambda e: e.activation(out=small[:, 25:26], in_=small[:, 24:25], func=AF.Ln, scale=1.0 / 256, bias=EPS),
                         reads=[b_small], writes=[b_small])
                    P.op("act", lambda e: e.activation(out=small[:, 26:27], in_=small[:, 25:26], func=AF.Exp, scale=-0.5),
                         reads=[b_small], writes=[b_small])
                    P.op("act", lambda e, tt=tt: e.activation(out=gl1[:], in_=tokB[tt][:], func=AF.Exp, scale=-1.0),
                         reads=[b_tokB[tt]], writes=[b_fin])
                    P.op("dve", lambda e: e.tensor_scalar(out=gl1[:], in0=gl1[:], scalar1=1.0, scalar2=None, op0=ALU.add), reads=[b_fin], writes=[b_fin])
                    P.op("dve", lambda e: e.reciprocal(out=gl1[:], in_=gl1[:]), reads=[b_fin], writes=[b_fin])
                    P.op("dve", lambda e, tt=tt: e.tensor_tensor(out=gl1[:], in0=gl1[:], in1=tokB[tt][:], op=ALU.mult),
                         reads=[b_fin, b_tokB[tt]], writes=[b_fin])
                    P.op("dve", lambda e: e.tensor_tensor(out=gl1[:], in0=gl1[:], in1=ggain_s[:], op=ALU.mult), reads=[b_fin, b_w], writes=[b_fin])
                    P.op("dve", lambda e, bo=bo, yb_=yb_: e.scalar_tensor_tensor(out=yt[yb_][:, 256:512], in0=banks[bo][:, 0:256], scalar=small[:, 26:27],
                                                                                 in1=gl1[:], op0=ALU.mult, op1=ALU.mult),
                         reads=[bb[bo], b_fin, b_small], writes=[b_yt[yb_]])

                chk(6)
                for kv in range(2):
                    lo = 64 * kv
                    bk = mmbank()
                    for hc in range(2):
                        for p_ in range(32):
                            P.op("pe", lambda e, bk=bk, lo=lo, hc=hc, p_=p_: e.matmul(
                                banks[bk][:, hc * 32:(hc + 1) * 32], lhsT=w1kv_s[lo:lo + 64, p_, hc * 128:(hc + 1) * 128],
                                rhs=cmpbuf[lo:lo + 64, p_:p_ + 497:16], start=(p_ == 0), stop=(p_ == 31)),
                                reads=[b_w, b_cmpbuf], writes=[bb[bk]], mode="r64")
                    for hc in range(2):
                        col = kv * 2 + hc
                        P.op("dve", lambda e, bk=bk, hc=hc, col=col: e.tensor_scalar(out=hsb[:, col, :], in0=banks[bk][:, hc * 32:(hc + 1) * 32],
                                                                                     scalar1=posb_s[:, col:col + 1], scalar2=None, op0=ALU.add),
                             reads=[bb[bk], b_w], writes=[b_hid])
                P.op("act", lambda e: e.activation(out=hex_[:], in_=hsb[:], func=AF.Exp, scale=-1.0), reads=[b_hid], writes=[b_hid])
                P.op("dve", lambda e: e.tensor_scalar(out=hex_[:], in0=hex_[:], scalar1=1.0, scalar2=None, op0=ALU.add), reads=[b_hid], writes=[b_hid])
                P.op("dve", lambda e: e.reciprocal(out=hex_[:], in_=hex_[:]), reads=[b_hid], writes=[b_hid])
                P.op("dve", lambda e: e.tensor_tensor(out=hidT[:], in0=hex_[:], in1=hsb[:], op=ALU.mult), reads=[b_hid], writes=[b_hid])
                bk = mmbank()
                for hc in range(2):
                    P.op("pe", lambda e, bk=bk, hc=hc: e.matmul(banks[bk][0:64, 0:32], lhsT=w2k_s[:, hc, :], rhs=hidT[:, hc, :],
                                                                start=(hc == 0), stop=(hc == 1)), reads=[b_w, b_hid], writes=[bb[bk]], mode="c64")
                P.op("dve", lambda e, bk=bk, ST=ST: e.tensor_copy(out=kcT[0:64, ST * 32:ST * 32 + 32], in_=banks[bk][0:64, 0:32]),
                     reads=[bb[bk]], writes=[b_kcT])
                for hc in range(2):
                    P.op("pe", lambda e, bk=bk, hc=hc: e.matmul(banks[bk][0:32, 64:128], lhsT=hidT[:, 2 + hc, :], rhs=w2v_s[:, hc, :],
                                                                start=(hc == 0), stop=(hc == 1)), reads=[b_w, b_hid], writes=[bb[bk]], mode="c32")
                P.op("dve", lambda e, bk=bk: e.tensor_copy(out=vcst[:], in_=banks[bk][0:32, 64:128]), reads=[bb[bk]], writes=[b_hid])
                pr = 32 * (ST % 4)
                P.dma("sp", lambda e, pr=pr, ST=ST: e.dma_start(out=vcM[pr:pr + 32, ST // 4, 0:64], in_=vcst[:]), reads=[b_hid], writes=[b_vcM])
                if ST == 0:
                    P.op("pool", lambda e: e.memset(vcM[0:1, 0, 0:193], 0.0), writes=[b_vcM])
                P.op("dve", lambda e: e.tensor_copy(out=cmpbuf[:, 0:16], in_=cmpbuf[:, 512:528]), reads=[b_cmpbuf], writes=[b_cmpbuf])

                if ST + 1 < NST:
                    norm_tile(ST + 1, 2)
                chk(7)
                def run_pipeline(jobs, depth=1):
                    pend = []
                    for jb in jobs:
                        jb[0]()
                        pend.append(jb)
                        if len(pend) > depth:
                            pend.pop(0)[1]()
                    for jb in pend:
                        jb[1]()

                def stage_A(tt):
                    i = ST * 4 + tt
                    par = i % 2
                    qv = qT[:, tt, :, :].rearrange("p h q -> p (h q)")
                    nnt = (8 * i + 7 + 127) // 128
                    jobs = []
                    for nt in range(nnt):
                        mrows = 128
                        st_ = {}

                        def qk(nt=nt, mrows=mrows, st_=st_):
                            sbk = stbank()
                            st_["sbk"] = sbk
                            mk = maskc[nt % 2]
                            P.op("pool", lambda e, mk=mk: e.affine_select(
                                out=mk[:], in_=zeros_b[:], pattern=[[0, 4], [1, 128]], compare_op=ALU.is_ge, fill=getfill(e),
                                base=128 * i - 2048 * nt - 15, channel_multiplier=-16), reads=[b_tab], writes=[b_maskc[nt % 2]])
                            P.op("pe", lambda e: e.matmul(banks[sbk][0:mrows, :], lhsT=kcT[:, nt * 128:nt * 128 + mrows], rhs=qv, start=True, stop=False),
                                 reads=[b_kcT, b_qT], writes=[bb[sbk]])
                            P.op("pe", lambda e: e.matmul(banks[sbk][0:mrows, :], lhsT=identb[:, 0:mrows], rhs=mk[:], start=False, stop=True),
                                 reads=[b_tab, b_maskc[nt % 2]], writes=[bb[sbk]])

                        def rest(nt=nt, mrows=mrows, st_=st_):
                            sbk = st_["sbk"]
                            pb = rr["pt"] % 3
                            rr["pt"] += 1
                            P.op("act", lambda e: e.activation(out=pT[pb][0:mrows, :], in_=banks[sbk][0:mrows, :], func=AF.Exp, scale=1.0),
                                 reads=[bb[sbk]], writes=[b_pT[pb]])
                            for h in range(4):
                                obk = 2 + h // 2
                                oc = (h % 2) * 256
                                P.op("pe", lambda e, obk=obk, oc=oc, h=h: e.matmul(
                                    banks[obk][:, oc:oc + 193], lhsT=pT[pb][0:mrows, h * 128:(h + 1) * 128], rhs=vcM[0:mrows, nt, 0:193],
                                    start=(nt == 0 and h % 2 == 0), stop=(nt == nnt - 1), skip_group_check=True),
                                    reads=[b_pT[pb], b_vcM], writes=[bb[obk]])
                        jobs.append((qk, rest))
                    run_pipeline(jobs)

                def stage_Adve(tt):
                    i = ST * 4 + tt
                    par = i % 2
                    bF = b_finA[par]
                    P.op("act", lambda e: e.activation(out=gsig[tt][:], in_=tokA[tt][:, 128:140], func=AF.Exp, scale=-1.0), reads=[b_tokA[tt]], writes=[b_gs[tt]])
                    P.op("dve", lambda e: e.tensor_scalar(out=gsig[tt][:], in0=gsig[tt][:], scalar1=1.0, scalar2=None, op0=ALU.add), reads=[b_gs[tt]], writes=[b_gs[tt]])
                    P.op("dve", lambda e: e.reciprocal(out=gsig[tt][:], in_=gsig[tt][:]), reads=[b_gs[tt]], writes=[b_gs[tt]])
                    for h in range(4):
                        obk = 2 + h // 2
                        oc = (h % 2) * 256
                        P.op("dve", lambda e, obk=obk, oc=oc, h=h: e.tensor_scalar(out=den[par][:, h:h + 1], in0=banks[obk][:, oc + 64:oc + 65],
                                                                                  scalar1=1e-30, scalar2=None, op0=ALU.max),
                             reads=[bb[obk]], writes=[bF])
                    P.op("dve", lambda e: e.reciprocal(out=den[par][:, 0:4], in_=den[par][:, 0:4]), reads=[bF], writes=[bF])
                    for h in range(4):
                        obk = 2 + h // 2
                        oc = (h % 2) * 256
                        P.op("dve", lambda e, h=h: e.tensor_tensor(out=coef[par][:, h:h + 1], in0=gsig[tt][:, 3 * h:3 * h + 1], in1=den[par][:, h:h + 1], op=ALU.mult),
                             reads=[bF, b_gs[tt]], writes=[bF])
                        P.op("dve", lambda e, obk=obk, oc=oc, h=h: e.tensor_scalar(out=acc[par][:, h * 64:(h + 1) * 64], in0=banks[obk][:, oc:oc + 64],
                                                                                  scalar1=coef[par][:, h:h + 1], scalar2=None, op0=ALU.mult),
                             reads=[bb[obk], bF], writes=[bF])

                    for h in range(4):
                        obk = 2 + h // 2
                        oc = (h % 2) * 256
                        if h == 0:
                            P.op("dve", lambda e, obk=obk, oc=oc: e.tensor_scalar(out=score[:], in0=banks[obk][:, oc + 65:oc + 193], scalar1=den[par][:, 0:1],
                                                                                 scalar2=None, op0=ALU.mult), reads=[bb[obk], bF], writes=[b_sel])
                        else:
                            P.op("dve", lambda e, obk=obk, oc=oc, h=h: e.scalar_tensor_tensor(out=score[:], in0=banks[obk][:, oc + 65:oc + 193],
                                                                                              scalar=den[par][:, h:h + 1], in1=score[:], op0=ALU.mult, op1=ALU.add),
                                 reads=[bb[obk], bF, b_sel], writes=[b_sel])
                    P.op("dve", lambda e: e.memset(score[:, 0:1], BIG), writes=[b_sel])
                    P.op("dve", lambda e: e.memset(score[:, 2 * i:2 * i + 1], BIG), writes=[b_sel])
                    if i > 0:
                        P.op("dve", lambda e: e.memset(score[0:64, 2 * i - 1:2 * i], BIG), writes=[b_sel])
                    P.op("dve", lambda e: e.memset(score[64:128, 2 * i + 1:2 * i + 2], BIG), writes=[b_sel])
                    P.op("dve", lambda e: e.max(out=m8[:, 0:8], in_=score[:]), reads=[b_sel], writes=[b_sel])
                    P.op("dve", lambda e: e.match_replace(out=swork[:], in_to_replace=m8[:, 0:8], in_values=score[:], imm_value=-BIG),
                         reads=[b_sel], writes=[b_sel])
                    P.op("dve", lambda e: e.max(out=m8[:, 8:16], in_=swork[:]), reads=[b_sel], writes=[b_sel])
                    P.op("dve", lambda e: e.tensor_scalar(out=selb[par][:], in0=score[:], scalar1=m8[:, 15:16], scalar2=NEGM, op0=ALU.is_lt, op1=ALU.mult),
                         reads=[b_sel], writes=[b_selb[par]])

                def stage_T(tt):
                    i = ST * 4 + tt
                    par = i % 2
                    sbt = mmbank()
                    P.op("pe", lambda e: e.transpose(banks[sbt][:].bitcast(BF16)[:, 0:128], selb[par][:], identb[:]), reads=[b_selb[par], b_tab], writes=[bb[sbt]], mode="tr")
                    for h in range(4):
                        P.op("dve", lambda e, h=h: e.tensor_copy(out=selT4[par][0][0:64, h, :], in_=banks[sbt][:].bitcast(BF16)[0:64, 0:128]),
                             reads=[bb[sbt]], writes=[b_selT[par]])
                        P.op("dve", lambda e, h=h: e.tensor_copy(out=selT4[par][1][64:128, h, :], in_=banks[sbt][:].bitcast(BF16)[64:128, 0:128]),
                             reads=[bb[sbt]], writes=[b_selT[par]])

                def stage_Bm(tt):
                    i = ST * 4 + tt
                    par = i % 2
                    bF = b_finA[par]
                    qv = qT[:, tt, :, :].rearrange("p h q -> p (h q)")

                    def mkjob(br, kt, idx, nk):
                        obk = 6 + br
                        if br == 0:
                            kop = kslcT[:, kt * 128:(kt + 1) * 128]; kb = b_kslc[kt]
                            vop = vslc[:, kt, 0:65]; vb = b_vslc[kt]
                        else:
                            rsl = (kt % 8) * 128
                            kop = kwinT[:, rsl:rsl + 128]; kb = b_kwin[kt % 8]
                            vop = vwin[:, kt % 8, 0:65]; vb = b_vwin[kt % 8]
                        extra = []
                        if br == 0:
                            gq_ = (2 * kt) // 64
                            m_ = kt % 32
                            extra.append((E4_s[:, m_, :], selT4[par][gq_][:, :, :].rearrange("p h q -> p (h q)"), [b_tab, b_selT[par]]))
                        if kt == i:
                            extra.append((identb[:], bdiag_s[:], [b_tab]))
                        if br == 1 and kt == i - 4:
                            extra.append((identb[:], bfar_s[:], [b_tab]))
                        st_ = {}

                        def qk():
                            sbk = stbank()
                            st_["sbk"] = sbk
                            P.op("pe", lambda e: e.matmul(banks[sbk][:, :], lhsT=kop, rhs=qv, start=True, stop=(len(extra) == 0)),
                                 reads=[kb, b_qT], writes=[bb[sbk]])
                            for xi, (l_, r_, rb_) in enumerate(extra):
                                P.op("pe", lambda e, l_=l_, r_=r_, lastx=(xi == len(extra) - 1): e.matmul(
                                    banks[sbk][:, :], lhsT=l_, rhs=r_, start=False, stop=lastx), reads=rb_, writes=[bb[sbk]])

                        def rest():
                            sbk = st_["sbk"]
                            pb = rr["pt"] % 3
                            rr["pt"] += 1
                            P.op("act", lambda e: e.activation(out=pT[pb][:], in_=banks[sbk][:, :], func=AF.Exp, scale=1.0),
                                 reads=[bb[sbk]], writes=[b_pT[pb]])
                            P.op("pe", lambda e: e.matmul(banks[obk][0:65, :], lhsT=vop, rhs=pT[pb][:], start=(idx == 0), stop=(idx == nk - 1)),
                                 reads=[b_pT[pb], vb], writes=[bb[obk]])
                        return (qk, rest)
                    kw = list(range(max(0, i - 4), i + 1))
                    ks = list(range(0, i + 1))
                    jobs_w = [mkjob(1, kt, idx, len(kw)) for idx, kt in enumerate(kw)]
                    jobs_s = [mkjob(0, kt, idx, len(ks)) for idx, kt in enumerate(ks)]
                    run_pipeline(jobs_w + jobs_s)

                def stage_Bc(tt):
                    for br in range(2):
                        obk = 6 + br
                        P.op("dve", lambda e, obk=obk, br=br: e.tensor_copy(out=(e1 if br == 0 else cs)[0:65, :], in_=banks[obk][0:65, :]),
                             reads=[bb[obk]], writes=[b_gl])

                def stage_Bf(tt):
                    i = ST * 4 + tt
                    par = i % 2
                    bF = b_finA[par]
                    for br in range(2):
                        tbk = mmbank()
                        for h in range(4):
                            P.op("pe", lambda e, tbk=tbk, br=br, h=h: e.transpose(
                                banks[tbk][:, h * 128:h * 128 + 65], (e1 if br == 0 else cs)[0:65, h * 128:(h + 1) * 128], identf[0:65, 0:65]),
                                reads=[b_gl, b_tab], writes=[bb[tbk]], mode="f32t")
                        for h in range(4):
                            dcol = 4 + br * 4 + h
                            P.op("dve", lambda e, tbk=tbk, h=h, dcol=dcol: e.tensor_scalar(out=den[par][:, dcol:dcol + 1], in0=banks[tbk][:, h * 128 + 64:h * 128 + 65],
                                                                                          scalar1=1e-30, scalar2=None, op0=ALU.max),
                                 reads=[bb[tbk]], writes=[bF])
                            P.op("dve", lambda e, dcol=dcol: e.reciprocal(out=den[par][:, dcol:dcol + 1], in_=den[par][:, dcol:dcol + 1]), reads=[bF], writes=[bF])
                            P.op("dve", lambda e, h=h, br=br, dcol=dcol: e.tensor_tensor(out=coef[par][:, dcol:dcol + 1], in0=gsig[tt][:, 3 * h + 1 + br:3 * h + 2 + br],
                                                                                        in1=den[par][:, dcol:dcol + 1], op=ALU.mult), reads=[bF, b_gs[tt]], writes=[bF])
                            P.op("dve", lambda e, tbk=tbk, h=h, dcol=dcol: e.scalar_tensor_tensor(
                                out=acc[par][:, h * 64:(h + 1) * 64], in0=banks[tbk][:, h * 128:h * 128 + 64], scalar=coef[par][:, dcol:dcol + 1],
                                in1=acc[par][:, h * 64:(h + 1) * 64], op0=ALU.mult, op1=ALU.add), reads=[bb[tbk], bF], writes=[bF])
                    if debug and i == NQT - 1:
                        P.dma("pool", lambda e: e.dma_start(out=dbg["oT0"], in_=e1[0:65, :]), reads=[b_gl])
                        P.dma("pool", lambda e: e.dma_start(out=dbg["oT1"], in_=cs[0:65, :]), reads=[b_gl])
                        P.dma("pool", lambda e: e.dma_start(out=dbg["acc"], in_=acc[par][:]), reads=[bF])
                        P.dma("pool", lambda e: e.dma_start(out=dbg["kcT"], in_=kcT[0:72]), reads=[b_kcT])
                        P.dma("pool", lambda e: e.dma_start(out=dbg["ocmp"], in_=vcM[:, :, 0:193]), reads=[b_vcM])
                        P.dma("pool", lambda e: e.dma_start(out=dbg["den"], in_=den[par][:]), reads=[bF])
                        P.dma("pool", lambda e: e.dma_start(out=dbg["gsig"], in_=gsig[par][:]), reads=[bF])
                        P.dma("pool", lambda e: e.dma_start(out=dbg["coef"], in_=coef[par][:]), reads=[bF])
                        P.dma("pool", lambda e: e.dma_start(out=dbg["pT"], in_=pT[(rr["pt"] - 1) % 3][:]), reads=[b_pT[(rr["pt"] - 1) % 3]])
                    yb_ = tt
                    P.op("act", lambda e: e.activation(out=szn[:], in_=tokA[tt][:, 140:396], func=AF.Exp, scale=-1.0), reads=[b_tokA[tt]], writes=[b_fin])
                    P.op("dve", lambda e: e.tensor_scalar(out=szn[:], in0=szn[:], scalar1=1.0, scalar2=None, op0=ALU.add), reads=[b_fin], writes=[b_fin])
                    P.op("dve", lambda e: e.reciprocal(out=szn[:], in_=szn[:]), reads=[b_fin], writes=[b_fin])
                    P.op("dve", lambda e: e.tensor_tensor(out=szn[:], in0=szn[:], in1=tokA[tt][:, 140:396], op=ALU.mult), reads=[b_fin, b_tokA[tt]], writes=[b_fin])
                    P.op("dve", lambda e: e.tensor_tensor(out=yt[yb_][:, 0:256], in0=szn[:], in1=acc[par][:], op=ALU.mult), reads=[b_fin, bF], writes=[b_yt[yb_]])
                    P.dma("pool", lambda e: e.dma_start(out=y_v[i], in_=yt[yb_][:]), reads=[b_yt[yb_]])

                stage_A(0)
                stage_Adve(0)
                stage_A(1)
                stage_T(0)
                stage_Adve(1)
                stage_Bm(0)
                stage_Bc(0)
                if ST + 1 < NST:
                    norm_tile(ST + 1, 3)
                stage_A(2)
                stage_T(1)
                stage_Bf(0)
                stage_Adve(2)
                stage_Bm(1)
                stage_Bc(1)
                stage_A(3)
                stage_T(2)
                stage_Bf(1)
                stage_Adve(3)
                stage_Bm(2)
                stage_Bc(2)
                stage_T(3)
                stage_Bf(2)
                stage_Bm(3)
                stage_Bc(3)
                stage_Bf(3)
        except _Stop:
            pass
        toks = [t_ for t_ in P.dma_last if t_ is not None]
        toks += [(e_, P.cnt[e_]) for e_ in ("pe", "act", "dve", "pool") if P.cnt[e_] > 0]
        for e_ in ("pool", "sp", "act", "dve", "pe"):
            P.wait_all(e_, toks)
        P.emit()
    return nc


import contextlib
import numpy as np
import ml_dtypes
import concourse.bass as bass
import concourse.mybir as mybir

F32 = mybir.dt.float32
BF16 = mybir.dt.bfloat16
AF = mybir.ActivationFunctionType
ALU = mybir.AluOpType
BF = ml_dtypes.bfloat16
D = 2048
KC = 16
EPS = 1e-6
CB = 512
NCB = D // CB


def host_inputs_l2(inp, ys, adas, b, qtr, TQ):
    t0 = qtr * TQ
    ycat = np.concatenate([ys[b * 4 + j][t0:t0 + TQ, 0:256] for j in range(4)] +
                          [ys[b * 4 + j][t0:t0 + TQ, 256:512] for j in range(4)], axis=1)
    w_in = inp["w_in"][0]
    b_in = inp["b_in"][0]
    return dict(
        x2=np.ascontiguousarray(inp["x"][b, t0:t0 + TQ]),
        yin=np.ascontiguousarray(ycat),
        ada_in=np.ascontiguousarray(adas[b * 4]),
        gainT2=np.ascontiguousarray(inp["norm_gain"][0].reshape(16, 128).T),
        wmg=np.ascontiguousarray(w_in[:, 6720:10816]),
        bmg=np.ascontiguousarray(b_in[6720:10816][None, :]),
        wbrn=np.ascontiguousarray(inp["w_br_nsa"][0]),
        wbrg=np.ascontiguousarray(inp["w_br_gla"][0]),
        wout=np.ascontiguousarray(inp["w_out"][0]),
        fgain=np.ascontiguousarray(np.broadcast_to(inp["final_norm_gain"][None, :], (128, D))),
        identb2=np.eye(128).astype(BF), identf2=np.eye(128, dtype=np.float32),
    )


def build_l2(TQ):
    NTT = 4
    NST = TQ // (128 * NTT)
    nc = bass.Bass("TRN2", target_bir_lowering=False)

    def din(name, shape, dt=F32):
        return nc.dram_tensor(name, list(shape), dt, kind="ExternalInput").ap()
    x = din("x2", [TQ, D]); yin = din("yin", [TQ, D], BF16)
    adain_d = din("ada_in", [128, 48])
    gainT_d = din("gainT2", [128, 16])
    wmg_d = din("wmg", [D, 4096]); bmg_d = din("bmg", [1, 4096])
    wbrn_d = din("wbrn", [1024, D]); wbrg_d = din("wbrg", [1024, D]); wout_d = din("wout", [D, D])
    fgain_d = din("fgain", [128, D]); identb_d = din("identb2", [128, 128], BF16); identf_d = din("identf2", [128, 128])
    out_d = nc.dram_tensor("out", [TQ, D], F32, kind="ExternalOutput").ap()
    wmg_b = nc.dram_tensor("wmg_b16", [D, 4096], BF16).ap()
    wbrn_b = nc.dram_tensor("wbrn_b16", [1024, D], BF16).ap()
    wbrg_b = nc.dram_tensor("wbrg_b16", [1024, D], BF16).ap()
    wout_b = nc.dram_tensor("wout_b16", [D, D], BF16).ap()

    P = Prog(nc)
    with contextlib.ExitStack() as st:
        def sb(name, shape, dt=F32):
            return st.enter_context(nc.sbuf_tensor("s2_" + name, list(shape), dt))
        xs = [sb("xs%d" % i, [128, D]) for i in range(2)]
        hT = sb("hT", [128, KC, 128 * NTT], BF16)
        yT = sb("yT", [128, KC, 128 * NTT], BF16)
        mtok = [sb("mtok%d" % i, [128, D], BF16) for i in range(NTT)]
        xnb = mtok[NTT - 1]; ytile = [mtok[NTT - 2], mtok[NTT - 3]]
        mT = sb("mT", [128, KC, 128], BF16)
        wmgb = [sb("wmgb%d" % i, [128, 2, KC, CB], BF16) for i in range(2)]
        wbrb = [sb("wbrb%d" % i, [128, 2, 8, CB], BF16) for i in range(2)]
        woutb = [sb("woutb%d" % i, [128, KC, 128], BF16) for i in range(2)]
        bmg_s = [sb("bmg_s%d" % i, [1, 2 * CB], BF16) for i in range(2)]; ones_s = sb("ones_s", [1, 128], BF16)
        onesf = sb("onesf", [128, 128])
        gate_bc = sb("gate_bc", [128, D]); fgain_s = sb("fgain_s", [128, D])
        xo = sb("xo", [128, 512])
        A_s = sb("A_s", [128, 16]); B_s = sb("B_s", [128, 16]); ada_s = sb("ada_s", [128, 48])
        cT_s = sb("cT_s", [128, 16]); gainT_s = sb("gainT_s", [128, 16]); bada_s = sb("bada_s", [128, 48])
        small = sb("small", [128, 32]); diag = [sb("diag%d" % i, [128, 128]) for i in range(2)]
        identb = sb("identb", [128, 128], BF16); identf = sb("identf", [128, 128])
        g1 = sb("g1", [128, CB]); g2 = sb("g2", [128, CB]); t1 = sb("t1", [128, CB])

        banks = [st.enter_context(nc.psum_tensor("b2ank%d" % i, [128, 512], F32)) for i in range(8)]
        bb = [Buf("bank%d" % i) for i in range(8)]
        B = Buf
        b_w = B("w"); b_AB = B("AB"); b_xs = [B("xs0"), B("xs1")]; b_small = B("small")
        b_hT = B("hT"); b_yT = B("yT"); b_mtok = [B("mtok%d" % i) for i in range(NTT)]
        b_xnb = b_mtok[NTT - 1]; b_yt = [b_mtok[NTT - 2], b_mtok[NTT - 3]]
        b_mT = B("mT"); b_wmg = [B("wmg0"), B("wmg1")]; b_wbr = [B("wbr0"), B("wbr1")]; b_wout = [B("wout0"), B("wout1")]
        b_gate = B("gate"); b_diag = [B("d0"), B("d1")]; b_g = B("g"); b_xo = B("xo"); b_tab = B("tab")

        def ld(q, out, in_, writes):
            return P.dma(q, lambda e: e.dma_start(out=out, in_=in_), writes=writes)
        ld("sp", gainT_s[:], gainT_d, [b_AB])
        ld("sp", fgain_s[:], fgain_d, [b_w]); ld("sp", identb[:], identb_d, [b_tab]); ld("sp", identf[:], identf_d, [b_tab])
        P.op("pool", lambda e: e.memset(ones_s[:], 1.0), writes=[b_w])
        b_wc = B("wcast")
        for h2 in range(2):
            for r0 in range(0, D, 1024):
                P.dma("pool", lambda e, h2=h2, r0=r0: e.dma_start(out=wmg_b[r0:r0 + 1024, h2 * 2048:(h2 + 1) * 2048], in_=wmg_d[r0:r0 + 1024, h2 * 2048:(h2 + 1) * 2048]), writes=[b_wc])
        P.dma("pool", lambda e: e.dma_start(out=wbrn_b, in_=wbrn_d), writes=[b_wc])
        P.dma("pool", lambda e: e.dma_start(out=wbrg_b, in_=wbrg_d), writes=[b_wc])
        for r0 in range(0, D, 1024):
            P.dma("pool", lambda e, r0=r0: e.dma_start(out=wout_b[r0:r0 + 1024, :], in_=wout_d[r0:r0 + 1024, :]), writes=[b_wc])
        P.op("pool", lambda e: e.memset(onesf[:], 1.0), writes=[b_w])
        ld("sp", ada_s[:], adain_d, [b_AB])
        P.op("dve", lambda e: e.tensor_scalar(out=A_s[:], in0=ada_s[:, 16:32], scalar1=1.0, scalar2=None, op0=ALU.add), reads=[b_AB], writes=[b_AB])
        P.op("dve", lambda e: e.tensor_tensor(out=A_s[:], in0=A_s[:], in1=gainT_s[:], op=ALU.mult), reads=[b_AB], writes=[b_AB])
        P.op("dve", lambda e: e.tensor_copy(out=B_s[:], in_=ada_s[:, 0:16]), reads=[b_AB], writes=[b_AB])
        for kc in range(KC):
            d_ = kc % 2
            P.op("dve", lambda e, kc=kc, d_=d_: e.tensor_scalar(out=diag[d_][:], in0=identf[:], scalar1=ada_s[:, 32 + kc:33 + kc], scalar2=None, op0=ALU.mult),
                 reads=[b_AB, b_tab], writes=[b_diag[d_]])
            bk = 1 + kc // 4
            P.op("pe", lambda e, kc=kc, d_=d_, bk=bk: e.matmul(banks[bk][:, (kc % 4) * 128:(kc % 4) * 128 + 128], lhsT=onesf[:], rhs=diag[d_][:],
                                                               start=(kc % 4 == 0), stop=(kc % 4 == 3), skip_group_check=True),
                 reads=[b_w, b_diag[d_]], writes=[bb[bk]])
        for q4 in range(4):
            P.op("dve", lambda e, q4=q4: e.tensor_copy(out=gate_bc[:, q4 * 512:(q4 + 1) * 512], in_=banks[1 + q4][:, :]), reads=[bb[1 + q4]], writes=[b_gate])

        x_v = x.rearrange("(t p) d -> t p d", p=128)
        y_v = yin.rearrange("(t p) d -> t p d", p=128)
        o_v = out_d.rearrange("(t p) d -> t p d", p=128)
        wmg_v = wmg_b.rearrange("(kc p) n -> p kc n", p=128)
        wbrn_v = wbrn_b.rearrange("(kc p) n -> p kc n", p=128)
        wbrg_v = wbrg_b.rearrange("(kc p) n -> p kc n", p=128)
        wout_v = wout_b.rearrange("(kc p) n -> p kc n", p=128)
        rr = {"mm": 0, "w": 0, "wo": 0, "x": 0, "ob": 0}

        def mmbank():
            i = rr["ob"] % 4
            rr["ob"] += 1
            return i

        for ST in range(NST):
            for tt in range(NTT):
                t = ST * NTT + tt
                xb_ = rr["x"] % 2
                rr["x"] += 1
                ld("sp", xs[xb_][:], x_v[t], [b_xs[xb_]])
                ld("sp", ytile[tt % 2][:], y_v[t], [b_yt[tt % 2]])
                P.op("act", lambda e, xb_=xb_, tt=tt: e.activation(out=xnb[:], in_=xs[xb_][:], func=AF.Square, accum_out=small[:, tt % 4:tt % 4 + 1]),
                     reads=[b_xs[xb_]], writes=[b_xnb, b_small])
                P.op("act", lambda e, tt=tt: e.activation(out=small[:, 8 + tt % 4:9 + tt % 4], in_=small[:, tt % 4:tt % 4 + 1], func=AF.Ln, scale=1.0 / D, bias=EPS),
                     reads=[b_small], writes=[b_small])
                P.op("act", lambda e, tt=tt: e.activation(out=small[:, 16 + tt % 4:17 + tt % 4], in_=small[:, 8 + tt % 4:9 + tt % 4], func=AF.Exp, scale=-0.5),
                     reads=[b_small], writes=[b_small])
                P.op("dve", lambda e, xb_=xb_, tt=tt: e.tensor_scalar(out=xnb[:], in0=xs[xb_][:], scalar1=small[:, 16 + tt % 4:17 + tt % 4], scalar2=None, op0=ALU.mult),
                     reads=[b_xs[xb_], b_small], writes=[b_xnb])
                for src_i, (src, sbuf_, dst, dbuf) in enumerate(((xnb, b_xnb, hT, b_hT), (ytile[tt % 2], b_yt[tt % 2], yT, b_yT))):
                    for half in range(2):
                        bk = 4 + 2 * src_i + half
                        pv = banks[bk][:].bitcast(BF16)
                        for c8 in range(8):
                            kc = half * 8 + c8
                            P.op("pe", lambda e, pv=pv, c8=c8, kc=kc, src=src: e.transpose(pv[:, c8 * 128:(c8 + 1) * 128], src[:, kc * 128:(kc + 1) * 128], identb[:]),
                                 reads=[sbuf_, b_tab], writes=[bb[bk]])
                        if src_i == 0:
                            for c8 in range(8):
                                kc = half * 8 + c8
                                P.op("dve", lambda e, pv=pv, c8=c8, kc=kc, tt=tt: e.tensor_scalar(
                                    out=hT[:, kc, tt * 128:(tt + 1) * 128], in0=pv[:, c8 * 128:(c8 + 1) * 128],
                                    scalar1=A_s[:, kc:kc + 1], scalar2=B_s[:, kc:kc + 1], op0=ALU.mult, op1=ALU.add),
                                    reads=[bb[bk], b_AB], writes=[b_hT])
                        else:
                            P.op("dve", lambda e, pv=pv, half=half, tt=tt: e.tensor_copy(
                                out=yT[:, half * 8:half * 8 + 8, tt * 128:(tt + 1) * 128], in_=pv.rearrange("p (c q) -> p c q", q=128)),
                                reads=[bb[bk]], writes=[b_yT])
            for cb in range(NCB):
                wb = rr["w"] % 2
                rr["w"] += 1
                c0 = cb * CB
                P.dma("sp", lambda e, wb=wb, c0=c0: e.dma_start(out=wmgb[wb][:, 0, :, :], in_=wmg_v[:, :, c0:c0 + CB]), reads=[b_wc], writes=[b_wmg[wb]])
                P.dma("sp", lambda e, wb=wb, c0=c0: e.dma_start(out=wmgb[wb][:, 1, :, :], in_=wmg_v[:, :, 2048 + c0:2048 + c0 + CB]), reads=[b_wc], writes=[b_wmg[wb]])
                P.dma("sp", lambda e, wb=wb, c0=c0: e.dma_start(out=wbrb[wb][:, 0, :, :], in_=wbrn_v[:, :, c0:c0 + CB]), reads=[b_wc], writes=[b_wbr[wb]])
                P.dma("sp", lambda e, wb=wb, c0=c0: e.dma_start(out=wbrb[wb][:, 1, :, :], in_=wbrg_v[:, :, c0:c0 + CB]), reads=[b_wc], writes=[b_wbr[wb]])
                ld("pool", bmg_s[wb][:, 0:CB], bmg_d[:, c0:c0 + CB], [b_wmg[wb]])
                ld("pool", bmg_s[wb][:, CB:2 * CB], bmg_d[:, 2048 + c0:2048 + c0 + CB], [b_wmg[wb]])
                for tt in range(NTT):
                    tsl = slice(tt * 128, tt * 128 + 128)
                    base = 4 * (rr["mm"] % 2)
                    rr["mm"] += 1
                    for gi in range(2):
                        bk = base + gi
                        for kc in range(KC):
                            P.op("pe", lambda e, bk=bk, gi=gi, kc=kc, tsl=tsl, wb=wb: e.matmul(
                                banks[bk][:, :], lhsT=hT[:, kc, tsl], rhs=wmgb[wb][:, gi, kc, :], start=(kc == 0), stop=False),
                                reads=[b_hT, b_wmg[wb]], writes=[bb[bk]])
                        P.op("pe", lambda e, bk=bk, gi=gi, wb=wb: e.matmul(
                            banks[bk][:, :], lhsT=ones_s[0:1, :], rhs=bmg_s[wb][0:1, gi * CB:(gi + 1) * CB], start=False, stop=True),
                            reads=[b_w, b_wmg[wb]], writes=[bb[bk]], mode="r32")
                    for gi in range(2):
                        bk = base + 2 + gi
                        for kc in range(8):
                            P.op("pe", lambda e, bk=bk, gi=gi, kc=kc, tsl=tsl, wb=wb: e.matmul(
                                banks[bk][:, :], lhsT=yT[:, gi * 8 + kc, tsl], rhs=wbrb[wb][:, gi, kc, :], start=(kc == 0), stop=(kc == 7)),
                                reads=[b_yT, b_wbr[wb]], writes=[bb[bk]])
                    P.op("act", lambda e, base=base: e.activation(out=g1[:], in_=banks[base][:, :], func=AF.Exp, scale=-1.0), reads=[bb[base]], writes=[b_g])
                    P.op("act", lambda e, base=base: e.activation(out=g2[:], in_=banks[base + 1][:, :], func=AF.Exp, scale=-1.0), reads=[bb[base + 1]], writes=[b_g])
                    P.op("dve", lambda e: e.tensor_scalar(out=g1[:], in0=g1[:], scalar1=1.0, scalar2=None, op0=ALU.add), reads=[b_g], writes=[b_g])
                    P.op("dve", lambda e: e.tensor_scalar(out=g2[:], in0=g2[:], scalar1=1.0, scalar2=None, op0=ALU.add), reads=[b_g], writes=[b_g])
                    P.op("dve", lambda e: e.reciprocal(out=g1[:], in_=g1[:]), reads=[b_g], writes=[b_g])
                    P.op("dve", lambda e: e.reciprocal(out=g2[:], in_=g2[:]), reads=[b_g], writes=[b_g])
                    P.op("dve", lambda e, base=base: e.tensor_tensor(out=t1[:], in0=banks[base + 2][:, :], in1=g1[:], op=ALU.mult), reads=[bb[base + 2], b_g], writes=[b_g])
                    P.op("dve", lambda e, base=base: e.tensor_tensor(out=g2[:], in0=banks[base + 3][:, :], in1=g2[:], op=ALU.mult), reads=[bb[base + 3], b_g], writes=[b_g])
                    P.op("dve", lambda e, tt=tt, c0=c0: e.tensor_tensor(out=mtok[tt][:, c0:c0 + CB], in0=t1[:], in1=g2[:], op=ALU.add), reads=[b_g], writes=[b_mtok[tt]])
            for tt in range(NTT):
                t = ST * NTT + tt
                for half in range(2):
                    bk = 4 + half
                    pv = banks[bk][:].bitcast(BF16)
                    for c8 in range(8):
                        kc = half * 8 + c8
                        P.op("pe", lambda e, pv=pv, c8=c8, kc=kc, tt=tt: e.transpose(pv[:, c8 * 128:(c8 + 1) * 128], mtok[tt][:, kc * 128:(kc + 1) * 128], identb[:]),
                             reads=[b_mtok[tt], b_tab], writes=[bb[bk]])
                    P.op("dve", lambda e, pv=pv, half=half: e.tensor_copy(out=mT[:, half * 8:half * 8 + 8, :], in_=pv.rearrange("p (c q) -> p c q", q=128)),
                         reads=[bb[bk]], writes=[b_mT])
                xb_ = rr["x"] % 2
                rr["x"] += 1
                ld("sp", xs[xb_][:], x_v[t], [b_xs[xb_]])
                for ob in range(D // 128):
                    wo = rr["wo"] % 2
                    rr["wo"] += 1
                    P.dma("sp", lambda e, wo=wo, ob=ob: e.dma_start(out=woutb[wo][:], in_=wout_v[:, :, ob * 128:(ob + 1) * 128]), reads=[b_wc], writes=[b_wout[wo]])
                    if ob % 4 == 0:
                        bk = mmbank()
                    oc = (ob % 4) * 128
                    for kc in range(KC):
                        P.op("pe", lambda e, bk=bk, kc=kc, wo=wo, oc=oc, ob=ob: e.matmul(banks[bk][:, oc:oc + 128], lhsT=mT[:, kc, :], rhs=woutb[wo][:, kc, :],
                                                                                 start=(kc == 0 and ob % 4 == 0), stop=(kc == KC - 1), skip_group_check=True),
                             reads=[b_mT, b_wout[wo]], writes=[bb[bk]])
                    if ob % 4 == 3:
                        osl = slice((ob // 4) * 512, (ob // 4) * 512 + 512)
                        P.op("dve", lambda e, bk=bk, osl=osl: e.tensor_tensor(out=xo[:, :], in0=banks[bk][:, :], in1=gate_bc[:, osl], op=ALU.mult),
                             reads=[bb[bk], b_gate], writes=[b_xo])
                        P.op("dve", lambda e, osl=osl, xb_=xb_: e.tensor_tensor(out=xs[xb_][:, osl], in0=xo[:, :], in1=xs[xb_][:, osl], op=ALU.add),
                             reads=[b_xo, b_xs[xb_]], writes=[b_xs[xb_]])
                P.op("act", lambda e, xb_=xb_, tt=tt: e.activation(out=hT[:, 0:4, :], in_=xs[xb_][:].rearrange("p (a b) -> p a b", a=4), func=AF.Square, accum_out=small[:, 24:25]),
                     reads=[b_xs[xb_]], writes=[b_hT, b_small])
                P.op("act", lambda e: e.activation(out=small[:, 25:26], in_=small[:, 24:25], func=AF.Ln, scale=1.0 / D, bias=EPS), reads=[b_small], writes=[b_small])
                P.op("act", lambda e: e.activation(out=small[:, 26:27], in_=small[:, 25:26], func=AF.Exp, scale=-0.5), reads=[b_small], writes=[b_small])
                P.op("dve", lambda e, xb_=xb_: e.scalar_tensor_tensor(out=xs[xb_][:], in0=xs[xb_][:], scalar=small[:, 26:27], in1=fgain_s[:], op0=ALU.mult, op1=ALU.mult),
                     reads=[b_small, b_w], writes=[b_xs[xb_]])
                P.dma("act", lambda e, xb_=xb_, t=t: e.dma_start(out=o_v[t], in_=xs[xb_][:]), reads=[b_xs[xb_]])
        toks = [t_ for t_ in P.dma_last if t_ is not None]
        toks += [(e_, P.cnt[e_]) for e_ in ("pe", "act", "dve", "pool") if P.cnt[e_] > 0]
        for e_ in ("pool", "sp", "act", "dve", "pe"):
            P.wait_all(e_, toks)
        P.emit()
    return nc


def kernel(**inputs):
    from concourse.bass_utils import run_bass_kernel_spmd
    inp = {k_: np.asarray(v) for k_, v in inputs.items()}
    Bn, T, Dm = inp["x"].shape
    nc1 = build_l1(T)
    in_maps = [host_inputs_l1(inp, c // 4, c % 4, T) for c in range(8)]
    res = run_bass_kernel_spmd(nc1, in_maps, core_ids=list(range(8)))
    ys = [np.asarray(r["y"]) for r in res.results]
    adas = [np.asarray(r["ada_out"]) for r in res.results]
    del in_maps
    TQ = T // 4
    nc2 = build_l2(TQ)
    in_maps2 = [host_inputs_l2(inp, ys, adas, c // 4, c % 4, TQ) for c in range(8)]
    res2 = run_bass_kernel_spmd(nc2, in_maps2, core_ids=list(range(8)))
    out = np.empty((Bn, T, Dm), np.float32)
    for c in range(8):
        out[c // 4, (c % 4) * TQ:(c % 4 + 1) * TQ] = np.asarray(res2.results[c]["out"])
    return out
```

```python
import contextlib

ENGS = ("pe", "act", "dve", "pool", "sp")
USE_MODE_DRAINS = False
N_DMA_SEMS = 28


class Buf:
    __slots__ = ("name", "w", "r")

    def __init__(self, name=""):
        self.name = name
        self.w = {}
        self.r = {}


class Prog:
    def __init__(self, nc):
        self.nc = nc
        self.streams = {e: [] for e in ENGS}
        self.cnt = {e: 0 for e in ENGS}
        self.known = {e: {} for e in ENGS}
        self.dma_i = 0
        self.dma_ip = 0
        self.dma_cnt = [0] * N_DMA_SEMS
        self.dma_last = [None] * N_DMA_SEMS
        self.n_ops = 0

    def _deps(self, reads, writes):
        deps = {}

        def add(d):
            for k, v in d.items():
                if deps.get(k, -1) < v:
                    deps[k] = v
        for b in reads:
            add(b.w)
        for b in writes:
            add(b.w)
            add(b.r)
        return deps

    def _commit(self, tok, reads, writes):
        k, v = tok
        for b in reads:
            b.r[k] = v
        for b in writes:
            b.w = {k: v}
            b.r = {}

    def _waits(self, eng, deps):
        out = []
        kn = self.known[eng]
        for k, v in deps.items():
            if eng == "pe" and k == "pe":
                continue
            if kn.get(k, -1) >= v:
                continue
            kn[k] = v
            out.append((k, v))
        return out

    def op(self, eng, fn, reads=(), writes=(), mode="full"):
        deps = self._deps(reads, writes)
        waits = self._waits(eng, deps)
        if eng == "pe":
            if USE_MODE_DRAINS and mode != getattr(self, "pe_mode", mode) and self.cnt["pe"] > 0:
                v = self.cnt["pe"]
                if self.known["pe"].get("pe", -1) < v:
                    self.known["pe"]["pe"] = v
                    waits.append(("pe", v))
            self.pe_mode = mode
        self.cnt[eng] += 1
        tok = (eng, self.cnt[eng])
        self.streams[eng].append((waits, fn, tok))
        self._commit(tok, reads, writes)
        self.n_ops += 1
        return tok

    def dma(self, q, fn, reads=(), writes=()):
        deps = self._deps(reads, writes)
        if q == "pool":
            s = self.dma_ip % 10
            self.dma_ip += 1
        else:
            s = 10 + self.dma_i % (N_DMA_SEMS - 10)
            self.dma_i += 1
        if self.dma_last[s] is not None:
            k, v = self.dma_last[s]
            if deps.get(k, -1) < v:
                deps[k] = v
        waits = self._waits(q, deps)
        self.dma_cnt[s] += 16
        tok = (("dma", s), self.dma_cnt[s])
        self.dma_last[s] = tok
        if q == "pool":
            self.last_pool = tok
        self.streams[q].append((waits, fn, tok))
        self._commit(tok, reads, writes)
        self.n_ops += 1
        return tok

    def wait_all(self, eng, toks):
        deps = {}
        for k, v in toks:
            if deps.get(k, -1) < v:
                deps[k] = v
        waits = self._waits(eng, deps)
        self.streams[eng].append((waits, None, None))

    def emit(self):
        nc = self.nc
        with contextlib.ExitStack() as st:
            sems = {}
            for e in ENGS:
                sems[e] = st.enter_context(nc.semaphore("s_" + e))
            for i in range(N_DMA_SEMS):
                sems[("dma", i)] = st.enter_context(nc.semaphore("s_dma%d" % i))
            block = st.enter_context(nc.Block())
            objs = {"pe": block.tensor, "act": block.scalar, "dve": block.vector,
                    "pool": block.gpsimd, "sp": block.sync}

            needed = {e: set() for e in ENGS}
            for e in ENGS:
                for (waits, fn, tok) in self.streams[e]:
                    for (k, v) in waits:
                        if not isinstance(k, tuple):
                            needed[k].add(v)
            rank = {e: {v: i + 1 for i, v in enumerate(sorted(needed[e]))} for e in ENGS}
            self.n_incs = {e: len(needed[e]) for e in ENGS}

            def mk(e):
                def body(eng):
                    for (waits, fn, tok) in self.streams[e]:
                        for (k, v) in waits:
                            if isinstance(k, tuple):
                                eng.wait_ge(sems[k], v)
                            else:
                                eng.wait_ge(sems[k], rank[k][v])
                        if fn is None:
                            continue
                        ins = fn(eng)
                        k, v = tok
                        if isinstance(k, tuple):
                            ins.then_inc(sems[k], 16)
                        elif v in needed[k]:
                            ins.then_inc(sems[k], 1)
                return body
            for e in ENGS:
                if self.streams[e]:
                    objs[e](mk(e))


import contextlib
import numpy as np
import ml_dtypes
import concourse.bass as bass
import concourse.mybir as mybir

F32 = mybir.dt.float32
BF16 = mybir.dt.bfloat16
AF = mybir.ActivationFunctionType
ALU = mybir.AluOpType
BF = ml_dtypes.bfloat16

D = 2048
KC = 16
NEGM = -30000.0
BIG = 1.0e30
EPS = 1e-6
NFC = 10
FCH_M = [64, 64, 64, 64, 128, 64, 64, 128, 128, 16]
FCH_OFF = [0, 64, 128, 192, 256, 384, 448, 512, 640, 768]
NF = 784
NTA = 396
NTB = 512
NT_ = NTA + NTB


def split3(a):
    a = np.asarray(a, np.float64)
    hi = a.astype(BF).astype(np.float64)
    mid = (a - hi).astype(BF).astype(np.float64)
    lo = (a - hi - mid).astype(BF).astype(np.float64)
    return hi.astype(BF), mid.astype(BF), lo.astype(BF)


def host_tables(T, g):
    slopes = 2.0 ** (-8.0 * np.arange(1, 17, dtype=np.float64) / 16)[4 * g:4 * g + 4]
    t = np.arange(T, dtype=np.float64)
    qaug = np.zeros((8, 4, T), BF)
    for h in range(4):
        a, b_, c_ = split3(-slopes[h] * t)
        qaug[0, h], qaug[1, h], qaug[2, h] = a, b_, c_
        a, b_, c_ = split3(np.full(T, 128.0 * slopes[h]))
        qaug[3, h], qaug[4, h], qaug[5, h] = a, b_, c_
        a, b_, c_ = split3(np.full(T, slopes[h]))
        qaug[6, h], qaug[7, h] = a, b_

    def kside(pos):
        k = np.zeros((8, len(pos)), BF)
        k[0:3] = 1.0
        k[3:6] = (pos // 128).astype(BF)
        k[6:8] = (pos % 128).astype(BF)
        return k
    kaug = kside(np.arange(T))
    ncp = np.arange(512)
    cend = np.maximum(16 * (ncp - 1) + 31, 0)
    kcaug = kside(cend)
    M = np.zeros((512, 128), np.float32)
    w = {-1: 1.0, 0: 2.0, 1: 2.0, 2: 2.0, 3: 1.0}
    for npr in range(1, 512):
        n = npr - 1
        for blk in range(128):
            dd = n - 4 * blk
            if dd in w:
                M[npr, blk] = w[dd]
    Mtab = M.reshape(4, 128, 128).transpose(1, 0, 2).astype(BF)
    E4 = np.zeros((128, 32, 128), BF)
    for m in range(32):
        for half in range(2):
            r = 2 * m + half
            for gq in range(2):
                E4[64 * gq + r, m, 64 * half:64 * half + 64] = 1.0
    p = np.arange(128)[:, None]
    q = np.arange(128)[None, :]
    bias_diag = np.where(q - p >= 0, 0.0, NEGM).astype(BF)
    bias_far = np.where(p - q - 1 >= 0, 0.0, NEGM).astype(BF)
    bias_diag = np.tile(bias_diag, (1, 4))
    bias_far = np.tile(bias_far, (1, 4))
    gmask = np.where(q - p >= 0, 1.0, 0.0).astype(BF)
    reset = np.ones((128, 512), BF)
    reset[:, ::128] = 0.0
    qaug = np.ascontiguousarray(qaug.reshape(8, 4, T // 128, 128).transpose(0, 2, 1, 3))
    return dict(qaug=qaug, kaug=kaug, kcaug=kcaug, Mtab=Mtab, E4=E4,
                bias_diag=bias_diag, bias_far=bias_far, gmask=gmask, reset=reset,
                identb=np.eye(128).astype(BF), identf=np.eye(128, dtype=np.float32))


def host_inputs_l1(inp, b, j, T):
    g = j
    x = np.ascontiguousarray(inp["x"][b, :T])
    w_in = inp["w_in"][0]
    b_in = inp["b_in"][0]
    offs = np.cumsum([0, 1024, 256, 256, 256, 256, 256, 256, 48, 1024, 512, 512, 1024, 16, 1024, 2048, 2048])
    o_q, o_ck, o_cv, o_sk, o_sv, o_wk, o_wv, o_g, o_z, o_gq, o_gk, o_gv, o_ga, o_gz = offs[:14]
    fcols = np.concatenate([
        np.arange(o_q + 256 * g, o_q + 256 * g + 256),
        np.arange(o_ck + 64 * g, o_ck + 64 * g + 64), np.arange(o_cv + 64 * g, o_cv + 64 * g + 64),
        np.arange(o_sk + 64 * g, o_sk + 64 * g + 64), np.arange(o_wk + 64 * g, o_wk + 64 * g + 64),
        np.arange(o_gq + 128 * j, o_gq + 128 * j + 128), np.arange(o_gk + 128 * j, o_gk + 128 * j + 128),
        np.arange(o_ga, o_ga + 16)])
    tcols = np.concatenate([
        np.arange(o_sv + 64 * g, o_sv + 64 * g + 64), np.arange(o_wv + 64 * g, o_wv + 64 * g + 64),
        np.arange(o_g + 12 * g, o_g + 12 * g + 12), np.arange(o_z + 256 * g, o_z + 256 * g + 256),
        np.arange(o_gv + 256 * j, o_gv + 256 * j + 256), np.arange(o_gz + 256 * j, o_gz + 256 * j + 256)])
    assert len(fcols) == NF and len(tcols) == NT_
    wf = np.ascontiguousarray(w_in[:, fcols])
    wt = np.ascontiguousarray(w_in[:, tcols])
    bf = np.zeros((128, NFC), np.float32)
    bfl = b_in[fcols]
    for ci in range(NFC):
        bf[:FCH_M[ci], ci] = bfl[FCH_OFF[ci]:FCH_OFF[ci] + FCH_M[ci]]
    bt = np.ascontiguousarray(b_in[tcols][None, :])
    w1kv = np.concatenate([inp["cmp_w1_k"][0].reshape(32, 64, 256).transpose(1, 0, 2),
                           inp["cmp_w1_v"][0].reshape(32, 64, 256).transpose(1, 0, 2)], axis=0)
    w2k = inp["cmp_w2_k"][0].reshape(2, 128, 64).transpose(1, 0, 2)
    w2v = inp["cmp_w2_v"][0].reshape(2, 128, 64).transpose(1, 0, 2)
    posT = np.concatenate([inp["cmp_pos_k"][0].T, inp["cmp_pos_v"][0].T], axis=0)
    d = dict(
        x=x,
        cT=np.ascontiguousarray(inp["c"][b].reshape(16, 128).T),
        w_ada=np.ascontiguousarray(inp["w_ada"][0]),
        b_ada=np.ascontiguousarray(inp["b_ada"][0].reshape(48, 128).T),
        gainT=np.ascontiguousarray(inp["norm_gain"][0].reshape(16, 128).T),
        wf=wf, wt=wt, bf=bf, bt=bt,
        w1kv=np.ascontiguousarray(w1kv), w2k=np.ascontiguousarray(w2k), w2v=np.ascontiguousarray(w2v),
        posT=np.ascontiguousarray(posT),
        walpha=np.ascontiguousarray(inp["gla_w_alpha"][0][:, 128 * j:128 * j + 128]),
        balpha=np.ascontiguousarray(inp["gla_b_alpha"][0][128 * j:128 * j + 128].reshape(128, 1)),
        ggain=np.ascontiguousarray(np.broadcast_to(inp["gla_norm_gain"][0][None, :], (128, 256))),
    )
    d.update(host_tables(T, g))
    return d


class _Stop(Exception):
    pass


def build_l1(T, debug=False, stage=99):
    NST = T // 512
    NQT = T // 128
    nc = bass.Bass("TRN2", target_bir_lowering=False)

    def din(name, shape, dt=F32):
        return nc.dram_tensor(name, list(shape), dt, kind="ExternalInput").ap()
    x = din("x", [T, D])
    cT_d = din("cT", [128, 16]); wada_d = din("w_ada", [D, 6144]); bada_d = din("b_ada", [128, 48])
    gainT_d = din("gainT", [128, 16])
    wf_d = din("wf", [D, NF]); wt_d = din("wt", [D, NT_]); bf_d = din("bf", [128, NFC]); bt_d = din("bt", [1, NT_])
    w1kv_d = din("w1kv", [128, 32, 256]); w2k_d = din("w2k", [128, 2, 64]); w2v_d = din("w2v", [128, 2, 64])
    posT_d = din("posT", [128, 32])
    walpha_d = din("walpha", [16, 128]); balpha_d = din("balpha", [128, 1]); ggain_d = din("ggain", [128, 256])
    qaug_d = din("qaug", [8, T // 128, 4, 128], BF16); kaug_d = din("kaug", [8, T], BF16); kcaug_d = din("kcaug", [8, 512], BF16)
    Mtab_d = din("Mtab", [128, 4, 128], BF16); E4_d = din("E4", [128, 32, 128], BF16)
    bdiag_d = din("bias_diag", [128, 512], BF16); bfar_d = din("bias_far", [128, 512], BF16)
    gmask_d = din("gmask", [128, 128], BF16); reset_d = din("reset", [128, 512], BF16)
    identb_d = din("identb", [128, 128], BF16); identf_d = din("identf", [128, 128])
    y_d = nc.dram_tensor("y", [T, 512], BF16, kind="ExternalOutput").ap()
    adao_d = nc.dram_tensor("ada_out", [128, 48], F32, kind="ExternalOutput").ap()
    dbg = {}
    if debug:
        dbg["qT"] = nc.dram_tensor("dbg_qT", [72, 4, 4, 128], BF16, kind="ExternalOutput").ap()
        dbg["tokA"] = nc.dram_tensor("dbg_tokA", [128, NTA], F32, kind="ExternalOutput").ap()
        dbg["ocmp"] = nc.dram_tensor("dbg_ocmp", [128, 4, 193], BF16, kind="ExternalOutput").ap()
        dbg["selb"] = nc.dram_tensor("dbg_selb", [128, 128], BF16, kind="ExternalOutput").ap()
        dbg["kcT"] = nc.dram_tensor("dbg_kcT", [72, 512], BF16, kind="ExternalOutput").ap()
        dbg["oT0"] = nc.dram_tensor("dbg_oT0", [65, 512], F32, kind="ExternalOutput").ap()
        dbg["oT1"] = nc.dram_tensor("dbg_oT1", [65, 512], F32, kind="ExternalOutput").ap()
        dbg["acc"] = nc.dram_tensor("dbg_acc", [128, 256], F32, kind="ExternalOutput").ap()
        dbg["pT"] = nc.dram_tensor("dbg_pT", [128, 512], BF16, kind="ExternalOutput").ap()
        dbg["den"] = nc.dram_tensor("dbg_den", [128, 12], F32, kind="ExternalOutput").ap()
        dbg["gsig"] = nc.dram_tensor("dbg_gsig", [128, 12], F32, kind="ExternalOutput").ap()
        dbg["coef"] = nc.dram_tensor("dbg_coef", [128, 12], F32, kind="ExternalOutput").ap()

    P = Prog(nc)
    with contextlib.ExitStack() as st:
        def sb(name, shape, dt=F32):
            return st.enter_context(nc.sbuf_tensor("sb_" + name, list(shape), dt))
        wf_s = sb("wf_s", [128, KC, NF], BF16); wt_s = sb("wt_s", [128, KC, NT_], BF16)
        bf_s = sb("bf_s", [128, NFC]); btb_s = sb("btb_s", [1, NT_], BF16)
        ones_s = sb("ones_s", [1, 128], BF16)
        xs = [sb("xs%d" % i, [128, D]) for i in range(2)]
        xnb = [sb("xnb%d" % i, [128, D], BF16) for i in range(2)]
        hT = sb("hT", [128, KC, 512], BF16)
        A_s = sb("A_s", [128, 16]); B_s = sb("B_s", [128, 16]); ada_s = sb("ada_s", [128, 48])
        cT_s = sb("cT_s", [128, 16]); gainT_s = sb("gainT_s", [128, 16]); bada_s = sb("bada_s", [128, 48])
        small = sb("small", [128, 64])
        qT = sb("qT", [128, 4, 4, 128], BF16)
        kslcT = sb("kslcT", [128, T], BF16)
        kwinT = sb("kwinT", [128, 1024], BF16)
        vslc = sb("vslc", [128, NQT, 66], BF16)
        vwin = sb("vwin", [128, 8, 66], BF16)
        cmpbuf = sb("cmpbuf", [128, 528], BF16)
        w1kv_s = sb("w1kv_s", [128, 32, 256], BF16)
        w2k_s = sb("w2k_s", [128, 2, 64], BF16); w2v_s = sb("w2v_s", [128, 2, 64], BF16)
        posT_s = sb("posT_s", [128, 32], BF16); posb_s = sb("posb_s", [128, 4])
        kcT = sb("kcT", [128, 512], BF16)
        vcM = sb("vcM", [128, 4, 194], BF16)
        E4_s = sb("E4_s", [128, 32, 128], BF16)
        bdiag_s = sb("bdiag_s", [128, 512], BF16); bfar_s = sb("bfar_s", [128, 512], BF16)
        gmask_s = sb("gmask_s", [128, 128], BF16); reset_s = sb("reset_s", [128, 512], BF16)
        identb = sb("identb", [128, 128], BF16); identf = sb("identf", [128, 128])
        walpha_s = sb("walpha_s", [16, 128], BF16); nbalpha_s = sb("nbalpha_s", [128, 1]); ggain_s = sb("ggain_s", [128, 256])
        gqT = sb("gqT", [128, 512], BF16); gkT = sb("gkT", [128, 512], BF16); gaT = sb("gaT", [16, 512], BF16)
        tokA = [sb("tokA%d" % i, [128, NTA]) for i in range(4)]
        tokB = [sb("tokB%d" % i, [128, 256]) for i in range(4)]
        gvb = [sb("gvb%d" % i, [128, 256], BF16) for i in range(4)]
        e1 = sb("e1", [128, 512]); spl = e1; cs = sb("cs", [128, 512])
        eb = sb("eb", [128, 512]); enb = e1
        qtl = sb("qtl", [128, 512], BF16); ktl = sb("ktl", [128, 512], BF16)
        ktok = sb("ktok", [128, 128], BF16); atm = sb("atm", [128, 128], BF16)
        S_s = sb("S_s", [128, 256]); Sb_s = sb("Sb_s", [128, 256], BF16); stmp = sb("stmp", [128, 256])
        gl1 = sb("gl1", [128, 256]); junk = gl1
        yt = [sb("yt%d" % i, [128, 512], BF16) for i in range(4)]
        hsb = sb("hsb", [128, 4, 32]); hex_ = sb("hex_", [128, 4, 32]); hidT = sb("hidT", [128, 4, 32], BF16)
        vcst = sb("vcst", [32, 64], BF16)
        pT = [sb("pT%d" % i, [128, 512], BF16) for i in range(3)]
        maskc = [sb("maskc%d" % i, [128, 512], BF16) for i in range(2)]
        zeros_b = sb("zeros_b", [128, 512], BF16)
        score = sb("score", [128, 128]); swork = sb("swork", [128, 128]); m8 = sb("m8", [128, 16])
        selb = [sb("selb%d" % i, [128, 128], BF16) for i in range(2)]; selT4 = [[sb("selT4%d_%d" % (i, g_), [128, 4, 128], BF16) for g_ in range(2)] for i in range(2)]
        oT_sb = None
        coef = [sb("coef%d" % i, [128, 12]) for i in range(2)]; den = [sb("den%d" % i, [128, 12]) for i in range(2)]; gsig = [sb("gsig%d" % i, [128, 12]) for i in range(4)]
        szn = gl1; acc = [sb("acc%d" % i, [128, 256]) for i in range(2)]

        banks = [st.enter_context(nc.psum_tensor("bank%d" % i, [128, 512], F32)) for i in range(8)]
        bb = [Buf("bank%d" % i) for i in range(8)]

        def B(name):
            return Buf(name)
        b_w = B("weights"); b_tab = B("tables"); b_hT = B("hT"); b_AB = B("AB")
        b_xs = [B("xs0"), B("xs1")]; b_xnb = [B("xnb0"), B("xnb1")]; b_small = B("small"); b_sm = [B("sm%d" % i) for i in range(4)]
        b_qT = B("qT"); b_kslc = [B("kslc%d" % i) for i in range(NQT)]
        b_kwin = [B("kwin%d" % i) for i in range(8)]; b_vslc = [B("vslc%d" % i) for i in range(NQT)]
        b_vwin = [B("vwin%d" % i) for i in range(8)]
        b_cmpbuf = B("cmpbuf"); b_kcT = B("kcT"); b_vcM = B("vcM"); b_hid = B("hid")
        b_gT = B("gT"); b_tokA = [B("tokA%d" % i) for i in range(4)]; b_tokB = [B("tokB%d" % i) for i in range(4)]
        b_gvb = [B("gvb%d" % i) for i in range(4)]
        b_gl = B("glatemps"); b_S = B("S"); b_glc = B("glachunk"); b_yt = [B("yt%d" % i) for i in range(4)]
        b_pT = [B("pT%d" % i) for i in range(3)]; b_maskc = [B("maskc0"), B("maskc1")]
        b_sel = B("sel"); b_selT = [B("selT0"), B("selT1")]; b_selb = [B("selb0"), B("selb1")]; b_oT = B("oTsb"); b_fin = B("fin"); b_finA = [B("finA0"), B("finA1")]; b_gs = [B("gs%d" % i) for i in range(4)]

        def chk(k):
            if stage == k:
                raise _Stop()
        try:
            def ld(q, out, in_, writes):
                return P.dma(q, lambda e: e.dma_start(out=out, in_=in_), writes=writes)
            wf_v = wf_d.rearrange("(kc p) n -> p kc n", p=128)
            wt_v = wt_d.rearrange("(kc p) n -> p kc n", p=128)
            for k4 in range(0, KC, 4):
                ld("pool", wf_s[:, k4:k4 + 4, :], wf_v[:, k4:k4 + 4, :], [b_w])
                ld("pool", wt_s[:, k4:k4 + 4, :], wt_v[:, k4:k4 + 4, :], [b_w])
            for pp in range(0, 32, 8):
                ld("pool", w1kv_s[:, pp:pp + 8, :], w1kv_d[:, pp:pp + 8, :], [b_w])
            ld("pool", w2k_s[:], w2k_d, [b_w]); ld("pool", w2v_s[:], w2v_d, [b_w])
            ld("pool", posT_s[:], posT_d, [b_w]); ld("pool", walpha_s[:], walpha_d, [b_w])
            ld("sp", bf_s[:], bf_d, [b_w]); ld("pool", btb_s[:], bt_d, [b_w])
            ld("sp", cT_s[:], cT_d, [b_AB]); ld("sp", gainT_s[:], gainT_d, [b_AB]); ld("sp", bada_s[:], bada_d, [b_AB])
            ld("sp", nbalpha_s[:], balpha_d, [b_w]); ld("sp", ggain_s[:], ggain_d, [b_w])
            ld("sp", E4_s[:], E4_d, [b_tab]); ld("sp", bdiag_s[:], bdiag_d, [b_tab]); ld("sp", bfar_s[:], bfar_d, [b_tab])
            ld("sp", gmask_s[:], gmask_d, [b_tab]); ld("sp", reset_s[:], reset_d, [b_tab])
            ld("sp", identb[:], identb_d, [b_tab]); ld("sp", identf[:], identf_d, [b_tab])
            ld("sp", vcM[:, :, 65:193], Mtab_d, [b_vcM])
            for i in range(NQT):
                pass
            P.op("pool", lambda e: e.memset(ones_s[:], 1.0), writes=[b_w])
            P.op("pool", lambda e: e.memset(zeros_b[:], 0.0), writes=[b_tab])
            for par_ in range(2):
                for g_ in range(2):
                    P.op("pool", lambda e, par_=par_, g_=g_: e.memset(selT4[par_][g_][:], 0.0), writes=[b_selT[par_]])
            P.op("pool", lambda e: e.memset(cmpbuf[:], 0.0), writes=[b_cmpbuf])
            P.op("pool", lambda e: e.memset(S_s[:], 0.0), writes=[b_S])
            P.op("pool", lambda e: e.memset(Sb_s[:], 0.0), writes=[b_S])
            P.op("pool", lambda e: e.memset(vcM[:, :, 0:64], 0.0), writes=[b_vcM])
            P.op("pool", lambda e: e.memset(vcM[:, :, 64:65], 1.0), writes=[b_vcM])
            P.op("pool", lambda e: e.memset(kcT[:, :], 0.0), writes=[b_kcT])
            ld("sp", kcT[64:72, :], kcaug_d, [b_kcT])
            P.op("pool", lambda e: e.memset(kslcT[:, :], 0.0), writes=b_kslc)
            ld("sp", kslcT[64:72, :], kaug_d, b_kslc)
            P.op("pool", lambda e: e.memset(kwinT[:, :], 0.0), writes=b_kwin)
            P.op("pool", lambda e: e.memset(qT[:].rearrange("p a b c -> p (a b c)"), 0.0), writes=[b_qT])
            for i in range(NQT):
                P.op("pool", lambda e, i=i: e.memset(vslc[:, i, 64:65], 1.0), writes=[b_vslc[i]])
            for i in range(8):
                P.op("pool", lambda e, i=i: e.memset(vwin[:, i, 64:65], 1.0), writes=[b_vwin[i]])
            P.op("dve", lambda e: e.tensor_scalar(out=nbalpha_s[:], in0=nbalpha_s[:], scalar1=-1.0, scalar2=None, op0=ALU.mult),
                 reads=[b_w], writes=[b_w])

            chk(0)
            wada_v = wada_d.rearrange("(kc p) n -> p kc n", p=128)
            for cc in range(48):
                xb_ = cc % 2
                ld("sp", xs[xb_][:].rearrange("p (kc n) -> p kc n", kc=16), wada_v[:, :, cc * 128:(cc + 1) * 128], [b_xs[xb_]])
                wv = xs[xb_][:].rearrange("p (kc n) -> p kc n", kc=16)
                for kc in range(KC):
                    P.op("pe", lambda e, kc=kc, wv=wv, cc=cc: e.matmul(banks[0][:, cc:cc + 1], lhsT=wv[:, kc, :], rhs=cT_s[:, kc:kc + 1],
                                                                      start=(kc == 0), stop=(kc == KC - 1)),
                         reads=[b_xs[xb_], b_AB], writes=[bb[0]], mode="f32")
            P.op("dve", lambda e: e.tensor_tensor(out=ada_s[:], in0=banks[0][:, 0:48], in1=bada_s[:], op=ALU.add),
                 reads=[bb[0], b_AB], writes=[b_AB])
            P.op("dve", lambda e: e.tensor_scalar(out=A_s[:], in0=ada_s[:, 16:32], scalar1=1.0, scalar2=None, op0=ALU.add),
                 reads=[b_AB], writes=[b_AB])
            P.op("dve", lambda e: e.tensor_tensor(out=A_s[:], in0=A_s[:], in1=gainT_s[:], op=ALU.mult), reads=[b_AB], writes=[b_AB])
            P.op("dve", lambda e: e.tensor_copy(out=B_s[:], in_=ada_s[:, 0:16]), reads=[b_AB], writes=[b_AB])
            P.dma("pool", lambda e: e.dma_start(out=adao_d, in_=ada_s[:]), reads=[b_AB])

            chk(1)
            for kv in range(2):
                lo = 64 * kv
                for hc in range(2):
                    col = kv * 2 + hc
                    for p_ in range(32):
                        P.op("pe", lambda e, lo=lo, hc=hc, p_=p_, col=col: e.matmul(
                            banks[1][:, col:col + 1], lhsT=w1kv_s[lo:lo + 64, p_, hc * 128:(hc + 1) * 128],
                            rhs=posT_s[lo:lo + 64, p_:p_ + 1], start=(p_ == 0), stop=(p_ == 31)),
                            reads=[b_w], writes=[bb[1]], mode="r64")
            P.op("dve", lambda e: e.tensor_copy(out=posb_s[:], in_=banks[1][:, 0:4]), reads=[bb[1]], writes=[b_w])

            chk(2)
            x_v = x.rearrange("(t p) d -> t p d", p=128)
            y_v = y_d.rearrange("(t p) n -> t p n", p=128)
            tokx = {}

            def load_x(t):
                if t < NQT:
                    tokx[t] = ld("sp", xs[t % 2][:], x_v[t], [b_xs[t % 2]])
            load_x(0); load_x(1)

            rr = {"mm": 0, "st": 0, "pt": 0}
            fillc = {}

            def getfill(e):
                if "r" not in fillc:
                    fillc["r"] = e.to_reg(NEGM)
                return fillc["r"]

            def mmbank():
                i = rr["mm"] % 2
                rr["mm"] += 1
                return i

            def stbank():
                i = 4 + rr["st"] % 2
                rr["st"] += 1
                return i

            def norm_tile(ST, tt):
                t = ST * 4 + tt
                xb_ = t % 2
                P.op("act", lambda e, xb_=xb_, tt=tt: e.activation(out=xnb[xb_][:], in_=xs[xb_][:], func=AF.Square,
                                                                   accum_out=small[:, tt:tt + 1]),
                     reads=[b_xs[xb_]], writes=[b_xnb[xb_], b_sm[tt]])
                P.op("act", lambda e, tt=tt: e.activation(out=small[:, 8 + tt:9 + tt], in_=small[:, tt:tt + 1], func=AF.Ln,
                                                          scale=1.0 / D, bias=EPS),
                     reads=[b_sm[tt]], writes=[b_sm[tt]])
                P.op("act", lambda e, tt=tt: e.activation(out=small[:, 16 + tt:17 + tt], in_=small[:, 8 + tt:9 + tt], func=AF.Exp,
                                                          scale=-0.5),
                     reads=[b_sm[tt]], writes=[b_sm[tt]])
                P.op("dve", lambda e, xb_=xb_, tt=tt: e.tensor_scalar(out=xnb[xb_][:], in0=xs[xb_][:], scalar1=small[:, 16 + tt:17 + tt],
                                                                      scalar2=None, op0=ALU.mult),
                     reads=[b_xs[xb_], b_sm[tt]], writes=[b_xnb[xb_]])
                load_x(t + 2)
                for half in range(2):
                    bk = 2 + half
                    pv = banks[bk][:].bitcast(BF16)
                    for c8 in range(8):
                        kc = half * 8 + c8
                        P.op("pe", lambda e, pv=pv, c8=c8, kc=kc, xb_=xb_: e.transpose(
                            pv[:, c8 * 128:(c8 + 1) * 128], xnb[xb_][:, kc * 128:(kc + 1) * 128], identb[:]),
                            reads=[b_xnb[xb_], b_tab], writes=[bb[bk]], mode="tr")
                    for c8 in range(8):
                        kc = half * 8 + c8
                        eng = "dve" if c8 % 2 == 0 else "pool"
                        eng = "dve"
                        P.op(eng, lambda e, pv=pv, c8=c8, kc=kc, tt=tt: e.tensor_scalar(
                            out=hT[:, kc, tt * 128:(tt + 1) * 128], in0=pv[:, c8 * 128:(c8 + 1) * 128],
                            scalar1=A_s[:, kc:kc + 1], scalar2=B_s[:, kc:kc + 1], op0=ALU.mult, op1=ALU.add),
                            reads=[bb[bk], b_AB], writes=[b_hT])

            for tt_ in range(4):
                norm_tile(0, tt_)
            for ST in range(NST):
                chk(3)
                tsl = slice(ST * 512, ST * 512 + 512)
                P.dma("sp", lambda e, ST=ST: e.dma_start(out=qT[64:72, :, :, :], in_=qaug_d[:, ST * 4:ST * 4 + 4, :, :]), writes=[b_qT])
                for ci in range(NFC):
                    M_ = FCH_M[ci]; off = FCH_OFF[ci]
                    bk = mmbank()
                    for kc in range(KC):
                        P.op("pe", lambda e, bk=bk, M_=M_, off=off, kc=kc: e.matmul(
                            banks[bk][0:M_, :], lhsT=wf_s[:, kc, off:off + M_], rhs=hT[:, kc, :], start=(kc == 0), stop=(kc == KC - 1)),
                            reads=[b_w, b_hT], writes=[bb[bk]], mode=("full" if M_ > 64 else "c%d" % (64 if M_ > 32 else 32)))
                    src = banks[bk]
                    if ci < 4:
                        P.op("act", lambda e, src=src, ci=ci: e.activation(out=qT[0:64, :, ci, :], in_=src[0:64, :].rearrange("p (t q) -> p t q", q=128), func=AF.Identity,
                                                                           bias=bf_s[0:64, ci:ci + 1], scale=1.0),
                             reads=[bb[bk], b_w], writes=[b_qT])
                        P.op("dve", lambda e, ci=ci: e.tensor_scalar(out=qT[0:64, :, ci, :], in0=qT[0:64, :, ci, :], scalar1=0.125, scalar2=None,
                                                                     op0=ALU.mult), reads=[b_qT], writes=[b_qT])
                    elif ci == 4:
                        P.op("act", lambda e, src=src: e.activation(out=cmpbuf[:, 16:528], in_=src[:, :], func=AF.Identity,
                                                                    bias=bf_s[:, 4:5], scale=1.0),
                             reads=[bb[bk], b_w], writes=[b_cmpbuf])
                    elif ci == 5:
                        P.op("act", lambda e, src=src, tsl=tsl: e.activation(out=kslcT[0:64, tsl], in_=src[0:64, :], func=AF.Identity,
                                                                             bias=bf_s[0:64, 5:6], scale=1.0),
                             reads=[bb[bk], b_w], writes=[b_kslc[ST * 4 + k] for k in range(4)])
                    elif ci == 6:
                        rs = (ST % 2) * 512
                        P.op("act", lambda e, src=src, rs=rs: e.activation(out=kwinT[0:64, rs:rs + 512], in_=src[0:64, :], func=AF.Identity,
                                                                           bias=bf_s[0:64, 6:7], scale=1.0),
                             reads=[bb[bk], b_w], writes=[b_kwin[(ST % 2) * 4 + k] for k in range(4)])
                        P.dma("sp", lambda e, rs=rs, tsl=tsl: e.dma_start(out=kwinT[64:72, rs:rs + 512], in_=kaug_d[:, tsl]),
                              writes=[b_kwin[(ST % 2) * 4 + k] for k in range(4)])
                    elif ci == 7:
                        P.op("act", lambda e, src=src: e.activation(out=gqT[:], in_=src[:, :], func=AF.Identity, bias=bf_s[:, 7:8], scale=1.0),
                             reads=[bb[bk], b_w], writes=[b_gT])
                    elif ci == 8:
                        P.op("act", lambda e, src=src: e.activation(out=gkT[:], in_=src[:, :], func=AF.Identity, bias=bf_s[:, 8:9], scale=1.0),
                             reads=[bb[bk], b_w], writes=[b_gT])
                    else:
                        P.op("act", lambda e, src=src: e.activation(out=gaT[:], in_=src[0:16, :], func=AF.Identity, bias=bf_s[0:16, 9:10], scale=1.0),
                             reads=[bb[bk], b_w], writes=[b_gT])
                bk = mmbank()
                P.op("pe", lambda e, bk=bk: e.matmul(banks[bk][:, :], lhsT=walpha_s[:, :], rhs=gaT[:, :], start=True, stop=True),
                     reads=[b_w, b_gT], writes=[bb[bk]], mode="r32")
                P.op("act", lambda e, bk=bk: e.activation(out=e1[:], in_=banks[bk][:, :], func=AF.Exp, scale=-1.0, bias=nbalpha_s[:, 0:1]),
                     reads=[bb[bk], b_w], writes=[b_gl])
                P.op("act", lambda e: e.activation(out=spl[:], in_=e1[:], func=AF.Ln, bias=1.0, scale=1.0), reads=[b_gl], writes=[b_gl])
                P.op("dve", lambda e: e.tensor_tensor_scan(out=cs[:], data0=reset_s[:], data1=spl[:], initial=0.0, op0=ALU.mult, op1=ALU.add),
                     reads=[b_gl, b_tab], writes=[b_gl])
                P.op("act", lambda e: e.activation(out=eb[:], in_=cs[:], func=AF.Exp, scale=-1.0 / 16), reads=[b_gl], writes=[b_gl])
                P.op("act", lambda e: e.activation(out=enb[:], in_=cs[:], func=AF.Exp, scale=1.0 / 16), reads=[b_gl], writes=[b_gl])
                P.op("dve", lambda e: e.scalar_tensor_tensor(out=qtl[:], in0=gqT[:], scalar=128.0 ** -0.5, in1=eb[:], op0=ALU.mult, op1=ALU.mult),
                     reads=[b_gl, b_gT], writes=[b_gl])
                P.op("dve", lambda e: e.tensor_tensor(out=ktl[:], in0=gkT[:], in1=enb[:], op=ALU.mult), reads=[b_gl, b_gT], writes=[b_gl])
                chk(4)
                for tt in range(4):
                    t = ST * 4 + tt
                    for grp in range(2):
                        if grp == 1:
                            chk(44)
                        if tt == 1:
                            chk(46)
                        bk = mmbank()
                        n0, n1 = (0, NTA) if grp == 0 else (NTA, NT_)
                        w_ = n1 - n0
                        for kc in range(KC):
                            P.op("pe", lambda e, bk=bk, kc=kc, tt=tt, n0=n0, n1=n1, w_=w_: e.matmul(
                                banks[bk][:, 0:w_], lhsT=hT[:, kc, tt * 128:(tt + 1) * 128], rhs=wt_s[:, kc, n0:n1], start=(kc == 0), stop=False),
                                reads=[b_w, b_hT], writes=[bb[bk]])
                        chk(41)
                        P.op("pe", lambda e, bk=bk, n0=n0, n1=n1, w_=w_: e.matmul(
                            banks[bk][:, 0:w_], lhsT=ones_s[0:1, :], rhs=btb_s[0:1, n0:n1], start=False, stop=True),
                            reads=[b_w], writes=[bb[bk]], mode="r32")
                        chk(42)
                        if grp == 0:
                            P.op("dve", lambda e, bk=bk, tt=tt: e.tensor_copy(out=tokA[tt][:], in_=banks[bk][:, 0:NTA]),
                                 reads=[bb[bk]], writes=[b_tokA[tt]])
                            chk(43)
                            P.op("dve", lambda e, tt=tt, t=t: e.tensor_copy(out=vslc[:, t, 0:64], in_=tokA[tt][:, 0:64]),
                                 reads=[b_tokA[tt]], writes=[b_vslc[t]])
                            P.op("dve", lambda e, tt=tt, t=t: e.tensor_copy(out=vwin[:, t % 8, 0:64], in_=tokA[tt][:, 64:128]),
                                 reads=[b_tokA[tt]], writes=[b_vwin[t % 8]])
                        else:
                            chk(45)
                            P.op("dve", lambda e, bk=bk, tt=tt: e.tensor_copy(out=tokB[tt][:], in_=banks[bk][:, 256:512]),
                                 reads=[bb[bk]], writes=[b_tokB[tt]])
                            P.op("dve", lambda e, bk=bk, tt=tt: e.tensor_copy(out=gvb[tt][:], in_=banks[bk][:, 0:256]),
                                 reads=[bb[bk]], writes=[b_gvb[tt]])
                if debug and ST == 0:
                    P.dma("pool", lambda e: e.dma_start(out=dbg["qT"], in_=qT[0:72]), reads=[b_qT])
                    P.dma("pool", lambda e: e.dma_start(out=dbg["tokA"], in_=tokA[0][:]), reads=[b_tokA[0]])

                if ST + 1 < NST:
                    norm_tile(ST + 1, 0)
                    norm_tile(ST + 1, 1)
                chk(5)
                bf6 = banks[6][:].bitcast(BF16)
                for tt in range(4):
                    cs_ = slice(tt * 128, tt * 128 + 128)
                    P.op("pe", lambda e, cs_=cs_: e.transpose(bf6[:, cs_], ktl[:, cs_], identb[:]), reads=[b_gl, b_tab], writes=[bb[6]], mode="tr")
                for tt in range(4):
                    cs_ = slice(tt * 128, tt * 128 + 128)
                    P.op("pe", lambda e, cs_=cs_: e.matmul(banks[7][:, cs_], lhsT=ktl[:, cs_], rhs=qtl[:, cs_], start=True, stop=True),
                         reads=[b_gl], writes=[bb[7]])
                P.op("dve", lambda e: e.tensor_copy(out=gkT[:], in_=bf6[:, 0:512]), reads=[bb[6], b_gl], writes=[b_gT])
                for tt in range(4):
                    cs_ = slice(tt * 128, tt * 128 + 128)
                    P.op("dve", lambda e, cs_=cs_: e.tensor_tensor(out=gqT[:, cs_], in0=banks[7][:, cs_], in1=gmask_s[:], op=ALU.mult),
                         reads=[bb[7], b_tab, b_gl], writes=[b_gT])
                for tt in range(4):
                    t = ST * 4 + tt
                    cs_ = slice(tt * 128, tt * 128 + 128)
                    bo = mmbank()
                    P.op("pe", lambda e, bo=bo, tt=tt, cs_=cs_: e.matmul(banks[bo][:, 0:256], lhsT=gqT[:, cs_], rhs=gvb[tt][:], start=True, stop=False),
                         reads=[b_gT, b_gvb[tt]], writes=[bb[bo]])
                    P.op("pe", lambda e, tt=tt, cs_=cs_: e.matmul(banks[6][:, 256:512], lhsT=gkT[:, cs_], rhs=gvb[tt][:], start=True, stop=True),
                         reads=[b_gT, b_gvb[tt]], writes=[bb[6]])
                    P.op("pe", lambda e, bo=bo, cs_=cs_: e.matmul(banks[bo][:, 0:256], lhsT=qtl[:, cs_], rhs=Sb_s[:], start=False, stop=True),
                         reads=[b_gl, b_S], writes=[bb[bo]])
                    P.op("dve", lambda e: e.tensor_tensor(out=stmp[:], in0=banks[6][:, 256:512], in1=S_s[:], op=ALU.add),
                         reads=[bb[6], b_S], writes=[b_glc])
                    last = tt * 128 + 127
                    P.op("dve", lambda e, last=last: e.tensor_scalar(out=S_s[:], in0=stmp[:], scalar1=eb[:, last:last + 1], scalar2=None, op0=ALU.mult),
                         reads=[b_glc, b_gl], writes=[b_S])
                    P.op("dve", lambda e: e.tensor_copy(out=Sb_s[:], in_=S_s[:]), reads=[b_S], writes=[b_S])
                    yb_ = tt
                    P.op("act", lambda e, bo=bo: e.activation(out=junk[:], in_=banks[bo][:, 0:256], func=AF.Square, accum_out=small[:, 24:25]),
                         reads=[bb[bo]], writes=[b_fin, b_small])
                    P.op("act", lambda e: e.activation(out=small[:, 25:26], in_=small[:, 24:25], func=AF.Ln, scale=1.0 / 256, bias=EPS),
                         reads=[b_small], writes=[b_small])
                    P.op("act", lambda e: e.activation(out=small[:, 26:27], in_=small[:, 25:26], func=AF.Exp, scale=-0.5),
                         reads=[b_small], writes=[b_small])
                    P.op("act", lambda e, tt=tt: e.activation(out=gl1[:], in_=tokB[tt][:], func=AF.Exp, scale=-1.0),
                         reads=[b_tokB[tt]], writes=[b_fin])
                    P.op("dve", lambda e: e.tensor_scalar(out=gl1[:], in0=gl1[:], scalar1=1.0, scalar2=None, op0=ALU.add), reads=[b_fin], writes=[b_fin])
                    P.op("dve", lambda e: e.reciprocal(out=gl1[:], in_=gl1[:]), reads=[b_fin], writes=[b_fin])
                    P.op("dve", lambda e, tt=tt: e.tensor_tensor(out=gl1[:], in0=gl1[:], in1=tokB[tt][:], op=ALU.mult),
                         reads=[b_fin, b_tokB[tt]], writes=[b_fin])
                    P.op("dve", lambda e: e.tensor_tensor(out=gl1[:], in0=gl1[:], in1=ggain_s[:], op=ALU.mult), reads=[b_fin, b_w], writes=[b_fin])
                    P.op("dve", lambda e, bo=bo, yb_=yb_: e.scalar_tensor_tensor(out=yt[yb_][:, 256:512], in0=banks[bo][:, 0:256], scalar=small[:, 26:27],
                                                                                 in1=gl1[:], op0=ALU.mult, op1=ALU.mult),
                         reads=[bb[bo], b_fin, b_small], writes=[b_yt[yb_]])

                chk(6)
                for kv in range(2):
                    lo = 64 * kv
                    bk = mmbank()
                    for hc in range(2):
                        for p_ in range(32):
                            P.op("pe", lambda e, bk=bk, lo=lo, hc=hc, p_=p_: e.matmul(
                                banks[bk][:, hc * 32:(hc + 1) * 32], lhsT=w1kv_s[lo:lo + 64, p_, hc * 128:(hc + 1) * 128],
                                rhs=cmpbuf[lo:lo + 64, p_:p_ + 497:16], start=(p_ == 0), stop=(p_ == 31)),
                                reads=[b_w, b_cmpbuf], writes=[bb[bk]], mode="r64")
                    for hc in range(2):
                        col = kv * 2 + hc
                        P.op("dve", lambda e, bk=bk, hc=hc, col=col: e.tensor_scalar(out=hsb[:, col, :], in0=banks[bk][:, hc * 32:(hc + 1) * 32],
                                                                                     scalar1=posb_s[:, col:col + 1], scalar2=None, op0=ALU.add),
                             reads=[bb[bk], b_w], writes=[b_hid])
                P.op("act", lambda e: e.activation(out=hex_[:], in_=hsb[:], func=AF.Exp, scale=-1.0), reads=[b_hid], writes=[b_hid])
                P.op("dve", lambda e: e.tensor_scalar(out=hex_[:], in0=hex_[:], scalar1=1.0, scalar2=None, op0=ALU.add), reads=[b_hid], writes=[b_hid])
                P.op("dve", lambda e: e.reciprocal(out=hex_[:], in_=hex_[:]), reads=[b_hid], writes=[b_hid])
                P.op("dve", lambda e: e.tensor_tensor(out=hidT[:], in0=hex_[:], in1=hsb[:], op=ALU.mult), reads=[b_hid], writes=[b_hid])
                bk = mmbank()
                for hc in range(2):
                    P.op("pe", lambda e, bk=bk, hc=hc: e.matmul(banks[bk][0:64, 0:32], lhsT=w2k_s[:, hc, :], rhs=hidT[:, hc, :],
                                                                start=(hc == 0), stop=(hc == 1)), reads=[b_w, b_hid], writes=[bb[bk]], mode="c64")
                P.op("dve", lambda e, bk=bk, ST=ST: e.tensor_copy(out=kcT[0:64, ST * 32:ST * 32 + 32], in_=banks[bk][0:64, 0:32]),
                     reads=[bb[bk]], writes=[b_kcT])
                for hc in range(2):
                    P.op("pe", lambda e, bk=bk, hc=hc: e.matmul(banks[bk][0:32, 64:128], lhsT=hidT[:, 2 + hc, :], rhs=w2v_s[:, hc, :],
                                                                start=(hc == 0), stop=(hc == 1)), reads=[b_w, b_hid], writes=[bb[bk]], mode="c32")
                P.op("dve", lambda e, bk=bk: e.tensor_copy(out=vcst[:], in_=banks[bk][0:32, 64:128]), reads=[bb[bk]], writes=[b_hid])
                pr = 32 * (ST % 4)
                P.dma("sp", lambda e, pr=pr, ST=ST: e.dma_start(out=vcM[pr:pr + 32, ST // 4, 0:64], in_=vcst[:]), reads=[b_hid], writes=[b_vcM])
                if ST == 0:
                    P.op("pool", lambda e: e.memset(vcM[0:1, 0, 0:193], 0.0), writes=[b_vcM])
                P.op("dve", lambda e: e.tensor_copy(out=cmpbuf[:, 0:16], in_=cmpbuf[:, 512:528]), reads=[b_cmpbuf], writes=[b_cmpbuf])

                if ST + 1 < NST:
                    norm_tile(ST + 1, 2)
                chk(7)
                def run_pipeline(jobs, depth=1):
                    pend = []
                    for jb in jobs:
                        jb[0]()
                        pend.append(jb)
                        if len(pend) > depth:
                            pend.pop(0)[1]()
                    for jb in pend:
                        jb[1]()

                def stage_A(tt):
                    i = ST * 4 + tt
                    par = i % 2
                    qv = qT[:, tt, :, :].rearrange("p h q -> p (h q)")
                    nnt = (8 * i + 7 + 127) // 128
                    jobs = []
                    for nt in range(nnt):
                        mrows = 128
                        st_ = {}

                        def qk(nt=nt, mrows=mrows, st_=st_):
                            sbk = stbank()
                            st_["sbk"] = sbk
                            mk = maskc[nt % 2]
                            P.op("pool", lambda e, mk=mk: e.affine_select(
                                out=mk[:], in_=zeros_b[:], pattern=[[0, 4], [1, 128]], compare_op=ALU.is_ge, fill=getfill(e),
                                base=128 * i - 2048 * nt - 15, channel_multiplier=-16), reads=[b_tab], writes=[b_maskc[nt % 2]])
                            P.op("pe", lambda e: e.matmul(banks[sbk][0:mrows, :], lhsT=kcT[:, nt * 128:nt * 128 + mrows], rhs=qv, start=True, stop=False),
                                 reads=[b_kcT, b_qT], writes=[bb[sbk]])
                            P.op("pe", lambda e: e.matmul(banks[sbk][0:mrows, :], lhsT=identb[:, 0:mrows], rhs=mk[:], start=False, stop=True),
                                 reads=[b_tab, b_maskc[nt % 2]], writes=[bb[sbk]])

                        def rest(nt=nt, mrows=mrows, st_=st_):
                            sbk = st_["sbk"]
                            pb = rr["pt"] % 3
                            rr["pt"] += 1
                            P.op("act", lambda e: e.activation(out=pT[pb][0:mrows, :], in_=banks[sbk][0:mrows, :], func=AF.Exp, scale=1.0),
                                 reads=[bb[sbk]], writes=[b_pT[pb]])
                            for h in range(4):
                                obk = 2 + h // 2
                                oc = (h % 2) * 256
                                P.op("pe", lambda e, obk=obk, oc=oc, h=h: e.matmul(
                                    banks[obk][:, oc:oc + 193], lhsT=pT[pb][0:mrows, h * 128:(h + 1) * 128], rhs=vcM[0:mrows, nt, 0:193],
                                    start=(nt == 0 and h % 2 == 0), stop=(nt == nnt - 1), skip_group_check=True),
                                    reads=[b_pT[pb], b_vcM], writes=[bb[obk]])
                        jobs.append((qk, rest))
                    run_pipeline(jobs)

                def stage_Adve(tt):
                    i = ST * 4 + tt
                    par = i % 2
                    bF = b_finA[par]
                    P.op("act", lambda e: e.activation(out=gsig[tt][:], in_=tokA[tt][:, 128:140], func=AF.Exp, scale=-1.0), reads=[b_tokA[tt]], writes=[b_gs[tt]])
                    P.op("dve", lambda e: e.tensor_scalar(out=gsig[tt][:], in0=gsig[tt][:], scalar1=1.0, scalar2=None, op0=ALU.add), reads=[b_gs[tt]], writes=[b_gs[tt]])
                    P.op("dve", lambda e: e.reciprocal(out=gsig[tt][:], in_=gsig[tt][:]), reads=[b_gs[tt]], writes=[b_gs[tt]])
                    for h in range(4):
                        obk = 2 + h // 2
                        oc = (h % 2) * 256
                        P.op("dve", lambda e, obk=obk, oc=oc, h=h: e.tensor_scalar(out=den[par][:, h:h + 1], in0=banks[obk][:, oc + 64:oc + 65],
                                                                                  scalar1=1e-30, scalar2=None, op0=ALU.max),
                             reads=[bb[obk]], writes=[bF])
                    P.op("dve", lambda e: e.reciprocal(out=den[par][:, 0:4], in_=den[par][:, 0:4]), reads=[bF], writes=[bF])
                    for h in range(4):
                        obk = 2 + h // 2
                        oc = (h % 2) * 256
                        P.op("dve", lambda e, h=h: e.tensor_tensor(out=coef[par][:, h:h + 1], in0=gsig[tt][:, 3 * h:3 * h + 1], in1=den[par][:, h:h + 1], op=ALU.mult),
                             reads=[bF, b_gs[tt]], writes=[bF])
                        P.op("dve", lambda e, obk=obk, oc=oc, h=h: e.tensor_scalar(out=acc[par][:, h * 64:(h + 1) * 64], in0=banks[obk][:, oc:oc + 64],
                                                                                  scalar1=coef[par][:, h:h + 1], scalar2=None, op0=ALU.mult),
                             reads=[bb[obk], bF], writes=[bF])

                    for h in range(4):
                        obk = 2 + h // 2
                        oc = (h % 2) * 256
                        if h == 0:
                            P.op("dve", lambda e, obk=obk, oc=oc: e.tensor_scalar(out=score[:], in0=banks[obk][:, oc + 65:oc + 193], scalar1=den[par][:, 0:1],
                                                                                 scalar2=None, op0=ALU.mult), reads=[bb[obk], bF], writes=[b_sel])
                        else:
                            P.op("dve", lambda e, obk=obk, oc=oc, h=h: e.scalar_tensor_tensor(out=score[:], in0=banks[obk][:, oc + 65:oc + 193],
                                                                                              scalar=den[par][:, h:h + 1], in1=score[:], op0=ALU.mult, op1=ALU.add),
                                 reads=[bb[obk], bF, b_sel], writes=[b_sel])
                    P.op("dve", lambda e: e.memset(score[:, 0:1], BIG), writes=[b_sel])
                    P.op("dve", lambda e: e.memset(score[:, 2 * i:2 * i + 1], BIG), writes=[b_sel])
                    if i > 0:
                        P.op("dve", lambda e: e.memset(score[0:64, 2 * i - 1:2 * i], BIG), writes=[b_sel])
                    P.op("dve", lambda e: e.memset(score[64:128, 2 * i + 1:2 * i + 2], BIG), writes=[b_sel])
                    P.op("dve", lambda e: e.max(out=m8[:, 0:8], in_=score[:]), reads=[b_sel], writes=[b_sel])
                    P.op("dve", lambda e: e.match_replace(out=swork[:], in_to_replace=m8[:, 0:8], in_values=score[:], imm_value=-BIG),
                         reads=[b_sel], writes=[b_sel])
                    P.op("dve", lambda e: e.max(out=m8[:, 8:16], in_=swork[:]), reads=[b_sel], writes=[b_sel])
                    P.op("dve", lambda e: e.tensor_scalar(out=selb[par][:], in0=score[:], scalar1=m8[:, 15:16], scalar2=NEGM, op0=ALU.is_lt, op1=ALU.mult),
                         reads=[b_sel], writes=[b_selb[par]])

                def stage_T(tt):
                    i = ST * 4 + tt
                    par = i % 2
                    sbt = mmbank()
                    P.op("pe", lambda e: e.transpose(banks[sbt][:].bitcast(BF16)[:, 0:128], selb[par][:], identb[:]), reads=[b_selb[par], b_tab], writes=[bb[sbt]], mode="tr")
                    for h in range(4):
                        P.op("dve", lambda e, h=h: e.tensor_copy(out=selT4[par][0][0:64, h, :], in_=banks[sbt][:].bitcast(BF16)[0:64, 0:128]),
                             reads=[bb[sbt]], writes=[b_selT[par]])
                        P.op("dve", lambda e, h=h: e.tensor_copy(out=selT4[par][1][64:128, h, :], in_=banks[sbt][:].bitcast(BF16)[64:128, 0:128]),
                             reads=[bb[sbt]], writes=[b_selT[par]])

                def stage_Bm(tt):
                    i = ST * 4 + tt
                    par = i % 2
                    bF = b_finA[par]
                    qv = qT[:, tt, :, :].rearrange("p h q -> p (h q)")

                    def mkjob(br, kt, idx, nk):
                        obk = 6 + br
                        if br == 0:
                            kop = kslcT[:, kt * 128:(kt + 1) * 128]; kb = b_kslc[kt]
                            vop = vslc[:, kt, 0:65]; vb = b_vslc[kt]
                        else:
                            rsl = (kt % 8) * 128
                            kop = kwinT[:, rsl:rsl + 128]; kb = b_kwin[kt % 8]
                            vop = vwin[:, kt % 8, 0:65]; vb = b_vwin[kt % 8]
                        extra = []
                        if br == 0:
                            gq_ = (2 * kt) // 64
                            m_ = kt % 32
                            extra.append((E4_s[:, m_, :], selT4[par][gq_][:, :, :].rearrange("p h q -> p (h q)"), [b_tab, b_selT[par]]))
                        if kt == i:
                            extra.append((identb[:], bdiag_s[:], [b_tab]))
                        if br == 1 and kt == i - 4:
                            extra.append((identb[:], bfar_s[:], [b_tab]))
                        st_ = {}

                        def qk():
                            sbk = stbank()
                            st_["sbk"] = sbk
                            P.op("pe", lambda e: e.matmul(banks[sbk][:, :], lhsT=kop, rhs=qv, start=True, stop=(len(extra) == 0)),
                                 reads=[kb, b_qT], writes=[bb[sbk]])
                            for xi, (l_, r_, rb_) in enumerate(extra):
                                P.op("pe", lambda e, l_=l_, r_=r_, lastx=(xi == len(extra) - 1): e.matmul(
                                    banks[sbk][:, :], lhsT=l_, rhs=r_, start=False, stop=lastx), reads=rb_, writes=[bb[sbk]])

                        def rest():
                            sbk = st_["sbk"]
                            pb = rr["pt"] % 3
                            rr["pt"] += 1
                            P.op("act", lambda e: e.activation(out=pT[pb][:], in_=banks[sbk][:, :], func=AF.Exp, scale=1.0),
                                 reads=[bb[sbk]], writes=[b_pT[pb]])
                            P.op("pe", lambda e: e.matmul(banks[obk][0:65, :], lhsT=vop, rhs=pT[pb][:], start=(idx == 0), stop=(idx == nk - 1)),
                                 reads=[b_pT[pb], vb], writes=[bb[obk]])
                        return (qk, rest)
                    kw = list(range(max(0, i - 4), i + 1))
                    ks = list(range(0, i + 1))
                    jobs_w = [mkjob(1, kt, idx, len(kw)) for idx, kt in enumerate(kw)]
                    jobs_s = [mkjob(0, kt, idx, len(ks)) for idx, kt in enumerate(ks)]
                    run_pipeline(jobs_w + jobs_s)

                def stage_Bc(tt):
                    for br in range(2):
                        obk = 6 + br
                        P.op("dve", lambda e, obk=obk, br=br: e.tensor_copy(out=(e1 if br == 0 else cs)[0:65, :], in_=banks[obk][0:65, :]),
                             reads=[bb[obk]], writes=[b_gl])

                def stage_Bf(tt):
                    i = ST * 4 + tt
                    par = i % 2
                    bF = b_finA[par]
                    for br in range(2):
                        tbk = mmbank()
                        for h in range(4):
                            P.op("pe", lambda e, tbk=tbk, br=br, h=h: e.transpose(
                                banks[tbk][:, h * 128:h * 128 + 65], (e1 if br == 0 else cs)[0:65, h * 128:(h + 1) * 128], identf[0:65, 0:65]),
                                reads=[b_gl, b_tab], writes=[bb[tbk]], mode="f32t")
                        for h in range(4):
                            dcol = 4 + br * 4 + h
                            P.op("dve", lambda e, tbk=tbk, h=h, dcol=dcol: e.tensor_scalar(out=den[par][:, dcol:dcol + 1], in0=banks[tbk][:, h * 128 + 64:h * 128 + 65],
                                                                                          scalar1=1e-30, scalar2=None, op0=ALU.max),
                                 reads=[bb[tbk]], writes=[bF])
                            P.op("dve", lambda e, dcol=dcol: e.reciprocal(out=den[par][:, dcol:dcol + 1], in_=den[par][:, dcol:dcol + 1]), reads=[bF], writes=[bF])
                            P.op("dve", lambda e, h=h, br=br, dcol=dcol: e.tensor_tensor(out=coef[par][:, dcol:dcol + 1], in0=gsig[tt][:, 3 * h + 1 + br:3 * h + 2 + br],
                                                                                        in1=den[par][:, dcol:dcol + 1], op=ALU.mult), reads=[bF, b_gs[tt]], writes=[bF])
                            P.op("dve", lambda e, tbk=tbk, h=h, dcol=dcol: e.scalar_tensor_tensor(
                                out=acc[par][:, h * 64:(h + 1) * 64], in0=banks[tbk][:, h * 128:h * 128 + 64], scalar=coef[par][:, dcol:dcol + 1],
                                in1=acc[par][:, h * 64:(h + 1) * 64], op0=ALU.mult, op1=ALU.add), reads=[bb[tbk], bF], writes=[bF])
                    if debug and i == NQT - 1:
                        P.dma("pool", lambda e: e.dma_start(out=dbg["oT0"], in_=e1[0:65, :]), reads=[b_gl])
                        P.dma("pool", lambda e: e.dma_start(out=dbg["oT1"], in_=cs[0:65, :]), reads=[b_gl])
                        P.dma("pool", lambda e: e.dma_start(out=dbg["acc"], in_=acc[par][:]), reads=[bF])
                        P.dma("pool", lambda e: e.dma_start(out=dbg["kcT"], in_=kcT[0:72]), reads=[b_kcT])
                        P.dma("pool", lambda e: e.dma_start(out=dbg["ocmp"], in_=vcM[:, :, 0:193]), reads=[b_vcM])
                        P.dma("pool", lambda e: e.dma_start(out=dbg["den"], in_=den[par][:]), reads=[bF])
                        P.dma("pool", lambda e: e.dma_start(out=dbg["gsig"], in_=gsig[par][:]), reads=[bF])
                        P.dma("pool", lambda e: e.dma_start(out=dbg["coef"], in_=coef[par][:]), reads=[bF])
                        P.dma("pool", lambda e: e.dma_start(out=dbg["pT"], in_=pT[(rr["pt"] - 1) % 3][:]), reads=[b_pT[(rr["pt"] - 1) % 3]])
                    yb_ = tt
                    P.op("act", lambda e: e.activation(out=szn[:], in_=tokA[tt][:, 140:396], func=AF.Exp, scale=-1.0), reads=[b_tokA[tt]], writes=[b_fin])
                    P.op("dve", lambda e: e.tensor_scalar(out=szn[:], in0=szn[:], scalar1=1.0, scalar2=None, op0=ALU.add), reads=[b_fin], writes=[b_fin])
                    P.op("dve", lambda e: e.reciprocal(out=szn[:], in_=szn[:]), reads=[b_fin], writes=[b_fin])
                    P.op("dve", lambda e: e.tensor_tensor(out=szn[:], in0=szn[:], in1=tokA[tt][:, 140:396], op=ALU.mult), reads=[b_fin, b_tokA[tt]], writes=[b_fin])
                    P.op("dve", lambda e: e.tensor_tensor(out=yt[yb_][:, 0:256], in0=szn[:], in1=acc[par][:], op=ALU.mult), reads=[b_fin, bF], writes=[b_yt[yb_]])
                    P.dma("pool", lambda e: e.dma_start(out=y_v[i], in_=yt[yb_][:]), reads=[b_yt[yb_]])

                stage_A(0)
                stage_Adve(0)
                stage_A(1)
                stage_T(0)
                stage_Adve(1)
                stage_Bm(0)
                stage_Bc(0)
                if ST + 1 < NST:
                    norm_tile(ST + 1, 3)
                stage_A(2)
                stage_T(1)
                stage_Bf(0)
                stage_Adve(2)
                stage_Bm(1)
                stage_Bc(1)
                stage_A(3)
                stage_T(2)
                stage_Bf(1)
                stage_Adve(3)
                stage_Bm(2)
                stage_Bc(2)
                stage_T(3)
                stage_Bf(2)
                stage_Bm(3)
                stage_Bc(3)
                stage_Bf(3)
        except _Stop:
            pass
        toks = [t_ for t_ in P.dma_last if t_ is not None]
        toks += [(e_, P.cnt[e_]) for e_ in ("pe", "act", "dve", "pool") if P.cnt[e_] > 0]
        for e_ in ("pool", "sp", "act", "dve", "pe"):
            P.wait_all(e_, toks)
        P.emit()
    return nc


import contextlib
import numpy as np
import ml_dtypes
import concourse.bass as bass
import concourse.mybir as mybir

F32 = mybir.dt.float32
BF16 = mybir.dt.bfloat16
AF = mybir.ActivationFunctionType
ALU = mybir.AluOpType
BF = ml_dtypes.bfloat16
D = 2048
KC = 16
EPS = 1e-6
CB = 512
NCB = D // CB


def host_inputs_l2(inp, ys, adas, b, qtr, TQ):
    t0 = qtr * TQ
    ycat = np.concatenate([ys[b * 4 + j][t0:t0 + TQ, 0:256] for j in range(4)] +
                          [ys[b * 4 + j][t0:t0 + TQ, 256:512] for j in range(4)], axis=1)
    w_in = inp["w_in"][0]
    b_in = inp["b_in"][0]
    return dict(
        x2=np.ascontiguousarray(inp["x"][b, t0:t0 + TQ]),
        yin=np.ascontiguousarray(ycat),
        ada_in=np.ascontiguousarray(adas[b * 4]),
        gainT2=np.ascontiguousarray(inp["norm_gain"][0].reshape(16, 128).T),
        wmg=np.ascontiguousarray(w_in[:, 6720:10816]),
        bmg=np.ascontiguousarray(b_in[6720:10816][None, :]),
        wbrn=np.ascontiguousarray(inp["w_br_nsa"][0]),
        wbrg=np.ascontiguousarray(inp["w_br_gla"][0]),
        wout=np.ascontiguousarray(inp["w_out"][0]),
        fgain=np.ascontiguousarray(np.broadcast_to(inp["final_norm_gain"][None, :], (128, D))),
        identb2=np.eye(128).astype(BF), identf2=np.eye(128, dtype=np.float32),
    )


def build_l2(TQ):
    NTT = 4
    NST = TQ // (128 * NTT)
    nc = bass.Bass("TRN2", target_bir_lowering=False)

    def din(name, shape, dt=F32):
        return nc.dram_tensor(name, list(shape), dt, kind="ExternalInput").ap()
    x = din("x2", [TQ, D]); yin = din("yin", [TQ, D], BF16)
    adain_d = din("ada_in", [128, 48])
    gainT_d = din("gainT2", [128, 16])
    wmg_d = din("wmg", [D, 4096]); bmg_d = din("bmg", [1, 4096])
    wbrn_d = din("wbrn", [1024, D]); wbrg_d = din("wbrg", [1024, D]); wout_d = din("wout", [D, D])
    fgain_d = din("fgain", [128, D]); identb_d = din("identb2", [128, 128], BF16); identf_d = din("identf2", [128, 128])
    out_d = nc.dram_tensor("out", [TQ, D], F32, kind="ExternalOutput").ap()
    wmg_b = nc.dram_tensor("wmg_b16", [D, 4096], BF16).ap()
    wbrn_b = nc.dram_tensor("wbrn_b16", [1024, D], BF16).ap()
    wbrg_b = nc.dram_tensor("wbrg_b16", [1024, D], BF16).ap()
    wout_b = nc.dram_tensor("wout_b16", [D, D], BF16).ap()

    P = Prog(nc)
    with contextlib.ExitStack() as st:
        def sb(name, shape, dt=F32):
            return st.enter_context(nc.sbuf_tensor("s2_" + name, list(shape), dt))
        xs = [sb("xs%d" % i, [128, D]) for i in range(2)]
        hT = sb("hT", [128, KC, 128 * NTT], BF16)
        yT = sb("yT", [128, KC, 128 * NTT], BF16)
        mtok = [sb("mtok%d" % i, [128, D], BF16) for i in range(NTT)]
        xnb = mtok[NTT - 1]; ytile = [mtok[NTT - 2], mtok[NTT - 3]]
        mT = sb("mT", [128, KC, 128], BF16)
        wmgb = [sb("wmgb%d" % i, [128, 2, KC, CB], BF16) for i in range(2)]
        wbrb = [sb("wbrb%d" % i, [128, 2, 8, CB], BF16) for i in range(2)]
        woutb = [sb("woutb%d" % i, [128, KC, 256], BF16) for i in range(2)]
        bmg_s = [sb("bmg_s%d" % i, [1, 2 * CB], BF16) for i in range(2)]; ones_s = sb("ones_s", [1, 128], BF16)
        gate_bc = sb("gate_bc", [128, D]); fgain_s = sb("fgain_s", [128, D])
        A_s = sb("A_s", [128, 16]); B_s = sb("B_s", [128, 16]); ada_s = sb("ada_s", [128, 48])
        cT_s = sb("cT_s", [128, 16]); gainT_s = sb("gainT_s", [128, 16]); bada_s = sb("bada_s", [128, 48])
        small = sb("small", [128, 32])
        identb = sb("identb", [128, 128], BF16); identf = sb("identf", [128, 128])
        g1 = sb("g1", [128, CB]); g2 = sb("g2", [128, CB]); t1 = sb("t1", [128, CB])
        xo = t1; diag = [g1[:, 0:128], g1[:, 128:256]]; onesf = g1[:, 256:384]

        banks = [st.enter_context(nc.psum_tensor("b2ank%d" % i, [128, 512], F32)) for i in range(8)]
        bb = [Buf("bank%d" % i) for i in range(8)]
        B = Buf
        b_w = B("w"); b_AB = B("AB"); b_xs = [B("xs0"), B("xs1")]; b_small = B("small")
        b_hT = B("hT"); b_yT = B("yT"); b_mtok = [B("mtok%d" % i) for i in range(NTT)]
        b_xnb = b_mtok[NTT - 1]; b_yt = [b_mtok[NTT - 2], b_mtok[NTT - 3]]
        b_mT = B("mT"); b_wmg = [B("wmg0"), B("wmg1")]; b_wbr = [B("wbr0"), B("wbr1")]; b_wout = [B("wout0"), B("wout1")]
        b_gate = B("gate"); b_g = B("g"); b_diag = [b_g, b_g]; b_xo = b_g; b_tab = B("tab")

        def ld(q, out, in_, writes):
            return P.dma(q, lambda e: e.dma_start(out=out, in_=in_), writes=writes)
        ld("sp", gainT_s[:], gainT_d, [b_AB])
        ld("sp", fgain_s[:], fgain_d, [b_w]); ld("sp", identb[:], identb_d, [b_tab]); ld("sp", identf[:], identf_d, [b_tab])
        P.op("pool", lambda e: e.memset(ones_s[:], 1.0), writes=[b_w])
        b_wc = B("wcast")
        for h2 in range(2):
            for r0 in range(0, D, 1024):
                P.dma("pool", lambda e, h2=h2, r0=r0: e.dma_start(out=wmg_b[r0:r0 + 1024, h2 * 2048:(h2 + 1) * 2048], in_=wmg_d[r0:r0 + 1024, h2 * 2048:(h2 + 1) * 2048]), writes=[b_wc])
        P.dma("pool", lambda e: e.dma_start(out=wbrn_b, in_=wbrn_d), writes=[b_wc])
        P.dma("pool", lambda e: e.dma_start(out=wbrg_b, in_=wbrg_d), writes=[b_wc])
        for r0 in range(0, D, 1024):
            P.dma("pool", lambda e, r0=r0: e.dma_start(out=wout_b[r0:r0 + 1024, :], in_=wout_d[r0:r0 + 1024, :]), writes=[b_wc])
        P.op("pool", lambda e: e.memset(onesf, 1.0), writes=[b_g])
        ld("sp", ada_s[:], adain_d, [b_AB])
        P.op("dve", lambda e: e.tensor_scalar(out=A_s[:], in0=ada_s[:, 16:32], scalar1=1.0, scalar2=None, op0=ALU.add), reads=[b_AB], writes=[b_AB])
        P.op("dve", lambda e: e.tensor_tensor(out=A_s[:], in0=A_s[:], in1=gainT_s[:], op=ALU.mult), reads=[b_AB], writes=[b_AB])
        P.op("dve", lambda e: e.tensor_copy(out=B_s[:], in_=ada_s[:, 0:16]), reads=[b_AB], writes=[b_AB])
        for kc in range(KC):
            d_ = kc % 2
            P.op("dve", lambda e, kc=kc, d_=d_: e.tensor_scalar(out=diag[d_], in0=identf[:], scalar1=ada_s[:, 32 + kc:33 + kc], scalar2=None, op0=ALU.mult),
                 reads=[b_AB, b_tab], writes=[b_diag[d_]])
            bk = 1 + kc // 4
            P.op("pe", lambda e, kc=kc, d_=d_, bk=bk: e.matmul(banks[bk][:, (kc % 4) * 128:(kc % 4) * 128 + 128], lhsT=onesf, rhs=diag[d_],
                                                               start=(kc % 4 == 0), stop=(kc % 4 == 3), skip_group_check=True),
                 reads=[b_g], writes=[bb[bk]])
        for q4 in range(4):
            P.op("dve", lambda e, q4=q4: e.tensor_copy(out=gate_bc[:, q4 * 512:(q4 + 1) * 512], in_=banks[1 + q4][:, :]), reads=[bb[1 + q4]], writes=[b_gate])

        x_v = x.rearrange("(t p) d -> t p d", p=128)
        y_v = yin.rearrange("(t p) d -> t p d", p=128)
        o_v = out_d.rearrange("(t p) d -> t p d", p=128)
        wmg_v = wmg_b.rearrange("(kc p) n -> p kc n", p=128)
        wbrn_v = wbrn_b.rearrange("(kc p) n -> p kc n", p=128)
        wbrg_v = wbrg_b.rearrange("(kc p) n -> p kc n", p=128)
        wout_v = wout_b.rearrange("(kc p) n -> p kc n", p=128)
        rr = {"mm": 0, "w": 0, "wo": 0, "x": 0, "ob": 0}

        def mmbank():
            i = rr["ob"] % 4
            rr["ob"] += 1
            return i

        for ST in range(NST):
            for tt in range(NTT):
                t = ST * NTT + tt
                xb_ = rr["x"] % 2
                rr["x"] += 1
                ld("sp", xs[xb_][:], x_v[t], [b_xs[xb_]])
                ld("sp", ytile[tt % 2][:], y_v[t], [b_yt[tt % 2]])
                P.op("act", lambda e, xb_=xb_, tt=tt: e.activation(out=xnb[:], in_=xs[xb_][:], func=AF.Square, accum_out=small[:, tt % 4:tt % 4 + 1]),
                     reads=[b_xs[xb_]], writes=[b_xnb, b_small])
                P.op("act", lambda e, tt=tt: e.activation(out=small[:, 8 + tt % 4:9 + tt % 4], in_=small[:, tt % 4:tt % 4 + 1], func=AF.Ln, scale=1.0 / D, bias=EPS),
                     reads=[b_small], writes=[b_small])
                P.op("act", lambda e, tt=tt: e.activation(out=small[:, 16 + tt % 4:17 + tt % 4], in_=small[:, 8 + tt % 4:9 + tt % 4], func=AF.Exp, scale=-0.5),
                     reads=[b_small], writes=[b_small])
                P.op("dve", lambda e, xb_=xb_, tt=tt: e.tensor_scalar(out=xnb[:], in0=xs[xb_][:], scalar1=small[:, 16 + tt % 4:17 + tt % 4], scalar2=None, op0=ALU.mult),
                     reads=[b_xs[xb_], b_small], writes=[b_xnb])
                for src_i, (src, sbuf_, dst, dbuf) in enumerate(((xnb, b_xnb, hT, b_hT), (ytile[tt % 2], b_yt[tt % 2], yT, b_yT))):
                    for half in range(2):
                        bk = 4 + 2 * src_i + half
                        pv = banks[bk][:].bitcast(BF16)
                        for c8 in range(8):
                            kc = half * 8 + c8
                            P.op("pe", lambda e, pv=pv, c8=c8, kc=kc, src=src: e.transpose(pv[:, c8 * 128:(c8 + 1) * 128], src[:, kc * 128:(kc + 1) * 128], identb[:]),
                                 reads=[sbuf_, b_tab], writes=[bb[bk]])
                        if src_i == 0:
                            for c8 in range(8):
                                kc = half * 8 + c8
                                P.op("dve", lambda e, pv=pv, c8=c8, kc=kc, tt=tt: e.tensor_scalar(
                                    out=hT[:, kc, tt * 128:(tt + 1) * 128], in0=pv[:, c8 * 128:(c8 + 1) * 128],
                                    scalar1=A_s[:, kc:kc + 1], scalar2=B_s[:, kc:kc + 1], op0=ALU.mult, op1=ALU.add),
                                    reads=[bb[bk], b_AB], writes=[b_hT])
                        else:
                            P.op("dve", lambda e, pv=pv, half=half, tt=tt: e.tensor_copy(
                                out=yT[:, half * 8:half * 8 + 8, tt * 128:(tt + 1) * 128], in_=pv.rearrange("p (c q) -> p c q", q=128)),
                                reads=[bb[bk]], writes=[b_yT])
            for cb in range(NCB):
                wb = rr["w"] % 2
                rr["w"] += 1
                c0 = cb * CB
                P.dma("sp", lambda e, wb=wb, c0=c0: e.dma_start(out=wmgb[wb][:, 0, :, :], in_=wmg_v[:, :, c0:c0 + CB]), reads=[b_wc], writes=[b_wmg[wb]])
                P.dma("sp", lambda e, wb=wb, c0=c0: e.dma_start(out=wmgb[wb][:, 1, :, :], in_=wmg_v[:, :, 2048 + c0:2048 + c0 + CB]), reads=[b_wc], writes=[b_wmg[wb]])
                P.dma("sp", lambda e, wb=wb, c0=c0: e.dma_start(out=wbrb[wb][:, 0, :, :], in_=wbrn_v[:, :, c0:c0 + CB]), reads=[b_wc], writes=[b_wbr[wb]])
                P.dma("sp", lambda e, wb=wb, c0=c0: e.dma_start(out=wbrb[wb][:, 1, :, :], in_=wbrg_v[:, :, c0:c0 + CB]), reads=[b_wc], writes=[b_wbr[wb]])
                ld("pool", bmg_s[wb][:, 0:CB], bmg_d[:, c0:c0 + CB], [b_wmg[wb]])
                ld("pool", bmg_s[wb][:, CB:2 * CB], bmg_d[:, 2048 + c0:2048 + c0 + CB], [b_wmg[wb]])
                for tt in range(NTT):
                    tsl = slice(tt * 128, tt * 128 + 128)
                    base = 4 * (rr["mm"] % 2)
                    rr["mm"] += 1
                    for gi in range(2):
                        bk = base + gi
                        for kc in range(KC):
                            P.op("pe", lambda e, bk=bk, gi=gi, kc=kc, tsl=tsl, wb=wb: e.matmul(
                                banks[bk][:, :], lhsT=hT[:, kc, tsl], rhs=wmgb[wb][:, gi, kc, :], start=(kc == 0), stop=False),
                                reads=[b_hT, b_wmg[wb]], writes=[bb[bk]])
                        P.op("pe", lambda e, bk=bk, gi=gi, wb=wb: e.matmul(
                            banks[bk][:, :], lhsT=ones_s[0:1, :], rhs=bmg_s[wb][0:1, gi * CB:(gi + 1) * CB], start=False, stop=True),
                            reads=[b_w, b_wmg[wb]], writes=[bb[bk]], mode="r32")
                    for gi in range(2):
                        bk = base + 2 + gi
                        for kc in range(8):
                            P.op("pe", lambda e, bk=bk, gi=gi, kc=kc, tsl=tsl, wb=wb: e.matmul(
                                banks[bk][:, :], lhsT=yT[:, gi * 8 + kc, tsl], rhs=wbrb[wb][:, gi, kc, :], start=(kc == 0), stop=(kc == 7)),
                                reads=[b_yT, b_wbr[wb]], writes=[bb[bk]])
                    P.op("act", lambda e, base=base: e.activation(out=g1[:], in_=banks[base][:, :], func=AF.Exp, scale=-1.0), reads=[bb[base]], writes=[b_g])
                    P.op("act", lambda e, base=base: e.activation(out=g2[:], in_=banks[base + 1][:, :], func=AF.Exp, scale=-1.0), reads=[bb[base + 1]], writes=[b_g])
                    P.op("dve", lambda e: e.tensor_scalar(out=g1[:], in0=g1[:], scalar1=1.0, scalar2=None, op0=ALU.add), reads=[b_g], writes=[b_g])
                    P.op("dve", lambda e: e.tensor_scalar(out=g2[:], in0=g2[:], scalar1=1.0, scalar2=None, op0=ALU.add), reads=[b_g], writes=[b_g])
                    P.op("dve", lambda e: e.reciprocal(out=g1[:], in_=g1[:]), reads=[b_g], writes=[b_g])
                    P.op("dve", lambda e: e.reciprocal(out=g2[:], in_=g2[:]), reads=[b_g], writes=[b_g])
                    P.op("dve", lambda e, base=base: e.tensor_tensor(out=t1[:], in0=banks[base + 2][:, :], in1=g1[:], op=ALU.mult), reads=[bb[base + 2], b_g], writes=[b_g])
                    P.op("dve", lambda e, base=base: e.tensor_tensor(out=g2[:], in0=banks[base + 3][:, :], in1=g2[:], op=ALU.mult), reads=[bb[base + 3], b_g], writes=[b_g])
                    P.op("dve", lambda e, tt=tt, c0=c0: e.tensor_tensor(out=mtok[tt][:, c0:c0 + CB], in0=t1[:], in1=g2[:], op=ALU.add), reads=[b_g], writes=[b_mtok[tt]])
            for tt in range(NTT):
                t = ST * NTT + tt
                for half in range(2):
                    bk = 4 + half
                    pv = banks[bk][:].bitcast(BF16)
                    for c8 in range(8):
                        kc = half * 8 + c8
                        P.op("pe", lambda e, pv=pv, c8=c8, kc=kc, tt=tt: e.transpose(pv[:, c8 * 128:(c8 + 1) * 128], mtok[tt][:, kc * 128:(kc + 1) * 128], identb[:]),
                             reads=[b_mtok[tt], b_tab], writes=[bb[bk]])
                    P.op("dve", lambda e, pv=pv, half=half: e.tensor_copy(out=mT[:, half * 8:half * 8 + 8, :], in_=pv.rearrange("p (c q) -> p c q", q=128)),
                         reads=[bb[bk]], writes=[b_mT])
                xb_ = rr["x"] % 2
                rr["x"] += 1
                ld("sp", xs[xb_][:], x_v[t], [b_xs[xb_]])
                for ob in range(8):
                    wo = rr["wo"] % 2
                    rr["wo"] += 1
                    P.dma("sp", lambda e, wo=wo, ob=ob: e.dma_start(out=woutb[wo][:], in_=wout_v[:, :, ob * 256:(ob + 1) * 256]), reads=[b_wc], writes=[b_wout[wo]])
                    if ob % 2 == 0:
                        bk = mmbank()
                    oc = (ob % 2) * 256
                    for kc in range(KC):
                        P.op("pe", lambda e, bk=bk, kc=kc, wo=wo, oc=oc, ob=ob: e.matmul(banks[bk][:, oc:oc + 256], lhsT=mT[:, kc, :], rhs=woutb[wo][:, kc, :],
                                                                                 start=(kc == 0 and ob % 2 == 0), stop=(kc == KC - 1), skip_group_check=True),
                             reads=[b_mT, b_wout[wo]], writes=[bb[bk]])
                    if ob % 2 == 1:
                        osl = slice((ob // 2) * 512, (ob // 2) * 512 + 512)
                        P.op("dve", lambda e, bk=bk, osl=osl: e.tensor_tensor(out=xo[:, :], in0=banks[bk][:, :], in1=gate_bc[:, osl], op=ALU.mult),
                             reads=[bb[bk], b_gate], writes=[b_xo])
                        P.op("dve", lambda e, osl=osl, xb_=xb_: e.tensor_tensor(out=xs[xb_][:, osl], in0=xo[:, :], in1=xs[xb_][:, osl], op=ALU.add),
                             reads=[b_xo, b_xs[xb_]], writes=[b_xs[xb_]])
                P.op("act", lambda e, xb_=xb_, tt=tt: e.activation(out=hT[:, 0:4, :], in_=xs[xb_][:].rearrange("p (a b) -> p a b", a=4), func=AF.Square, accum_out=small[:, 24:25]),
                     reads=[b_xs[xb_]], writes=[b_hT, b_small])
                P.op("act", lambda e: e.activation(out=small[:, 25:26], in_=small[:, 24:25], func=AF.Ln, scale=1.0 / D, bias=EPS), reads=[b_small], writes=[b_small])
                P.op("act", lambda e: e.activation(out=small[:, 26:27], in_=small[:, 25:26], func=AF.Exp, scale=-0.5), reads=[b_small], writes=[b_small])
                P.op("dve", lambda e, xb_=xb_: e.scalar_tensor_tensor(out=xs[xb_][:], in0=xs[xb_][:], scalar=small[:, 26:27], in1=fgain_s[:], op0=ALU.mult, op1=ALU.mult),
                     reads=[b_small, b_w], writes=[b_xs[xb_]])
                P.dma("act", lambda e, xb_=xb_, t=t: e.dma_start(out=o_v[t], in_=xs[xb_][:]), reads=[b_xs[xb_]])
        toks = [t_ for t_ in P.dma_last if t_ is not None]
        toks += [(e_, P.cnt[e_]) for e_ in ("pe", "act", "dve", "pool") if P.cnt[e_] > 0]
        for e_ in ("pool", "sp", "act", "dve", "pe"):
            P.wait_all(e_, toks)
        P.emit()
    return nc


def kernel(**inputs):
    from concourse.bass_utils import run_bass_kernel_spmd
    inp = {k_: np.asarray(v) for k_, v in inputs.items()}
    Bn, T, Dm = inp["x"].shape
    nc1 = build_l1(T)
    in_maps = [host_inputs_l1(inp, c // 4, c % 4, T) for c in range(8)]
    res = run_bass_kernel_spmd(nc1, in_maps, core_ids=list(range(8)))
    ys = [np.asarray(r["y"]) for r in res.results]
    adas = [np.asarray(r["ada_out"]) for r in res.results]
    del in_maps
    TQ = T // 4
    nc2 = build_l2(TQ)
    in_maps2 = [host_inputs_l2(inp, ys, adas, c // 4, c % 4, TQ) for c in range(8)]
    res2 = run_bass_kernel_spmd(nc2, in_maps2, core_ids=list(range(8)))
    out = np.empty((Bn, T, Dm), np.float32)
    for c in range(8):
        out[c // 4, (c % 4) * TQ:(c % 4 + 1) * TQ] = np.asarray(res2.results[c]["out"])
    return out
```
